# Optimizing a Trainium2 kernel written in Bass

```python
import jax, jax.numpy as jnp
from jax import lax
import numpy as np

D_MODEL = 1024
BATCH = 16
SEQ = 2048
DEPTH = 2

N_Q_HEADS = 8
N_KV_HEADS = 2
HEAD_DIM = 64
Q_GROUP = N_Q_HEADS // N_KV_HEADS
ATT_WIDTH = N_Q_HEADS * HEAD_DIM
KV_WIDTH = N_KV_HEADS * HEAD_DIM
WINDOW = 128
ATT_BLOCK = 128
SGU_WIDTH = D_MODEL // 2
SGU_GROUPS = 8
SGU_GROUP_DIM = SGU_WIDTH // SGU_GROUPS
SGU_CHUNK = 128
N_BRANCHES = 2
IN_WIDTH = ATT_WIDTH + 2 * KV_WIDTH + 2 * SGU_WIDTH + N_BRANCHES * D_MODEL
D_FF = 256 * ((8 * D_MODEL // 3 + 255) // 256)
CONV_WIDTH = 3
NORM_EPS = 1e-6
NEG_INF = -1e30

kernel_name = "hybrid_gated_swa_sgu_convffn"


def rmsnorm(x, gain):
    xf = x.astype(jnp.float32)
    y = xf * lax.rsqrt(jnp.mean(xf * xf, axis=-1, keepdims=True) + NORM_EPS)
    return (y * gain.astype(jnp.float32)).astype(x.dtype)


def alibi_slopes():
    return jnp.exp2(-8.0 * (jnp.arange(N_Q_HEADS, dtype=jnp.float32) + 1.0) / N_Q_HEADS)


def sliding_window_attention(q, k, v, q_gain, k_gain, sinks):
    B, S, _ = q.shape
    nb = S // ATT_BLOCK
    q = rmsnorm(q.reshape(B, S, N_Q_HEADS, HEAD_DIM), q_gain)
    k = rmsnorm(k.reshape(B, S, N_KV_HEADS, HEAD_DIM), k_gain)
    v = v.reshape(B, S, N_KV_HEADS, HEAD_DIM)
    qb = q.reshape(B, nb, ATT_BLOCK, N_KV_HEADS, Q_GROUP, HEAD_DIM)

    def band(t):
        tp = jnp.pad(t, ((0, 0), (ATT_BLOCK, 0), (0, 0), (0, 0)))
        tp = tp.reshape(B, nb + 1, ATT_BLOCK, N_KV_HEADS, HEAD_DIM)
        return jnp.concatenate([tp[:, :-1], tp[:, 1:]], axis=2)

    kb, vb = band(k), band(v)
    scores = jnp.einsum('bnqhgd,bnkhd->bnhgqk', qb, kb,
                        preferred_element_type=jnp.float32) * (HEAD_DIM ** -0.5)
    qi = jnp.arange(ATT_BLOCK)[:, None]
    kj = jnp.arange(2 * ATT_BLOCK)[None, :]
    dist = qi + ATT_BLOCK - kj
    key_pos = jnp.arange(nb)[:, None] * ATT_BLOCK - ATT_BLOCK + kj
    valid = ((dist >= 0) & (dist < WINDOW))[None] & (key_pos >= 0)[:, None, :]
    slopes = alibi_slopes().reshape(N_KV_HEADS, Q_GROUP)
    alibi = -slopes[:, :, None, None] * dist.astype(jnp.float32)[None, None]
    scores = jnp.where(valid[None, :, None, None], scores + alibi[None, None], NEG_INF)
    sink = jnp.broadcast_to(
        sinks.astype(jnp.float32).reshape(1, 1, N_KV_HEADS, Q_GROUP, 1, 1),
        scores.shape[:-1] + (1,))
    probs = jax.nn.softmax(jnp.concatenate([scores, sink], axis=-1), axis=-1)[..., :-1]
    out = jnp.einsum('bnhgqk,bnkhd->bnqhgd', probs.astype(v.dtype), vb)
    return out.reshape(B, S, ATT_WIDTH)


def chunked_spatial_gating(u, v, sgu_gain, w_s, b_s):
    B, S, _ = u.shape
    nc = S // SGU_CHUNK
    u = jax.nn.gelu(u)
    v = rmsnorm(jax.nn.gelu(v), sgu_gain)
    vc = v.reshape(B, nc, SGU_CHUNK, SGU_GROUPS, SGU_GROUP_DIM)
    causal = jnp.tril(jnp.ones((SGU_CHUNK, SGU_CHUNK), dtype=bool))
    w = jnp.where(causal[None], w_s, 0)
    mixed = jnp.einsum('gij,bcjgd->bcigd', w, vc) + b_s.T[:, :, None]
    return u * mixed.reshape(B, S, SGU_WIDTH)


def conv_gated_mlp(h, w_up, conv_w, conv_b, w_down):
    S = h.shape[1]
    z = h @ w_up
    zp = jnp.pad(z, ((0, 0), (CONV_WIDTH - 1, 0), (0, 0)))
    zc = conv_b
    for tap in range(CONV_WIDTH):
        zc = zc + conv_w[tap] * zp[:, tap:tap + S]
    gate, val = jnp.split(zc, 2, axis=-1)
    return (jax.nn.silu(gate) * val) @ w_down


def setup_inputs(seed: int = 0) -> dict:
    key = jax.random.key(seed)
    ks = jax.random.split(key, 20)
    f32 = jnp.float32

    def nrm(k, shape, scale):
        return jax.random.normal(k, shape, f32) * scale

    return {
        "x": nrm(ks[0], (BATCH, SEQ, D_MODEL), 1.0),
        "mix_norm": 1.0 + nrm(ks[1], (DEPTH, D_MODEL), 0.05),
        "w_in": nrm(ks[2], (DEPTH, D_MODEL, IN_WIDTH), D_MODEL ** -0.5),
        "q_norm": 1.0 + nrm(ks[3], (DEPTH, HEAD_DIM), 0.05),
        "k_norm": 1.0 + nrm(ks[4], (DEPTH, HEAD_DIM), 0.05),
        "sinks": nrm(ks[5], (DEPTH, N_Q_HEADS), 0.5),
        "sgu_norm": 1.0 + nrm(ks[6], (DEPTH, SGU_WIDTH), 0.05),
        "w_s": nrm(ks[7], (DEPTH, SGU_GROUPS, SGU_CHUNK, SGU_CHUNK), SGU_CHUNK ** -0.5),
        "b_s": 1.0 + nrm(ks[8], (DEPTH, SGU_GROUPS, SGU_CHUNK), 0.1),
        "w_oa": nrm(ks[9], (DEPTH, ATT_WIDTH, D_MODEL), ATT_WIDTH ** -0.5),
        "w_ob": nrm(ks[10], (DEPTH, SGU_WIDTH, D_MODEL), SGU_WIDTH ** -0.5),
        "w_out": nrm(ks[11], (DEPTH, D_MODEL, D_MODEL), D_MODEL ** -0.5),
        "ffn_norm": 1.0 + nrm(ks[12], (DEPTH, D_MODEL), 0.05),
        "w_up": nrm(ks[13], (DEPTH, D_MODEL, 2 * D_FF), D_MODEL ** -0.5),
        "conv_w": nrm(ks[14], (DEPTH, CONV_WIDTH, 2 * D_FF), CONV_WIDTH ** -0.5),
        "conv_b": nrm(ks[15], (DEPTH, 2 * D_FF), 0.02),
        "w_down": nrm(ks[16], (DEPTH, D_FF, D_MODEL), D_FF ** -0.5),
    }


def reference(x, mix_norm, w_in, q_norm, k_norm, sinks, sgu_norm, w_s, b_s,
              w_oa, w_ob, w_out, ffn_norm, w_up, conv_w, conv_b, w_down):
    splits = (ATT_WIDTH,
              ATT_WIDTH + KV_WIDTH,
              ATT_WIDTH + 2 * KV_WIDTH,
              ATT_WIDTH + 2 * KV_WIDTH + SGU_WIDTH,
              ATT_WIDTH + 2 * KV_WIDTH + 2 * SGU_WIDTH,
              ATT_WIDTH + 2 * KV_WIDTH + 2 * SGU_WIDTH + D_MODEL)
    for l in range(DEPTH):
        h = rmsnorm(x, mix_norm[l])
        proj = h @ w_in[l]
        q, k, v, su, sv, g_a, g_b = jnp.split(proj, splits, axis=-1)
        y_att = sliding_window_attention(q, k, v, q_norm[l], k_norm[l], sinks[l])
        y_sgu = chunked_spatial_gating(su, sv, sgu_norm[l], w_s[l], b_s[l])
        merged = (jax.nn.sigmoid(g_a) * (y_att @ w_oa[l])
                  + jax.nn.sigmoid(g_b) * (y_sgu @ w_ob[l]))
        x = x + merged @ w_out[l]
        x = x + conv_gated_mlp(rmsnorm(x, ffn_norm[l]), w_up[l], conv_w[l], conv_b[l], w_down[l])
    return x
```

```python
from contextlib import ExitStack

import numpy as np
import concourse.bass as bass
import concourse.mybir as mybir
from concourse.bass_utils import run_bass_kernel_spmd

F32 = mybir.dt.float32
BF16 = mybir.dt.bfloat16
AF = mybir.ActivationFunctionType
ALU = mybir.AluOpType

D = 1024
NL = 2
NH = 8
NKV = 2
HD = 64
DFF = 2816
NJ = DFF // 128
T = 512
NB = T // 128
EPS = 1e-6
NBUF = 5
SLAB = 4096

SLABS = ([("A", 4096), ("B", 2048), ("C", 4096), ("D", 4096), ("E", 4096), ("G", 4096),
          ("OAB0", 4096), ("F", 4096), ("H", 4096), ("OAB1", 4096), ("OUT0", 4096), ("OUT1", 4096)]
         + [("UP%d" % i, 4096) for i in range(11)] + [("DN%d" % c, 2816) for c in range(8)])
SLAB_OFF = {}
_o = 0
for _n, _s in SLABS:
    SLAB_OFF[_n] = (_o, _s)
    _o += _s
WPL = _o

C_VEC = 0
C_QK = 384
C_SGUG = 392
C_BSB = C_SGUG + 1024
C_ABIAS = C_BSB + 1024
C_TRIL = C_ABIAS + 2048
NCST = C_TRIL + 128


class Buf:
    __slots__ = ("name", "lw", "rd", "excl")

    def __init__(self, name, excl=False):
        self.name = name
        self.lw = None
        self.rd = []
        self.excl = excl


class Sched:
    ENG = ("pe", "act", "dve", "pool", "sp")

    def __init__(self):
        self.streams = {e: [] for e in self.ENG}
        self.cnt = {}
        self.waited = {e: {} for e in self.ENG}
        self.semnames = []

    def newsem(self, key):
        self.cnt[key] = 0
        self.semnames.append(key)

    def _waits(self, eng, reads, writes):
        deps = {}
        def add(ev, raw):
            if ev is None:
                return
            k, v = ev
            if k == eng and (eng == "pe" or not raw):
                return
            if deps.get(k, 0) < v:
                deps[k] = v
        for b in reads:
            add(b.lw, True)
            if b.excl:
                for ev in b.rd:
                    add(ev, False)
        for b in writes:
            add(b.lw, False)
            for ev in b.rd:
                add(ev, False)
        for k, v in deps.items():
            if self.waited[eng].get(k, 0) >= v:
                continue
            self.waited[eng][k] = v
            self.streams[eng].append(("wait", k, v))

    def _commit(self, ev, reads, writes):
        for b in reads:
            b.rd.append(ev)
        for b in writes:
            b.lw = ev
            b.rd = []

    def op(self, eng, fn, reads=(), writes=()):
        self._waits(eng, reads, writes)
        self.cnt[eng] += 1
        self.streams[eng].append(("op", fn, eng, 1))
        self._commit((eng, self.cnt[eng]), reads, writes)

    def dma(self, eng, sem, fn, reads=(), writes=()):
        self._waits(eng, reads, writes)
        self.cnt[sem] += 16
        self.streams[eng].append(("op", fn, sem, 16))
        self._commit((sem, self.cnt[sem]), reads, writes)

    def wait_all(self, eng, bufs):
        self._waits(eng, bufs, bufs)

    def replay(self, eng, handle, sems):
        for it in self.streams[eng]:
            if it[0] == "wait":
                handle.wait_ge(sems[it[1]], it[2])
            else:
                ins = it[1](handle)
                ins.then_inc(sems[it[2]], it[3])


class Rot:
    def __init__(self, tiles, name):
        self.tiles = tiles
        self.bufs = [Buf("%s%d" % (name, i)) for i in range(len(tiles))]
        self.i = 0

    def next(self):
        i = self.i
        self.i = (i + 1) % len(self.tiles)
        return self.tiles[i], self.bufs[i]


def build(n_seq, seq_len, layers=(0, 1)):
    nc = bass.Bass("TRN2", target_bir_lowering=False)
    S = Sched()
    tiles_per_seq = seq_len // T
    n_tiles = n_seq * tiles_per_seq
    xT_d = nc.dram_tensor("xT", [n_seq, D, seq_len], F32, kind="ExternalInput").ap()
    wts_d = nc.dram_tensor("wts", [NL, 128, WPL], F32, kind="ExternalInput").ap()
    cst_d = nc.dram_tensor("cst", [128, NCST], F32, kind="ExternalInput").ap()
    srow_d = nc.dram_tensor("srow", [1, 2048], F32, kind="ExternalInput").ap()
    wst_d = nc.dram_tensor("wst", [128, 2048], F32, kind="ExternalInput").ap()
    yT_d = nc.dram_tensor("yT", [n_seq, D, seq_len], F32, kind="ExternalOutput").ap()

    es = ExitStack()
    with es:
        def sb(name, shape, dt):
            return es.enter_context(nc.sbuf_tensor("s_" + name, shape, dt))

        cst = sb("cst", [128, NCST], F32)
        srow = sb("srow", [128, 2048], BF16)
        wsT = sb("wsT", [128, 16, 128], BF16)
        ones = sb("ones", [128, 128], BF16)
        xres = [sb("xres%d" % i, [128, 8, T], F32) for i in range(2)]
        hT = sb("hT", [128, 8, T], BF16)
        sqt = sb("sqt", [128, 4, T], BF16)
        NFR = 10
        frt = sb("frt", [128, NFR, 512], F32)
        qT = sb("qT", [64, 8, T], BF16)
        kT = [sb("kT%d" % l, [64, 2, T + 128], BF16) for l in range(NL)]
        vtok = [sb("vtok%d" % l, [128, NB + 1, 128], BF16) for l in range(NL)]
        qsq = sb("qsq", [64, 2, T], BF16)
        uT = sb("uT", [128, 4, T], BF16)
        junk = sb("junk", [128, 512], BF16)
        ssv = sb("ssv", [128, 8], F32)
        lnv = sb("lnv", [128, 8], F32)
        rv = sb("rv", [128, 8], F32)
        vn = sb("vn", [128, NB, 512], BF16)
        pT = sb("pT", [128, 6, 512], BF16)
        rden = sb("rden", [128, 2, 256], F32)
        yattT = sb("yattT", [128, 4, T], BF16)
        ysguT = sb("ysguT", [128, 4, T], BF16)
        mergedT = sb("mergedT", [128, 8, T], BF16)
        actT = sb("actT", [128, NJ, T], BF16)
        halo = [sb("halo%d" % l, [128, 2 * NJ, 2], F32) for l in range(NL)]
        wslab = sb("wslab", [128, NBUF, SLAB], BF16)
        psb = [es.enter_context(nc.psum_tensor("ps%d" % i, [128, 512], F32)) for i in range(8)]

        b_cst = Buf("cst")
        b_srow = Buf("srow")
        b_wsT = Buf("wsT")
        b_ones = Buf("ones")
        b_xres = [[Buf("xres%d_%d" % (i, c)) for c in range(8)] for i in range(2)]
        b_hT = [Buf("hT%d" % c) for c in range(8)]
        b_qT = [Buf("qT%d" % h) for h in range(8)]
        b_kprev = [Buf("kprev%d" % l) for l in range(NL)]
        b_kcur = [[Buf("kcur%d_%d" % (l, g)) for g in range(2)] for l in range(NL)]
        b_vprev = [Buf("vprev%d" % l) for l in range(NL)]
        b_vcur = [Buf("vcur%d" % l) for l in range(NL)]
        b_uT = [Buf("uT%d" % c) for c in range(4)]
        b_junk = Buf("junk")
        b_ssv = [Buf("ssv%d" % i) for i in range(8)]
        b_lnv = [Buf("lnv%d" % i) for i in range(8)]
        b_rv = [Buf("rv%d" % i) for i in range(8)]
        b_vn = [Buf("vn%d" % b) for b in range(NB)]
        b_yatt = [[Buf("yatt%d_%d" % (c, b)) for b in range(NB)] for c in range(4)]
        b_ysgu = [Buf("ysgu%d" % b) for b in range(NB)]
        b_merged = [Buf("merged%d" % c) for c in range(8)]
        b_actT = [Buf("actT%d" % j) for j in range(NJ)]
        b_halo = [[Buf("halo%d_%d" % (l, ch)) for ch in range(2 * NJ)] for l in range(NL)]
        b_wslab = [Buf("wslab%d" % i) for i in range(NBUF)]
        b_ps = [Buf("ps%d" % i, excl=True) for i in range(8)]

        sqr = Rot([sqt[:, i, :] for i in range(4)], "sq")
        qsqr = Rot([qsq[:, i, :] for i in range(2)], "qsq")
        fr = Rot([frt[:, i, :] for i in range(NFR)], "fr")
        rqr = rq2r = gvr = efr = tmr = sar = sbr = t1r = t2r = agr = avr = sgr = fr
        srowf = frt[0:1, 0:4, :]
        wstage = frt[:, 4:8, :]
        pTr = Rot([pT[:, i, :] for i in range(6)], "pT")
        rdr = Rot([rden[:, i, :] for i in range(2)], "rden")
        psr = Rot([p[:, :] for p in psb[0:7]], "psr")
        psr.bufs = b_ps[0:7]
        ss_ps, b_ss = psb[7][:, :], b_ps[7]
        small_i = [0]

        for e in ("pe", "act", "dve", "pool"):
            S.newsem(e)
        for i in range(NBUF):
            S.newsem("w%d" % i)
        for k in ("cst0", "cst1", "cst2", "xl0", "xl1", "xs0", "xs1"):
            S.newsem(k)

        def act(out, in_, func, reads, writes, bias=None, scale=None, accum_out=None):
            kw = {}
            if bias is not None:
                kw["bias"] = bias
            if scale is not None:
                kw["scale"] = scale
            if accum_out is not None:
                kw["accum_out"] = accum_out
            S.op("act", lambda e: e.activation(out=out, in_=in_, func=func, **kw), reads, writes)

        def tt(out, in0, in1, op, reads, writes, eng="dve"):
            S.op(eng, lambda e: e.tensor_tensor(out=out, in0=in0, in1=in1, op=op), reads, writes)

        def stt(out, in0, scalar, in1, op0, op1, reads, writes, eng="dve"):
            S.op(eng, lambda e: e.scalar_tensor_tensor(out=out, in0=in0, scalar=scalar, in1=in1,
                                                        op0=op0, op1=op1), reads, writes)

        def cp(out, in_, reads, writes, eng="dve"):
            S.op(eng, lambda e: e.tensor_copy(out=out, in_=in_), reads, writes)

        def mm_group(out, pairs, reads, writes):
            def fn(e):
                ins = None
                n = len(pairs)
                for i, (l, r) in enumerate(pairs):
                    ins = e.matmul(out, l, r, start=(i == 0), stop=(i == n - 1))
                return ins
            S.op("pe", fn, reads, writes)

        def mm_multi(groups, reads, writes):
            def fn(e):
                ins = None
                for out, pairs in groups:
                    n = len(pairs)
                    for i, (l, r) in enumerate(pairs):
                        ins = e.matmul(out, l, r, start=(i == 0), stop=(i == n - 1))
                return ins
            S.op("pe", fn, reads, writes)

        passes = [(ti, l) for ti in range(n_tiles) for l in layers]
        wseq = [(l, nm) for (_, l) in passes for (nm, _) in SLABS]
        wstate = {"issue": 0, "acq": 0}

        def w_issue():
            i = wstate["issue"]
            if i >= len(wseq):
                return
            wstate["issue"] = i + 1
            l, nm = wseq[i]
            off, n = SLAB_OFF[nm]
            slot = i % NBUF
            o = wslab[:, slot, 0:n]
            src = wts_d[l, :, off:off + n]
            S.dma("pool", "w%d" % slot, lambda e: e.dma_start(out=o, in_=src), (), (b_wslab[slot],))

        def w_acquire(expect):
            i = wstate["acq"]
            wstate["acq"] = i + 1
            assert wseq[i][1] == expect, (wseq[i], expect)
            slot = i % NBUF
            return wslab[:, slot, :], b_wslab[slot]

        def w_release(n=1):
            for _ in range(n):
                w_issue()

        S.dma("sp", "cst0", lambda e: e.dma_start(out=cst[:, :], in_=cst_d[:, :]), (), (b_cst,))
        S.dma("sp", "cst1", lambda e: e.dma_start(out=srowf, in_=srow_d.rearrange("o (a n) -> o a n", a=4)),
              (), tuple(fr.bufs[0:4]))
        S.dma("sp", "cst2", lambda e: e.dma_start(out=wstage, in_=wst_d.rearrange("p (a n) -> p a n", a=4)),
              (), tuple(fr.bufs[4:8]))
        for _ in range(NBUF):
            w_issue()
        S.op("dve", lambda e: e.memset(ones[:, :], 1.0), (), (b_ones,))
        S.op("dve", lambda e: e.memset(srow[:, :], 0.0), (), (b_srow,))
        act(srow[0:1, :].rearrange("o (a n) -> o a n", a=4), srowf, AF.Exp, tuple(fr.bufs[0:4]), (b_srow,))
        act(cst[:, C_ABIAS:C_ABIAS + 2048], cst[:, C_ABIAS:C_ABIAS + 2048], AF.Exp, (b_cst,), (b_cst,))
        for i in range(16):
            tt(wsT[:, i, :], wstage[:, i // 4, (i % 4) * 128:(i % 4 + 1) * 128], cst[:, C_TRIL:C_TRIL + 128], ALU.mult,
               (fr.bufs[4 + i // 4], b_cst), (b_wsT,))

        def rms_sq_act(xr, bxr, c):
            sq, bsq = sqr.next()
            act(sq, xr[:, c, :], AF.Square, (bxr[c],), (bsq,))
            return sq, bsq

        def rms_sq_mm(sqb, c):
            sq, bsq = sqb
            S.op("pe", (lambda e: e.matmul(ss_ps, ones[:, :], sq, start=(c == 0), stop=(c == 7))),
                 (bsq, b_ones), (b_ss,))

        def rms_finish(xr, bxr, gcol):
            ps, bps = ss_ps, b_ss
            rtmp, b_rtmp = fr.next()
            act(rtmp, ps, AF.Ln, (bps,), (b_rtmp,), bias=EPS, scale=1.0 / D)
            rstd, b_rstd = fr.next()
            act(rstd, rtmp, AF.Exp, (b_rtmp,), (b_rstd,), scale=-0.5)
            for c in range(8):
                stt(hT[:, c, :], xr[:, c, :], cst[:, gcol + c:gcol + c + 1], rstd, ALU.mult, ALU.mult,
                    (bxr[c], b_rstd, b_cst), (b_hT[c],))

        def headnorm(ps, bps, gcolumn, out, bout):
            sq, bsq = qsqr.next()
            act(sq[0:64, :], ps[0:64, :], AF.Square, (bps,), (bsq,))
            ps2, bps2 = psr.next()
            S.op("pe", lambda e: e.matmul(ps2[0:64, :], ones[0:64, 0:64], sq[0:64, :], start=True, stop=True),
                 (bsq, b_ones), (bps2,))
            r1, br1 = rqr.next()
            act(r1[0:64, :], ps2[0:64, :], AF.Ln, (bps2,), (br1,), bias=EPS, scale=1.0 / HD)
            r2, br2 = rq2r.next()
            act(r2[0:64, :], r1[0:64, :], AF.Exp, (br1,), (br2,), scale=-0.5)
            stt(out, ps[0:64, :], cst[0:64, gcolumn:gcolumn + 1], r2[0:64, :], ALU.mult, ALU.mult,
                (bps, br2, b_cst), (bout,))

        def emit_xload(ti):
            s_idx = ti // tiles_per_seq
            t0 = (ti % tiles_per_seq) * T
            xi = ti % 2
            src = xT_d[s_idx].rearrange("(c p) t -> p c t", p=128)[:, :, t0:t0 + T]
            dstt = xres[xi][:, :, :]
            S.dma("sp", "xl%d" % xi, lambda e: e.dma_start(out=dstt, in_=src), (), tuple(b_xres[xi]))

        def kouter(outs, lhs_fn, reads_w, wr_bufs):
            for k in range(8):
                groups = [(o, lhs_fn(i, k), hT[:, k, :]) for i, o in enumerate(outs)]
                def fn(e, groups=groups, k=k):
                    ins = None
                    for (o, l_, r_) in groups:
                        ins = e.matmul(o, l_, r_, start=(k == 0), stop=(k == 7))
                    return ins
                S.op("pe", fn, (*reads_w, b_hT[k]), tuple(wr_bufs))

        def run_pass(ti, l, first_layer, last_layer, nxt):
            _CUR[0], _CUR[1] = ti, l
            s_idx = ti // tiles_per_seq
            tt_i = ti % tiles_per_seq
            t0 = tt_i * T
            first_in_seq = tt_i == 0
            last_in_seq = tt_i == tiles_per_seq - 1
            xi = ti % 2
            xr = xres[xi]
            bxr = b_xres[xi]
            vbase = C_VEC + 192 * l
            G1, G2 = vbase, vbase + 8
            CW0, CW1, CW2, CB = vbase + 16, vbase + 60, vbase + 104, vbase + 148
            QG, KG = C_QK + 2 * l, C_QK + 2 * l + 1

            _ck(0)
            rms_finish(xr, bxr, G1)
            _ck(1)

            wA, bA = w_acquire("A")
            vA = wA.rearrange("p (k n) -> p k n", k=8)
            qps = [psr.next() for _ in range(4)]
            kouter([p[0][0:64, :] for p in qps], lambda i, k: vA[:, k, i * 64:(i + 1) * 64], (bA,),
                   [p[1] for p in qps])
            for h in range(4):
                headnorm(qps[h][0], qps[h][1], QG, qT[:, h, :], b_qT[h])
            for h in range(4, 8):
                ps, bps = psr.next()
                mm_group(ps[0:64, :], [(vA[:, k, h * 64:(h + 1) * 64], hT[:, k, :]) for k in range(8)],
                         (bA, *b_hT), (bps,))
                headnorm(ps, bps, QG, qT[:, h, :], b_qT[h])
            w_release()
            _ck(2)

            wB, bB = w_acquire("B")
            vB = wB[:, 0:2048].rearrange("p (k n) -> p k n", k=8)
            for g in range(2):
                ps, bps = psr.next()
                mm_group(ps[0:64, :], [(vB[:, k, g * 64:(g + 1) * 64], hT[:, k, :]) for k in range(8)],
                         (bB, *b_hT), (bps,))
                headnorm(ps, bps, KG, kT[l][:, g, 128:128 + T], b_kcur[l][g])
            ps, bps = psr.next()
            mm_multi([(ps[:, b * 128:(b + 1) * 128],
                       [(hT[:, k, b * 128:(b + 1) * 128], vB[:, k, 128:256]) for k in range(8)])
                      for b in range(NB)], (bB, *b_hT), (bps,))
            cp(vtok[l][:, 1:NB + 1, :], ps.rearrange("p (b n) -> p b n", b=NB), (bps,), (b_vcur[l],))
            w_release()

            def sgu_block(b):
                ps, bps = psr.next()
                groups = []
                for gp in range(4):
                    for sl in range(2):
                        g = 2 * gp + sl
                        groups.append((ps[64 * sl:64 * sl + 64, gp * 128:(gp + 1) * 128],
                                       [(vn[:, b, g * 64:(g + 1) * 64], wsT[:, l * 8 + g, :])]))
                mm_multi(groups, (b_vn[b], b_wsT), (bps,))
                tm, btm = tmr.next()
                tt(tm, ps, cst[:, C_BSB + 512 * l:C_BSB + 512 * l + 512], ALU.add, (bps, b_cst), (btm,))
                tt(ysguT[:, :, b * 128:(b + 1) * 128], tm.rearrange("p (a n) -> p a n", a=4),
                   uT[:, :, b * 128:(b + 1) * 128], ALU.mult, (btm, *b_uT), (b_ysgu[b],))

            def att_stage1(b, g):
                halves = []
                if not (first_in_seq and b == 0):
                    halves.append(0)
                halves.append(1)
                pts = {}
                for hf in halves:
                    ps, bps = psr.next()
                    kcols = slice(128 * (b + hf), 128 * (b + hf) + 128)
                    kb = [b_kcur[l][g]] + ([b_kprev[l]] if (b == 0 and hf == 0) else [])
                    lhs_ = kT[l][:, g, kcols]
                    rhs_ = qT[:, 4 * g:4 * g + 4, b * 128:(b + 1) * 128]
                    out_ = ps.rearrange("p (a n) -> p a n", a=4)
                    S.op("pe", (lambda e, out_=out_, lhs_=lhs_, rhs_=rhs_: e.matmul(
                        out_, lhs_, rhs_, start=True, stop=True)),
                        (*kb, *b_qT[4 * g:4 * g + 4]), (bps,))
                    e_, be_ = efr.next()
                    act(e_, ps, AF.Exp, (bps,), (be_,), scale=0.125)
                    p_, bp_ = pTr.next()
                    col = C_ABIAS + (g * 2 + hf) * 512
                    tt(p_, e_, cst[:, col:col + 512], ALU.mult, (be_, b_cst), (bp_,), eng="pool")
                    pts[hf] = (p_, bp_)
                return halves, pts

            def att_stage2(b, g, halves, pts):
                yd, byd = psr.next()
                groups = []
                sbase = ((l * 2 + g) * 2) * 256
                for sl in range(2):
                    ypairs, dpairs = [], []
                    for hf in halves:
                        p_ = pts[hf][0]
                        rhs = p_.rearrange("p (pr s n) -> p pr s n", pr=2, s=2)[:, :, sl, :]
                        ypairs.append((vtok[l][:, b + hf, g * 64:(g + 1) * 64], rhs))
                        dpairs.append((ones[:, 0:64], rhs))
                    dpairs.append((ones[:, 0:64],
                                   srow[:, sbase + sl * 256:sbase + sl * 256 + 256].rearrange("p (a n) -> p a n", a=2)))
                    groups.append((yd[64 * sl:64 * sl + 64, 0:256].rearrange("p (a n) -> p a n", a=2), ypairs))
                    groups.append((yd[64 * sl:64 * sl + 64, 256:512].rearrange("p (a n) -> p a n", a=2), dpairs))
                vb = [b_vcur[l]] + ([b_vprev[l]] if b == 0 and 0 in halves else [])
                mm_multi(groups, (*[pts[hf][1] for hf in halves], *vb, b_ones, b_srow), (byd,))
                rd, brd = rdr.next()
                S.op("dve", lambda e, rd=rd, yd=yd: e.reciprocal(out=rd, in_=yd[:, 256:512]), (byd,), (brd,))
                tt(yattT[:, 2 * g:2 * g + 2, b * 128:(b + 1) * 128],
                   yd[:, 0:256].rearrange("p (a n) -> p a n", a=2),
                   rd.rearrange("p (a n) -> p a n", a=2), ALU.mult, (byd, brd),
                   (b_yatt[2 * g][b], b_yatt[2 * g + 1][b]))

            att_pre = [att_stage1(0, 0), att_stage1(0, 1)]

            wC, bC = w_acquire("C")
            vC = wC.rearrange("p (k n) -> p k n", k=8)
            for c in range(4):
                ps, bps = psr.next()
                mm_group(ps, [(vC[:, k, c * 128:(c + 1) * 128], hT[:, k, :]) for k in range(8)],
                         (bC, *b_hT), (bps,))
                act(uT[:, c, :], ps, AF.Gelu_apprx_tanh, (bps,), (b_uT[c],))
            w_release()

            wD, bD = w_acquire("D")
            vD = wD.rearrange("p (k n) -> p k n", k=8)
            for b in range(NB):
                ps, bps = psr.next()
                mm_group(ps, [(hT[:, k, b * 128:(b + 1) * 128], vD[:, k, :]) for k in range(8)],
                         (bD, *b_hT), (bps,))
                g_, bg_ = gvr.next()
                act(g_, ps, AF.Gelu_apprx_tanh, (bps,), (bg_,))
                si = small_i[0]
                small_i[0] = (si + 1) % 8
                act(junk[:, :], g_, AF.Square, (bg_,), (b_junk, b_ssv[si]), accum_out=ssv[:, si:si + 1])
                act(lnv[:, si:si + 1], ssv[:, si:si + 1], AF.Ln, (b_ssv[si],), (b_lnv[si],), bias=EPS, scale=1.0 / 512)
                act(rv[:, si:si + 1], lnv[:, si:si + 1], AF.Exp, (b_lnv[si],), (b_rv[si],), scale=-0.5)
                stt(vn[:, b, :], g_, rv[:, si:si + 1], cst[:, C_SGUG + 512 * l:C_SGUG + 512 * l + 512],
                    ALU.mult, ALU.mult, (bg_, b_rv[si], b_cst), (b_vn[b],))
            w_release()
            _ck(3)

            _ck(4)
            its = [(b, g) for b in range(NB) for g in range(2)]
            pend = list(att_pre)
            for i, (b, g) in enumerate(its):
                if i + 2 < len(its):
                    pend.append(att_stage1(*its[i + 2]))
                if g == 0:
                    sgu_block(b)
                att_stage2(b, g, *pend.pop(0))
            if not last_in_seq:
                cp(kT[l][:, :, 0:128], kT[l][:, :, T:T + 128], (*b_kcur[l],), (b_kprev[l],))
                cp(vtok[l][:, 0, :], vtok[l][:, NB, :], (b_vcur[l],), (b_vprev[l],))

            _ck(5)
            for half in range(2):
                wE, bE = w_acquire("EF"[half])
                wG, bG = w_acquire("GH"[half])
                wO, bO = w_acquire("OAB%d" % half)
                vE = wE.rearrange("p (k n) -> p k n", k=8)
                vG = wG.rearrange("p (k n) -> p k n", k=8)
                vO = wO.rearrange("p (m k n) -> p m k n", m=2, k=4)
                for c4 in range(4):
                    c = 4 * half + c4
                    cs = slice(c4 * 128, (c4 + 1) * 128)
                    pga, bpga = psr.next()
                    mm_group(pga, [(vE[:, k, cs], hT[:, k, :]) for k in range(8)], (bE, *b_hT), (bpga,))
                    pgb, bpgb = psr.next()
                    mm_group(pgb, [(vG[:, k, cs], hT[:, k, :]) for k in range(8)], (bG, *b_hT), (bpgb,))
                    pa, bpa = psr.next()
                    mm_group(pa, [(vO[:, 0, kc, cs], yattT[:, kc, :]) for kc in range(4)],
                             (bO, *[b_yatt[kc][b] for kc in range(4) for b in range(NB)]), (bpa,))
                    pb, bpb = psr.next()
                    mm_group(pb, [(vO[:, 1, kc, cs], ysguT[:, kc, :]) for kc in range(4)], (bO, *b_ysgu), (bpb,))
                    sa, bsa = sar.next()
                    act(sa, pga, AF.Sigmoid, (bpga,), (bsa,))
                    sb_, bsb_ = sbr.next()
                    act(sb_, pgb, AF.Sigmoid, (bpgb,), (bsb_,))
                    t1, bt1 = t1r.next()
                    tt(t1, pa, sa, ALU.mult, (bpa, bsa), (bt1,))
                    t2, bt2 = t2r.next()
                    tt(t2, pb, sb_, ALU.mult, (bpb, bsb_), (bt2,))
                    tt(mergedT[:, c, :], t1, t2, ALU.add, (bt1, bt2), (b_merged[c],), eng="pool")
                w_release(3)
            sqbs = {}
            for half in range(2):
                wO, bO = w_acquire("OUT%d" % half)
                vO = wO.rearrange("p (k n) -> p k n", k=8)
                for c4 in range(4):
                    c = 4 * half + c4
                    po, bpo = psr.next()
                    mm_group(po, [(vO[:, k, c4 * 128:(c4 + 1) * 128], mergedT[:, k, :]) for k in range(8)],
                             (bO, *b_merged), (bpo,))
                    if c >= 1:
                        rms_sq_mm(sqbs[c - 1], c - 1)
                    tt(xr[:, c, :], po, xr[:, c, :], ALU.add, (bpo, bxr[c]), (bxr[c],))
                    sqbs[c] = rms_sq_act(xr, bxr, c)
                w_release()
            rms_sq_mm(sqbs[7], 7)

            _ck(6)
            rms_finish(xr, bxr, G2)
            if last_layer and nxt is not None:
                emit_xload(nxt[0])

            def ffn_epilogue(j, pg, bpg, pv, bpv):
                ag, bag = agr.next()
                av, bav = avr.next()
                items = ((pg, bpg, ag, bag, j), (pv, bpv, av, bav, NJ + j))
                for (ps, bps, a_, ba_, ch) in items:
                    act(a_, ps, AF.Identity, (bps, b_cst), (ba_,),
                        bias=cst[:, CB + ch:CB + ch + 1], scale=cst[:, CW2 + ch:CW2 + ch + 1])
                for (ps, bps, a_, ba_, ch) in items:
                    stt(a_[:, 1:T], ps[:, 0:T - 1], cst[:, CW1 + ch:CW1 + ch + 1], a_[:, 1:T], ALU.mult, ALU.add,
                        (bps, ba_, b_cst), (ba_,))
                for (ps, bps, a_, ba_, ch) in items:
                    stt(a_[:, 2:T], ps[:, 0:T - 2], cst[:, CW0 + ch:CW0 + ch + 1], a_[:, 2:T], ALU.mult, ALU.add,
                        (bps, ba_, b_cst), (ba_,))
                if not first_in_seq:
                    for (ps, bps, a_, ba_, ch) in items:
                        stt(a_[:, 0:2], halo[l][:, ch, 0:2], cst[:, CW0 + ch:CW0 + ch + 1], a_[:, 0:2],
                            ALU.mult, ALU.add, (b_halo[l][ch], ba_, b_cst), (ba_,))
                    for (ps, bps, a_, ba_, ch) in items:
                        stt(a_[:, 0:1], halo[l][:, ch, 1:2], cst[:, CW1 + ch:CW1 + ch + 1], a_[:, 0:1],
                            ALU.mult, ALU.add, (b_halo[l][ch], ba_, b_cst), (ba_,))
                if not last_in_seq:
                    for (ps, bps, a_, ba_, ch) in items:
                        act(halo[l][:, ch, :], ps[:, T - 2:T], AF.Identity, (bps,), (b_halo[l][ch],))
                sg, bsg = sgr.next()
                act(sg, ag, AF.Silu, (bag,), (bsg,))
                tt(actT[:, j, :], sg, av, ALU.mult, (bsg, bav), (b_actT[j],), eng="pool")

            for i in range(11):
                wU, bU = w_acquire("UP%d" % i)
                vU = wU.rearrange("p (k n) -> p k n", k=8)
                if i == 0:
                    pss = [psr.next() for _ in range(4)]
                    offs = [0, 256, 128, 384]
                    kouter([p[0] for p in pss], lambda q, k: vU[:, k, offs[q]:offs[q] + 128], (bU,),
                           [p[1] for p in pss])
                    ffn_epilogue(0, pss[0][0], pss[0][1], pss[1][0], pss[1][1])
                    ffn_epilogue(1, pss[2][0], pss[2][1], pss[3][0], pss[3][1])
                else:
                    for jj in range(2):
                        j = 2 * i + jj
                        pg, bpg = psr.next()
                        mm_group(pg, [(vU[:, k, jj * 128:(jj + 1) * 128], hT[:, k, :]) for k in range(8)],
                                 (bU, *b_hT), (bpg,))
                        pv, bpv = psr.next()
                        mm_group(pv, [(vU[:, k, 256 + jj * 128:256 + (jj + 1) * 128], hT[:, k, :]) for k in range(8)],
                                 (bU, *b_hT), (bpv,))
                        ffn_epilogue(j, pg, bpg, pv, bpv)
                w_release()
            _ck(7)
            if nxt is not None:
                nxr, nbxr = xres[nxt[0] % 2], b_xres[nxt[0] % 2]
            sqbs = {}
            for c in range(8):
                wDn, bDn = w_acquire("DN%d" % c)
                vDn = wDn[:, 0:2816].rearrange("p (k n) -> p k n", k=NJ)
                pd, bpd = psr.next()
                mm_group(pd, [(vDn[:, j, :], actT[:, j, :]) for j in range(NJ)], (bDn, *b_actT), (bpd,))
                if nxt is not None and c >= 1:
                    rms_sq_mm(sqbs[c - 1], c - 1)
                tt(xr[:, c, :], pd, xr[:, c, :], ALU.add, (bpd, bxr[c]), (bxr[c],))
                if nxt is not None:
                    sqbs[c] = rms_sq_act(nxr, nbxr, c)
                w_release()
            if nxt is not None:
                rms_sq_mm(sqbs[7], 7)

            if last_layer:
                dst = yT_d[s_idx].rearrange("(c p) t -> p c t", p=128)[:, :, t0:t0 + T]
                S.dma("sp", "xs%d" % xi, lambda e: e.dma_start(out=dst, in_=xr[:, :, :]), tuple(bxr), ())

        plist = [(ti, li) for ti in range(n_tiles) for li in range(len(layers))]
        emit_xload(0)
        sq0 = [rms_sq_act(xres[0], b_xres[0], c) for c in range(4)]
        for c in range(8):
            rms_sq_mm(sq0[c] if c < 4 else rms_sq_act(xres[0], b_xres[0], c), c)
        try:
            for pi, (ti, li) in enumerate(plist):
                nxt = plist[pi + 1] if pi + 1 < len(plist) else None
                run_pass(ti, layers[li], li == 0, li == len(layers) - 1, nxt)
        except _Stop:
            dst = yT_d[0].rearrange("(c p) t -> p c t", p=128)[:, :, 0:T]
            S.dma("sp", "xs0", lambda e: e.dma_start(out=dst, in_=xres[0][:, :, :]), tuple(b_xres[0]), ())
        S.wait_all("sp", [b for i in range(2) for b in b_xres[i]])

        sems = {k: es.enter_context(nc.semaphore(k)) for k in S.semnames}
        block = es.enter_context(nc.Block())

        @block.tensor
        def _(e):
            S.replay("pe", e, sems)

        @block.scalar
        def _(e):
            S.replay("act", e, sems)

        @block.vector
        def _(e):
            S.replay("dve", e, sems)

        @block.gpsimd
        def _(e):
            S.replay("pool", e, sems)

        @block.sync
        def _(e):
            S.replay("sp", e, sems)
    return nc


def _pkn(w):
    kc = w.shape[0] // 128
    return np.ascontiguousarray(w.reshape(kc, 128, -1).transpose(1, 0, 2).reshape(128, -1))


def pack_weights(w_in, w_oa, w_ob, w_out, w_up, w_down):
    out = np.empty((NL, 128, WPL), np.float32)
    for l in range(NL):
        parts = {
            "A": _pkn(w_in[l][:, 0:512]), "B": _pkn(w_in[l][:, 512:768]),
            "C": _pkn(w_in[l][:, 768:1280]), "D": _pkn(w_in[l][:, 1280:1792]),
            "E": _pkn(w_in[l][:, 1792:2304]), "F": _pkn(w_in[l][:, 2304:2816]),
            "G": _pkn(w_in[l][:, 2816:3328]), "H": _pkn(w_in[l][:, 3328:3840]),
            "OAB0": np.concatenate([_pkn(w_oa[l][:, 0:512]), _pkn(w_ob[l][:, 0:512])], axis=1),
            "OAB1": np.concatenate([_pkn(w_oa[l][:, 512:1024]), _pkn(w_ob[l][:, 512:1024])], axis=1),
            "OUT0": _pkn(w_out[l][:, 0:512]), "OUT1": _pkn(w_out[l][:, 512:1024]),
        }
        for i in range(11):
            parts["UP%d" % i] = _pkn(np.concatenate(
                [w_up[l][:, 256 * i:256 * i + 256], w_up[l][:, DFF + 256 * i:DFF + 256 * i + 256]], axis=1))
        for c in range(8):
            parts["DN%d" % c] = _pkn(w_down[l][:, 128 * c:128 * c + 128])
        for nm, n in SLABS:
            off, _ = SLAB_OFF[nm]
            assert parts[nm].shape == (128, n), (nm, parts[nm].shape)
            out[l, :, off:off + n] = parts[nm]
    return out


def pack_consts(mix_norm, q_norm, k_norm, sinks, sgu_norm, w_s, b_s, ffn_norm, conv_w, conv_b):
    cst = np.zeros((128, NCST), np.float32)
    for l in range(NL):
        vb = C_VEC + 192 * l
        cst[:, vb:vb + 8] = mix_norm[l].reshape(8, 128).T
        cst[:, vb + 8:vb + 16] = ffn_norm[l].reshape(8, 128).T
        for tap in range(3):
            cst[:, vb + 16 + 44 * tap:vb + 16 + 44 * (tap + 1)] = conv_w[l, tap].reshape(44, 128).T
        cst[:, vb + 148:vb + 192] = conv_b[l].reshape(44, 128).T
        cst[0:64, C_QK + 2 * l] = q_norm[l]
        cst[0:64, C_QK + 2 * l + 1] = k_norm[l]
        cst[:, C_SGUG + 512 * l:C_SGUG + 512 * (l + 1)] = sgu_norm[l][None, :]
        for gp in range(4):
            cst[0:64, C_BSB + 512 * l + gp * 128:C_BSB + 512 * l + (gp + 1) * 128] = b_s[l, 2 * gp][None, :]
            cst[64:128, C_BSB + 512 * l + gp * 128:C_BSB + 512 * l + (gp + 1) * 128] = b_s[l, 2 * gp + 1][None, :]
    k = np.arange(128)[:, None]
    q = np.arange(128)[None, :]
    for g in range(2):
        for j in range(4):
            slope = 2.0 ** (-(4 * g + j + 1))
            dist_prev = q + 128 - k
            dist_cur = q - k
            bp = np.where(dist_prev < 128, -slope * dist_prev, -30000.0)
            bc = np.where(dist_cur >= 0, -slope * dist_cur, -30000.0)
            cst[:, C_ABIAS + (g * 2 + 0) * 512 + j * 128:C_ABIAS + (g * 2 + 0) * 512 + (j + 1) * 128] = bp
            cst[:, C_ABIAS + (g * 2 + 1) * 512 + j * 128:C_ABIAS + (g * 2 + 1) * 512 + (j + 1) * 128] = bc
    cst[:, C_TRIL:C_TRIL + 128] = (k <= q).astype(np.float32)
    srow = np.zeros((1, 2048), np.float32)
    for l in range(NL):
        for g in range(2):
            for sl in range(2):
                for pr in range(2):
                    base = (((l * 2 + g) * 2 + sl) * 2 + pr) * 128
                    srow[0, base:base + 128] = sinks[l, 4 * g + 2 * pr + sl]
    wst = np.ascontiguousarray(np.transpose(w_s, (3, 0, 1, 2)).reshape(128, NL * 8 * 128)).astype(np.float32)
    return cst, srow, wst


_NC_CACHE = {}
DBG_STOP = None


class _Stop(Exception):
    pass


_CUR = [0, 0]


def _ck(k):
    if DBG_STOP is not None and DBG_STOP == (_CUR[0], _CUR[1], k):
        raise _Stop()


def run(x, params, n_cores, layers=(0, 1)):
    B, S_, _ = x.shape
    n_seq = B // n_cores
    key = (n_seq, S_, tuple(layers))
    if key not in _NC_CACHE:
        _NC_CACHE[key] = build(n_seq, S_, layers)
    nc = _NC_CACHE[key]
    wts = pack_weights(params["w_in"], params["w_oa"], params["w_ob"], params["w_out"], params["w_up"],
                       params["w_down"])
    cst, srow, wst = pack_consts(params["mix_norm"], params["q_norm"], params["k_norm"], params["sinks"],
                                 params["sgu_norm"], params["w_s"], params["b_s"], params["ffn_norm"],
                                 params["conv_w"], params["conv_b"])
    in_maps = []
    for c in range(n_cores):
        xc = np.ascontiguousarray(np.transpose(x[c * n_seq:(c + 1) * n_seq], (0, 2, 1)))
        in_maps.append({"xT": xc, "wts": wts, "cst": cst, "srow": srow, "wst": wst})
    res = run_bass_kernel_spmd(nc, in_maps, core_ids=list(range(n_cores)))
    outs = [np.transpose(r["yT"], (0, 2, 1)) for r in res.results]
    return np.ascontiguousarray(np.concatenate(outs, axis=0)).astype(np.float32)


def kernel(**inputs):
    inputs = {k: np.asarray(v) for k, v in inputs.items()}
    x = inputs.pop("x").astype(np.float32)
    params = {k: v.astype(np.float32) for k, v in inputs.items()}
    return run(x, params, 8)
```

```python
from contextlib import ExitStack

import numpy as np
import concourse.bass as bass
import concourse.mybir as mybir
from concourse.bass_utils import run_bass_kernel_spmd

F32 = mybir.dt.float32
BF16 = mybir.dt.bfloat16
AF = mybir.ActivationFunctionType
ALU = mybir.AluOpType

D = 1024
NL = 2
NH = 8
NKV = 2
HD = 64
DFF = 2816
NJ = DFF // 128
T = 512
NB = T // 128
EPS = 1e-6
NBUF = 5
SLAB = 4096

SLABS = ([("A0", 2048), ("B", 2048), ("A1", 2048), ("C", 4096), ("D", 4096), ("E", 4096), ("G", 4096),
          ("OAB0", 4096), ("F", 4096), ("H", 4096), ("OAB1", 4096), ("OUT0", 4096), ("OUT1", 4096)]
         + [("UP%d" % i, 4096) for i in range(11)] + [("DN%d" % c, 2816) for c in range(8)])
SLAB_OFF = {}
_o = 0
for _n, _s in SLABS:
    SLAB_OFF[_n] = (_o, _s)
    _o += _s
WPL = _o

C_VEC = 0
C_QK = 384
C_SGUG = 392
C_BSB = C_SGUG + 1024
C_ABIAS = C_BSB + 1024
C_TRIL = C_ABIAS + 2048
NCST = C_TRIL + 128


class Buf:
    __slots__ = ("name", "lw", "rd", "excl")

    def __init__(self, name, excl=False):
        self.name = name
        self.lw = None
        self.rd = []
        self.excl = excl


class Sched:
    ENG = ("pe", "act", "dve", "pool", "sp")

    def __init__(self):
        self.streams = {e: [] for e in self.ENG}
        self.cnt = {}
        self.waited = {e: {} for e in self.ENG}
        self.semnames = []

    def newsem(self, key):
        self.cnt[key] = 0
        self.semnames.append(key)

    def _waits(self, eng, reads, writes):
        deps = {}
        def add(ev, raw):
            if ev is None:
                return
            k, v = ev
            if k == eng and (eng == "pe" or not raw):
                return
            if deps.get(k, 0) < v:
                deps[k] = v
        for b in reads:
            add(b.lw, True)
            if b.excl:
                for ev in b.rd:
                    add(ev, False)
        for b in writes:
            add(b.lw, False)
            for ev in b.rd:
                add(ev, False)
        for k, v in deps.items():
            if self.waited[eng].get(k, 0) >= v:
                continue
            self.waited[eng][k] = v
            self.streams[eng].append(("wait", k, v))

    def _commit(self, ev, reads, writes):
        for b in reads:
            b.rd.append(ev)
        for b in writes:
            b.lw = ev
            b.rd = []

    def op(self, eng, fn, reads=(), writes=()):
        self._waits(eng, reads, writes)
        self.cnt[eng] += 1
        self.streams[eng].append(("op", fn, eng, 1))
        self._commit((eng, self.cnt[eng]), reads, writes)

    def dma(self, eng, sem, fn, reads=(), writes=()):
        self._waits(eng, reads, writes)
        self.cnt[sem] += 16
        self.streams[eng].append(("op", fn, sem, 16))
        self._commit((sem, self.cnt[sem]), reads, writes)

    def wait_all(self, eng, bufs):
        self._waits(eng, bufs, bufs)

    def replay(self, eng, handle, sems):
        for it in self.streams[eng]:
            if it[0] == "wait":
                handle.wait_ge(sems[it[1]], it[2])
            else:
                ins = it[1](handle)
                ins.then_inc(sems[it[2]], it[3])


class Rot:
    def __init__(self, tiles, name):
        self.tiles = tiles
        self.bufs = [Buf("%s%d" % (name, i)) for i in range(len(tiles))]
        self.i = 0

    def next(self):
        i = self.i
        self.i = (i + 1) % len(self.tiles)
        return self.tiles[i], self.bufs[i]


def build(n_seq, seq_len, layers=(0, 1)):
    nc = bass.Bass("TRN2", target_bir_lowering=False)
    S = Sched()
    tiles_per_seq = seq_len // T
    n_tiles = n_seq * tiles_per_seq
    xT_d = nc.dram_tensor("xT", [n_seq, D, seq_len], F32, kind="ExternalInput").ap()
    wts_d = nc.dram_tensor("wts", [NL, 128, WPL], F32, kind="ExternalInput").ap()
    cst_d = nc.dram_tensor("cst", [128, NCST], F32, kind="ExternalInput").ap()
    srow_d = nc.dram_tensor("srow", [1, 2048], F32, kind="ExternalInput").ap()
    wst_d = nc.dram_tensor("wst", [128, 2048], F32, kind="ExternalInput").ap()
    yT_d = nc.dram_tensor("yT", [n_seq, D, seq_len], F32, kind="ExternalOutput").ap()

    es = ExitStack()
    with es:
        def sb(name, shape, dt):
            return es.enter_context(nc.sbuf_tensor("s_" + name, shape, dt))

        cst = sb("cst", [128, NCST], F32)
        srow = sb("srow", [128, 2048], BF16)
        wsT = sb("wsT", [128, 16, 128], BF16)
        ones = sb("ones", [128, 128], BF16)
        xres = [sb("xres%d" % i, [128, 8, T], F32) for i in range(2)]
        hT = sb("hT", [128, 8, T], BF16)
        sqt = sb("sqt", [128, 4, T], BF16)
        NFR = 10
        frt = sb("frt", [128, NFR, 512], F32)
        qT = sb("qT", [64, 8, T], BF16)
        kT = [sb("kT%d" % l, [64, 2, T + 128], BF16) for l in range(NL)]
        vtok = [sb("vtok%d" % l, [128, NB + 1, 128], BF16) for l in range(NL)]
        qsq = sb("qsq", [64, 5, T], BF16)
        uT = sb("uT", [128, 4, T], BF16)
        junk = sb("junk", [128, 512], BF16)
        ssv = sb("ssv", [128, 8], F32)
        lnv = sb("lnv", [128, 8], F32)
        rv = sb("rv", [128, 8], F32)
        vn = sb("vn", [128, NB, 512], BF16)
        pT = sb("pT", [128, 6, 512], BF16)
        rden = sb("rden", [128, 2, 256], F32)
        yattT = sb("yattT", [128, 4, T], BF16)
        ysguT = sb("ysguT", [128, 4, T], BF16)
        mergedT = sb("mergedT", [128, 8, T], BF16)
        actT = sb("actT", [128, NJ, T], BF16)
        halo = [sb("halo%d" % l, [128, 2 * NJ, 2], F32) for l in range(NL)]
        wslab = sb("wslab", [128, NBUF, SLAB], BF16)
        psb = [es.enter_context(nc.psum_tensor("ps%d" % i, [128, 512], F32)) for i in range(8)]

        b_cst = Buf("cst")
        b_srow = Buf("srow")
        b_wsT = Buf("wsT")
        b_ones = Buf("ones")
        b_xres = [[Buf("xres%d_%d" % (i, c)) for c in range(8)] for i in range(2)]
        b_hT = [Buf("hT%d" % c) for c in range(8)]
        b_qT = [Buf("qT%d" % h) for h in range(8)]
        b_kprev = [Buf("kprev%d" % l) for l in range(NL)]
        b_kcur = [[Buf("kcur%d_%d" % (l, g)) for g in range(2)] for l in range(NL)]
        b_vprev = [Buf("vprev%d" % l) for l in range(NL)]
        b_vcur = [Buf("vcur%d" % l) for l in range(NL)]
        b_uT = [Buf("uT%d" % c) for c in range(4)]
        b_junk = Buf("junk")
        b_ssv = [Buf("ssv%d" % i) for i in range(8)]
        b_lnv = [Buf("lnv%d" % i) for i in range(8)]
        b_rv = [Buf("rv%d" % i) for i in range(8)]
        b_vn = [Buf("vn%d" % b) for b in range(NB)]
        b_yatt = [[Buf("yatt%d_%d" % (c, b)) for b in range(NB)] for c in range(4)]
        b_ysgu = [Buf("ysgu%d" % b) for b in range(NB)]
        b_merged = [Buf("merged%d" % c) for c in range(8)]
        b_actT = [Buf("actT%d" % j) for j in range(NJ)]
        b_halo = [[Buf("halo%d_%d" % (l, ch)) for ch in range(2 * NJ)] for l in range(NL)]
        b_wslab = [Buf("wslab%d" % i) for i in range(NBUF)]
        b_ps = [Buf("ps%d" % i, excl=True) for i in range(8)]

        sqr = Rot([sqt[:, i, :] for i in range(4)], "sq")
        qsqr = Rot([qsq[:, i, :] for i in range(5)], "qsq")
        fr = Rot([frt[:, i, :] for i in range(NFR)], "fr")
        rqr = rq2r = gvr = efr = tmr = sar = sbr = t1r = t2r = agr = avr = sgr = fr
        srowf = frt[0:1, 0:4, :]
        wstage = frt[:, 4:8, :]
        pTr = Rot([pT[:, i, :] for i in range(6)], "pT")
        rdr = Rot([rden[:, i, :] for i in range(2)], "rden")
        psr = Rot([p[:, :] for p in psb[0:7]], "psr")
        psr.bufs = b_ps[0:7]
        ss_ps, b_ss = psb[7][:, :], b_ps[7]
        small_i = [0]

        for e in ("pe", "act", "dve", "pool"):
            S.newsem(e)
        for i in range(NBUF):
            S.newsem("w%d" % i)
        for k in ("cst0", "cst1", "cst2", "xl0", "xl1", "xs0", "xs1"):
            S.newsem(k)

        def act(out, in_, func, reads, writes, bias=None, scale=None, accum_out=None):
            kw = {}
            if bias is not None:
                kw["bias"] = bias
            if scale is not None:
                kw["scale"] = scale
            if accum_out is not None:
                kw["accum_out"] = accum_out
            S.op("act", lambda e: e.activation(out=out, in_=in_, func=func, **kw), reads, writes)

        def tt(out, in0, in1, op, reads, writes, eng="dve"):
            S.op(eng, lambda e: e.tensor_tensor(out=out, in0=in0, in1=in1, op=op), reads, writes)

        def stt(out, in0, scalar, in1, op0, op1, reads, writes, eng="dve"):
            S.op(eng, lambda e: e.scalar_tensor_tensor(out=out, in0=in0, scalar=scalar, in1=in1,
                                                        op0=op0, op1=op1), reads, writes)

        def cp(out, in_, reads, writes, eng="dve"):
            S.op(eng, lambda e: e.tensor_copy(out=out, in_=in_), reads, writes)

        def mm_group(out, pairs, reads, writes):
            def fn(e):
                ins = None
                n = len(pairs)
                for i, (l, r) in enumerate(pairs):
                    ins = e.matmul(out, l, r, start=(i == 0), stop=(i == n - 1))
                return ins
            S.op("pe", fn, reads, writes)

        def mm_part(out, pairs, reads, writes, first, last):
            def fn(e):
                ins = None
                n = len(pairs)
                for i, (l, r) in enumerate(pairs):
                    ins = e.matmul(out, l, r, start=(first and i == 0), stop=(last and i == n - 1))
                return ins
            S.op("pe", fn, reads, writes)

        def mm_multi(groups, reads, writes):
            def fn(e):
                ins = None
                for out, pairs in groups:
                    n = len(pairs)
                    for i, (l, r) in enumerate(pairs):
                        ins = e.matmul(out, l, r, start=(i == 0), stop=(i == n - 1))
                return ins
            S.op("pe", fn, reads, writes)

        passes = [(ti, l) for ti in range(n_tiles) for l in layers]
        wseq = [(l, nm) for (_, l) in passes for (nm, _) in SLABS]
        wstate = {"issue": 0, "acq": 0}

        def w_issue():
            i = wstate["issue"]
            if i >= len(wseq):
                return
            wstate["issue"] = i + 1
            l, nm = wseq[i]
            off, n = SLAB_OFF[nm]
            slot = i % NBUF
            o = wslab[:, slot, 0:n]
            src = wts_d[l, :, off:off + n]
            S.dma("pool", "w%d" % slot, lambda e: e.dma_start(out=o, in_=src), (), (b_wslab[slot],))

        def w_acquire(expect):
            i = wstate["acq"]
            wstate["acq"] = i + 1
            assert wseq[i][1] == expect, (wseq[i], expect)
            slot = i % NBUF
            return wslab[:, slot, :], b_wslab[slot]

        def w_release(n=1):
            for _ in range(n):
                w_issue()

        S.dma("sp", "cst0", lambda e: e.dma_start(out=cst[:, :], in_=cst_d[:, :]), (), (b_cst,))
        S.dma("sp", "cst1", lambda e: e.dma_start(out=srowf, in_=srow_d.rearrange("o (a n) -> o a n", a=4)),
              (), tuple(fr.bufs[0:4]))
        S.dma("sp", "cst2", lambda e: e.dma_start(out=wstage, in_=wst_d.rearrange("p (a n) -> p a n", a=4)),
              (), tuple(fr.bufs[4:8]))
        for _ in range(NBUF):
            w_issue()
        S.op("dve", lambda e: e.memset(ones[:, :], 1.0), (), (b_ones,))
        S.op("dve", lambda e: e.memset(srow[:, :], 0.0), (), (b_srow,))
        act(srow[0:1, :].rearrange("o (a n) -> o a n", a=4), srowf, AF.Exp, tuple(fr.bufs[0:4]), (b_srow,))
        act(cst[:, C_ABIAS:C_ABIAS + 2048], cst[:, C_ABIAS:C_ABIAS + 2048], AF.Exp, (b_cst,), (b_cst,))
        for i in range(16):
            tt(wsT[:, i, :], wstage[:, i // 4, (i % 4) * 128:(i % 4 + 1) * 128], cst[:, C_TRIL:C_TRIL + 128], ALU.mult,
               (fr.bufs[4 + i // 4], b_cst), (b_wsT,))

        def rms_sq_act(xr, bxr, c):
            sq, bsq = sqr.next()
            act(sq, xr[:, c, :], AF.Square, (bxr[c],), (bsq,))
            return sq, bsq

        def rms_sq_mm(sqb, c):
            sq, bsq = sqb
            S.op("pe", (lambda e: e.matmul(ss_ps, ones[:, :], sq, start=(c == 0), stop=(c == 7))),
                 (bsq, b_ones), (b_ss,))

        def rms_finish(xr, bxr, gcol):
            ps, bps = ss_ps, b_ss
            rtmp, b_rtmp = fr.next()
            act(rtmp, ps, AF.Ln, (bps,), (b_rtmp,), bias=EPS, scale=1.0 / D)
            rstd, b_rstd = fr.next()
            act(rstd, rtmp, AF.Exp, (b_rtmp,), (b_rstd,), scale=-0.5)
            for c in range(8):
                stt(hT[:, c, :], xr[:, c, :], cst[:, gcol + c:gcol + c + 1], rstd, ALU.mult, ALU.mult,
                    (bxr[c], b_rstd, b_cst), (b_hT[c],))

        def headnorm_a(ps, bps, gcolumn, out, bout):
            sq, bsq = qsqr.next()
            act(sq[0:64, :], ps[0:64, :], AF.Square, (bps,), (bsq,))
            return lambda: headnorm_b(ps, bps, gcolumn, out, bout, sq, bsq)

        def headnorm_b(ps, bps, gcolumn, out, bout, sq, bsq):
            ps2, bps2 = psr.next()
            S.op("pe", lambda e: e.matmul(ps2[0:64, :], ones[0:64, 0:64], sq[0:64, :], start=True, stop=True),
                 (bsq, b_ones), (bps2,))
            r1, br1 = rqr.next()
            act(r1[0:64, :], ps2[0:64, :], AF.Ln, (bps2,), (br1,), bias=EPS, scale=1.0 / HD)
            r2, br2 = rq2r.next()
            act(r2[0:64, :], r1[0:64, :], AF.Exp, (br1,), (br2,), scale=-0.5)
            stt(out, ps[0:64, :], cst[0:64, gcolumn:gcolumn + 1], r2[0:64, :], ALU.mult, ALU.mult,
                (bps, br2, b_cst), (bout,))

        def emit_xload(ti):
            s_idx = ti // tiles_per_seq
            t0 = (ti % tiles_per_seq) * T
            xi = ti % 2
            src = xT_d[s_idx].rearrange("(c p) t -> p c t", p=128)[:, :, t0:t0 + T]
            dstt = xres[xi][:, :, :]
            S.dma("sp", "xl%d" % xi, lambda e: e.dma_start(out=dstt, in_=src), (), tuple(b_xres[xi]))

        def kouter(outs, lhs_fn, reads_w, wr_bufs):
            for k in range(8):
                groups = [(o, lhs_fn(i, k), hT[:, k, :]) for i, o in enumerate(outs)]
                def fn(e, groups=groups, k=k):
                    ins = None
                    for (o, l_, r_) in groups:
                        ins = e.matmul(o, l_, r_, start=(k == 0), stop=(k == 7))
                    return ins
                S.op("pe", fn, (*reads_w, b_hT[k]), tuple(wr_bufs))

        def run_pass(ti, l, first_layer, last_layer, nxt):
            _CUR[0], _CUR[1] = ti, l
            s_idx = ti // tiles_per_seq
            tt_i = ti % tiles_per_seq
            t0 = tt_i * T
            first_in_seq = tt_i == 0
            last_in_seq = tt_i == tiles_per_seq - 1
            xi = ti % 2
            xr = xres[xi]
            bxr = b_xres[xi]
            vbase = C_VEC + 192 * l
            G1, G2 = vbase, vbase + 8
            CW0, CW1, CW2, CB = vbase + 16, vbase + 60, vbase + 104, vbase + 148
            QG, KG = C_QK + 2 * l, C_QK + 2 * l + 1

            _ck(0)
            rms_finish(xr, bxr, G1)
            _ck(1)

            pend_hn = []

            def flush_hn():
                while pend_hn:
                    pend_hn.pop(0)()

            def proj_head(vW, bW, cols, gcolumn, out, bout):
                ps, bps = psr.next()
                mm_group(ps[0:64, :], [(vW[:, k, cols], hT[:, k, :]) for k in range(8)], (bW, *b_hT), (bps,))
                part_b = headnorm_a(ps, bps, gcolumn, out, bout)
                flush_hn()
                pend_hn.append(part_b)

            wA0, bA0 = w_acquire("A0")
            vA0 = wA0[:, 0:2048].rearrange("p (k n) -> p k n", k=8)
            qps = [psr.next() for _ in range(4)]
            kouter([p[0][0:64, :] for p in qps], lambda i, k: vA0[:, k, i * 64:(i + 1) * 64], (bA0,),
                   [p[1] for p in qps])
            for h in range(4):
                pend_hn.append(headnorm_a(qps[h][0], qps[h][1], QG, qT[:, h, :], b_qT[h]))
            w_release()
            _ck(2)

            wB, bB = w_acquire("B")
            vB = wB[:, 0:2048].rearrange("p (k n) -> p k n", k=8)
            for g in range(2):
                proj_head(vB, bB, slice(g * 64, (g + 1) * 64), KG, kT[l][:, g, 128:128 + T], b_kcur[l][g])
            ps, bps = psr.next()
            mm_multi([(ps[:, b * 128:(b + 1) * 128],
                       [(hT[:, k, b * 128:(b + 1) * 128], vB[:, k, 128:256]) for k in range(8)])
                      for b in range(NB)], (bB, *b_hT), (bps,))
            flush_hn()
            cp(vtok[l][:, 1:NB + 1, :], ps.rearrange("p (b n) -> p b n", b=NB), (bps,), (b_vcur[l],))
            w_release()

            def sgu_block(b):
                ps, bps = psr.next()
                groups = []
                for gp in range(4):
                    for sl in range(2):
                        g = 2 * gp + sl
                        groups.append((ps[64 * sl:64 * sl + 64, gp * 128:(gp + 1) * 128],
                                       [(vn[:, b, g * 64:(g + 1) * 64], wsT[:, l * 8 + g, :])]))
                mm_multi(groups, (b_vn[b], b_wsT), (bps,))
                tm, btm = tmr.next()
                tt(tm, ps, cst[:, C_BSB + 512 * l:C_BSB + 512 * l + 512], ALU.add, (bps, b_cst), (btm,))
                tt(ysguT[:, :, b * 128:(b + 1) * 128], tm.rearrange("p (a n) -> p a n", a=4),
                   uT[:, :, b * 128:(b + 1) * 128], ALU.mult, (btm, *b_uT), (b_ysgu[b],))

            def att_stage1(b, g):
                halves = []
                if not (first_in_seq and b == 0):
                    halves.append(0)
                halves.append(1)
                pts = {}
                for hf in halves:
                    ps, bps = psr.next()
                    kcols = slice(128 * (b + hf), 128 * (b + hf) + 128)
                    kb = [b_kcur[l][g]] + ([b_kprev[l]] if (b == 0 and hf == 0) else [])
                    lhs_ = kT[l][:, g, kcols]
                    rhs_ = qT[:, 4 * g:4 * g + 4, b * 128:(b + 1) * 128]
                    out_ = ps.rearrange("p (a n) -> p a n", a=4)
                    S.op("pe", (lambda e, out_=out_, lhs_=lhs_, rhs_=rhs_: e.matmul(
                        out_, lhs_, rhs_, start=True, stop=True)),
                        (*kb, *b_qT[4 * g:4 * g + 4]), (bps,))
                    e_, be_ = efr.next()
                    act(e_, ps, AF.Exp, (bps,), (be_,), scale=0.125)
                    p_, bp_ = pTr.next()
                    col = C_ABIAS + (g * 2 + hf) * 512
                    tt(p_, e_, cst[:, col:col + 512], ALU.mult, (be_, b_cst), (bp_,), eng="pool")
                    pts[hf] = (p_, bp_)
                return halves, pts

            def att_stage2(b, g, halves, pts):
                yd, byd = psr.next()
                groups = []
                sbase = ((l * 2 + g) * 2) * 256
                for sl in range(2):
                    ypairs, dpairs = [], []
                    for hf in halves:
                        p_ = pts[hf][0]
                        rhs = p_.rearrange("p (pr s n) -> p pr s n", pr=2, s=2)[:, :, sl, :]
                        ypairs.append((vtok[l][:, b + hf, g * 64:(g + 1) * 64], rhs))
                        dpairs.append((ones[:, 0:64], rhs))
                    dpairs.append((ones[:, 0:64],
                                   srow[:, sbase + sl * 256:sbase + sl * 256 + 256].rearrange("p (a n) -> p a n", a=2)))
                    groups.append((yd[64 * sl:64 * sl + 64, 0:256].rearrange("p (a n) -> p a n", a=2), ypairs))
                    groups.append((yd[64 * sl:64 * sl + 64, 256:512].rearrange("p (a n) -> p a n", a=2), dpairs))
                vb = [b_vcur[l]] + ([b_vprev[l]] if b == 0 and 0 in halves else [])
                mm_multi(groups, (*[pts[hf][1] for hf in halves], *vb, b_ones, b_srow), (byd,))
                rd, brd = rdr.next()
                S.op("dve", lambda e, rd=rd, yd=yd: e.reciprocal(out=rd, in_=yd[:, 256:512]), (byd,), (brd,))
                tt(yattT[:, 2 * g:2 * g + 2, b * 128:(b + 1) * 128],
                   yd[:, 0:256].rearrange("p (a n) -> p a n", a=2),
                   rd.rearrange("p (a n) -> p a n", a=2), ALU.mult, (byd, brd),
                   (b_yatt[2 * g][b], b_yatt[2 * g + 1][b]))

            _ck(4)
            slabs = {}

            def get_slab(nm):
                if nm not in slabs:
                    w_, b_ = w_acquire(nm)
                    n_ = 2048 if nm == "A1" else 4096
                    slabs[nm] = (w_[:, 0:n_].rearrange("p (k n) -> p k n", k=8), b_)
                return slabs[nm]

            def u_qhead(h):
                vA1, bA1 = get_slab("A1")
                proj_head(vA1, bA1, slice((h - 4) * 64, (h - 3) * 64), QG, qT[:, h, :], b_qT[h])
                if h == 7:
                    w_release()

            def u_su(c):
                vC, bC = get_slab("C")
                flush_hn()
                ps, bps = psr.next()
                mm_group(ps, [(vC[:, k, c * 128:(c + 1) * 128], hT[:, k, :]) for k in range(8)],
                         (bC, *b_hT), (bps,))
                act(uT[:, c, :], ps, AF.Gelu_apprx_tanh, (bps,), (b_uT[c],))
                if c == 3:
                    w_release()

            def u_sv(b):
                vD, bD = get_slab("D")
                ps, bps = psr.next()
                mm_group(ps, [(hT[:, k, b * 128:(b + 1) * 128], vD[:, k, :]) for k in range(8)],
                         (bD, *b_hT), (bps,))
                g_, bg_ = gvr.next()
                act(g_, ps, AF.Gelu_apprx_tanh, (bps,), (bg_,))
                si = small_i[0]
                small_i[0] = (si + 1) % 8
                act(junk[:, :], g_, AF.Square, (bg_,), (b_junk, b_ssv[si]), accum_out=ssv[:, si:si + 1])
                act(lnv[:, si:si + 1], ssv[:, si:si + 1], AF.Ln, (b_ssv[si],), (b_lnv[si],), bias=EPS, scale=1.0 / 512)
                act(rv[:, si:si + 1], lnv[:, si:si + 1], AF.Exp, (b_lnv[si],), (b_rv[si],), scale=-0.5)
                stt(vn[:, b, :], g_, rv[:, si:si + 1], cst[:, C_SGUG + 512 * l:C_SGUG + 512 * l + 512],
                    ALU.mult, ALU.mult, (bg_, b_rv[si], b_cst), (b_vn[b],))
                if b == NB - 1:
                    w_release()

            units = ([lambda h=h: u_qhead(h) for h in range(4, 8)] + [lambda c=c: u_su(c) for c in range(4)]
                     + [lambda b=b: u_sv(b) for b in range(NB)])
            its = [(b, 0) for b in range(NB)] + [(b, 1) for b in range(NB)]
            pend = [att_stage1(*its[0]), att_stage1(*its[1])]
            for i, (b, g) in enumerate(its):
                if g == 0:
                    for _ in range(3):
                        units.pop(0)()
                else:
                    sgu_block(b)
                if i + 2 < len(its):
                    pend.append(att_stage1(*its[i + 2]))
                att_stage2(b, g, *pend.pop(0))
            assert not units
            _ck(3)
            if not last_in_seq:
                cp(kT[l][:, :, 0:128], kT[l][:, :, T:T + 128], (*b_kcur[l],), (b_kprev[l],))
                cp(vtok[l][:, 0, :], vtok[l][:, NB, :], (b_vcur[l],), (b_vprev[l],))

            _ck(5)
            for half in range(2):
                wE, bE = w_acquire("EF"[half])
                wG, bG = w_acquire("GH"[half])
                wO, bO = w_acquire("OAB%d" % half)
                vE = wE.rearrange("p (k n) -> p k n", k=8)
                vG = wG.rearrange("p (k n) -> p k n", k=8)
                vO = wO.rearrange("p (m k n) -> p m k n", m=2, k=4)
                for c4 in range(4):
                    c = 4 * half + c4
                    cs = slice(c4 * 128, (c4 + 1) * 128)
                    pga, bpga = psr.next()
                    mm_group(pga, [(vE[:, k, cs], hT[:, k, :]) for k in range(8)], (bE, *b_hT), (bpga,))
                    pgb, bpgb = psr.next()
                    mm_group(pgb, [(vG[:, k, cs], hT[:, k, :]) for k in range(8)], (bG, *b_hT), (bpgb,))
                    pa, bpa = psr.next()
                    mm_group(pa, [(vO[:, 0, kc, cs], yattT[:, kc, :]) for kc in range(4)],
                             (bO, *[b_yatt[kc][b] for kc in range(4) for b in range(NB)]), (bpa,))
                    pb, bpb = psr.next()
                    mm_group(pb, [(vO[:, 1, kc, cs], ysguT[:, kc, :]) for kc in range(4)], (bO, *b_ysgu), (bpb,))
                    sa, bsa = sar.next()
                    act(sa, pga, AF.Sigmoid, (bpga,), (bsa,))
                    sb_, bsb_ = sbr.next()
                    act(sb_, pgb, AF.Sigmoid, (bpgb,), (bsb_,))
                    t1, bt1 = t1r.next()
                    tt(t1, pa, sa, ALU.mult, (bpa, bsa), (bt1,))
                    t2, bt2 = t2r.next()
                    tt(t2, pb, sb_, ALU.mult, (bpb, bsb_), (bt2,))
                    tt(mergedT[:, c, :], t1, t2, ALU.add, (bt1, bt2), (b_merged[c],), eng="pool")
                w_release(3)
            sqbs = {}
            for half in range(2):
                wO, bO = w_acquire("OUT%d" % half)
                vO = wO.rearrange("p (k n) -> p k n", k=8)
                for c4 in range(4):
                    c = 4 * half + c4
                    po, bpo = psr.next()
                    mm_group(po, [(vO[:, k, c4 * 128:(c4 + 1) * 128], mergedT[:, k, :]) for k in range(8)],
                             (bO, *b_merged), (bpo,))
                    if c >= 1:
                        rms_sq_mm(sqbs[c - 1], c - 1)
                    tt(xr[:, c, :], po, xr[:, c, :], ALU.add, (bpo, bxr[c]), (bxr[c],))
                    sqbs[c] = rms_sq_act(xr, bxr, c)
                w_release()
            rms_sq_mm(sqbs[7], 7)

            _ck(6)
            rms_finish(xr, bxr, G2)
            if last_layer and nxt is not None:
                emit_xload(nxt[0])

            def ffn_epilogue(j, pg, bpg, pv, bpv):
                ag, bag = agr.next()
                av, bav = avr.next()
                items = ((pg, bpg, ag, bag, j), (pv, bpv, av, bav, NJ + j))
                for (ps, bps, a_, ba_, ch) in items:
                    act(a_, ps, AF.Identity, (bps, b_cst), (ba_,),
                        bias=cst[:, CB + ch:CB + ch + 1], scale=cst[:, CW2 + ch:CW2 + ch + 1])
                for (ps, bps, a_, ba_, ch) in items:
                    stt(a_[:, 1:T], ps[:, 0:T - 1], cst[:, CW1 + ch:CW1 + ch + 1], a_[:, 1:T], ALU.mult, ALU.add,
                        (bps, ba_, b_cst), (ba_,))
                for (ps, bps, a_, ba_, ch) in items:
                    stt(a_[:, 2:T], ps[:, 0:T - 2], cst[:, CW0 + ch:CW0 + ch + 1], a_[:, 2:T], ALU.mult, ALU.add,
                        (bps, ba_, b_cst), (ba_,))
                if not first_in_seq:
                    for (ps, bps, a_, ba_, ch) in items:
                        stt(a_[:, 0:2], halo[l][:, ch, 0:2], cst[:, CW0 + ch:CW0 + ch + 1], a_[:, 0:2],
                            ALU.mult, ALU.add, (b_halo[l][ch], ba_, b_cst), (ba_,))
                    for (ps, bps, a_, ba_, ch) in items:
                        stt(a_[:, 0:1], halo[l][:, ch, 1:2], cst[:, CW1 + ch:CW1 + ch + 1], a_[:, 0:1],
                            ALU.mult, ALU.add, (b_halo[l][ch], ba_, b_cst), (ba_,))
                if not last_in_seq:
                    for (ps, bps, a_, ba_, ch) in items:
                        act(halo[l][:, ch, :], ps[:, T - 2:T], AF.Identity, (bps,), (b_halo[l][ch],))
                sg, bsg = sgr.next()
                act(sg, ag, AF.Silu, (bag,), (bsg,))
                tt(actT[:, j, :], sg, av, ALU.mult, (bsg, bav), (b_actT[j],), eng="pool")

            for i in range(11):
                wU, bU = w_acquire("UP%d" % i)
                vU = wU.rearrange("p (k n) -> p k n", k=8)
                if i == 0:
                    pss = [psr.next() for _ in range(4)]
                    offs = [0, 256, 128, 384]
                    kouter([p[0] for p in pss], lambda q, k: vU[:, k, offs[q]:offs[q] + 128], (bU,),
                           [p[1] for p in pss])
                    ffn_epilogue(0, pss[0][0], pss[0][1], pss[1][0], pss[1][1])
                    ffn_epilogue(1, pss[2][0], pss[2][1], pss[3][0], pss[3][1])
                else:
                    for jj in range(2):
                        j = 2 * i + jj
                        pg, bpg = psr.next()
                        mm_group(pg, [(vU[:, k, jj * 128:(jj + 1) * 128], hT[:, k, :]) for k in range(8)],
                                 (bU, *b_hT), (bpg,))
                        pv, bpv = psr.next()
                        mm_group(pv, [(vU[:, k, 256 + jj * 128:256 + (jj + 1) * 128], hT[:, k, :]) for k in range(8)],
                                 (bU, *b_hT), (bpv,))
                        ffn_epilogue(j, pg, bpg, pv, bpv)
                w_release()
            _ck(7)
            if nxt is not None:
                nxr, nbxr = xres[nxt[0] % 2], b_xres[nxt[0] % 2]
            sqbs = {}
            for c in range(8):
                wDn, bDn = w_acquire("DN%d" % c)
                vDn = wDn[:, 0:2816].rearrange("p (k n) -> p k n", k=NJ)
                pd, bpd = psr.next()
                if c == 0:
                    mm_part(pd, [(vDn[:, j, :], actT[:, j, :]) for j in range(16)], (bDn, *b_actT[0:16]), (bpd,), True, False)
                    mm_part(pd, [(vDn[:, j, :], actT[:, j, :]) for j in range(16, NJ)], (bDn, *b_actT[16:NJ]), (bpd,), False, True)
                else:
                    mm_group(pd, [(vDn[:, j, :], actT[:, j, :]) for j in range(NJ)], (bDn, *b_actT), (bpd,))
                if nxt is not None and c >= 1:
                    rms_sq_mm(sqbs[c - 1], c - 1)
                tt(xr[:, c, :], pd, xr[:, c, :], ALU.add, (bpd, bxr[c]), (bxr[c],))
                if nxt is not None:
                    sqbs[c] = rms_sq_act(nxr, nbxr, c)
                w_release()
            if nxt is not None:
                rms_sq_mm(sqbs[7], 7)

            if last_layer:
                dst = yT_d[s_idx].rearrange("(c p) t -> p c t", p=128)[:, :, t0:t0 + T]
                S.dma("sp", "xs%d" % xi, lambda e: e.dma_start(out=dst, in_=xr[:, :, :]), tuple(bxr), ())

        plist = [(ti, li) for ti in range(n_tiles) for li in range(len(layers))]
        emit_xload(0)
        sq0 = [rms_sq_act(xres[0], b_xres[0], c) for c in range(4)]
        for c in range(8):
            rms_sq_mm(sq0[c] if c < 4 else rms_sq_act(xres[0], b_xres[0], c), c)
        try:
            for pi, (ti, li) in enumerate(plist):
                nxt = plist[pi + 1] if pi + 1 < len(plist) else None
                run_pass(ti, layers[li], li == 0, li == len(layers) - 1, nxt)
        except _Stop:
            dst = yT_d[0].rearrange("(c p) t -> p c t", p=128)[:, :, 0:T]
            S.dma("sp", "xs0", lambda e: e.dma_start(out=dst, in_=xres[0][:, :, :]), tuple(b_xres[0]), ())
        S.wait_all("sp", [b for i in range(2) for b in b_xres[i]])

        sems = {k: es.enter_context(nc.semaphore(k)) for k in S.semnames}
        block = es.enter_context(nc.Block())

        @block.tensor
        def _(e):
            S.replay("pe", e, sems)

        @block.scalar
        def _(e):
            S.replay("act", e, sems)

        @block.vector
        def _(e):
            S.replay("dve", e, sems)

        @block.gpsimd
        def _(e):
            S.replay("pool", e, sems)

        @block.sync
        def _(e):
            S.replay("sp", e, sems)
    return nc


def _pkn(w):
    kc = w.shape[0] // 128
    return np.ascontiguousarray(w.reshape(kc, 128, -1).transpose(1, 0, 2).reshape(128, -1))


def pack_weights(w_in, w_oa, w_ob, w_out, w_up, w_down):
    out = np.empty((NL, 128, WPL), np.float32)
    for l in range(NL):
        parts = {
            "A0": _pkn(w_in[l][:, 0:256]), "A1": _pkn(w_in[l][:, 256:512]), "B": _pkn(w_in[l][:, 512:768]),
            "C": _pkn(w_in[l][:, 768:1280]), "D": _pkn(w_in[l][:, 1280:1792]),
            "E": _pkn(w_in[l][:, 1792:2304]), "F": _pkn(w_in[l][:, 2304:2816]),
            "G": _pkn(w_in[l][:, 2816:3328]), "H": _pkn(w_in[l][:, 3328:3840]),
            "OAB0": np.concatenate([_pkn(w_oa[l][:, 0:512]), _pkn(w_ob[l][:, 0:512])], axis=1),
            "OAB1": np.concatenate([_pkn(w_oa[l][:, 512:1024]), _pkn(w_ob[l][:, 512:1024])], axis=1),
            "OUT0": _pkn(w_out[l][:, 0:512]), "OUT1": _pkn(w_out[l][:, 512:1024]),
        }
        for i in range(11):
            parts["UP%d" % i] = _pkn(np.concatenate(
                [w_up[l][:, 256 * i:256 * i + 256], w_up[l][:, DFF + 256 * i:DFF + 256 * i + 256]], axis=1))
        for c in range(8):
            parts["DN%d" % c] = _pkn(w_down[l][:, 128 * c:128 * c + 128])
        for nm, n in SLABS:
            off, _ = SLAB_OFF[nm]
            assert parts[nm].shape == (128, n), (nm, parts[nm].shape)
            out[l, :, off:off + n] = parts[nm]
    return out


def pack_consts(mix_norm, q_norm, k_norm, sinks, sgu_norm, w_s, b_s, ffn_norm, conv_w, conv_b):
    cst = np.zeros((128, NCST), np.float32)
    for l in range(NL):
        vb = C_VEC + 192 * l
        cst[:, vb:vb + 8] = mix_norm[l].reshape(8, 128).T
        cst[:, vb + 8:vb + 16] = ffn_norm[l].reshape(8, 128).T
        for tap in range(3):
            cst[:, vb + 16 + 44 * tap:vb + 16 + 44 * (tap + 1)] = conv_w[l, tap].reshape(44, 128).T
        cst[:, vb + 148:vb + 192] = conv_b[l].reshape(44, 128).T
        cst[0:64, C_QK + 2 * l] = q_norm[l]
        cst[0:64, C_QK + 2 * l + 1] = k_norm[l]
        cst[:, C_SGUG + 512 * l:C_SGUG + 512 * (l + 1)] = sgu_norm[l][None, :]
        for gp in range(4):
            cst[0:64, C_BSB + 512 * l + gp * 128:C_BSB + 512 * l + (gp + 1) * 128] = b_s[l, 2 * gp][None, :]
            cst[64:128, C_BSB + 512 * l + gp * 128:C_BSB + 512 * l + (gp + 1) * 128] = b_s[l, 2 * gp + 1][None, :]
    k = np.arange(128)[:, None]
    q = np.arange(128)[None, :]
    for g in range(2):
        for j in range(4):
            slope = 2.0 ** (-(4 * g + j + 1))
            dist_prev = q + 128 - k
            dist_cur = q - k
            bp = np.where(dist_prev < 128, -slope * dist_prev, -30000.0)
            bc = np.where(dist_cur >= 0, -slope * dist_cur, -30000.0)
            cst[:, C_ABIAS + (g * 2 + 0) * 512 + j * 128:C_ABIAS + (g * 2 + 0) * 512 + (j + 1) * 128] = bp
            cst[:, C_ABIAS + (g * 2 + 1) * 512 + j * 128:C_ABIAS + (g * 2 + 1) * 512 + (j + 1) * 128] = bc
    cst[:, C_TRIL:C_TRIL + 128] = (k <= q).astype(np.float32)
    srow = np.zeros((1, 2048), np.float32)
    for l in range(NL):
        for g in range(2):
            for sl in range(2):
                for pr in range(2):
                    base = (((l * 2 + g) * 2 + sl) * 2 + pr) * 128
                    srow[0, base:base + 128] = sinks[l, 4 * g + 2 * pr + sl]
    wst = np.ascontiguousarray(np.transpose(w_s, (3, 0, 1, 2)).reshape(128, NL * 8 * 128)).astype(np.float32)
    return cst, srow, wst


_NC_CACHE = {}
DBG_STOP = None


class _Stop(Exception):
    pass


_CUR = [0, 0]


def _ck(k):
    if DBG_STOP is not None and DBG_STOP == (_CUR[0], _CUR[1], k):
        raise _Stop()


def run(x, params, n_cores, layers=(0, 1)):
    B, S_, _ = x.shape
    n_seq = B // n_cores
    key = (n_seq, S_, tuple(layers))
    if key not in _NC_CACHE:
        _NC_CACHE[key] = build(n_seq, S_, layers)
    nc = _NC_CACHE[key]
    wts = pack_weights(params["w_in"], params["w_oa"], params["w_ob"], params["w_out"], params["w_up"],
                       params["w_down"])
    cst, srow, wst = pack_consts(params["mix_norm"], params["q_norm"], params["k_norm"], params["sinks"],
                                 params["sgu_norm"], params["w_s"], params["b_s"], params["ffn_norm"],
                                 params["conv_w"], params["conv_b"])
    in_maps = []
    for c in range(n_cores):
        xc = np.ascontiguousarray(np.transpose(x[c * n_seq:(c + 1) * n_seq], (0, 2, 1)))
        in_maps.append({"xT": xc, "wts": wts, "cst": cst, "srow": srow, "wst": wst})
    res = run_bass_kernel_spmd(nc, in_maps, core_ids=list(range(n_cores)))
    outs = [np.transpose(r["yT"], (0, 2, 1)) for r in res.results]
    return np.ascontiguousarray(np.concatenate(outs, axis=0)).astype(np.float32)


def kernel(**inputs):
    inputs = {k: np.asarray(v) for k, v in inputs.items()}
    x = inputs.pop("x").astype(np.float32)
    params = {k: v.astype(np.float32) for k, v in inputs.items()}
    return run(x, params, 8)
```

```python
from contextlib import ExitStack

import numpy as np
import concourse.bass as bass
import concourse.mybir as mybir
from concourse.bass_utils import run_bass_kernel_spmd

F32 = mybir.dt.float32
BF16 = mybir.dt.bfloat16
AF = mybir.ActivationFunctionType
ALU = mybir.AluOpType

D = 1024
NL = 2
NH = 8
NKV = 2
HD = 64
DFF = 2816
NJ = DFF // 128
T = 512
NB = T // 128
EPS = 1e-6
NBUF = 7
SLAB = 4096

SLABS = ([("A0", 2048), ("B", 2048), ("A1", 2048), ("C", 4096), ("D", 4096), ("E", 4096), ("G", 4096),
          ("OAB0", 4096), ("F", 4096), ("H", 4096), ("OAB1", 4096), ("OUT0", 4096), ("OUT1", 4096)]
         + [("UP%d" % i, 4096) for i in range(11)] + [("DN%d" % c, 2816) for c in range(8)])
SLAB_OFF = {}
_o = 0
for _n, _s in SLABS:
    SLAB_OFF[_n] = (_o, _s)
    _o += _s
WPL = _o

C_VEC = 0
C_QK = 384
C_SGUG = 392
C_BSB = C_SGUG + 1024
C_ABIAS = C_BSB + 1024
C_TRIL = C_ABIAS + 2048
NCST = C_TRIL + 128


class Buf:
    __slots__ = ("name", "lw", "rd", "excl")

    def __init__(self, name, excl=False):
        self.name = name
        self.lw = None
        self.rd = []
        self.excl = excl


class Sched:
    ENG = ("pe", "act", "dve", "pool", "sp")

    def __init__(self):
        self.streams = {e: [] for e in self.ENG}
        self.cnt = {}
        self.waited = {e: {} for e in self.ENG}
        self.semnames = []

    def newsem(self, key):
        self.cnt[key] = 0
        self.semnames.append(key)

    def _waits(self, eng, reads, writes):
        deps = {}
        def add(ev, raw):
            if ev is None:
                return
            k, v = ev
            if k == eng and (eng == "pe" or not raw):
                return
            if deps.get(k, 0) < v:
                deps[k] = v
        for b in reads:
            add(b.lw, True)
            if b.excl:
                for ev in b.rd:
                    add(ev, False)
        for b in writes:
            add(b.lw, False)
            for ev in b.rd:
                add(ev, False)
        for k, v in deps.items():
            if self.waited[eng].get(k, 0) >= v:
                continue
            self.waited[eng][k] = v
            self.streams[eng].append(("wait", k, v))

    def _commit(self, ev, reads, writes):
        for b in reads:
            b.rd.append(ev)
        for b in writes:
            b.lw = ev
            b.rd = []

    def op(self, eng, fn, reads=(), writes=()):
        self._waits(eng, reads, writes)
        self.cnt[eng] += 1
        self.streams[eng].append(("op", fn, eng, 1))
        self._commit((eng, self.cnt[eng]), reads, writes)

    def dma(self, eng, sem, fn, reads=(), writes=()):
        self._waits(eng, reads, writes)
        self.cnt[sem] += 16
        self.streams[eng].append(("op", fn, sem, 16))
        self._commit((sem, self.cnt[sem]), reads, writes)

    def wait_all(self, eng, bufs):
        self._waits(eng, bufs, bufs)

    def replay(self, eng, handle, sems):
        for it in self.streams[eng]:
            if it[0] == "wait":
                handle.wait_ge(sems[it[1]], it[2])
            else:
                ins = it[1](handle)
                ins.then_inc(sems[it[2]], it[3])


class Rot:
    def __init__(self, tiles, name):
        self.tiles = tiles
        self.bufs = [Buf("%s%d" % (name, i)) for i in range(len(tiles))]
        self.i = 0

    def next(self):
        i = self.i
        self.i = (i + 1) % len(self.tiles)
        return self.tiles[i], self.bufs[i]


def build(n_seq, seq_len, layers=(0, 1)):
    nc = bass.Bass("TRN2", target_bir_lowering=False)
    S = Sched()
    tiles_per_seq = seq_len // T
    n_tiles = n_seq * tiles_per_seq
    xT_d = nc.dram_tensor("xT", [n_seq, D, seq_len], F32, kind="ExternalInput").ap()
    wts_d = nc.dram_tensor("wts", [NL, 128, WPL], F32, kind="ExternalInput").ap()
    cst_d = nc.dram_tensor("cst", [128, NCST], F32, kind="ExternalInput").ap()
    srow_d = nc.dram_tensor("srow", [1, 2048], F32, kind="ExternalInput").ap()
    wst_d = nc.dram_tensor("wst", [128, 2048], F32, kind="ExternalInput").ap()
    yT_d = nc.dram_tensor("yT", [n_seq, D, seq_len], F32, kind="ExternalOutput").ap()

    es = ExitStack()
    with es:
        def sb(name, shape, dt):
            return es.enter_context(nc.sbuf_tensor("s_" + name, shape, dt))

        cst = sb("cst", [128, NCST], F32)
        srow = sb("srow", [128, 2048], BF16)
        wsT = sb("wsT", [128, 16, 128], BF16)
        ones = sb("ones", [128, 128], BF16)
        xres = [sb("xres%d" % i, [128, 8, T], F32) for i in range(2)]
        hT = sb("hT", [128, 8, T], BF16)
        sqt = sb("sqt", [128, 4, T], BF16)
        NFR = 10
        frt = sb("frt", [128, NFR, 512], F32)
        qT = sb("qT", [64, 8, T], BF16)
        kT = [sb("kT%d" % l, [64, 2, T + 128], BF16) for l in range(NL)]
        vtok = [sb("vtok%d" % l, [128, NB + 1, 128], BF16) for l in range(NL)]
        qsq = sb("qsq", [64, 5, T], BF16)
        junk = sb("junk", [128, 512], BF16)
        ssv = sb("ssv", [128, 8], F32)
        lnv = sb("lnv", [128, 8], F32)
        rv = sb("rv", [128, 8], F32)
        vn = sb("vn", [128, NB, 512], BF16)
        pT = sb("pT", [128, 6, 512], BF16)
        rden = sb("rden", [128, 2, 256], F32)
        actT = sb("actT", [128, NJ, T], BF16)
        mergedT = actT[:, 0:8, :]
        yattT = actT[:, 8:12, :]
        ysguT = actT[:, 12:16, :]
        uT = actT[:, 16:20, :]
        halo = [sb("halo%d" % l, [128, 2 * NJ, 2], F32) for l in range(NL)]
        wslab = sb("wslab", [128, NBUF, SLAB], BF16)
        psb = [es.enter_context(nc.psum_tensor("ps%d" % i, [128, 512], F32)) for i in range(8)]

        b_cst = Buf("cst")
        b_srow = Buf("srow")
        b_wsT = Buf("wsT")
        b_ones = Buf("ones")
        b_xres = [[Buf("xres%d_%d" % (i, c)) for c in range(8)] for i in range(2)]
        b_hT = [Buf("hT%d" % c) for c in range(8)]
        b_qT = [Buf("qT%d" % h) for h in range(8)]
        b_kprev = [Buf("kprev%d" % l) for l in range(NL)]
        b_kcur = [[Buf("kcur%d_%d" % (l, g)) for g in range(2)] for l in range(NL)]
        b_vprev = [Buf("vprev%d" % l) for l in range(NL)]
        b_vcur = [Buf("vcur%d" % l) for l in range(NL)]
        b_junk = Buf("junk")
        b_ssv = [Buf("ssv%d" % i) for i in range(8)]
        b_lnv = [Buf("lnv%d" % i) for i in range(8)]
        b_rv = [Buf("rv%d" % i) for i in range(8)]
        b_vn = [Buf("vn%d" % b) for b in range(NB)]
        b_actT = [Buf("actT%d" % j) for j in range(NJ)]
        b_merged = b_actT[0:8]
        b_yatt = [[b_actT[8 + c]] * NB for c in range(4)]
        b_uT = b_actT[16:20]
        b_halo = [[Buf("halo%d_%d" % (l, ch)) for ch in range(2 * NJ)] for l in range(NL)]
        b_wslab = [Buf("wslab%d" % i) for i in range(NBUF)]
        b_ps = [Buf("ps%d" % i, excl=True) for i in range(8)]

        sqr = Rot([sqt[:, i, :] for i in range(4)], "sq")
        qsqr = Rot([qsq[:, i, :] for i in range(5)], "qsq")
        fr = Rot([frt[:, i, :] for i in range(NFR)], "fr")
        rqr = rq2r = gvr = efr = tmr = sar = sbr = t1r = t2r = agr = avr = sgr = fr
        srowf = frt[0:1, 0:4, :]
        wstage = frt[:, 4:8, :]
        pTr = Rot([pT[:, i, :] for i in range(6)], "pT")
        rdr = Rot([rden[:, i, :] for i in range(2)], "rden")
        psr = Rot([p[:, :] for p in psb[0:7]], "psr")
        psr.bufs = b_ps[0:7]
        ss_ps, b_ss = psb[7][:, :], b_ps[7]
        small_i = [0]

        for e in ("pe", "act", "dve", "pool"):
            S.newsem(e)
        for i in range(NBUF):
            S.newsem("w%d" % i)
        for k in ("cst0", "cst1", "cst2", "xl0", "xl1", "xs0", "xs1"):
            S.newsem(k)

        def act(out, in_, func, reads, writes, bias=None, scale=None, accum_out=None):
            kw = {}
            if bias is not None:
                kw["bias"] = bias
            if scale is not None:
                kw["scale"] = scale
            if accum_out is not None:
                kw["accum_out"] = accum_out
            S.op("act", lambda e: e.activation(out=out, in_=in_, func=func, **kw), reads, writes)

        def tt(out, in0, in1, op, reads, writes, eng="dve"):
            S.op(eng, lambda e: e.tensor_tensor(out=out, in0=in0, in1=in1, op=op), reads, writes)

        def stt(out, in0, scalar, in1, op0, op1, reads, writes, eng="dve"):
            S.op(eng, lambda e: e.scalar_tensor_tensor(out=out, in0=in0, scalar=scalar, in1=in1,
                                                        op0=op0, op1=op1), reads, writes)

        def cp(out, in_, reads, writes, eng="dve"):
            S.op(eng, lambda e: e.tensor_copy(out=out, in_=in_), reads, writes)

        def mm_group(out, pairs, reads, writes):
            def fn(e):
                ins = None
                n = len(pairs)
                for i, (l, r) in enumerate(pairs):
                    ins = e.matmul(out, l, r, start=(i == 0), stop=(i == n - 1))
                return ins
            S.op("pe", fn, reads, writes)

        def mm_part(out, pairs, reads, writes, first, last):
            def fn(e):
                ins = None
                n = len(pairs)
                for i, (l, r) in enumerate(pairs):
                    ins = e.matmul(out, l, r, start=(first and i == 0), stop=(last and i == n - 1))
                return ins
            S.op("pe", fn, reads, writes)

        def mm_multi(groups, reads, writes):
            def fn(e):
                ins = None
                for out, pairs in groups:
                    n = len(pairs)
                    for i, (l, r) in enumerate(pairs):
                        ins = e.matmul(out, l, r, start=(i == 0), stop=(i == n - 1))
                return ins
            S.op("pe", fn, reads, writes)

        passes = [(ti, l) for ti in range(n_tiles) for l in layers]
        wseq = [(l, nm) for (_, l) in passes for (nm, _) in SLABS]
        wstate = {"issue": 0, "acq": 0}

        def w_issue():
            i = wstate["issue"]
            if i >= len(wseq):
                return
            wstate["issue"] = i + 1
            l, nm = wseq[i]
            off, n = SLAB_OFF[nm]
            slot = i % NBUF
            o = wslab[:, slot, 0:n]
            src = wts_d[l, :, off:off + n]
            S.dma("pool", "w%d" % slot, lambda e: e.dma_start(out=o, in_=src), (), (b_wslab[slot],))

        def w_acquire(expect):
            i = wstate["acq"]
            wstate["acq"] = i + 1
            assert wseq[i][1] == expect, (wseq[i], expect)
            slot = i % NBUF
            return wslab[:, slot, :], b_wslab[slot]

        def w_release(n=1):
            for _ in range(n):
                w_issue()

        S.dma("sp", "cst0", lambda e: e.dma_start(out=cst[:, :], in_=cst_d[:, :]), (), (b_cst,))
        S.dma("sp", "cst1", lambda e: e.dma_start(out=srowf, in_=srow_d.rearrange("o (a n) -> o a n", a=4)),
              (), tuple(fr.bufs[0:4]))
        S.dma("sp", "cst2", lambda e: e.dma_start(out=wstage, in_=wst_d.rearrange("p (a n) -> p a n", a=4)),
              (), tuple(fr.bufs[4:8]))
        for _ in range(NBUF):
            w_issue()
        S.op("dve", lambda e: e.memset(ones[:, :], 1.0), (), (b_ones,))
        S.op("dve", lambda e: e.memset(srow[:, :], 0.0), (), (b_srow,))
        act(srow[0:1, :].rearrange("o (a n) -> o a n", a=4), srowf, AF.Exp, tuple(fr.bufs[0:4]), (b_srow,))
        act(cst[:, C_ABIAS:C_ABIAS + 2048], cst[:, C_ABIAS:C_ABIAS + 2048], AF.Exp, (b_cst,), (b_cst,))
        for i in range(16):
            tt(wsT[:, i, :], wstage[:, i // 4, (i % 4) * 128:(i % 4 + 1) * 128], cst[:, C_TRIL:C_TRIL + 128], ALU.mult,
               (fr.bufs[4 + i // 4], b_cst), (b_wsT,))

        def rms_sq_act(xr, bxr, c):
            sq, bsq = sqr.next()
            act(sq, xr[:, c, :], AF.Square, (bxr[c],), (bsq,))
            return sq, bsq

        def rms_sq_mm(sqb, c):
            sq, bsq = sqb
            S.op("pe", (lambda e: e.matmul(ss_ps, ones[:, :], sq, start=(c == 0), stop=(c == 7))),
                 (bsq, b_ones), (b_ss,))

        def rms_finish(xr, bxr, gcol):
            ps, bps = ss_ps, b_ss
            rtmp, b_rtmp = fr.next()
            act(rtmp, ps, AF.Ln, (bps,), (b_rtmp,), bias=EPS, scale=1.0 / D)
            rstd, b_rstd = fr.next()
            act(rstd, rtmp, AF.Exp, (b_rtmp,), (b_rstd,), scale=-0.5)
            for c in range(8):
                stt(hT[:, c, :], xr[:, c, :], cst[:, gcol + c:gcol + c + 1], rstd, ALU.mult, ALU.mult,
                    (bxr[c], b_rstd, b_cst), (b_hT[c],))

        def headnorm_a(ps, bps, gcolumn, out, bout):
            sq, bsq = qsqr.next()
            act(sq[0:64, :], ps[0:64, :], AF.Square, (bps,), (bsq,))
            return lambda: headnorm_b(ps, bps, gcolumn, out, bout, sq, bsq)

        def headnorm_b(ps, bps, gcolumn, out, bout, sq, bsq):
            ps2, bps2 = psr.next()
            S.op("pe", lambda e: e.matmul(ps2[0:64, :], ones[0:64, 0:64], sq[0:64, :], start=True, stop=True),
                 (bsq, b_ones), (bps2,))
            r1, br1 = rqr.next()
            act(r1[0:64, :], ps2[0:64, :], AF.Ln, (bps2,), (br1,), bias=EPS, scale=1.0 / HD)
            r2, br2 = rq2r.next()
            act(r2[0:64, :], r1[0:64, :], AF.Exp, (br1,), (br2,), scale=-0.5)
            stt(out, ps[0:64, :], cst[0:64, gcolumn:gcolumn + 1], r2[0:64, :], ALU.mult, ALU.mult,
                (bps, br2, b_cst), (bout,))

        def emit_xload(ti):
            s_idx = ti // tiles_per_seq
            t0 = (ti % tiles_per_seq) * T
            xi = ti % 2
            src = xT_d[s_idx].rearrange("(c p) t -> p c t", p=128)[:, :, t0:t0 + T]
            dstt = xres[xi][:, :, :]
            S.dma("sp", "xl%d" % xi, lambda e: e.dma_start(out=dstt, in_=src), (), tuple(b_xres[xi]))

        def kouter(outs, lhs_fn, reads_w, wr_bufs):
            for k in range(8):
                groups = [(o, lhs_fn(i, k), hT[:, k, :]) for i, o in enumerate(outs)]
                def fn(e, groups=groups, k=k):
                    ins = None
                    for (o, l_, r_) in groups:
                        ins = e.matmul(o, l_, r_, start=(k == 0), stop=(k == 7))
                    return ins
                S.op("pe", fn, (*reads_w, b_hT[k]), tuple(wr_bufs))

        def run_pass(ti, l, first_layer, last_layer, nxt):
            _CUR[0], _CUR[1] = ti, l
            s_idx = ti // tiles_per_seq
            tt_i = ti % tiles_per_seq
            t0 = tt_i * T
            first_in_seq = tt_i == 0
            last_in_seq = tt_i == tiles_per_seq - 1
            xi = ti % 2
            xr = xres[xi]
            bxr = b_xres[xi]
            vbase = C_VEC + 192 * l
            G1, G2 = vbase, vbase + 8
            CW0, CW1, CW2, CB = vbase + 16, vbase + 60, vbase + 104, vbase + 148
            QG, KG = C_QK + 2 * l, C_QK + 2 * l + 1

            _ck(0)
            rms_finish(xr, bxr, G1)
            _ck(1)

            pend_hn = []

            def flush_hn():
                while pend_hn:
                    pend_hn.pop(0)()

            def proj_head(vW, bW, cols, gcolumn, out, bout):
                ps, bps = psr.next()
                mm_group(ps[0:64, :], [(vW[:, k, cols], hT[:, k, :]) for k in range(8)], (bW, *b_hT), (bps,))
                part_b = headnorm_a(ps, bps, gcolumn, out, bout)
                flush_hn()
                pend_hn.append(part_b)

            wA0, bA0 = w_acquire("A0")
            vA0 = wA0[:, 0:2048].rearrange("p (k n) -> p k n", k=8)
            qps = [psr.next() for _ in range(4)]
            kouter([p[0][0:64, :] for p in qps], lambda i, k: vA0[:, k, i * 64:(i + 1) * 64], (bA0,),
                   [p[1] for p in qps])
            for h in range(4):
                pend_hn.append(headnorm_a(qps[h][0], qps[h][1], QG, qT[:, h, :], b_qT[h]))
            w_release()
            _ck(2)

            wB, bB = w_acquire("B")
            vB = wB[:, 0:2048].rearrange("p (k n) -> p k n", k=8)
            for g in range(2):
                proj_head(vB, bB, slice(g * 64, (g + 1) * 64), KG, kT[l][:, g, 128:128 + T], b_kcur[l][g])
            ps, bps = psr.next()
            mm_multi([(ps[:, b * 128:(b + 1) * 128],
                       [(hT[:, k, b * 128:(b + 1) * 128], vB[:, k, 128:256]) for k in range(8)])
                      for b in range(NB)], (bB, *b_hT), (bps,))
            flush_hn()
            cp(vtok[l][:, 1:NB + 1, :], ps.rearrange("p (b n) -> p b n", b=NB), (bps,), (b_vcur[l],))
            w_release()

            def sgu_block(b):
                ps, bps = psr.next()
                groups = []
                for gp in range(4):
                    for sl in range(2):
                        g = 2 * gp + sl
                        groups.append((ps[64 * sl:64 * sl + 64, gp * 128:(gp + 1) * 128],
                                       [(vn[:, b, g * 64:(g + 1) * 64], wsT[:, l * 8 + g, :])]))
                mm_multi(groups, (b_vn[b], b_wsT), (bps,))
                tm, btm = tmr.next()
                tt(tm, ps, cst[:, C_BSB + 512 * l:C_BSB + 512 * l + 512], ALU.add, (bps, b_cst), (btm,))
                tt(ysguT[:, :, b * 128:(b + 1) * 128], tm.rearrange("p (a n) -> p a n", a=4),
                   uT[:, :, b * 128:(b + 1) * 128], ALU.mult, (btm, *b_uT), tuple(b_actT[12:16]))

            def att_stage1(b, g):
                halves = []
                if not (first_in_seq and b == 0):
                    halves.append(0)
                halves.append(1)
                pts = {}
                for hf in halves:
                    ps, bps = psr.next()
                    kcols = slice(128 * (b + hf), 128 * (b + hf) + 128)
                    kb = [b_kcur[l][g]] + ([b_kprev[l]] if (b == 0 and hf == 0) else [])
                    lhs_ = kT[l][:, g, kcols]
                    rhs_ = qT[:, 4 * g:4 * g + 4, b * 128:(b + 1) * 128]
                    out_ = ps.rearrange("p (a n) -> p a n", a=4)
                    S.op("pe", (lambda e, out_=out_, lhs_=lhs_, rhs_=rhs_: e.matmul(
                        out_, lhs_, rhs_, start=True, stop=True)),
                        (*kb, *b_qT[4 * g:4 * g + 4]), (bps,))
                    e_, be_ = efr.next()
                    act(e_, ps, AF.Exp, (bps,), (be_,), scale=0.125)
                    p_, bp_ = pTr.next()
                    col = C_ABIAS + (g * 2 + hf) * 512
                    tt(p_, e_, cst[:, col:col + 512], ALU.mult, (be_, b_cst), (bp_,), eng="pool")
                    pts[hf] = (p_, bp_)
                return halves, pts

            def att_stage2(b, g, halves, pts):
                yd, byd = psr.next()
                groups = []
                sbase = ((l * 2 + g) * 2) * 256
                for sl in range(2):
                    ypairs, dpairs = [], []
                    for hf in halves:
                        p_ = pts[hf][0]
                        rhs = p_.rearrange("p (pr s n) -> p pr s n", pr=2, s=2)[:, :, sl, :]
                        ypairs.append((vtok[l][:, b + hf, g * 64:(g + 1) * 64], rhs))
                        dpairs.append((ones[:, 0:64], rhs))
                    dpairs.append((ones[:, 0:64],
                                   srow[:, sbase + sl * 256:sbase + sl * 256 + 256].rearrange("p (a n) -> p a n", a=2)))
                    groups.append((yd[64 * sl:64 * sl + 64, 0:256].rearrange("p (a n) -> p a n", a=2), ypairs))
                    groups.append((yd[64 * sl:64 * sl + 64, 256:512].rearrange("p (a n) -> p a n", a=2), dpairs))
                vb = [b_vcur[l]] + ([b_vprev[l]] if b == 0 and 0 in halves else [])
                mm_multi(groups, (*[pts[hf][1] for hf in halves], *vb, b_ones, b_srow), (byd,))
                rd, brd = rdr.next()
                S.op("dve", lambda e, rd=rd, yd=yd: e.reciprocal(out=rd, in_=yd[:, 256:512]), (byd,), (brd,))
                tt(yattT[:, 2 * g:2 * g + 2, b * 128:(b + 1) * 128],
                   yd[:, 0:256].rearrange("p (a n) -> p a n", a=2),
                   rd.rearrange("p (a n) -> p a n", a=2), ALU.mult, (byd, brd),
                   (b_yatt[2 * g][b], b_yatt[2 * g + 1][b]))

            _ck(4)
            slabs = {}

            def get_slab(nm):
                if nm not in slabs:
                    w_, b_ = w_acquire(nm)
                    n_ = 2048 if nm == "A1" else 4096
                    slabs[nm] = (w_[:, 0:n_].rearrange("p (k n) -> p k n", k=8), b_)
                return slabs[nm]

            def u_qhead(h):
                vA1, bA1 = get_slab("A1")
                proj_head(vA1, bA1, slice((h - 4) * 64, (h - 3) * 64), QG, qT[:, h, :], b_qT[h])
                if h == 7:
                    w_release()

            def u_su(c):
                vC, bC = get_slab("C")
                flush_hn()
                ps, bps = psr.next()
                mm_group(ps, [(vC[:, k, c * 128:(c + 1) * 128], hT[:, k, :]) for k in range(8)],
                         (bC, *b_hT), (bps,))
                act(uT[:, c, :], ps, AF.Gelu_apprx_tanh, (bps,), (b_uT[c],))
                if c == 3:
                    w_release()

            def u_sv(b):
                vD, bD = get_slab("D")
                ps, bps = psr.next()
                mm_group(ps, [(hT[:, k, b * 128:(b + 1) * 128], vD[:, k, :]) for k in range(8)],
                         (bD, *b_hT), (bps,))
                g_, bg_ = gvr.next()
                act(g_, ps, AF.Gelu_apprx_tanh, (bps,), (bg_,))
                si = small_i[0]
                small_i[0] = (si + 1) % 8
                act(junk[:, :], g_, AF.Square, (bg_,), (b_junk, b_ssv[si]), accum_out=ssv[:, si:si + 1])
                act(lnv[:, si:si + 1], ssv[:, si:si + 1], AF.Ln, (b_ssv[si],), (b_lnv[si],), bias=EPS, scale=1.0 / 512)
                act(rv[:, si:si + 1], lnv[:, si:si + 1], AF.Exp, (b_lnv[si],), (b_rv[si],), scale=-0.5)
                stt(vn[:, b, :], g_, rv[:, si:si + 1], cst[:, C_SGUG + 512 * l:C_SGUG + 512 * l + 512],
                    ALU.mult, ALU.mult, (bg_, b_rv[si], b_cst), (b_vn[b],))
                if b == NB - 1:
                    w_release()

            units = ([lambda h=h: u_qhead(h) for h in range(4, 8)] + [lambda c=c: u_su(c) for c in range(4)]
                     + [lambda b=b: u_sv(b) for b in range(NB)])
            its = [(b, 0) for b in range(NB)] + [(b, 1) for b in range(NB)]
            pend = [att_stage1(*its[0]), att_stage1(*its[1])]
            for i, (b, g) in enumerate(its):
                if g == 0:
                    for _ in range(3):
                        units.pop(0)()
                else:
                    sgu_block(b)
                if i + 2 < len(its):
                    pend.append(att_stage1(*its[i + 2]))
                att_stage2(b, g, *pend.pop(0))
            assert not units
            _ck(3)
            if not last_in_seq:
                cp(kT[l][:, :, 0:128], kT[l][:, :, T:T + 128], (*b_kcur[l],), (b_kprev[l],))
                cp(vtok[l][:, 0, :], vtok[l][:, NB, :], (b_vcur[l],), (b_vprev[l],))

            _ck(5)
            for half in range(2):
                wE, bE = w_acquire("EF"[half])
                wG, bG = w_acquire("GH"[half])
                wO, bO = w_acquire("OAB%d" % half)
                vE = wE.rearrange("p (k n) -> p k n", k=8)
                vG = wG.rearrange("p (k n) -> p k n", k=8)
                vO = wO.rearrange("p (m k n) -> p m k n", m=2, k=4)
                for c4 in range(4):
                    c = 4 * half + c4
                    cs = slice(c4 * 128, (c4 + 1) * 128)
                    pga, bpga = psr.next()
                    mm_group(pga, [(vE[:, k, cs], hT[:, k, :]) for k in range(8)], (bE, *b_hT), (bpga,))
                    pgb, bpgb = psr.next()
                    mm_group(pgb, [(vG[:, k, cs], hT[:, k, :]) for k in range(8)], (bG, *b_hT), (bpgb,))
                    pa, bpa = psr.next()
                    mm_group(pa, [(vO[:, 0, kc, cs], yattT[:, kc, :]) for kc in range(4)],
                             (bO, *b_actT[8:12]), (bpa,))
                    pb, bpb = psr.next()
                    mm_group(pb, [(vO[:, 1, kc, cs], ysguT[:, kc, :]) for kc in range(4)], (bO, *b_actT[12:16]), (bpb,))
                    sa, bsa = sar.next()
                    act(sa, pga, AF.Sigmoid, (bpga,), (bsa,))
                    sb_, bsb_ = sbr.next()
                    act(sb_, pgb, AF.Sigmoid, (bpgb,), (bsb_,))
                    t1, bt1 = t1r.next()
                    tt(t1, pa, sa, ALU.mult, (bpa, bsa), (bt1,))
                    t2, bt2 = t2r.next()
                    tt(t2, pb, sb_, ALU.mult, (bpb, bsb_), (bt2,))
                    tt(mergedT[:, c, :], t1, t2, ALU.add, (bt1, bt2), (b_merged[c],), eng="pool")
                w_release(3)
            sqbs = {}
            for half in range(2):
                wO, bO = w_acquire("OUT%d" % half)
                vO = wO.rearrange("p (k n) -> p k n", k=8)
                for c4 in range(4):
                    c = 4 * half + c4
                    po, bpo = psr.next()
                    mm_group(po, [(vO[:, k, c4 * 128:(c4 + 1) * 128], mergedT[:, k, :]) for k in range(8)],
                             (bO, *b_merged), (bpo,))
                    if c >= 1:
                        rms_sq_mm(sqbs[c - 1], c - 1)
                    tt(xr[:, c, :], po, xr[:, c, :], ALU.add, (bpo, bxr[c]), (bxr[c],))
                    sqbs[c] = rms_sq_act(xr, bxr, c)
                w_release()
            rms_sq_mm(sqbs[7], 7)

            _ck(6)
            rms_finish(xr, bxr, G2)
            if last_layer and nxt is not None:
                emit_xload(nxt[0])

            def ffn_epilogue(j, pg, bpg, pv, bpv):
                ag, bag = agr.next()
                av, bav = avr.next()
                items = ((pg, bpg, ag, bag, j), (pv, bpv, av, bav, NJ + j))
                for (ps, bps, a_, ba_, ch) in items:
                    act(a_, ps, AF.Identity, (bps, b_cst), (ba_,),
                        bias=cst[:, CB + ch:CB + ch + 1], scale=cst[:, CW2 + ch:CW2 + ch + 1])
                for (ps, bps, a_, ba_, ch) in items:
                    stt(a_[:, 1:T], ps[:, 0:T - 1], cst[:, CW1 + ch:CW1 + ch + 1], a_[:, 1:T], ALU.mult, ALU.add,
                        (bps, ba_, b_cst), (ba_,))
                for (ps, bps, a_, ba_, ch) in items:
                    stt(a_[:, 2:T], ps[:, 0:T - 2], cst[:, CW0 + ch:CW0 + ch + 1], a_[:, 2:T], ALU.mult, ALU.add,
                        (bps, ba_, b_cst), (ba_,))
                if not first_in_seq:
                    for (ps, bps, a_, ba_, ch) in items:
                        stt(a_[:, 0:2], halo[l][:, ch, 0:2], cst[:, CW0 + ch:CW0 + ch + 1], a_[:, 0:2],
                            ALU.mult, ALU.add, (b_halo[l][ch], ba_, b_cst), (ba_,))
                    for (ps, bps, a_, ba_, ch) in items:
                        stt(a_[:, 0:1], halo[l][:, ch, 1:2], cst[:, CW1 + ch:CW1 + ch + 1], a_[:, 0:1],
                            ALU.mult, ALU.add, (b_halo[l][ch], ba_, b_cst), (ba_,))
                if not last_in_seq:
                    for (ps, bps, a_, ba_, ch) in items:
                        act(halo[l][:, ch, :], ps[:, T - 2:T], AF.Identity, (bps,), (b_halo[l][ch],))
                sg, bsg = sgr.next()
                act(sg, ag, AF.Silu, (bag,), (bsg,))
                tt(actT[:, j, :], sg, av, ALU.mult, (bsg, bav), (b_actT[j],), eng="pool")

            for i in range(11):
                wU, bU = w_acquire("UP%d" % i)
                vU = wU.rearrange("p (k n) -> p k n", k=8)
                if i == 0:
                    pss = [psr.next() for _ in range(4)]
                    offs = [0, 256, 128, 384]
                    kouter([p[0] for p in pss], lambda q, k: vU[:, k, offs[q]:offs[q] + 128], (bU,),
                           [p[1] for p in pss])
                    ffn_epilogue(0, pss[0][0], pss[0][1], pss[1][0], pss[1][1])
                    ffn_epilogue(1, pss[2][0], pss[2][1], pss[3][0], pss[3][1])
                else:
                    for jj in range(2):
                        j = 2 * i + jj
                        pg, bpg = psr.next()
                        mm_group(pg, [(vU[:, k, jj * 128:(jj + 1) * 128], hT[:, k, :]) for k in range(8)],
                                 (bU, *b_hT), (bpg,))
                        pv, bpv = psr.next()
                        mm_group(pv, [(vU[:, k, 256 + jj * 128:256 + (jj + 1) * 128], hT[:, k, :]) for k in range(8)],
                                 (bU, *b_hT), (bpv,))
                        ffn_epilogue(j, pg, bpg, pv, bpv)
                w_release()
            _ck(7)
            if nxt is not None:
                nxr, nbxr = xres[nxt[0] % 2], b_xres[nxt[0] % 2]
            sqbs = {}
            for c in range(8):
                wDn, bDn = w_acquire("DN%d" % c)
                vDn = wDn[:, 0:2816].rearrange("p (k n) -> p k n", k=NJ)
                pd, bpd = psr.next()
                if c == 0:
                    mm_part(pd, [(vDn[:, j, :], actT[:, j, :]) for j in range(16)], (bDn, *b_actT[0:16]), (bpd,), True, False)
                    mm_part(pd, [(vDn[:, j, :], actT[:, j, :]) for j in range(16, NJ)], (bDn, *b_actT[16:NJ]), (bpd,), False, True)
                else:
                    mm_group(pd, [(vDn[:, j, :], actT[:, j, :]) for j in range(NJ)], (bDn, *b_actT), (bpd,))
                if nxt is not None and c >= 1:
                    rms_sq_mm(sqbs[c - 1], c - 1)
                tt(xr[:, c, :], pd, xr[:, c, :], ALU.add, (bpd, bxr[c]), (bxr[c],))
                if nxt is not None:
                    sqbs[c] = rms_sq_act(nxr, nbxr, c)
                w_release()
            if nxt is not None:
                rms_sq_mm(sqbs[7], 7)

            if last_layer:
                dst = yT_d[s_idx].rearrange("(c p) t -> p c t", p=128)[:, :, t0:t0 + T]
                S.dma("sp", "xs%d" % xi, lambda e: e.dma_start(out=dst, in_=xr[:, :, :]), tuple(bxr), ())

        plist = [(ti, li) for ti in range(n_tiles) for li in range(len(layers))]
        emit_xload(0)
        sq0 = [rms_sq_act(xres[0], b_xres[0], c) for c in range(4)]
        for c in range(8):
            rms_sq_mm(sq0[c] if c < 4 else rms_sq_act(xres[0], b_xres[0], c), c)
        try:
            for pi, (ti, li) in enumerate(plist):
                nxt = plist[pi + 1] if pi + 1 < len(plist) else None
                run_pass(ti, layers[li], li == 0, li == len(layers) - 1, nxt)
        except _Stop:
            dst = yT_d[0].rearrange("(c p) t -> p c t", p=128)[:, :, 0:T]
            S.dma("sp", "xs0", lambda e: e.dma_start(out=dst, in_=xres[0][:, :, :]), tuple(b_xres[0]), ())
        S.wait_all("sp", [b for i in range(2) for b in b_xres[i]])

        sems = {k: es.enter_context(nc.semaphore(k)) for k in S.semnames}
        block = es.enter_context(nc.Block())

        @block.tensor
        def _(e):
            S.replay("pe", e, sems)

        @block.scalar
        def _(e):
            S.replay("act", e, sems)

        @block.vector
        def _(e):
            S.replay("dve", e, sems)

        @block.gpsimd
        def _(e):
            S.replay("pool", e, sems)

        @block.sync
        def _(e):
            S.replay("sp", e, sems)
    return nc


def _pkn(w):
    kc = w.shape[0] // 128
    return np.ascontiguousarray(w.reshape(kc, 128, -1).transpose(1, 0, 2).reshape(128, -1))


def pack_weights(w_in, w_oa, w_ob, w_out, w_up, w_down):
    out = np.empty((NL, 128, WPL), np.float32)
    for l in range(NL):
        parts = {
            "A0": _pkn(w_in[l][:, 0:256]), "A1": _pkn(w_in[l][:, 256:512]), "B": _pkn(w_in[l][:, 512:768]),
            "C": _pkn(w_in[l][:, 768:1280]), "D": _pkn(w_in[l][:, 1280:1792]),
            "E": _pkn(w_in[l][:, 1792:2304]), "F": _pkn(w_in[l][:, 2304:2816]),
            "G": _pkn(w_in[l][:, 2816:3328]), "H": _pkn(w_in[l][:, 3328:3840]),
            "OAB0": np.concatenate([_pkn(w_oa[l][:, 0:512]), _pkn(w_ob[l][:, 0:512])], axis=1),
            "OAB1": np.concatenate([_pkn(w_oa[l][:, 512:1024]), _pkn(w_ob[l][:, 512:1024])], axis=1),
            "OUT0": _pkn(w_out[l][:, 0:512]), "OUT1": _pkn(w_out[l][:, 512:1024]),
        }
        for i in range(11):
            parts["UP%d" % i] = _pkn(np.concatenate(
                [w_up[l][:, 256 * i:256 * i + 256], w_up[l][:, DFF + 256 * i:DFF + 256 * i + 256]], axis=1))
        for c in range(8):
            parts["DN%d" % c] = _pkn(w_down[l][:, 128 * c:128 * c + 128])
        for nm, n in SLABS:
            off, _ = SLAB_OFF[nm]
            assert parts[nm].shape == (128, n), (nm, parts[nm].shape)
            out[l, :, off:off + n] = parts[nm]
    return out


def pack_consts(mix_norm, q_norm, k_norm, sinks, sgu_norm, w_s, b_s, ffn_norm, conv_w, conv_b):
    cst = np.zeros((128, NCST), np.float32)
    for l in range(NL):
        vb = C_VEC + 192 * l
        cst[:, vb:vb + 8] = mix_norm[l].reshape(8, 128).T
        cst[:, vb + 8:vb + 16] = ffn_norm[l].reshape(8, 128).T
        for tap in range(3):
            cst[:, vb + 16 + 44 * tap:vb + 16 + 44 * (tap + 1)] = conv_w[l, tap].reshape(44, 128).T
        cst[:, vb + 148:vb + 192] = conv_b[l].reshape(44, 128).T
        cst[0:64, C_QK + 2 * l] = q_norm[l]
        cst[0:64, C_QK + 2 * l + 1] = k_norm[l]
        cst[:, C_SGUG + 512 * l:C_SGUG + 512 * (l + 1)] = sgu_norm[l][None, :]
        for gp in range(4):
            cst[0:64, C_BSB + 512 * l + gp * 128:C_BSB + 512 * l + (gp + 1) * 128] = b_s[l, 2 * gp][None, :]
            cst[64:128, C_BSB + 512 * l + gp * 128:C_BSB + 512 * l + (gp + 1) * 128] = b_s[l, 2 * gp + 1][None, :]
    k = np.arange(128)[:, None]
    q = np.arange(128)[None, :]
    for g in range(2):
        for j in range(4):
            slope = 2.0 ** (-(4 * g + j + 1))
            dist_prev = q + 128 - k
            dist_cur = q - k
            bp = np.where(dist_prev < 128, -slope * dist_prev, -30000.0)
            bc = np.where(dist_cur >= 0, -slope * dist_cur, -30000.0)
            cst[:, C_ABIAS + (g * 2 + 0) * 512 + j * 128:C_ABIAS + (g * 2 + 0) * 512 + (j + 1) * 128] = bp
            cst[:, C_ABIAS + (g * 2 + 1) * 512 + j * 128:C_ABIAS + (g * 2 + 1) * 512 + (j + 1) * 128] = bc
    cst[:, C_TRIL:C_TRIL + 128] = (k <= q).astype(np.float32)
    srow = np.zeros((1, 2048), np.float32)
    for l in range(NL):
        for g in range(2):
            for sl in range(2):
                for pr in range(2):
                    base = (((l * 2 + g) * 2 + sl) * 2 + pr) * 128
                    srow[0, base:base + 128] = sinks[l, 4 * g + 2 * pr + sl]
    wst = np.ascontiguousarray(np.transpose(w_s, (3, 0, 1, 2)).reshape(128, NL * 8 * 128)).astype(np.float32)
    return cst, srow, wst


_NC_CACHE = {}
DBG_STOP = None


class _Stop(Exception):
    pass


_CUR = [0, 0]


def _ck(k):
    if DBG_STOP is not None and DBG_STOP == (_CUR[0], _CUR[1], k):
        raise _Stop()


def run(x, params, n_cores, layers=(0, 1)):
    B, S_, _ = x.shape
    n_seq = B // n_cores
    key = (n_seq, S_, tuple(layers))
    if key not in _NC_CACHE:
        _NC_CACHE[key] = build(n_seq, S_, layers)
    nc = _NC_CACHE[key]
    wts = pack_weights(params["w_in"], params["w_oa"], params["w_ob"], params["w_out"], params["w_up"],
                       params["w_down"])
    cst, srow, wst = pack_consts(params["mix_norm"], params["q_norm"], params["k_norm"], params["sinks"],
                                 params["sgu_norm"], params["w_s"], params["b_s"], params["ffn_norm"],
                                 params["conv_w"], params["conv_b"])
    in_maps = []
    for c in range(n_cores):
        xc = np.ascontiguousarray(np.transpose(x[c * n_seq:(c + 1) * n_seq], (0, 2, 1)))
        in_maps.append({"xT": xc, "wts": wts, "cst": cst, "srow": srow, "wst": wst})
    res = run_bass_kernel_spmd(nc, in_maps, core_ids=list(range(n_cores)))
    outs = [np.transpose(r["yT"], (0, 2, 1)) for r in res.results]
    return np.ascontiguousarray(np.concatenate(outs, axis=0)).astype(np.float32)


def kernel(**inputs):
    inputs = {k: np.asarray(v) for k, v in inputs.items()}
    x = inputs.pop("x").astype(np.float32)
    params = {k: v.astype(np.float32) for k, v in inputs.items()}
    return run(x, params, 8)
```

```python
from contextlib import ExitStack

import numpy as np
import concourse.bass as bass
import concourse.mybir as mybir
from concourse.bass_utils import run_bass_kernel_spmd

F32 = mybir.dt.float32
BF16 = mybir.dt.bfloat16
AF = mybir.ActivationFunctionType
ALU = mybir.AluOpType

D = 1024
NL = 2
NH = 8
NKV = 2
HD = 64
DFF = 2816
NJ = DFF // 128
T = 512
NB = T // 128
EPS = 1e-6
NBUF = 7
SLAB = 4096

SLABS = ([("A0", 2048), ("B", 2048), ("A1", 2048), ("C", 4096), ("D", 4096), ("E", 4096), ("G", 4096),
          ("OAB0", 4096), ("F", 4096), ("H", 4096), ("OAB1", 4096), ("OUT0", 4096), ("OUT1", 4096)]
         + [("UP%d" % i, 4096) for i in range(11)] + [("DN%d" % c, 2816) for c in range(8)])
SLAB_OFF = {}
_o = 0
for _n, _s in SLABS:
    SLAB_OFF[_n] = (_o, _s)
    _o += _s
WPL = _o

C_VEC = 0
C_QK = 384
C_SGUG = 392
C_BSB = C_SGUG + 1024
C_ABIAS = C_BSB + 1024
C_TRIL = C_ABIAS + 2048
NCST = C_TRIL + 128


class Buf:
    __slots__ = ("name", "lw", "rd", "excl")

    def __init__(self, name, excl=False):
        self.name = name
        self.lw = None
        self.rd = []
        self.excl = excl


class Sched:
    ENG = ("pe", "act", "dve", "pool", "sp")

    def __init__(self):
        self.streams = {e: [] for e in self.ENG}
        self.cnt = {}
        self.waited = {e: {} for e in self.ENG}
        self.semnames = []

    def newsem(self, key):
        self.cnt[key] = 0
        self.semnames.append(key)

    def _waits(self, eng, reads, writes):
        deps = {}
        def add(ev, raw):
            if ev is None:
                return
            k, v = ev
            if k == eng and (eng == "pe" or not raw):
                return
            if deps.get(k, 0) < v:
                deps[k] = v
        for b in reads:
            add(b.lw, True)
            if b.excl:
                for ev in b.rd:
                    add(ev, False)
        for b in writes:
            add(b.lw, False)
            for ev in b.rd:
                add(ev, False)
        for k, v in deps.items():
            if self.waited[eng].get(k, 0) >= v:
                continue
            self.waited[eng][k] = v
            self.streams[eng].append(("wait", k, v))

    def _commit(self, ev, reads, writes):
        for b in reads:
            b.rd.append(ev)
        for b in writes:
            b.lw = ev
            b.rd = []

    def op(self, eng, fn, reads=(), writes=()):
        self._waits(eng, reads, writes)
        self.cnt[eng] += 1
        self.streams[eng].append(("op", fn, eng, 1))
        self._commit((eng, self.cnt[eng]), reads, writes)

    def dma(self, eng, sem, fn, reads=(), writes=()):
        self._waits(eng, reads, writes)
        self.cnt[sem] += 16
        self.streams[eng].append(("op", fn, sem, 16))
        self._commit((sem, self.cnt[sem]), reads, writes)

    def wait_all(self, eng, bufs):
        self._waits(eng, bufs, bufs)

    def replay(self, eng, handle, sems):
        for it in self.streams[eng]:
            if it[0] == "wait":
                handle.wait_ge(sems[it[1]], it[2])
            else:
                ins = it[1](handle)
                ins.then_inc(sems[it[2]], it[3])


class Rot:
    def __init__(self, tiles, name):
        self.tiles = tiles
        self.bufs = [Buf("%s%d" % (name, i)) for i in range(len(tiles))]
        self.i = 0

    def next(self):
        i = self.i
        self.i = (i + 1) % len(self.tiles)
        return self.tiles[i], self.bufs[i]


def build(n_seq, seq_len, layers=(0, 1)):
    nc = bass.Bass("TRN2", target_bir_lowering=False)
    S = Sched()
    tiles_per_seq = seq_len // T
    n_tiles = n_seq * tiles_per_seq
    xT_d = nc.dram_tensor("xT", [n_seq, D, seq_len], F32, kind="ExternalInput").ap()
    wts_d = nc.dram_tensor("wts", [NL, 128, WPL], F32, kind="ExternalInput").ap()
    cst_d = nc.dram_tensor("cst", [128, NCST], F32, kind="ExternalInput").ap()
    srow_d = nc.dram_tensor("srow", [1, 2048], F32, kind="ExternalInput").ap()
    wst_d = nc.dram_tensor("wst", [128, 2048], F32, kind="ExternalInput").ap()
    yT_d = nc.dram_tensor("yT", [n_seq, D, seq_len], F32, kind="ExternalOutput").ap()

    es = ExitStack()
    with es:
        def sb(name, shape, dt):
            return es.enter_context(nc.sbuf_tensor("s_" + name, shape, dt))

        cst = sb("cst", [128, NCST], F32)
        srow = sb("srow", [128, 2048], BF16)
        wsT = sb("wsT", [128, 16, 128], BF16)
        ones = sb("ones", [128, 128], BF16)
        xres = [sb("xres%d" % i, [128, 8, T], F32) for i in range(2)]
        hT = sb("hT", [128, 8, T], BF16)
        sqt = sb("sqt", [128, 4, T], BF16)
        NFR = 10
        frt = sb("frt", [128, NFR, 512], F32)
        qT = sb("qT", [64, 8, T], BF16)
        kT = [sb("kT%d" % l, [64, 2, T + 128], BF16) for l in range(NL)]
        vtok = [sb("vtok%d" % l, [128, NB + 1, 128], BF16) for l in range(NL)]
        qsq = sb("qsq", [64, 5, T], BF16)
        junk = sb("junk", [128, 512], BF16)
        ssv = sb("ssv", [128, 8], F32)
        lnv = sb("lnv", [128, 8], F32)
        rv = sb("rv", [128, 8], F32)
        vn = sb("vn", [128, NB, 512], BF16)
        pT = sb("pT", [128, 6, 512], BF16)
        rden = sb("rden", [128, 2, 256], F32)
        actT = sb("actT", [128, NJ, T], BF16)
        mergedT = actT[:, 0:8, :]
        yattT = actT[:, 8:12, :]
        ysguT = actT[:, 12:16, :]
        uT = actT[:, 16:20, :]
        halo = [sb("halo%d" % l, [128, 2 * NJ, 2], F32) for l in range(NL)]
        hc = sb("hc", [128, 2 * NJ, 2], F32)
        hctmp = sb("hctmp", [128, 2 * NJ], F32)
        b_hc = Buf("hc")
        b_hctmp = Buf("hctmp")
        wslab = sb("wslab", [128, NBUF, SLAB], BF16)
        psb = [es.enter_context(nc.psum_tensor("ps%d" % i, [128, 512], F32)) for i in range(8)]

        b_cst = Buf("cst")
        b_srow = Buf("srow")
        b_wsT = Buf("wsT")
        b_ones = Buf("ones")
        b_xres = [[Buf("xres%d_%d" % (i, c)) for c in range(8)] for i in range(2)]
        b_hT = [Buf("hT%d" % c) for c in range(8)]
        b_qT = [Buf("qT%d" % h) for h in range(8)]
        b_kprev = [Buf("kprev%d" % l) for l in range(NL)]
        b_kcur = [[Buf("kcur%d_%d" % (l, g)) for g in range(2)] for l in range(NL)]
        b_vprev = [Buf("vprev%d" % l) for l in range(NL)]
        b_vcur = [Buf("vcur%d" % l) for l in range(NL)]
        b_junk = Buf("junk")
        b_ssv = [Buf("ssv%d" % i) for i in range(8)]
        b_lnv = [Buf("lnv%d" % i) for i in range(8)]
        b_rv = [Buf("rv%d" % i) for i in range(8)]
        b_vn = [Buf("vn%d" % b) for b in range(NB)]
        b_actT = [Buf("actT%d" % j) for j in range(NJ)]
        b_merged = b_actT[0:8]
        b_yatt = [[b_actT[8 + c]] * NB for c in range(4)]
        b_uT = b_actT[16:20]
        b_halo = [[Buf("halo%d_%d" % (l, ch)) for ch in range(2 * NJ)] for l in range(NL)]
        b_wslab = [Buf("wslab%d" % i) for i in range(NBUF)]
        b_ps = [Buf("ps%d" % i, excl=True) for i in range(8)]

        sqr = Rot([sqt[:, i, :] for i in range(4)], "sq")
        qsqr = Rot([qsq[:, i, :] for i in range(5)], "qsq")
        fr = Rot([frt[:, i, :] for i in range(NFR)], "fr")
        rqr = rq2r = gvr = efr = tmr = sar = sbr = t1r = t2r = agr = avr = sgr = fr
        srowf = frt[0:1, 0:4, :]
        wstage = frt[:, 4:8, :]
        pTr = Rot([pT[:, i, :] for i in range(6)], "pT")
        rdr = Rot([rden[:, i, :] for i in range(2)], "rden")
        psr = Rot([p[:, :] for p in psb[0:7]], "psr")
        psr.bufs = b_ps[0:7]
        ss_ps, b_ss = psb[7][:, :], b_ps[7]
        small_i = [0]

        for e in ("pe", "act", "dve", "pool"):
            S.newsem(e)
        for i in range(NBUF):
            S.newsem("w%d" % i)
        for k in ("cst0", "cst1", "cst2", "xl0", "xl1", "xs0", "xs1"):
            S.newsem(k)

        def act(out, in_, func, reads, writes, bias=None, scale=None, accum_out=None):
            kw = {}
            if bias is not None:
                kw["bias"] = bias
            if scale is not None:
                kw["scale"] = scale
            if accum_out is not None:
                kw["accum_out"] = accum_out
            S.op("act", lambda e: e.activation(out=out, in_=in_, func=func, **kw), reads, writes)

        def tt(out, in0, in1, op, reads, writes, eng="dve"):
            S.op(eng, lambda e: e.tensor_tensor(out=out, in0=in0, in1=in1, op=op), reads, writes)

        def stt(out, in0, scalar, in1, op0, op1, reads, writes, eng="dve"):
            S.op(eng, lambda e: e.scalar_tensor_tensor(out=out, in0=in0, scalar=scalar, in1=in1,
                                                        op0=op0, op1=op1), reads, writes)

        def cp(out, in_, reads, writes, eng="dve"):
            S.op(eng, lambda e: e.tensor_copy(out=out, in_=in_), reads, writes)

        def mm_group(out, pairs, reads, writes):
            def fn(e):
                ins = None
                n = len(pairs)
                for i, (l, r) in enumerate(pairs):
                    ins = e.matmul(out, l, r, start=(i == 0), stop=(i == n - 1))
                return ins
            S.op("pe", fn, reads, writes)

        def mm_part(out, pairs, reads, writes, first, last):
            def fn(e):
                ins = None
                n = len(pairs)
                for i, (l, r) in enumerate(pairs):
                    ins = e.matmul(out, l, r, start=(first and i == 0), stop=(last and i == n - 1))
                return ins
            S.op("pe", fn, reads, writes)

        def mm_multi(groups, reads, writes):
            def fn(e):
                ins = None
                for out, pairs in groups:
                    n = len(pairs)
                    for i, (l, r) in enumerate(pairs):
                        ins = e.matmul(out, l, r, start=(i == 0), stop=(i == n - 1))
                return ins
            S.op("pe", fn, reads, writes)

        passes = [(ti, l) for ti in range(n_tiles) for l in layers]
        wseq = [(l, nm) for (_, l) in passes for (nm, _) in SLABS]
        wstate = {"issue": 0, "acq": 0}

        def w_issue():
            i = wstate["issue"]
            if i >= len(wseq):
                return
            wstate["issue"] = i + 1
            l, nm = wseq[i]
            off, n = SLAB_OFF[nm]
            slot = i % NBUF
            o = wslab[:, slot, 0:n]
            src = wts_d[l, :, off:off + n]
            S.dma("pool", "w%d" % slot, lambda e: e.dma_start(out=o, in_=src), (), (b_wslab[slot],))

        def w_acquire(expect):
            i = wstate["acq"]
            wstate["acq"] = i + 1
            assert wseq[i][1] == expect, (wseq[i], expect)
            slot = i % NBUF
            return wslab[:, slot, :], b_wslab[slot]

        def w_release(n=1):
            for _ in range(n):
                w_issue()

        S.dma("sp", "cst0", lambda e: e.dma_start(out=cst[:, :], in_=cst_d[:, :]), (), (b_cst,))
        S.dma("sp", "cst1", lambda e: e.dma_start(out=srowf, in_=srow_d.rearrange("o (a n) -> o a n", a=4)),
              (), tuple(fr.bufs[0:4]))
        S.dma("sp", "cst2", lambda e: e.dma_start(out=wstage, in_=wst_d.rearrange("p (a n) -> p a n", a=4)),
              (), tuple(fr.bufs[4:8]))
        for _ in range(NBUF):
            w_issue()
        S.op("dve", lambda e: e.memset(ones[:, :], 1.0), (), (b_ones,))
        S.op("dve", lambda e: e.memset(srow[:, :], 0.0), (), (b_srow,))
        act(srow[0:1, :].rearrange("o (a n) -> o a n", a=4), srowf, AF.Exp, tuple(fr.bufs[0:4]), (b_srow,))
        act(cst[:, C_ABIAS:C_ABIAS + 2048], cst[:, C_ABIAS:C_ABIAS + 2048], AF.Exp, (b_cst,), (b_cst,))
        for i in range(16):
            tt(wsT[:, i, :], wstage[:, i // 4, (i % 4) * 128:(i % 4 + 1) * 128], cst[:, C_TRIL:C_TRIL + 128], ALU.mult,
               (fr.bufs[4 + i // 4], b_cst), (b_wsT,))

        def rms_sq_act(xr, bxr, c):
            sq, bsq = sqr.next()
            act(sq, xr[:, c, :], AF.Square, (bxr[c],), (bsq,))
            return sq, bsq

        def rms_sq_mm(sqb, c):
            sq, bsq = sqb
            S.op("pe", (lambda e: e.matmul(ss_ps, ones[:, :], sq, start=(c == 0), stop=(c == 7))),
                 (bsq, b_ones), (b_ss,))

        def rms_finish(xr, bxr, gcol):
            ps, bps = ss_ps, b_ss
            rtmp, b_rtmp = fr.next()
            act(rtmp, ps, AF.Ln, (bps,), (b_rtmp,), bias=EPS, scale=1.0 / D)
            rstd, b_rstd = fr.next()
            act(rstd, rtmp, AF.Exp, (b_rtmp,), (b_rstd,), scale=-0.5)
            for c in range(8):
                stt(hT[:, c, :], xr[:, c, :], cst[:, gcol + c:gcol + c + 1], rstd, ALU.mult, ALU.mult,
                    (bxr[c], b_rstd, b_cst), (b_hT[c],))

        def headnorm_a(ps, bps, gcolumn, out, bout):
            sq, bsq = qsqr.next()
            act(sq[0:64, :], ps[0:64, :], AF.Square, (bps,), (bsq,))
            return lambda: headnorm_b(ps, bps, gcolumn, out, bout, sq, bsq)

        def headnorm_b(ps, bps, gcolumn, out, bout, sq, bsq):
            ps2, bps2 = psr.next()
            S.op("pe", lambda e: e.matmul(ps2[0:64, :], ones[0:64, 0:64], sq[0:64, :], start=True, stop=True),
                 (bsq, b_ones), (bps2,))
            r1, br1 = rqr.next()
            act(r1[0:64, :], ps2[0:64, :], AF.Ln, (bps2,), (br1,), bias=EPS, scale=1.0 / HD)
            r2, br2 = rq2r.next()
            act(r2[0:64, :], r1[0:64, :], AF.Exp, (br1,), (br2,), scale=-0.5)
            stt(out, ps[0:64, :], cst[0:64, gcolumn:gcolumn + 1], r2[0:64, :], ALU.mult, ALU.mult,
                (bps, br2, b_cst), (bout,))

        def emit_xload(ti):
            s_idx = ti // tiles_per_seq
            t0 = (ti % tiles_per_seq) * T
            xi = ti % 2
            src = xT_d[s_idx].rearrange("(c p) t -> p c t", p=128)[:, :, t0:t0 + T]
            dstt = xres[xi][:, :, :]
            S.dma("sp", "xl%d" % xi, lambda e: e.dma_start(out=dstt, in_=src), (), tuple(b_xres[xi]))

        def kouter(outs, lhs_fn, reads_w, wr_bufs):
            for k in range(8):
                groups = [(o, lhs_fn(i, k), hT[:, k, :]) for i, o in enumerate(outs)]
                def fn(e, groups=groups, k=k):
                    ins = None
                    for (o, l_, r_) in groups:
                        ins = e.matmul(o, l_, r_, start=(k == 0), stop=(k == 7))
                    return ins
                S.op("pe", fn, (*reads_w, b_hT[k]), tuple(wr_bufs))

        def run_pass(ti, l, first_layer, last_layer, nxt):
            _CUR[0], _CUR[1] = ti, l
            s_idx = ti // tiles_per_seq
            tt_i = ti % tiles_per_seq
            t0 = tt_i * T
            first_in_seq = tt_i == 0
            last_in_seq = tt_i == tiles_per_seq - 1
            xi = ti % 2
            xr = xres[xi]
            bxr = b_xres[xi]
            vbase = C_VEC + 192 * l
            G1, G2 = vbase, vbase + 8
            CW0, CW1, CW2, CB = vbase + 16, vbase + 60, vbase + 104, vbase + 148
            QG, KG = C_QK + 2 * l, C_QK + 2 * l + 1

            _ck(0)
            rms_finish(xr, bxr, G1)
            _ck(1)

            pend_hn = []

            def flush_hn():
                while pend_hn:
                    pend_hn.pop(0)()

            def proj_head(vW, bW, cols, gcolumn, out, bout):
                ps, bps = psr.next()
                mm_group(ps[0:64, :], [(vW[:, k, cols], hT[:, k, :]) for k in range(8)], (bW, *b_hT), (bps,))
                part_b = headnorm_a(ps, bps, gcolumn, out, bout)
                flush_hn()
                pend_hn.append(part_b)

            wA0, bA0 = w_acquire("A0")
            vA0 = wA0[:, 0:2048].rearrange("p (k n) -> p k n", k=8)
            qps = [psr.next() for _ in range(4)]
            kouter([p[0][0:64, :] for p in qps], lambda i, k: vA0[:, k, i * 64:(i + 1) * 64], (bA0,),
                   [p[1] for p in qps])
            for h in range(4):
                pend_hn.append(headnorm_a(qps[h][0], qps[h][1], QG, qT[:, h, :], b_qT[h]))
            w_release()
            _ck(2)

            wB, bB = w_acquire("B")
            vB = wB[:, 0:2048].rearrange("p (k n) -> p k n", k=8)
            for g in range(2):
                proj_head(vB, bB, slice(g * 64, (g + 1) * 64), KG, kT[l][:, g, 128:128 + T], b_kcur[l][g])
            ps, bps = psr.next()
            mm_multi([(ps[:, b * 128:(b + 1) * 128],
                       [(hT[:, k, b * 128:(b + 1) * 128], vB[:, k, 128:256]) for k in range(8)])
                      for b in range(NB)], (bB, *b_hT), (bps,))
            flush_hn()
            cp(vtok[l][:, 1:NB + 1, :], ps.rearrange("p (b n) -> p b n", b=NB), (bps,), (b_vcur[l],))
            w_release()

            def sgu_block(b):
                ps, bps = psr.next()
                groups = []
                for gp in range(4):
                    for sl in range(2):
                        g = 2 * gp + sl
                        groups.append((ps[64 * sl:64 * sl + 64, gp * 128:(gp + 1) * 128],
                                       [(vn[:, b, g * 64:(g + 1) * 64], wsT[:, l * 8 + g, :])]))
                mm_multi(groups, (b_vn[b], b_wsT), (bps,))
                tm, btm = tmr.next()
                tt(tm, ps, cst[:, C_BSB + 512 * l:C_BSB + 512 * l + 512], ALU.add, (bps, b_cst), (btm,))
                tt(ysguT[:, :, b * 128:(b + 1) * 128], tm.rearrange("p (a n) -> p a n", a=4),
                   uT[:, :, b * 128:(b + 1) * 128], ALU.mult, (btm, *b_uT), tuple(b_actT[12:16]))

            def att_stage1(b, g):
                halves = []
                if not (first_in_seq and b == 0):
                    halves.append(0)
                halves.append(1)
                pts = {}
                for hf in halves:
                    ps, bps = psr.next()
                    kcols = slice(128 * (b + hf), 128 * (b + hf) + 128)
                    kb = [b_kcur[l][g]] + ([b_kprev[l]] if (b == 0 and hf == 0) else [])
                    lhs_ = kT[l][:, g, kcols]
                    rhs_ = qT[:, 4 * g:4 * g + 4, b * 128:(b + 1) * 128]
                    out_ = ps.rearrange("p (a n) -> p a n", a=4)
                    S.op("pe", (lambda e, out_=out_, lhs_=lhs_, rhs_=rhs_: e.matmul(
                        out_, lhs_, rhs_, start=True, stop=True)),
                        (*kb, *b_qT[4 * g:4 * g + 4]), (bps,))
                    e_, be_ = efr.next()
                    act(e_, ps, AF.Exp, (bps,), (be_,), scale=0.125)
                    p_, bp_ = pTr.next()
                    col = C_ABIAS + (g * 2 + hf) * 512
                    tt(p_, e_, cst[:, col:col + 512], ALU.mult, (be_, b_cst), (bp_,), eng="pool")
                    pts[hf] = (p_, bp_)
                return halves, pts

            def att_stage2(b, g, halves, pts):
                yd, byd = psr.next()
                groups = []
                sbase = ((l * 2 + g) * 2) * 256
                for sl in range(2):
                    ypairs, dpairs = [], []
                    for hf in halves:
                        p_ = pts[hf][0]
                        rhs = p_.rearrange("p (pr s n) -> p pr s n", pr=2, s=2)[:, :, sl, :]
                        ypairs.append((vtok[l][:, b + hf, g * 64:(g + 1) * 64], rhs))
                        dpairs.append((ones[:, 0:64], rhs))
                    dpairs.append((ones[:, 0:64],
                                   srow[:, sbase + sl * 256:sbase + sl * 256 + 256].rearrange("p (a n) -> p a n", a=2)))
                    groups.append((yd[64 * sl:64 * sl + 64, 0:256].rearrange("p (a n) -> p a n", a=2), ypairs))
                    groups.append((yd[64 * sl:64 * sl + 64, 256:512].rearrange("p (a n) -> p a n", a=2), dpairs))
                vb = [b_vcur[l]] + ([b_vprev[l]] if b == 0 and 0 in halves else [])
                mm_multi(groups, (*[pts[hf][1] for hf in halves], *vb, b_ones, b_srow), (byd,))
                rd, brd = rdr.next()
                S.op("dve", lambda e, rd=rd, yd=yd: e.reciprocal(out=rd, in_=yd[:, 256:512]), (byd,), (brd,))
                tt(yattT[:, 2 * g:2 * g + 2, b * 128:(b + 1) * 128],
                   yd[:, 0:256].rearrange("p (a n) -> p a n", a=2),
                   rd.rearrange("p (a n) -> p a n", a=2), ALU.mult, (byd, brd),
                   (b_yatt[2 * g][b], b_yatt[2 * g + 1][b]))

            _ck(4)
            slabs = {}

            def get_slab(nm):
                if nm not in slabs:
                    w_, b_ = w_acquire(nm)
                    n_ = 2048 if nm == "A1" else 4096
                    slabs[nm] = (w_[:, 0:n_].rearrange("p (k n) -> p k n", k=8), b_)
                return slabs[nm]

            def u_qhead(h):
                vA1, bA1 = get_slab("A1")
                proj_head(vA1, bA1, slice((h - 4) * 64, (h - 3) * 64), QG, qT[:, h, :], b_qT[h])
                if h == 7:
                    w_release()

            def u_su(c):
                vC, bC = get_slab("C")
                flush_hn()
                ps, bps = psr.next()
                mm_group(ps, [(vC[:, k, c * 128:(c + 1) * 128], hT[:, k, :]) for k in range(8)],
                         (bC, *b_hT), (bps,))
                act(uT[:, c, :], ps, AF.Gelu_apprx_tanh, (bps,), (b_uT[c],))
                if c == 3:
                    w_release()

            def u_sv(b):
                vD, bD = get_slab("D")
                ps, bps = psr.next()
                mm_group(ps, [(hT[:, k, b * 128:(b + 1) * 128], vD[:, k, :]) for k in range(8)],
                         (bD, *b_hT), (bps,))
                g_, bg_ = gvr.next()
                act(g_, ps, AF.Gelu_apprx_tanh, (bps,), (bg_,))
                si = small_i[0]
                small_i[0] = (si + 1) % 8
                act(junk[:, :], g_, AF.Square, (bg_,), (b_junk, b_ssv[si]), accum_out=ssv[:, si:si + 1])
                act(lnv[:, si:si + 1], ssv[:, si:si + 1], AF.Ln, (b_ssv[si],), (b_lnv[si],), bias=EPS, scale=1.0 / 512)
                act(rv[:, si:si + 1], lnv[:, si:si + 1], AF.Exp, (b_lnv[si],), (b_rv[si],), scale=-0.5)
                stt(vn[:, b, :], g_, rv[:, si:si + 1], cst[:, C_SGUG + 512 * l:C_SGUG + 512 * l + 512],
                    ALU.mult, ALU.mult, (bg_, b_rv[si], b_cst), (b_vn[b],))
                if b == NB - 1:
                    w_release()

            units = ([lambda h=h: u_qhead(h) for h in range(4, 8)] + [lambda c=c: u_su(c) for c in range(4)]
                     + [lambda b=b: u_sv(b) for b in range(NB)])
            its = [(b, 0) for b in range(NB)] + [(b, 1) for b in range(NB)]
            pend = [att_stage1(*its[0]), att_stage1(*its[1])]
            for i, (b, g) in enumerate(its):
                if g == 0:
                    for _ in range(3):
                        units.pop(0)()
                else:
                    sgu_block(b)
                if i + 2 < len(its):
                    pend.append(att_stage1(*its[i + 2]))
                att_stage2(b, g, *pend.pop(0))
            assert not units
            _ck(3)
            if not last_in_seq:
                cp(kT[l][:, :, 0:128], kT[l][:, :, T:T + 128], (*b_kcur[l],), (b_kprev[l],))
                cp(vtok[l][:, 0, :], vtok[l][:, NB, :], (b_vcur[l],), (b_vprev[l],))

            _ck(5)
            for half in range(2):
                wE, bE = w_acquire("EF"[half])
                wG, bG = w_acquire("GH"[half])
                wO, bO = w_acquire("OAB%d" % half)
                vE = wE.rearrange("p (k n) -> p k n", k=8)
                vG = wG.rearrange("p (k n) -> p k n", k=8)
                vO = wO.rearrange("p (m k n) -> p m k n", m=2, k=4)
                for c4 in range(4):
                    c = 4 * half + c4
                    cs = slice(c4 * 128, (c4 + 1) * 128)
                    pga, bpga = psr.next()
                    mm_group(pga, [(vE[:, k, cs], hT[:, k, :]) for k in range(8)], (bE, *b_hT), (bpga,))
                    pgb, bpgb = psr.next()
                    mm_group(pgb, [(vG[:, k, cs], hT[:, k, :]) for k in range(8)], (bG, *b_hT), (bpgb,))
                    pa, bpa = psr.next()
                    mm_group(pa, [(vO[:, 0, kc, cs], yattT[:, kc, :]) for kc in range(4)],
                             (bO, *b_actT[8:12]), (bpa,))
                    pb, bpb = psr.next()
                    mm_group(pb, [(vO[:, 1, kc, cs], ysguT[:, kc, :]) for kc in range(4)], (bO, *b_actT[12:16]), (bpb,))
                    sa, bsa = sar.next()
                    act(sa, pga, AF.Sigmoid, (bpga,), (bsa,))
                    sb_, bsb_ = sbr.next()
                    act(sb_, pgb, AF.Sigmoid, (bpgb,), (bsb_,))
                    t1, bt1 = t1r.next()
                    tt(t1, pa, sa, ALU.mult, (bpa, bsa), (bt1,))
                    t2, bt2 = t2r.next()
                    tt(t2, pb, sb_, ALU.mult, (bpb, bsb_), (bt2,))
                    tt(mergedT[:, c, :], t1, t2, ALU.add, (bt1, bt2), (b_merged[c],), eng="pool")
                w_release(3)
            sqbs = {}
            for half in range(2):
                wO, bO = w_acquire("OUT%d" % half)
                vO = wO.rearrange("p (k n) -> p k n", k=8)
                for c4 in range(4):
                    c = 4 * half + c4
                    po, bpo = psr.next()
                    mm_group(po, [(vO[:, k, c4 * 128:(c4 + 1) * 128], mergedT[:, k, :]) for k in range(8)],
                             (bO, *b_merged), (bpo,))
                    if c >= 1:
                        rms_sq_mm(sqbs[c - 1], c - 1)
                    tt(xr[:, c, :], po, xr[:, c, :], ALU.add, (bpo, bxr[c]), (bxr[c],))
                    sqbs[c] = rms_sq_act(xr, bxr, c)
                w_release()
            rms_sq_mm(sqbs[7], 7)

            _ck(6)
            rms_finish(xr, bxr, G2)
            if last_layer and nxt is not None:
                emit_xload(nxt[0])

            def ffn_epilogue(j, pg, bpg, pv, bpv):
                ag, bag = agr.next()
                av, bav = avr.next()
                items = ((pg, bpg, ag, bag, j), (pv, bpv, av, bav, NJ + j))
                for (ps, bps, a_, ba_, ch) in items:
                    act(a_, ps, AF.Identity, (bps, b_cst), (ba_,),
                        bias=cst[:, CB + ch:CB + ch + 1], scale=cst[:, CW2 + ch:CW2 + ch + 1])
                for (ps, bps, a_, ba_, ch) in items:
                    stt(a_[:, 1:T], ps[:, 0:T - 1], cst[:, CW1 + ch:CW1 + ch + 1], a_[:, 1:T], ALU.mult, ALU.add,
                        (bps, ba_, b_cst), (ba_,))
                for (ps, bps, a_, ba_, ch) in items:
                    stt(a_[:, 2:T], ps[:, 0:T - 2], cst[:, CW0 + ch:CW0 + ch + 1], a_[:, 2:T], ALU.mult, ALU.add,
                        (bps, ba_, b_cst), (ba_,))
                if not first_in_seq:
                    for (ps, bps, a_, ba_, ch) in items:
                        tt(a_[:, 0:2], a_[:, 0:2], hc[:, ch, :], ALU.add, (ba_, b_hc), (ba_,), eng="pool")
                if not last_in_seq:
                    for (ps, bps, a_, ba_, ch) in items:
                        act(halo[l][:, ch, :], ps[:, T - 2:T], AF.Identity, (bps,), (b_halo[l][ch],))
                sg, bsg = sgr.next()
                act(sg, ag, AF.Silu, (bag,), (bsg,))
                tt(actT[:, j, :], sg, av, ALU.mult, (bsg, bav), (b_actT[j],), eng="pool")

            if not first_in_seq:
                bh = tuple(b_halo[l])
                tt(hc[:, :, 1], halo[l][:, :, 1], cst[:, CW0:CW0 + 44], ALU.mult, (*bh, b_cst), (b_hc,))
                tt(hc[:, :, 0], halo[l][:, :, 0], cst[:, CW0:CW0 + 44], ALU.mult, (*bh, b_cst), (b_hc,))
                tt(hctmp[:, :], halo[l][:, :, 1], cst[:, CW1:CW1 + 44], ALU.mult, (*bh, b_cst), (b_hctmp,))
                tt(hc[:, :, 0], hc[:, :, 0], hctmp[:, :], ALU.add, (b_hc, b_hctmp), (b_hc,))
            for i in range(11):
                wU, bU = w_acquire("UP%d" % i)
                vU = wU.rearrange("p (k n) -> p k n", k=8)
                if i == 0:
                    pss = [psr.next() for _ in range(4)]
                    offs = [0, 256, 128, 384]
                    kouter([p[0] for p in pss], lambda q, k: vU[:, k, offs[q]:offs[q] + 128], (bU,),
                           [p[1] for p in pss])
                    ffn_epilogue(0, pss[0][0], pss[0][1], pss[1][0], pss[1][1])
                    ffn_epilogue(1, pss[2][0], pss[2][1], pss[3][0], pss[3][1])
                else:
                    for jj in range(2):
                        j = 2 * i + jj
                        pg, bpg = psr.next()
                        mm_group(pg, [(vU[:, k, jj * 128:(jj + 1) * 128], hT[:, k, :]) for k in range(8)],
                                 (bU, *b_hT), (bpg,))
                        pv, bpv = psr.next()
                        mm_group(pv, [(vU[:, k, 256 + jj * 128:256 + (jj + 1) * 128], hT[:, k, :]) for k in range(8)],
                                 (bU, *b_hT), (bpv,))
                        ffn_epilogue(j, pg, bpg, pv, bpv)
                w_release()
            _ck(7)
            if nxt is not None:
                nxr, nbxr = xres[nxt[0] % 2], b_xres[nxt[0] % 2]
            sqbs = {}
            for c in range(8):
                wDn, bDn = w_acquire("DN%d" % c)
                vDn = wDn[:, 0:2816].rearrange("p (k n) -> p k n", k=NJ)
                pd, bpd = psr.next()
                if c == 0:
                    mm_part(pd, [(vDn[:, j, :], actT[:, j, :]) for j in range(16)], (bDn, *b_actT[0:16]), (bpd,), True, False)
                    mm_part(pd, [(vDn[:, j, :], actT[:, j, :]) for j in range(16, NJ)], (bDn, *b_actT[16:NJ]), (bpd,), False, True)
                else:
                    mm_group(pd, [(vDn[:, j, :], actT[:, j, :]) for j in range(NJ)], (bDn, *b_actT), (bpd,))
                if nxt is not None and c >= 1:
                    rms_sq_mm(sqbs[c - 1], c - 1)
                tt(xr[:, c, :], pd, xr[:, c, :], ALU.add, (bpd, bxr[c]), (bxr[c],))
                if nxt is not None:
                    sqbs[c] = rms_sq_act(nxr, nbxr, c)
                w_release()
            if nxt is not None:
                rms_sq_mm(sqbs[7], 7)

            if last_layer:
                dst = yT_d[s_idx].rearrange("(c p) t -> p c t", p=128)[:, :, t0:t0 + T]
                S.dma("sp", "xs%d" % xi, lambda e: e.dma_start(out=dst, in_=xr[:, :, :]), tuple(bxr), ())

        plist = [(ti, li) for ti in range(n_tiles) for li in range(len(layers))]
        emit_xload(0)
        sq0 = [rms_sq_act(xres[0], b_xres[0], c) for c in range(4)]
        for c in range(8):
            rms_sq_mm(sq0[c] if c < 4 else rms_sq_act(xres[0], b_xres[0], c), c)
        try:
            for pi, (ti, li) in enumerate(plist):
                nxt = plist[pi + 1] if pi + 1 < len(plist) else None
                run_pass(ti, layers[li], li == 0, li == len(layers) - 1, nxt)
        except _Stop:
            dst = yT_d[0].rearrange("(c p) t -> p c t", p=128)[:, :, 0:T]
            S.dma("sp", "xs0", lambda e: e.dma_start(out=dst, in_=xres[0][:, :, :]), tuple(b_xres[0]), ())
        S.wait_all("sp", [b for i in range(2) for b in b_xres[i]])

        sems = {k: es.enter_context(nc.semaphore(k)) for k in S.semnames}
        block = es.enter_context(nc.Block())

        @block.tensor
        def _(e):
            S.replay("pe", e, sems)

        @block.scalar
        def _(e):
            S.replay("act", e, sems)

        @block.vector
        def _(e):
            S.replay("dve", e, sems)

        @block.gpsimd
        def _(e):
            S.replay("pool", e, sems)

        @block.sync
        def _(e):
            S.replay("sp", e, sems)
    return nc


def _pkn(w):
    kc = w.shape[0] // 128
    return np.ascontiguousarray(w.reshape(kc, 128, -1).transpose(1, 0, 2).reshape(128, -1))


def pack_weights(w_in, w_oa, w_ob, w_out, w_up, w_down):
    out = np.empty((NL, 128, WPL), np.float32)
    for l in range(NL):
        parts = {
            "A0": _pkn(w_in[l][:, 0:256]), "A1": _pkn(w_in[l][:, 256:512]), "B": _pkn(w_in[l][:, 512:768]),
            "C": _pkn(w_in[l][:, 768:1280]), "D": _pkn(w_in[l][:, 1280:1792]),
            "E": _pkn(w_in[l][:, 1792:2304]), "F": _pkn(w_in[l][:, 2304:2816]),
            "G": _pkn(w_in[l][:, 2816:3328]), "H": _pkn(w_in[l][:, 3328:3840]),
            "OAB0": np.concatenate([_pkn(w_oa[l][:, 0:512]), _pkn(w_ob[l][:, 0:512])], axis=1),
            "OAB1": np.concatenate([_pkn(w_oa[l][:, 512:1024]), _pkn(w_ob[l][:, 512:1024])], axis=1),
            "OUT0": _pkn(w_out[l][:, 0:512]), "OUT1": _pkn(w_out[l][:, 512:1024]),
        }
        for i in range(11):
            parts["UP%d" % i] = _pkn(np.concatenate(
                [w_up[l][:, 256 * i:256 * i + 256], w_up[l][:, DFF + 256 * i:DFF + 256 * i + 256]], axis=1))
        for c in range(8):
            parts["DN%d" % c] = _pkn(w_down[l][:, 128 * c:128 * c + 128])
        for nm, n in SLABS:
            off, _ = SLAB_OFF[nm]
            assert parts[nm].shape == (128, n), (nm, parts[nm].shape)
            out[l, :, off:off + n] = parts[nm]
    return out


def pack_consts(mix_norm, q_norm, k_norm, sinks, sgu_norm, w_s, b_s, ffn_norm, conv_w, conv_b):
    cst = np.zeros((128, NCST), np.float32)
    for l in range(NL):
        vb = C_VEC + 192 * l
        cst[:, vb:vb + 8] = mix_norm[l].reshape(8, 128).T
        cst[:, vb + 8:vb + 16] = ffn_norm[l].reshape(8, 128).T
        for tap in range(3):
            cst[:, vb + 16 + 44 * tap:vb + 16 + 44 * (tap + 1)] = conv_w[l, tap].reshape(44, 128).T
        cst[:, vb + 148:vb + 192] = conv_b[l].reshape(44, 128).T
        cst[0:64, C_QK + 2 * l] = q_norm[l]
        cst[0:64, C_QK + 2 * l + 1] = k_norm[l]
        cst[:, C_SGUG + 512 * l:C_SGUG + 512 * (l + 1)] = sgu_norm[l][None, :]
        for gp in range(4):
            cst[0:64, C_BSB + 512 * l + gp * 128:C_BSB + 512 * l + (gp + 1) * 128] = b_s[l, 2 * gp][None, :]
            cst[64:128, C_BSB + 512 * l + gp * 128:C_BSB + 512 * l + (gp + 1) * 128] = b_s[l, 2 * gp + 1][None, :]
    k = np.arange(128)[:, None]
    q = np.arange(128)[None, :]
    for g in range(2):
        for j in range(4):
            slope = 2.0 ** (-(4 * g + j + 1))
            dist_prev = q + 128 - k
            dist_cur = q - k
            bp = np.where(dist_prev < 128, -slope * dist_prev, -30000.0)
            bc = np.where(dist_cur >= 0, -slope * dist_cur, -30000.0)
            cst[:, C_ABIAS + (g * 2 + 0) * 512 + j * 128:C_ABIAS + (g * 2 + 0) * 512 + (j + 1) * 128] = bp
            cst[:, C_ABIAS + (g * 2 + 1) * 512 + j * 128:C_ABIAS + (g * 2 + 1) * 512 + (j + 1) * 128] = bc
    cst[:, C_TRIL:C_TRIL + 128] = (k <= q).astype(np.float32)
    srow = np.zeros((1, 2048), np.float32)
    for l in range(NL):
        for g in range(2):
            for sl in range(2):
                for pr in range(2):
                    base = (((l * 2 + g) * 2 + sl) * 2 + pr) * 128
                    srow[0, base:base + 128] = sinks[l, 4 * g + 2 * pr + sl]
    wst = np.ascontiguousarray(np.transpose(w_s, (3, 0, 1, 2)).reshape(128, NL * 8 * 128)).astype(np.float32)
    return cst, srow, wst


_NC_CACHE = {}
DBG_STOP = None


class _Stop(Exception):
    pass


_CUR = [0, 0]


def _ck(k):
    if DBG_STOP is not None and DBG_STOP == (_CUR[0], _CUR[1], k):
        raise _Stop()


def run(x, params, n_cores, layers=(0, 1)):
    B, S_, _ = x.shape
    n_seq = B // n_cores
    key = (n_seq, S_, tuple(layers))
    if key not in _NC_CACHE:
        _NC_CACHE[key] = build(n_seq, S_, layers)
    nc = _NC_CACHE[key]
    wts = pack_weights(params["w_in"], params["w_oa"], params["w_ob"], params["w_out"], params["w_up"],
                       params["w_down"])
    cst, srow, wst = pack_consts(params["mix_norm"], params["q_norm"], params["k_norm"], params["sinks"],
                                 params["sgu_norm"], params["w_s"], params["b_s"], params["ffn_norm"],
                                 params["conv_w"], params["conv_b"])
    in_maps = []
    for c in range(n_cores):
        xc = np.ascontiguousarray(np.transpose(x[c * n_seq:(c + 1) * n_seq], (0, 2, 1)))
        in_maps.append({"xT": xc, "wts": wts, "cst": cst, "srow": srow, "wst": wst})
    res = run_bass_kernel_spmd(nc, in_maps, core_ids=list(range(n_cores)))
    outs = [np.transpose(r["yT"], (0, 2, 1)) for r in res.results]
    return np.ascontiguousarray(np.concatenate(outs, axis=0)).astype(np.float32)


def kernel(**inputs):
    inputs = {k: np.asarray(v) for k, v in inputs.items()}
    x = inputs.pop("x").astype(np.float32)
    params = {k: v.astype(np.float32) for k, v in inputs.items()}
    return run(x, params, 8)
```

```python
from contextlib import ExitStack

import numpy as np
import concourse.bass as bass
import concourse.mybir as mybir
from concourse.bass_utils import run_bass_kernel_spmd

F32 = mybir.dt.float32
BF16 = mybir.dt.bfloat16
AF = mybir.ActivationFunctionType
ALU = mybir.AluOpType

D = 1024
NL = 2
NH = 8
NKV = 2
HD = 64
DFF = 2816
NJ = DFF // 128
T = 512
NB = T // 128
EPS = 1e-6
NBUF = 7
SLAB = 4096

SLABS = ([("A0", 2048), ("B", 2048), ("A1", 2048), ("C", 4096), ("D", 4096), ("E", 4096), ("G", 4096),
          ("OAB0", 4096), ("F", 4096), ("H", 4096), ("OAB1", 4096), ("OUT0", 4096), ("OUT1", 4096)]
         + [("UP%d" % i, 4096) for i in range(11)] + [("DN%d" % c, 2816) for c in range(8)])
SLAB_OFF = {}
_o = 0
for _n, _s in SLABS:
    SLAB_OFF[_n] = (_o, _s)
    _o += _s
WPL = _o

C_VEC = 0
C_QK = 384
C_SGUG = 392
C_BSB = C_SGUG + 1024
C_ABIAS = C_BSB + 1024
C_TRIL = C_ABIAS + 2048
NCST = C_TRIL + 128


class Buf:
    __slots__ = ("name", "lw", "rd", "excl")

    def __init__(self, name, excl=False):
        self.name = name
        self.lw = None
        self.rd = []
        self.excl = excl


class Sched:
    ENG = ("pe", "act", "dve", "pool", "sp")

    def __init__(self):
        self.streams = {e: [] for e in self.ENG}
        self.cnt = {}
        self.waited = {e: {} for e in self.ENG}
        self.semnames = []

    def newsem(self, key):
        self.cnt[key] = 0
        self.semnames.append(key)

    def _waits(self, eng, reads, writes):
        deps = {}
        def add(ev, raw):
            if ev is None:
                return
            k, v = ev
            if k == eng and (eng == "pe" or not raw):
                return
            if deps.get(k, 0) < v:
                deps[k] = v
        for b in reads:
            add(b.lw, True)
            if b.excl:
                for ev in b.rd:
                    add(ev, False)
        for b in writes:
            add(b.lw, False)
            for ev in b.rd:
                add(ev, False)
        for k, v in deps.items():
            if self.waited[eng].get(k, 0) >= v:
                continue
            self.waited[eng][k] = v
            self.streams[eng].append(("wait", k, v))

    def _commit(self, ev, reads, writes):
        for b in reads:
            b.rd.append(ev)
        for b in writes:
            b.lw = ev
            b.rd = []

    def op(self, eng, fn, reads=(), writes=()):
        self._waits(eng, reads, writes)
        self.cnt[eng] += 1
        self.streams[eng].append(("op", fn, eng, 1))
        self._commit((eng, self.cnt[eng]), reads, writes)

    def dma(self, eng, sem, fn, reads=(), writes=()):
        self._waits(eng, reads, writes)
        self.cnt[sem] += 16
        self.streams[eng].append(("op", fn, sem, 16))
        self._commit((sem, self.cnt[sem]), reads, writes)

    def wait_all(self, eng, bufs):
        self._waits(eng, bufs, bufs)

    def replay(self, eng, handle, sems):
        for it in self.streams[eng]:
            if it[0] == "wait":
                handle.wait_ge(sems[it[1]], it[2])
            else:
                ins = it[1](handle)
                ins.then_inc(sems[it[2]], it[3])


class Rot:
    def __init__(self, tiles, name):
        self.tiles = tiles
        self.bufs = [Buf("%s%d" % (name, i)) for i in range(len(tiles))]
        self.i = 0

    def next(self):
        i = self.i
        self.i = (i + 1) % len(self.tiles)
        return self.tiles[i], self.bufs[i]


def build(n_seq, seq_len, layers=(0, 1)):
    nc = bass.Bass("TRN2", target_bir_lowering=False)
    S = Sched()
    tiles_per_seq = seq_len // T
    n_tiles = n_seq * tiles_per_seq
    xT_d = nc.dram_tensor("xT", [n_seq, D, seq_len], F32, kind="ExternalInput").ap()
    wts_d = nc.dram_tensor("wts", [NL, 128, WPL], F32, kind="ExternalInput").ap()
    cst_d = nc.dram_tensor("cst", [128, NCST], F32, kind="ExternalInput").ap()
    srow_d = nc.dram_tensor("srow", [1, 2048], F32, kind="ExternalInput").ap()
    wst_d = nc.dram_tensor("wst", [128, 2048], F32, kind="ExternalInput").ap()
    yT_d = nc.dram_tensor("yT", [n_seq, D, seq_len], F32, kind="ExternalOutput").ap()

    es = ExitStack()
    with es:
        def sb(name, shape, dt):
            return es.enter_context(nc.sbuf_tensor("s_" + name, shape, dt))

        cst = sb("cst", [128, NCST], F32)
        srow = sb("srow", [128, 2048], BF16)
        wsT = sb("wsT", [128, 16, 128], BF16)
        ones = sb("ones", [128, 128], BF16)
        xres = [sb("xres%d" % i, [128, 8, T], F32) for i in range(2)]
        hT = sb("hT", [128, 8, T], BF16)
        sqt = sb("sqt", [128, 4, T], BF16)
        NFR = 10
        frt = sb("frt", [128, NFR, 512], F32)
        qT = sb("qT", [64, 8, T], BF16)
        kT = [sb("kT%d" % l, [64, 2, T + 128], BF16) for l in range(NL)]
        vtok = [sb("vtok%d" % l, [128, NB + 1, 128], BF16) for l in range(NL)]
        qsq = sb("qsq", [64, 5, T], BF16)
        junk = sb("junk", [128, 512], BF16)
        ssv = sb("ssv", [128, 8], F32)
        lnv = sb("lnv", [128, 8], F32)
        rv = sb("rv", [128, 8], F32)
        vn = sb("vn", [128, NB, 512], BF16)
        pT = sb("pT", [128, 6, 512], BF16)
        rden = sb("rden", [128, 2, 256], F32)
        actT = sb("actT", [128, NJ, T], BF16)
        mergedT = actT[:, 0:8, :]
        yattT = actT[:, 8:12, :]
        ysguT = actT[:, 12:16, :]
        uT = actT[:, 16:20, :]
        halo = [sb("halo%d" % l, [128, 2 * NJ, 2], F32) for l in range(NL)]
        hc = sb("hc", [128, 2 * NJ, 2], F32)
        hctmp = sb("hctmp", [128, 2 * NJ], F32)
        b_hc = Buf("hc")
        b_hctmp = Buf("hctmp")
        wslab = sb("wslab", [128, NBUF, SLAB], BF16)
        psb = [es.enter_context(nc.psum_tensor("ps%d" % i, [128, 512], F32)) for i in range(8)]

        b_cst = Buf("cst")
        b_srow = Buf("srow")
        b_wsT = Buf("wsT")
        b_ones = Buf("ones")
        b_xres = [[Buf("xres%d_%d" % (i, c)) for c in range(8)] for i in range(2)]
        b_hT = [Buf("hT%d" % c) for c in range(8)]
        b_qT = [Buf("qT%d" % h) for h in range(8)]
        b_kprev = [Buf("kprev%d" % l) for l in range(NL)]
        b_kcur = [[Buf("kcur%d_%d" % (l, g)) for g in range(2)] for l in range(NL)]
        b_vprev = [Buf("vprev%d" % l) for l in range(NL)]
        b_vcur = [Buf("vcur%d" % l) for l in range(NL)]
        b_junk = Buf("junk")
        b_ssv = [Buf("ssv%d" % i) for i in range(8)]
        b_lnv = [Buf("lnv%d" % i) for i in range(8)]
        b_rv = [Buf("rv%d" % i) for i in range(8)]
        b_vn = [Buf("vn%d" % b) for b in range(NB)]
        b_actT = [Buf("actT%d" % j) for j in range(NJ)]
        b_merged = b_actT[0:8]
        b_yatt = [[b_actT[8 + c]] * NB for c in range(4)]
        b_uT = b_actT[16:20]
        b_halo = [[Buf("halo%d_%d" % (l, ch)) for ch in range(2 * NJ)] for l in range(NL)]
        b_wslab = [Buf("wslab%d" % i) for i in range(NBUF)]
        b_ps = [Buf("ps%d" % i, excl=True) for i in range(8)]

        sqr = Rot([sqt[:, i, :] for i in range(4)], "sq")
        qsqr = Rot([qsq[:, i, :] for i in range(5)], "qsq")
        fr = Rot([frt[:, i, :] for i in range(NFR)], "fr")
        rqr = rq2r = gvr = efr = tmr = sar = sbr = t1r = t2r = agr = avr = sgr = fr
        srowf = frt[0:1, 0:4, :]
        wstage = frt[:, 4:8, :]
        pTr = Rot([pT[:, i, :] for i in range(6)], "pT")
        rdr = Rot([rden[:, i, :] for i in range(2)], "rden")
        psr = Rot([p[:, :] for p in psb[0:7]], "psr")
        psr.bufs = b_ps[0:7]
        ss_ps, b_ss = psb[7][:, :], b_ps[7]
        small_i = [0]

        for e in ("pe", "act", "dve", "pool"):
            S.newsem(e)
        for i in range(NBUF):
            S.newsem("w%d" % i)
        for k in ("cst0", "cst1", "cst2", "xl0", "xl1", "xs0", "xs1"):
            S.newsem(k)

        def act(out, in_, func, reads, writes, bias=None, scale=None, accum_out=None):
            kw = {}
            if bias is not None:
                kw["bias"] = bias
            if scale is not None:
                kw["scale"] = scale
            if accum_out is not None:
                kw["accum_out"] = accum_out
            S.op("act", lambda e: e.activation(out=out, in_=in_, func=func, **kw), reads, writes)

        def tt(out, in0, in1, op, reads, writes, eng="dve"):
            S.op(eng, lambda e: e.tensor_tensor(out=out, in0=in0, in1=in1, op=op), reads, writes)

        def stt(out, in0, scalar, in1, op0, op1, reads, writes, eng="dve"):
            S.op(eng, lambda e: e.scalar_tensor_tensor(out=out, in0=in0, scalar=scalar, in1=in1,
                                                        op0=op0, op1=op1), reads, writes)

        def cp(out, in_, reads, writes, eng="dve"):
            S.op(eng, lambda e: e.tensor_copy(out=out, in_=in_), reads, writes)

        def mm_group(out, pairs, reads, writes):
            def fn(e):
                ins = None
                n = len(pairs)
                for i, (l, r) in enumerate(pairs):
                    ins = e.matmul(out, l, r, start=(i == 0), stop=(i == n - 1))
                return ins
            S.op("pe", fn, reads, writes)

        def mm_part(out, pairs, reads, writes, first, last):
            def fn(e):
                ins = None
                n = len(pairs)
                for i, (l, r) in enumerate(pairs):
                    ins = e.matmul(out, l, r, start=(first and i == 0), stop=(last and i == n - 1))
                return ins
            S.op("pe", fn, reads, writes)

        def mm_multi(groups, reads, writes):
            def fn(e):
                ins = None
                for out, pairs in groups:
                    n = len(pairs)
                    for i, (l, r) in enumerate(pairs):
                        ins = e.matmul(out, l, r, start=(i == 0), stop=(i == n - 1))
                return ins
            S.op("pe", fn, reads, writes)

        passes = [(ti, l) for ti in range(n_tiles) for l in layers]
        wseq = [(l, nm) for (_, l) in passes for (nm, _) in SLABS]
        wstate = {"issue": 0, "acq": 0}

        def w_issue():
            i = wstate["issue"]
            if i >= len(wseq):
                return
            wstate["issue"] = i + 1
            l, nm = wseq[i]
            off, n = SLAB_OFF[nm]
            slot = i % NBUF
            o = wslab[:, slot, 0:n]
            src = wts_d[l, :, off:off + n]
            S.dma("pool", "w%d" % slot, lambda e: e.dma_start(out=o, in_=src), (), (b_wslab[slot],))

        def w_acquire(expect):
            i = wstate["acq"]
            wstate["acq"] = i + 1
            assert wseq[i][1] == expect, (wseq[i], expect)
            slot = i % NBUF
            return wslab[:, slot, :], b_wslab[slot]

        def w_release(n=1):
            for _ in range(n):
                w_issue()

        S.dma("sp", "cst0", lambda e: e.dma_start(out=cst[:, :], in_=cst_d[:, :]), (), (b_cst,))
        S.dma("sp", "cst1", lambda e: e.dma_start(out=srowf, in_=srow_d.rearrange("o (a n) -> o a n", a=4)),
              (), tuple(fr.bufs[0:4]))
        S.dma("sp", "cst2", lambda e: e.dma_start(out=wstage, in_=wst_d.rearrange("p (a n) -> p a n", a=4)),
              (), tuple(fr.bufs[4:8]))
        for _ in range(NBUF):
            w_issue()
        S.op("dve", lambda e: e.memset(ones[:, :], 1.0), (), (b_ones,))
        S.op("dve", lambda e: e.memset(srow[:, :], 0.0), (), (b_srow,))
        act(srow[0:1, :].rearrange("o (a n) -> o a n", a=4), srowf, AF.Exp, tuple(fr.bufs[0:4]), (b_srow,))
        act(cst[:, C_ABIAS:C_ABIAS + 2048], cst[:, C_ABIAS:C_ABIAS + 2048], AF.Exp, (b_cst,), (b_cst,))
        for i in range(16):
            tt(wsT[:, i, :], wstage[:, i // 4, (i % 4) * 128:(i % 4 + 1) * 128], cst[:, C_TRIL:C_TRIL + 128], ALU.mult,
               (fr.bufs[4 + i // 4], b_cst), (b_wsT,))

        def rms_sq_act(xr, bxr, c):
            sq, bsq = sqr.next()
            act(sq, xr[:, c, :], AF.Square, (bxr[c],), (bsq,))
            return sq, bsq

        def rms_sq_mm(sqb, c):
            sq, bsq = sqb
            S.op("pe", (lambda e: e.matmul(ss_ps, ones[:, :], sq, start=(c == 0), stop=(c == 7))),
                 (bsq, b_ones), (b_ss,))

        def rms_finish(xr, bxr, gcol):
            ps, bps = ss_ps, b_ss
            rtmp, b_rtmp = fr.next()
            act(rtmp, ps, AF.Ln, (bps,), (b_rtmp,), bias=EPS, scale=1.0 / D)
            rstd, b_rstd = fr.next()
            act(rstd, rtmp, AF.Exp, (b_rtmp,), (b_rstd,), scale=-0.5)
            for c in range(8):
                stt(hT[:, c, :], xr[:, c, :], cst[:, gcol + c:gcol + c + 1], rstd, ALU.mult, ALU.mult,
                    (bxr[c], b_rstd, b_cst), (b_hT[c],))

        def headnorm_a(ps, bps, gcolumn, out, bout):
            sq, bsq = qsqr.next()
            act(sq[0:64, :], ps[0:64, :], AF.Square, (bps,), (bsq,))
            return lambda: headnorm_b(ps, bps, gcolumn, out, bout, sq, bsq)

        def headnorm_b(ps, bps, gcolumn, out, bout, sq, bsq):
            ps2, bps2 = psr.next()
            S.op("pe", lambda e: e.matmul(ps2[0:64, :], ones[0:64, 0:64], sq[0:64, :], start=True, stop=True),
                 (bsq, b_ones), (bps2,))
            r1, br1 = rqr.next()
            act(r1[0:64, :], ps2[0:64, :], AF.Ln, (bps2,), (br1,), bias=EPS, scale=1.0 / HD)
            r2, br2 = rq2r.next()
            act(r2[0:64, :], r1[0:64, :], AF.Exp, (br1,), (br2,), scale=-0.5)
            stt(out, ps[0:64, :], cst[0:64, gcolumn:gcolumn + 1], r2[0:64, :], ALU.mult, ALU.mult,
                (bps, br2, b_cst), (bout,))

        def emit_xload(ti):
            s_idx = ti // tiles_per_seq
            t0 = (ti % tiles_per_seq) * T
            xi = ti % 2
            src = xT_d[s_idx].rearrange("(c p) t -> p c t", p=128)[:, :, t0:t0 + T]
            dstt = xres[xi][:, :, :]
            S.dma("sp", "xl%d" % xi, lambda e: e.dma_start(out=dstt, in_=src), (), tuple(b_xres[xi]))

        def kouter(outs, lhs_fn, reads_w, wr_bufs):
            for k in range(8):
                groups = [(o, lhs_fn(i, k), hT[:, k, :]) for i, o in enumerate(outs)]
                def fn(e, groups=groups, k=k):
                    ins = None
                    for (o, l_, r_) in groups:
                        ins = e.matmul(o, l_, r_, start=(k == 0), stop=(k == 7))
                    return ins
                S.op("pe", fn, (*reads_w, b_hT[k]), tuple(wr_bufs))

        def run_pass(ti, l, first_layer, last_layer, nxt):
            _CUR[0], _CUR[1] = ti, l
            s_idx = ti // tiles_per_seq
            tt_i = ti % tiles_per_seq
            t0 = tt_i * T
            first_in_seq = tt_i == 0
            last_in_seq = tt_i == tiles_per_seq - 1
            xi = ti % 2
            xr = xres[xi]
            bxr = b_xres[xi]
            vbase = C_VEC + 192 * l
            G1, G2 = vbase, vbase + 8
            CW0, CW1, CW2, CB = vbase + 16, vbase + 60, vbase + 104, vbase + 148
            QG, KG = C_QK + 2 * l, C_QK + 2 * l + 1

            _ck(0)
            rms_finish(xr, bxr, G1)
            _ck(1)

            pend_hn = []

            def flush_hn():
                while pend_hn:
                    pend_hn.pop(0)()

            def proj_head(vW, bW, cols, gcolumn, out, bout):
                ps, bps = psr.next()
                mm_group(ps[0:64, :], [(vW[:, k, cols], hT[:, k, :]) for k in range(8)], (bW, *b_hT), (bps,))
                part_b = headnorm_a(ps, bps, gcolumn, out, bout)
                flush_hn()
                pend_hn.append(part_b)

            wA0, bA0 = w_acquire("A0")
            vA0 = wA0[:, 0:2048].rearrange("p (k n) -> p k n", k=8)
            qps = [psr.next() for _ in range(4)]
            kouter([p[0][0:64, :] for p in qps], lambda i, k: vA0[:, k, i * 64:(i + 1) * 64], (bA0,),
                   [p[1] for p in qps])
            for h in range(4):
                pend_hn.append(headnorm_a(qps[h][0], qps[h][1], QG, qT[:, h, :], b_qT[h]))
            w_release()
            _ck(2)

            wB, bB = w_acquire("B")
            vB = wB[:, 0:2048].rearrange("p (k n) -> p k n", k=8)
            for g in range(2):
                proj_head(vB, bB, slice(g * 64, (g + 1) * 64), KG, kT[l][:, g, 128:128 + T], b_kcur[l][g])
            ps, bps = psr.next()
            mm_multi([(ps[:, b * 128:(b + 1) * 128],
                       [(hT[:, k, b * 128:(b + 1) * 128], vB[:, k, 128:256]) for k in range(8)])
                      for b in range(NB)], (bB, *b_hT), (bps,))
            flush_hn()
            cp(vtok[l][:, 1:NB + 1, :], ps.rearrange("p (b n) -> p b n", b=NB), (bps,), (b_vcur[l],))
            w_release()

            def sgu_block(b):
                ps, bps = psr.next()
                groups = []
                for gp in range(4):
                    for sl in range(2):
                        g = 2 * gp + sl
                        groups.append((ps[64 * sl:64 * sl + 64, gp * 128:(gp + 1) * 128],
                                       [(vn[:, b, g * 64:(g + 1) * 64], wsT[:, l * 8 + g, :])]))
                mm_multi(groups, (b_vn[b], b_wsT), (bps,))
                tm, btm = tmr.next()
                tt(tm, ps, cst[:, C_BSB + 512 * l:C_BSB + 512 * l + 512], ALU.add, (bps, b_cst), (btm,))
                tt(ysguT[:, :, b * 128:(b + 1) * 128], tm.rearrange("p (a n) -> p a n", a=4),
                   uT[:, :, b * 128:(b + 1) * 128], ALU.mult, (btm, *b_uT), tuple(b_actT[12:16]))

            def att_stage1(b, g):
                halves = []
                if not (first_in_seq and b == 0):
                    halves.append(0)
                halves.append(1)
                pts = {}
                for hf in halves:
                    ps, bps = psr.next()
                    kcols = slice(128 * (b + hf), 128 * (b + hf) + 128)
                    kb = [b_kcur[l][g]] + ([b_kprev[l]] if (b == 0 and hf == 0) else [])
                    lhs_ = kT[l][:, g, kcols]
                    rhs_ = qT[:, 4 * g:4 * g + 4, b * 128:(b + 1) * 128]
                    out_ = ps.rearrange("p (a n) -> p a n", a=4)
                    S.op("pe", (lambda e, out_=out_, lhs_=lhs_, rhs_=rhs_: e.matmul(
                        out_, lhs_, rhs_, start=True, stop=True)),
                        (*kb, *b_qT[4 * g:4 * g + 4]), (bps,))
                    e_, be_ = efr.next()
                    act(e_, ps, AF.Exp, (bps,), (be_,), scale=0.125)
                    p_, bp_ = pTr.next()
                    col = C_ABIAS + (g * 2 + hf) * 512
                    tt(p_, e_, cst[:, col:col + 512], ALU.mult, (be_, b_cst), (bp_,), eng="pool")
                    pts[hf] = (p_, bp_)
                return halves, pts

            def att_stage2(b, g, halves, pts):
                yd, byd = psr.next()
                groups = []
                sbase = ((l * 2 + g) * 2) * 256
                for sl in range(2):
                    ypairs, dpairs = [], []
                    for hf in halves:
                        p_ = pts[hf][0]
                        rhs = p_.rearrange("p (pr s n) -> p pr s n", pr=2, s=2)[:, :, sl, :]
                        ypairs.append((vtok[l][:, b + hf, g * 64:(g + 1) * 64], rhs))
                        dpairs.append((ones[:, 0:64], rhs))
                    dpairs.append((ones[:, 0:64],
                                   srow[:, sbase + sl * 256:sbase + sl * 256 + 256].rearrange("p (a n) -> p a n", a=2)))
                    groups.append((yd[64 * sl:64 * sl + 64, 0:256].rearrange("p (a n) -> p a n", a=2), ypairs))
                    groups.append((yd[64 * sl:64 * sl + 64, 256:512].rearrange("p (a n) -> p a n", a=2), dpairs))
                vb = [b_vcur[l]] + ([b_vprev[l]] if b == 0 and 0 in halves else [])
                mm_multi(groups, (*[pts[hf][1] for hf in halves], *vb, b_ones, b_srow), (byd,))
                rd, brd = rdr.next()
                S.op("dve", lambda e, rd=rd, yd=yd: e.reciprocal(out=rd, in_=yd[:, 256:512]), (byd,), (brd,))
                tt(yattT[:, 2 * g:2 * g + 2, b * 128:(b + 1) * 128],
                   yd[:, 0:256].rearrange("p (a n) -> p a n", a=2),
                   rd.rearrange("p (a n) -> p a n", a=2), ALU.mult, (byd, brd),
                   (b_yatt[2 * g][b], b_yatt[2 * g + 1][b]))

            _ck(4)
            slabs = {}

            def get_slab(nm):
                if nm not in slabs:
                    w_, b_ = w_acquire(nm)
                    n_ = 2048 if nm == "A1" else 4096
                    slabs[nm] = (w_[:, 0:n_].rearrange("p (k n) -> p k n", k=8), b_)
                return slabs[nm]

            def u_qhead(h):
                vA1, bA1 = get_slab("A1")
                proj_head(vA1, bA1, slice((h - 4) * 64, (h - 3) * 64), QG, qT[:, h, :], b_qT[h])
                if h == 7:
                    w_release()

            def u_su(c):
                vC, bC = get_slab("C")
                flush_hn()
                ps, bps = psr.next()
                mm_group(ps, [(vC[:, k, c * 128:(c + 1) * 128], hT[:, k, :]) for k in range(8)],
                         (bC, *b_hT), (bps,))
                act(uT[:, c, :], ps, AF.Gelu_apprx_tanh, (bps,), (b_uT[c],))
                if c == 3:
                    w_release()

            def u_sv(b):
                vD, bD = get_slab("D")
                ps, bps = psr.next()
                mm_group(ps, [(hT[:, k, b * 128:(b + 1) * 128], vD[:, k, :]) for k in range(8)],
                         (bD, *b_hT), (bps,))
                g_, bg_ = gvr.next()
                act(g_, ps, AF.Gelu_apprx_tanh, (bps,), (bg_,))
                si = small_i[0]
                small_i[0] = (si + 1) % 8
                act(junk[:, :], g_, AF.Square, (bg_,), (b_junk, b_ssv[si]), accum_out=ssv[:, si:si + 1])
                act(lnv[:, si:si + 1], ssv[:, si:si + 1], AF.Ln, (b_ssv[si],), (b_lnv[si],), bias=EPS, scale=1.0 / 512)
                act(rv[:, si:si + 1], lnv[:, si:si + 1], AF.Exp, (b_lnv[si],), (b_rv[si],), scale=-0.5)
                stt(vn[:, b, :], g_, rv[:, si:si + 1], cst[:, C_SGUG + 512 * l:C_SGUG + 512 * l + 512],
                    ALU.mult, ALU.mult, (bg_, b_rv[si], b_cst), (b_vn[b],))
                if b == NB - 1:
                    w_release()

            units = ([lambda h=h: u_qhead(h) for h in range(4, 8)] + [lambda c=c: u_su(c) for c in range(4)]
                     + [lambda b=b: u_sv(b) for b in range(NB)])
            its = [(b, 0) for b in range(NB)] + [(b, 1) for b in range(NB)]
            pend = [att_stage1(*its[0]), att_stage1(*its[1])]
            for i, (b, g) in enumerate(its):
                if g == 0:
                    for _ in range(3):
                        units.pop(0)()
                else:
                    sgu_block(b)
                if i + 2 < len(its):
                    pend.append(att_stage1(*its[i + 2]))
                att_stage2(b, g, *pend.pop(0))
            assert not units
            _ck(3)
            if not last_in_seq:
                cp(kT[l][:, :, 0:128], kT[l][:, :, T:T + 128], (*b_kcur[l],), (b_kprev[l],))
                cp(vtok[l][:, 0, :], vtok[l][:, NB, :], (b_vcur[l],), (b_vprev[l],))

            _ck(5)
            for half in range(2):
                wE, bE = w_acquire("EF"[half])
                wG, bG = w_acquire("GH"[half])
                wO, bO = w_acquire("OAB%d" % half)
                vE = wE.rearrange("p (k n) -> p k n", k=8)
                vG = wG.rearrange("p (k n) -> p k n", k=8)
                vO = wO.rearrange("p (m k n) -> p m k n", m=2, k=4)
                for c4 in range(4):
                    c = 4 * half + c4
                    cs = slice(c4 * 128, (c4 + 1) * 128)
                    pga, bpga = psr.next()
                    mm_group(pga, [(vE[:, k, cs], hT[:, k, :]) for k in range(8)], (bE, *b_hT), (bpga,))
                    pgb, bpgb = psr.next()
                    mm_group(pgb, [(vG[:, k, cs], hT[:, k, :]) for k in range(8)], (bG, *b_hT), (bpgb,))
                    pa, bpa = psr.next()
                    mm_group(pa, [(vO[:, 0, kc, cs], yattT[:, kc, :]) for kc in range(4)],
                             (bO, *b_actT[8:12]), (bpa,))
                    pb, bpb = psr.next()
                    mm_group(pb, [(vO[:, 1, kc, cs], ysguT[:, kc, :]) for kc in range(4)], (bO, *b_actT[12:16]), (bpb,))
                    sa, bsa = sar.next()
                    act(sa, pga, AF.Sigmoid, (bpga,), (bsa,))
                    sb_, bsb_ = sbr.next()
                    act(sb_, pgb, AF.Sigmoid, (bpgb,), (bsb_,))
                    t1, bt1 = t1r.next()
                    tt(t1, pa, sa, ALU.mult, (bpa, bsa), (bt1,))
                    t2, bt2 = t2r.next()
                    tt(t2, pb, sb_, ALU.mult, (bpb, bsb_), (bt2,))
                    tt(mergedT[:, c, :], t1, t2, ALU.add, (bt1, bt2), (b_merged[c],), eng="pool")
                w_release(3)
            sqbs = {}
            for half in range(2):
                wO, bO = w_acquire("OUT%d" % half)
                vO = wO.rearrange("p (k n) -> p k n", k=8)
                for c4 in range(4):
                    c = 4 * half + c4
                    po, bpo = psr.next()
                    mm_group(po, [(vO[:, k, c4 * 128:(c4 + 1) * 128], mergedT[:, k, :]) for k in range(8)],
                             (bO, *b_merged), (bpo,))
                    if c >= 1:
                        rms_sq_mm(sqbs[c - 1], c - 1)
                    tt(xr[:, c, :], po, xr[:, c, :], ALU.add, (bpo, bxr[c]), (bxr[c],))
                    sqbs[c] = rms_sq_act(xr, bxr, c)
                w_release()
            rms_sq_mm(sqbs[7], 7)

            _ck(6)
            rms_finish(xr, bxr, G2)
            if last_layer and nxt is not None:
                emit_xload(nxt[0])

            def ffn_epilogue(j, pg, bpg, pv, bpv):
                ag, bag = agr.next()
                av, bav = avr.next()
                items = ((pg, bpg, ag, bag, j), (pv, bpv, av, bav, NJ + j))
                for (ps, bps, a_, ba_, ch) in items:
                    act(a_, ps, AF.Identity, (bps, b_cst), (ba_,),
                        bias=cst[:, CB + ch:CB + ch + 1], scale=cst[:, CW2 + ch:CW2 + ch + 1])
                for (ps, bps, a_, ba_, ch) in items:
                    stt(a_[:, 1:T], ps[:, 0:T - 1], cst[:, CW1 + ch:CW1 + ch + 1], a_[:, 1:T], ALU.mult, ALU.add,
                        (bps, ba_, b_cst), (ba_,))
                for (ps, bps, a_, ba_, ch) in items:
                    stt(a_[:, 2:T], ps[:, 0:T - 2], cst[:, CW0 + ch:CW0 + ch + 1], a_[:, 2:T], ALU.mult, ALU.add,
                        (bps, ba_, b_cst), (ba_,))
                if not first_in_seq:
                    for (ps, bps, a_, ba_, ch) in items:
                        tt(a_[:, 0:2], a_[:, 0:2], hc[:, ch, :], ALU.add, (ba_, b_hc), (ba_,), eng="pool")
                if not last_in_seq:
                    for (ps, bps, a_, ba_, ch) in items:
                        act(halo[l][:, ch, :], ps[:, T - 2:T], AF.Identity, (bps,), (b_halo[l][ch],))
                sg, bsg = sgr.next()
                act(sg, ag, AF.Silu, (bag,), (bsg,))
                tt(actT[:, j, :], sg, av, ALU.mult, (bsg, bav), (b_actT[j],), eng="pool")

            if not first_in_seq:
                bh = tuple(b_halo[l])
                tt(hc[:, :, 1], halo[l][:, :, 1], cst[:, CW0:CW0 + 44], ALU.mult, (*bh, b_cst), (b_hc,))
                tt(hc[:, :, 0], halo[l][:, :, 0], cst[:, CW0:CW0 + 44], ALU.mult, (*bh, b_cst), (b_hc,))
                tt(hctmp[:, :], halo[l][:, :, 1], cst[:, CW1:CW1 + 44], ALU.mult, (*bh, b_cst), (b_hctmp,))
                tt(hc[:, :, 0], hc[:, :, 0], hctmp[:, :], ALU.add, (b_hc, b_hctmp), (b_hc,))
            for i in range(11):
                wU, bU = w_acquire("UP%d" % i)
                vU = wU.rearrange("p (k n) -> p k n", k=8)
                if i == 0:
                    pss = [psr.next() for _ in range(4)]
                    offs = [0, 256, 128, 384]
                    kouter([p[0] for p in pss], lambda q, k: vU[:, k, offs[q]:offs[q] + 128], (bU,),
                           [p[1] for p in pss])
                    ffn_epilogue(0, pss[0][0], pss[0][1], pss[1][0], pss[1][1])
                    ffn_epilogue(1, pss[2][0], pss[2][1], pss[3][0], pss[3][1])
                else:
                    for jj in range(2):
                        j = 2 * i + jj
                        pg, bpg = psr.next()
                        mm_group(pg, [(vU[:, k, jj * 128:(jj + 1) * 128], hT[:, k, :]) for k in range(8)],
                                 (bU, *b_hT), (bpg,))
                        pv, bpv = psr.next()
                        mm_group(pv, [(vU[:, k, 256 + jj * 128:256 + (jj + 1) * 128], hT[:, k, :]) for k in range(8)],
                                 (bU, *b_hT), (bpv,))
                        ffn_epilogue(j, pg, bpg, pv, bpv)
                w_release()
            _ck(7)
            if nxt is not None:
                nxr, nbxr = xres[nxt[0] % 2], b_xres[nxt[0] % 2]
            sqbs = {}

            def down_tail(c, pd, bpd):
                if nxt is not None and c >= 1:
                    rms_sq_mm(sqbs[c - 1], c - 1)
                tt(xr[:, c, :], pd, xr[:, c, :], ALU.add, (bpd, bxr[c]), (bxr[c],))
                if nxt is not None:
                    sqbs[c] = rms_sq_act(nxr, nbxr, c)

            JS = 14
            first = []
            for c in range(4):
                wDn, bDn = w_acquire("DN%d" % c)
                vDn = wDn[:, 0:2816].rearrange("p (k n) -> p k n", k=NJ)
                pd, bpd = psr.next()
                mm_part(pd, [(vDn[:, j, :], actT[:, j, :]) for j in range(JS)], (bDn, *b_actT[0:JS]), (bpd,), True, False)
                first.append((vDn, bDn, pd, bpd))
            for c in range(4):
                vDn, bDn, pd, bpd = first[c]
                mm_part(pd, [(vDn[:, j, :], actT[:, j, :]) for j in range(JS, NJ)], (bDn, *b_actT[JS:NJ]), (bpd,), False, True)
                down_tail(c, pd, bpd)
                w_release()
            for c in range(4, 8):
                wDn, bDn = w_acquire("DN%d" % c)
                vDn = wDn[:, 0:2816].rearrange("p (k n) -> p k n", k=NJ)
                pd, bpd = psr.next()
                mm_group(pd, [(vDn[:, j, :], actT[:, j, :]) for j in range(NJ)], (bDn, *b_actT), (bpd,))
                down_tail(c, pd, bpd)
                w_release()
            if nxt is not None:
                rms_sq_mm(sqbs[7], 7)

            if last_layer:
                dst = yT_d[s_idx].rearrange("(c p) t -> p c t", p=128)[:, :, t0:t0 + T]
                S.dma("sp", "xs%d" % xi, lambda e: e.dma_start(out=dst, in_=xr[:, :, :]), tuple(bxr), ())

        plist = [(ti, li) for ti in range(n_tiles) for li in range(len(layers))]
        emit_xload(0)
        sq0 = [rms_sq_act(xres[0], b_xres[0], c) for c in range(4)]
        for c in range(8):
            rms_sq_mm(sq0[c] if c < 4 else rms_sq_act(xres[0], b_xres[0], c), c)
        try:
            for pi, (ti, li) in enumerate(plist):
                nxt = plist[pi + 1] if pi + 1 < len(plist) else None
                run_pass(ti, layers[li], li == 0, li == len(layers) - 1, nxt)
        except _Stop:
            dst = yT_d[0].rearrange("(c p) t -> p c t", p=128)[:, :, 0:T]
            S.dma("sp", "xs0", lambda e: e.dma_start(out=dst, in_=xres[0][:, :, :]), tuple(b_xres[0]), ())
        S.wait_all("sp", [b for i in range(2) for b in b_xres[i]])

        sems = {k: es.enter_context(nc.semaphore(k)) for k in S.semnames}
        block = es.enter_context(nc.Block())

        @block.tensor
        def _(e):
            S.replay("pe", e, sems)

        @block.scalar
        def _(e):
            S.replay("act", e, sems)

        @block.vector
        def _(e):
            S.replay("dve", e, sems)

        @block.gpsimd
        def _(e):
            S.replay("pool", e, sems)

        @block.sync
        def _(e):
            S.replay("sp", e, sems)
    return nc


def _pkn(w):
    kc = w.shape[0] // 128
    return np.ascontiguousarray(w.reshape(kc, 128, -1).transpose(1, 0, 2).reshape(128, -1))


def pack_weights(w_in, w_oa, w_ob, w_out, w_up, w_down):
    out = np.empty((NL, 128, WPL), np.float32)
    for l in range(NL):
        parts = {
            "A0": _pkn(w_in[l][:, 0:256]), "A1": _pkn(w_in[l][:, 256:512]), "B": _pkn(w_in[l][:, 512:768]),
            "C": _pkn(w_in[l][:, 768:1280]), "D": _pkn(w_in[l][:, 1280:1792]),
            "E": _pkn(w_in[l][:, 1792:2304]), "F": _pkn(w_in[l][:, 2304:2816]),
            "G": _pkn(w_in[l][:, 2816:3328]), "H": _pkn(w_in[l][:, 3328:3840]),
            "OAB0": np.concatenate([_pkn(w_oa[l][:, 0:512]), _pkn(w_ob[l][:, 0:512])], axis=1),
            "OAB1": np.concatenate([_pkn(w_oa[l][:, 512:1024]), _pkn(w_ob[l][:, 512:1024])], axis=1),
            "OUT0": _pkn(w_out[l][:, 0:512]), "OUT1": _pkn(w_out[l][:, 512:1024]),
        }
        for i in range(11):
            parts["UP%d" % i] = _pkn(np.concatenate(
                [w_up[l][:, 256 * i:256 * i + 256], w_up[l][:, DFF + 256 * i:DFF + 256 * i + 256]], axis=1))
        for c in range(8):
            parts["DN%d" % c] = _pkn(w_down[l][:, 128 * c:128 * c + 128])
        for nm, n in SLABS:
            off, _ = SLAB_OFF[nm]
            assert parts[nm].shape == (128, n), (nm, parts[nm].shape)
            out[l, :, off:off + n] = parts[nm]
    return out


def pack_consts(mix_norm, q_norm, k_norm, sinks, sgu_norm, w_s, b_s, ffn_norm, conv_w, conv_b):
    cst = np.zeros((128, NCST), np.float32)
    for l in range(NL):
        vb = C_VEC + 192 * l
        cst[:, vb:vb + 8] = mix_norm[l].reshape(8, 128).T
        cst[:, vb + 8:vb + 16] = ffn_norm[l].reshape(8, 128).T
        for tap in range(3):
            cst[:, vb + 16 + 44 * tap:vb + 16 + 44 * (tap + 1)] = conv_w[l, tap].reshape(44, 128).T
        cst[:, vb + 148:vb + 192] = conv_b[l].reshape(44, 128).T
        cst[0:64, C_QK + 2 * l] = q_norm[l]
        cst[0:64, C_QK + 2 * l + 1] = k_norm[l]
        cst[:, C_SGUG + 512 * l:C_SGUG + 512 * (l + 1)] = sgu_norm[l][None, :]
        for gp in range(4):
            cst[0:64, C_BSB + 512 * l + gp * 128:C_BSB + 512 * l + (gp + 1) * 128] = b_s[l, 2 * gp][None, :]
            cst[64:128, C_BSB + 512 * l + gp * 128:C_BSB + 512 * l + (gp + 1) * 128] = b_s[l, 2 * gp + 1][None, :]
    k = np.arange(128)[:, None]
    q = np.arange(128)[None, :]
    for g in range(2):
        for j in range(4):
            slope = 2.0 ** (-(4 * g + j + 1))
            dist_prev = q + 128 - k
            dist_cur = q - k
            bp = np.where(dist_prev < 128, -slope * dist_prev, -30000.0)
            bc = np.where(dist_cur >= 0, -slope * dist_cur, -30000.0)
            cst[:, C_ABIAS + (g * 2 + 0) * 512 + j * 128:C_ABIAS + (g * 2 + 0) * 512 + (j + 1) * 128] = bp
            cst[:, C_ABIAS + (g * 2 + 1) * 512 + j * 128:C_ABIAS + (g * 2 + 1) * 512 + (j + 1) * 128] = bc
    cst[:, C_TRIL:C_TRIL + 128] = (k <= q).astype(np.float32)
    srow = np.zeros((1, 2048), np.float32)
    for l in range(NL):
        for g in range(2):
            for sl in range(2):
                for pr in range(2):
                    base = (((l * 2 + g) * 2 + sl) * 2 + pr) * 128
                    srow[0, base:base + 128] = sinks[l, 4 * g + 2 * pr + sl]
    wst = np.ascontiguousarray(np.transpose(w_s, (3, 0, 1, 2)).reshape(128, NL * 8 * 128)).astype(np.float32)
    return cst, srow, wst


_NC_CACHE = {}
DBG_STOP = None


class _Stop(Exception):
    pass


_CUR = [0, 0]


def _ck(k):
    if DBG_STOP is not None and DBG_STOP == (_CUR[0], _CUR[1], k):
        raise _Stop()


def run(x, params, n_cores, layers=(0, 1)):
    B, S_, _ = x.shape
    n_seq = B // n_cores
    key = (n_seq, S_, tuple(layers))
    if key not in _NC_CACHE:
        _NC_CACHE[key] = build(n_seq, S_, layers)
    nc = _NC_CACHE[key]
    wts = pack_weights(params["w_in"], params["w_oa"], params["w_ob"], params["w_out"], params["w_up"],
                       params["w_down"])
    cst, srow, wst = pack_consts(params["mix_norm"], params["q_norm"], params["k_norm"], params["sinks"],
                                 params["sgu_norm"], params["w_s"], params["b_s"], params["ffn_norm"],
                                 params["conv_w"], params["conv_b"])
    in_maps = []
    for c in range(n_cores):
        xc = np.ascontiguousarray(np.transpose(x[c * n_seq:(c + 1) * n_seq], (0, 2, 1)))
        in_maps.append({"xT": xc, "wts": wts, "cst": cst, "srow": srow, "wst": wst})
    res = run_bass_kernel_spmd(nc, in_maps, core_ids=list(range(n_cores)))
    outs = [np.transpose(r["yT"], (0, 2, 1)) for r in res.results]
    return np.ascontiguousarray(np.concatenate(outs, axis=0)).astype(np.float32)


def kernel(**inputs):
    inputs = {k: np.asarray(v) for k, v in inputs.items()}
    x = inputs.pop("x").astype(np.float32)
    params = {k: v.astype(np.float32) for k, v in inputs.items()}
    return run(x, params, 8)
```

```python
from contextlib import ExitStack

import numpy as np
import concourse.bass as bass
import concourse.mybir as mybir
from concourse.bass_utils import run_bass_kernel_spmd

F32 = mybir.dt.float32
BF16 = mybir.dt.bfloat16
AF = mybir.ActivationFunctionType
ALU = mybir.AluOpType

D = 1024
NL = 2
NH = 8
NKV = 2
HD = 64
DFF = 2816
NJ = DFF // 128
T = 512
NB = T // 128
EPS = 1e-6
NBUF = 7
SLAB = 4096

SLABS = ([("A0", 2048), ("B", 2048), ("A1", 2048), ("C", 4096), ("D", 4096), ("E", 4096), ("G", 4096),
          ("OAB0", 4096), ("F", 4096), ("H", 4096), ("OAB1", 4096), ("OUT0", 4096), ("OUT1", 4096)]
         + [("UP%d" % i, 4096) for i in range(11)] + [("DN%d" % c, 2816) for c in range(8)])
SLAB_OFF = {}
_o = 0
for _n, _s in SLABS:
    SLAB_OFF[_n] = (_o, _s)
    _o += _s
WPL = _o

C_VEC = 0
C_QK = 384
C_SGUG = 392
C_BSB = C_SGUG + 1024
C_ABIAS = C_BSB + 1024
C_TRIL = C_ABIAS + 2048
NCST = C_TRIL + 128


class Buf:
    __slots__ = ("name", "lw", "rd", "excl")

    def __init__(self, name, excl=False):
        self.name = name
        self.lw = None
        self.rd = []
        self.excl = excl


class Sched:
    ENG = ("pe", "act", "dve", "pool", "sp")

    def __init__(self):
        self.streams = {e: [] for e in self.ENG}
        self.cnt = {}
        self.waited = {e: {} for e in self.ENG}
        self.semnames = []

    def newsem(self, key):
        self.cnt[key] = 0
        self.semnames.append(key)

    def _waits(self, eng, reads, writes):
        deps = {}
        def add(ev, raw):
            if ev is None:
                return
            k, v = ev
            if k == eng and (eng == "pe" or not raw):
                return
            if deps.get(k, 0) < v:
                deps[k] = v
        for b in reads:
            add(b.lw, True)
            if b.excl:
                for ev in b.rd:
                    add(ev, False)
        for b in writes:
            add(b.lw, False)
            for ev in b.rd:
                add(ev, False)
        for k, v in deps.items():
            if self.waited[eng].get(k, 0) >= v:
                continue
            self.waited[eng][k] = v
            self.streams[eng].append(("wait", k, v))

    def _commit(self, ev, reads, writes):
        for b in reads:
            b.rd.append(ev)
        for b in writes:
            b.lw = ev
            b.rd = []

    def op(self, eng, fn, reads=(), writes=()):
        self._waits(eng, reads, writes)
        self.cnt[eng] += 1
        self.streams[eng].append(("op", fn, eng, 1))
        self._commit((eng, self.cnt[eng]), reads, writes)

    def dma(self, eng, sem, fn, reads=(), writes=()):
        self._waits(eng, reads, writes)
        self.cnt[sem] += 16
        self.streams[eng].append(("op", fn, sem, 16))
        self._commit((sem, self.cnt[sem]), reads, writes)

    def wait_all(self, eng, bufs):
        self._waits(eng, bufs, bufs)

    def replay(self, eng, handle, sems):
        for it in self.streams[eng]:
            if it[0] == "wait":
                handle.wait_ge(sems[it[1]], it[2])
            else:
                ins = it[1](handle)
                ins.then_inc(sems[it[2]], it[3])


class Rot:
    def __init__(self, tiles, name):
        self.tiles = tiles
        self.bufs = [Buf("%s%d" % (name, i)) for i in range(len(tiles))]
        self.i = 0

    def next(self):
        i = self.i
        self.i = (i + 1) % len(self.tiles)
        return self.tiles[i], self.bufs[i]


def build(n_seq, seq_len, layers=(0, 1)):
    nc = bass.Bass("TRN2", target_bir_lowering=False)
    S = Sched()
    tiles_per_seq = seq_len // T
    n_tiles = n_seq * tiles_per_seq
    xT_d = nc.dram_tensor("xT", [n_seq, D, seq_len], F32, kind="ExternalInput").ap()
    wts_d = nc.dram_tensor("wts", [NL, 128, WPL], F32, kind="ExternalInput").ap()
    cst_d = nc.dram_tensor("cst", [128, NCST], F32, kind="ExternalInput").ap()
    srow_d = nc.dram_tensor("srow", [1, 2048], F32, kind="ExternalInput").ap()
    wst_d = nc.dram_tensor("wst", [128, 2048], F32, kind="ExternalInput").ap()
    yT_d = nc.dram_tensor("yT", [n_seq, D, seq_len], F32, kind="ExternalOutput").ap()
    wbf_d = nc.dram_tensor("wbf", [NL, 128, WPL], BF16).ap()

    es = ExitStack()
    with es:
        def sb(name, shape, dt):
            return es.enter_context(nc.sbuf_tensor("s_" + name, shape, dt))

        cst = sb("cst", [128, NCST], F32)
        srow = sb("srow", [128, 2048], BF16)
        wsT = sb("wsT", [128, 16, 128], BF16)
        ones = sb("ones", [128, 128], BF16)
        xres = [sb("xres%d" % i, [128, 8, T], F32) for i in range(2)]
        hT = sb("hT", [128, 8, T], BF16)
        sqt = sb("sqt", [128, 4, T], BF16)
        NFR = 10
        frt = sb("frt", [128, NFR, 512], F32)
        qT = sb("qT", [64, 8, T], BF16)
        kT = [sb("kT%d" % l, [64, 2, T + 128], BF16) for l in range(NL)]
        vtok = [sb("vtok%d" % l, [128, NB + 1, 128], BF16) for l in range(NL)]
        qsq = sb("qsq", [64, 5, T], BF16)
        junk = sb("junk", [128, 512], BF16)
        ssv = sb("ssv", [128, 8], F32)
        lnv = sb("lnv", [128, 8], F32)
        rv = sb("rv", [128, 8], F32)
        vn = sb("vn", [128, NB, 512], BF16)
        pT = sb("pT", [128, 6, 512], BF16)
        rden = sb("rden", [128, 2, 256], F32)
        actT = sb("actT", [128, NJ, T], BF16)
        mergedT = actT[:, 0:8, :]
        yattT = actT[:, 8:12, :]
        ysguT = actT[:, 12:16, :]
        uT = actT[:, 16:20, :]
        halo = [sb("halo%d" % l, [128, 2 * NJ, 2], F32) for l in range(NL)]
        hc = sb("hc", [128, 2 * NJ, 2], F32)
        hctmp = sb("hctmp", [128, 2 * NJ], F32)
        b_hc = Buf("hc")
        b_hctmp = Buf("hctmp")
        wslab = sb("wslab", [128, NBUF, SLAB], BF16)
        psb = [es.enter_context(nc.psum_tensor("ps%d" % i, [128, 512], F32)) for i in range(8)]

        b_cst = Buf("cst")
        b_srow = Buf("srow")
        b_wsT = Buf("wsT")
        b_ones = Buf("ones")
        b_xres = [[Buf("xres%d_%d" % (i, c)) for c in range(8)] for i in range(2)]
        b_hT = [Buf("hT%d" % c) for c in range(8)]
        b_qT = [Buf("qT%d" % h) for h in range(8)]
        b_kprev = [Buf("kprev%d" % l) for l in range(NL)]
        b_kcur = [[Buf("kcur%d_%d" % (l, g)) for g in range(2)] for l in range(NL)]
        b_vprev = [Buf("vprev%d" % l) for l in range(NL)]
        b_vcur = [Buf("vcur%d" % l) for l in range(NL)]
        b_junk = Buf("junk")
        b_ssv = [Buf("ssv%d" % i) for i in range(8)]
        b_lnv = [Buf("lnv%d" % i) for i in range(8)]
        b_rv = [Buf("rv%d" % i) for i in range(8)]
        b_vn = [Buf("vn%d" % b) for b in range(NB)]
        b_actT = [Buf("actT%d" % j) for j in range(NJ)]
        b_merged = b_actT[0:8]
        b_yatt = [[b_actT[8 + c]] * NB for c in range(4)]
        b_uT = b_actT[16:20]
        b_halo = [[Buf("halo%d_%d" % (l, ch)) for ch in range(2 * NJ)] for l in range(NL)]
        b_wslab = [Buf("wslab%d" % i) for i in range(NBUF)]
        b_ps = [Buf("ps%d" % i, excl=True) for i in range(8)]

        sqr = Rot([sqt[:, i, :] for i in range(4)], "sq")
        qsqr = Rot([qsq[:, i, :] for i in range(5)], "qsq")
        fr = Rot([frt[:, i, :] for i in range(NFR)], "fr")
        rqr = rq2r = gvr = efr = tmr = sar = sbr = t1r = t2r = agr = avr = sgr = fr
        srowf = frt[0:1, 0:4, :]
        wstage = frt[:, 4:8, :]
        pTr = Rot([pT[:, i, :] for i in range(6)], "pT")
        rdr = Rot([rden[:, i, :] for i in range(2)], "rden")
        psr = Rot([p[:, :] for p in psb[0:7]], "psr")
        psr.bufs = b_ps[0:7]
        ss_ps, b_ss = psb[7][:, :], b_ps[7]
        small_i = [0]

        for e in ("pe", "act", "dve", "pool"):
            S.newsem(e)
        for i in range(NBUF):
            S.newsem("w%d" % i)
            S.newsem("ws%d" % i)
        b_wd = {(l, nm): Buf("wd%d_%s" % (l, nm)) for l in range(NL) for (nm, _) in SLABS}
        for k in ("cst0", "cst1", "cst2", "xl0", "xl1", "xs0", "xs1"):
            S.newsem(k)

        def act(out, in_, func, reads, writes, bias=None, scale=None, accum_out=None):
            kw = {}
            if bias is not None:
                kw["bias"] = bias
            if scale is not None:
                kw["scale"] = scale
            if accum_out is not None:
                kw["accum_out"] = accum_out
            S.op("act", lambda e: e.activation(out=out, in_=in_, func=func, **kw), reads, writes)

        def tt(out, in0, in1, op, reads, writes, eng="dve"):
            S.op(eng, lambda e: e.tensor_tensor(out=out, in0=in0, in1=in1, op=op), reads, writes)

        def stt(out, in0, scalar, in1, op0, op1, reads, writes, eng="dve"):
            S.op(eng, lambda e: e.scalar_tensor_tensor(out=out, in0=in0, scalar=scalar, in1=in1,
                                                        op0=op0, op1=op1), reads, writes)

        def cp(out, in_, reads, writes, eng="dve"):
            S.op(eng, lambda e: e.tensor_copy(out=out, in_=in_), reads, writes)

        def mm_group(out, pairs, reads, writes):
            def fn(e):
                ins = None
                n = len(pairs)
                for i, (l, r) in enumerate(pairs):
                    ins = e.matmul(out, l, r, start=(i == 0), stop=(i == n - 1))
                return ins
            S.op("pe", fn, reads, writes)

        def mm_part(out, pairs, reads, writes, first, last):
            def fn(e):
                ins = None
                n = len(pairs)
                for i, (l, r) in enumerate(pairs):
                    ins = e.matmul(out, l, r, start=(first and i == 0), stop=(last and i == n - 1))
                return ins
            S.op("pe", fn, reads, writes)

        def mm_multi(groups, reads, writes):
            def fn(e):
                ins = None
                for out, pairs in groups:
                    n = len(pairs)
                    for i, (l, r) in enumerate(pairs):
                        ins = e.matmul(out, l, r, start=(i == 0), stop=(i == n - 1))
                return ins
            S.op("pe", fn, reads, writes)

        passes = [(ti, l) for ti in range(n_tiles) for l in layers]
        wseq = [(l, nm) for (_, l) in passes for (nm, _) in SLABS]
        wstate = {"issue": 0, "acq": 0}

        def w_issue():
            i = wstate["issue"]
            if i >= len(wseq):
                return
            wstate["issue"] = i + 1
            l, nm = wseq[i]
            off, n = SLAB_OFF[nm]
            slot = i % NBUF
            o = wslab[:, slot, 0:n]
            if passes[i // len(SLABS)][0] == 0:
                src = wts_d[l, :, off:off + n]
                S.dma("pool", "w%d" % slot, lambda e: e.dma_start(out=o, in_=src), (), (b_wslab[slot],))
            else:
                src = wbf_d[l, :, off:off + n]
                S.dma("sp", "w%d" % slot, lambda e: e.dma_start(out=o, in_=src), (b_wd[(l, nm)],),
                      (b_wslab[slot],))

        def w_acquire(expect):
            i = wstate["acq"]
            wstate["acq"] = i + 1
            assert wseq[i][1] == expect, (wseq[i], expect)
            slot = i % NBUF
            if passes[i // len(SLABS)][0] == 0 and n_tiles > 1:
                l, nm = wseq[i]
                off, n = SLAB_OFF[nm]
                dst = wbf_d[l, :, off:off + n]
                srcs = wslab[:, slot, 0:n]
                S.dma("sp", "ws%d" % slot, lambda e: e.dma_start(out=dst, in_=srcs), (b_wslab[slot],),
                      (b_wd[(l, nm)],))
            return wslab[:, slot, :], b_wslab[slot]

        def w_release(n=1):
            for _ in range(n):
                w_issue()

        S.dma("sp", "cst0", lambda e: e.dma_start(out=cst[:, :], in_=cst_d[:, :]), (), (b_cst,))
        S.dma("sp", "cst1", lambda e: e.dma_start(out=srowf, in_=srow_d.rearrange("o (a n) -> o a n", a=4)),
              (), tuple(fr.bufs[0:4]))
        S.dma("sp", "cst2", lambda e: e.dma_start(out=wstage, in_=wst_d.rearrange("p (a n) -> p a n", a=4)),
              (), tuple(fr.bufs[4:8]))
        for _ in range(NBUF):
            w_issue()
        S.op("dve", lambda e: e.memset(ones[:, :], 1.0), (), (b_ones,))
        S.op("dve", lambda e: e.memset(srow[:, :], 0.0), (), (b_srow,))
        act(srow[0:1, :].rearrange("o (a n) -> o a n", a=4), srowf, AF.Exp, tuple(fr.bufs[0:4]), (b_srow,))
        act(cst[:, C_ABIAS:C_ABIAS + 2048], cst[:, C_ABIAS:C_ABIAS + 2048], AF.Exp, (b_cst,), (b_cst,))
        for i in range(16):
            tt(wsT[:, i, :], wstage[:, i // 4, (i % 4) * 128:(i % 4 + 1) * 128], cst[:, C_TRIL:C_TRIL + 128], ALU.mult,
               (fr.bufs[4 + i // 4], b_cst), (b_wsT,))

        def rms_sq_act(xr, bxr, c):
            sq, bsq = sqr.next()
            act(sq, xr[:, c, :], AF.Square, (bxr[c],), (bsq,))
            return sq, bsq

        def rms_sq_mm(sqb, c):
            sq, bsq = sqb
            S.op("pe", (lambda e: e.matmul(ss_ps, ones[:, :], sq, start=(c == 0), stop=(c == 7))),
                 (bsq, b_ones), (b_ss,))

        def rms_finish(xr, bxr, gcol):
            ps, bps = ss_ps, b_ss
            rtmp, b_rtmp = fr.next()
            act(rtmp, ps, AF.Ln, (bps,), (b_rtmp,), bias=EPS, scale=1.0 / D)
            rstd, b_rstd = fr.next()
            act(rstd, rtmp, AF.Exp, (b_rtmp,), (b_rstd,), scale=-0.5)
            for c in range(8):
                stt(hT[:, c, :], xr[:, c, :], cst[:, gcol + c:gcol + c + 1], rstd, ALU.mult, ALU.mult,
                    (bxr[c], b_rstd, b_cst), (b_hT[c],))

        def headnorm_a(ps, bps, gcolumn, out, bout):
            sq, bsq = qsqr.next()
            act(sq[0:64, :], ps[0:64, :], AF.Square, (bps,), (bsq,))
            return lambda: headnorm_b(ps, bps, gcolumn, out, bout, sq, bsq)

        def headnorm_b(ps, bps, gcolumn, out, bout, sq, bsq):
            ps2, bps2 = psr.next()
            S.op("pe", lambda e: e.matmul(ps2[0:64, :], ones[0:64, 0:64], sq[0:64, :], start=True, stop=True),
                 (bsq, b_ones), (bps2,))
            r1, br1 = rqr.next()
            act(r1[0:64, :], ps2[0:64, :], AF.Ln, (bps2,), (br1,), bias=EPS, scale=1.0 / HD)
            r2, br2 = rq2r.next()
            act(r2[0:64, :], r1[0:64, :], AF.Exp, (br1,), (br2,), scale=-0.5)
            stt(out, ps[0:64, :], cst[0:64, gcolumn:gcolumn + 1], r2[0:64, :], ALU.mult, ALU.mult,
                (bps, br2, b_cst), (bout,))

        def emit_xload(ti):
            s_idx = ti // tiles_per_seq
            t0 = (ti % tiles_per_seq) * T
            xi = ti % 2
            src = xT_d[s_idx].rearrange("(c p) t -> p c t", p=128)[:, :, t0:t0 + T]
            dstt = xres[xi][:, :, :]
            S.dma("sp", "xl%d" % xi, lambda e: e.dma_start(out=dstt, in_=src), (), tuple(b_xres[xi]))

        def kouter(outs, lhs_fn, reads_w, wr_bufs):
            for k in range(8):
                groups = [(o, lhs_fn(i, k), hT[:, k, :]) for i, o in enumerate(outs)]
                def fn(e, groups=groups, k=k):
                    ins = None
                    for (o, l_, r_) in groups:
                        ins = e.matmul(o, l_, r_, start=(k == 0), stop=(k == 7))
                    return ins
                S.op("pe", fn, (*reads_w, b_hT[k]), tuple(wr_bufs))

        def run_pass(ti, l, first_layer, last_layer, nxt):
            _CUR[0], _CUR[1] = ti, l
            s_idx = ti // tiles_per_seq
            tt_i = ti % tiles_per_seq
            t0 = tt_i * T
            first_in_seq = tt_i == 0
            last_in_seq = tt_i == tiles_per_seq - 1
            xi = ti % 2
            xr = xres[xi]
            bxr = b_xres[xi]
            vbase = C_VEC + 192 * l
            G1, G2 = vbase, vbase + 8
            CW0, CW1, CW2, CB = vbase + 16, vbase + 60, vbase + 104, vbase + 148
            QG, KG = C_QK + 2 * l, C_QK + 2 * l + 1

            _ck(0)
            rms_finish(xr, bxr, G1)
            _ck(1)

            pend_hn = []

            def flush_hn():
                while pend_hn:
                    pend_hn.pop(0)()

            def proj_head(vW, bW, cols, gcolumn, out, bout):
                ps, bps = psr.next()
                mm_group(ps[0:64, :], [(vW[:, k, cols], hT[:, k, :]) for k in range(8)], (bW, *b_hT), (bps,))
                part_b = headnorm_a(ps, bps, gcolumn, out, bout)
                flush_hn()
                pend_hn.append(part_b)

            wA0, bA0 = w_acquire("A0")
            vA0 = wA0[:, 0:2048].rearrange("p (k n) -> p k n", k=8)
            qps = [psr.next() for _ in range(4)]
            kouter([p[0][0:64, :] for p in qps], lambda i, k: vA0[:, k, i * 64:(i + 1) * 64], (bA0,),
                   [p[1] for p in qps])
            for h in range(4):
                pend_hn.append(headnorm_a(qps[h][0], qps[h][1], QG, qT[:, h, :], b_qT[h]))
            w_release()
            _ck(2)

            wB, bB = w_acquire("B")
            vB = wB[:, 0:2048].rearrange("p (k n) -> p k n", k=8)
            for g in range(2):
                proj_head(vB, bB, slice(g * 64, (g + 1) * 64), KG, kT[l][:, g, 128:128 + T], b_kcur[l][g])
            ps, bps = psr.next()
            mm_multi([(ps[:, b * 128:(b + 1) * 128],
                       [(hT[:, k, b * 128:(b + 1) * 128], vB[:, k, 128:256]) for k in range(8)])
                      for b in range(NB)], (bB, *b_hT), (bps,))
            flush_hn()
            cp(vtok[l][:, 1:NB + 1, :], ps.rearrange("p (b n) -> p b n", b=NB), (bps,), (b_vcur[l],))
            w_release()

            def sgu_block(b):
                ps, bps = psr.next()
                groups = []
                for gp in range(4):
                    for sl in range(2):
                        g = 2 * gp + sl
                        groups.append((ps[64 * sl:64 * sl + 64, gp * 128:(gp + 1) * 128],
                                       [(vn[:, b, g * 64:(g + 1) * 64], wsT[:, l * 8 + g, :])]))
                mm_multi(groups, (b_vn[b], b_wsT), (bps,))
                tm, btm = tmr.next()
                tt(tm, ps, cst[:, C_BSB + 512 * l:C_BSB + 512 * l + 512], ALU.add, (bps, b_cst), (btm,))
                tt(ysguT[:, :, b * 128:(b + 1) * 128], tm.rearrange("p (a n) -> p a n", a=4),
                   uT[:, :, b * 128:(b + 1) * 128], ALU.mult, (btm, *b_uT), tuple(b_actT[12:16]))

            def att_stage1(b, g):
                halves = []
                if not (first_in_seq and b == 0):
                    halves.append(0)
                halves.append(1)
                pts = {}
                for hf in halves:
                    ps, bps = psr.next()
                    kcols = slice(128 * (b + hf), 128 * (b + hf) + 128)
                    kb = [b_kcur[l][g]] + ([b_kprev[l]] if (b == 0 and hf == 0) else [])
                    lhs_ = kT[l][:, g, kcols]
                    rhs_ = qT[:, 4 * g:4 * g + 4, b * 128:(b + 1) * 128]
                    out_ = ps.rearrange("p (a n) -> p a n", a=4)
                    S.op("pe", (lambda e, out_=out_, lhs_=lhs_, rhs_=rhs_: e.matmul(
                        out_, lhs_, rhs_, start=True, stop=True)),
                        (*kb, *b_qT[4 * g:4 * g + 4]), (bps,))
                    e_, be_ = efr.next()
                    act(e_, ps, AF.Exp, (bps,), (be_,), scale=0.125)
                    p_, bp_ = pTr.next()
                    col = C_ABIAS + (g * 2 + hf) * 512
                    tt(p_, e_, cst[:, col:col + 512], ALU.mult, (be_, b_cst), (bp_,), eng="pool")
                    pts[hf] = (p_, bp_)
                return halves, pts

            def att_stage2(b, g, halves, pts):
                yd, byd = psr.next()
                groups = []
                sbase = ((l * 2 + g) * 2) * 256
                for sl in range(2):
                    ypairs, dpairs = [], []
                    for hf in halves:
                        p_ = pts[hf][0]
                        rhs = p_.rearrange("p (pr s n) -> p pr s n", pr=2, s=2)[:, :, sl, :]
                        ypairs.append((vtok[l][:, b + hf, g * 64:(g + 1) * 64], rhs))
                        dpairs.append((ones[:, 0:64], rhs))
                    dpairs.append((ones[:, 0:64],
                                   srow[:, sbase + sl * 256:sbase + sl * 256 + 256].rearrange("p (a n) -> p a n", a=2)))
                    groups.append((yd[64 * sl:64 * sl + 64, 0:256].rearrange("p (a n) -> p a n", a=2), ypairs))
                    groups.append((yd[64 * sl:64 * sl + 64, 256:512].rearrange("p (a n) -> p a n", a=2), dpairs))
                vb = [b_vcur[l]] + ([b_vprev[l]] if b == 0 and 0 in halves else [])
                mm_multi(groups, (*[pts[hf][1] for hf in halves], *vb, b_ones, b_srow), (byd,))
                rd, brd = rdr.next()
                S.op("dve", lambda e, rd=rd, yd=yd: e.reciprocal(out=rd, in_=yd[:, 256:512]), (byd,), (brd,))
                tt(yattT[:, 2 * g:2 * g + 2, b * 128:(b + 1) * 128],
                   yd[:, 0:256].rearrange("p (a n) -> p a n", a=2),
                   rd.rearrange("p (a n) -> p a n", a=2), ALU.mult, (byd, brd),
                   (b_yatt[2 * g][b], b_yatt[2 * g + 1][b]))

            _ck(4)
            slabs = {}

            def get_slab(nm):
                if nm not in slabs:
                    w_, b_ = w_acquire(nm)
                    n_ = 2048 if nm == "A1" else 4096
                    slabs[nm] = (w_[:, 0:n_].rearrange("p (k n) -> p k n", k=8), b_)
                return slabs[nm]

            def u_qhead(h):
                vA1, bA1 = get_slab("A1")
                proj_head(vA1, bA1, slice((h - 4) * 64, (h - 3) * 64), QG, qT[:, h, :], b_qT[h])
                if h == 7:
                    w_release()

            def u_su(c):
                vC, bC = get_slab("C")
                flush_hn()
                ps, bps = psr.next()
                mm_group(ps, [(vC[:, k, c * 128:(c + 1) * 128], hT[:, k, :]) for k in range(8)],
                         (bC, *b_hT), (bps,))
                act(uT[:, c, :], ps, AF.Gelu_apprx_tanh, (bps,), (b_uT[c],))
                if c == 3:
                    w_release()

            def u_sv(b):
                vD, bD = get_slab("D")
                ps, bps = psr.next()
                mm_group(ps, [(hT[:, k, b * 128:(b + 1) * 128], vD[:, k, :]) for k in range(8)],
                         (bD, *b_hT), (bps,))
                g_, bg_ = gvr.next()
                act(g_, ps, AF.Gelu_apprx_tanh, (bps,), (bg_,))
                si = small_i[0]
                small_i[0] = (si + 1) % 8
                act(junk[:, :], g_, AF.Square, (bg_,), (b_junk, b_ssv[si]), accum_out=ssv[:, si:si + 1])
                act(lnv[:, si:si + 1], ssv[:, si:si + 1], AF.Ln, (b_ssv[si],), (b_lnv[si],), bias=EPS, scale=1.0 / 512)
                act(rv[:, si:si + 1], lnv[:, si:si + 1], AF.Exp, (b_lnv[si],), (b_rv[si],), scale=-0.5)
                stt(vn[:, b, :], g_, rv[:, si:si + 1], cst[:, C_SGUG + 512 * l:C_SGUG + 512 * l + 512],
                    ALU.mult, ALU.mult, (bg_, b_rv[si], b_cst), (b_vn[b],))
                if b == NB - 1:
                    w_release()

            units = ([lambda h=h: u_qhead(h) for h in range(4, 8)] + [lambda c=c: u_su(c) for c in range(4)]
                     + [lambda b=b: u_sv(b) for b in range(NB)])
            its = [(b, 0) for b in range(NB)] + [(b, 1) for b in range(NB)]
            pend = [att_stage1(*its[0]), att_stage1(*its[1])]
            for i, (b, g) in enumerate(its):
                if g == 0:
                    for _ in range(3):
                        units.pop(0)()
                else:
                    sgu_block(b)
                if i + 2 < len(its):
                    pend.append(att_stage1(*its[i + 2]))
                att_stage2(b, g, *pend.pop(0))
            assert not units
            _ck(3)
            if not last_in_seq:
                cp(kT[l][:, :, 0:128], kT[l][:, :, T:T + 128], (*b_kcur[l],), (b_kprev[l],))
                cp(vtok[l][:, 0, :], vtok[l][:, NB, :], (b_vcur[l],), (b_vprev[l],))

            _ck(5)
            for half in range(2):
                wE, bE = w_acquire("EF"[half])
                wG, bG = w_acquire("GH"[half])
                wO, bO = w_acquire("OAB%d" % half)
                vE = wE.rearrange("p (k n) -> p k n", k=8)
                vG = wG.rearrange("p (k n) -> p k n", k=8)
                vO = wO.rearrange("p (m k n) -> p m k n", m=2, k=4)
                for c4 in range(4):
                    c = 4 * half + c4
                    cs = slice(c4 * 128, (c4 + 1) * 128)
                    pga, bpga = psr.next()
                    mm_group(pga, [(vE[:, k, cs], hT[:, k, :]) for k in range(8)], (bE, *b_hT), (bpga,))
                    pgb, bpgb = psr.next()
                    mm_group(pgb, [(vG[:, k, cs], hT[:, k, :]) for k in range(8)], (bG, *b_hT), (bpgb,))
                    pa, bpa = psr.next()
                    mm_group(pa, [(vO[:, 0, kc, cs], yattT[:, kc, :]) for kc in range(4)],
                             (bO, *b_actT[8:12]), (bpa,))
                    pb, bpb = psr.next()
                    mm_group(pb, [(vO[:, 1, kc, cs], ysguT[:, kc, :]) for kc in range(4)], (bO, *b_actT[12:16]), (bpb,))
                    sa, bsa = sar.next()
                    act(sa, pga, AF.Sigmoid, (bpga,), (bsa,))
                    sb_, bsb_ = sbr.next()
                    act(sb_, pgb, AF.Sigmoid, (bpgb,), (bsb_,))
                    t1, bt1 = t1r.next()
                    tt(t1, pa, sa, ALU.mult, (bpa, bsa), (bt1,))
                    t2, bt2 = t2r.next()
                    tt(t2, pb, sb_, ALU.mult, (bpb, bsb_), (bt2,))
                    tt(mergedT[:, c, :], t1, t2, ALU.add, (bt1, bt2), (b_merged[c],), eng="pool")
                w_release(3)
            sqbs = {}
            for half in range(2):
                wO, bO = w_acquire("OUT%d" % half)
                vO = wO.rearrange("p (k n) -> p k n", k=8)
                for c4 in range(4):
                    c = 4 * half + c4
                    po, bpo = psr.next()
                    mm_group(po, [(vO[:, k, c4 * 128:(c4 + 1) * 128], mergedT[:, k, :]) for k in range(8)],
                             (bO, *b_merged), (bpo,))
                    if c >= 1:
                        rms_sq_mm(sqbs[c - 1], c - 1)
                    tt(xr[:, c, :], po, xr[:, c, :], ALU.add, (bpo, bxr[c]), (bxr[c],))
                    sqbs[c] = rms_sq_act(xr, bxr, c)
                w_release()
            rms_sq_mm(sqbs[7], 7)

            _ck(6)
            rms_finish(xr, bxr, G2)
            if last_layer and nxt is not None:
                emit_xload(nxt[0])

            def ffn_epilogue(j, pg, bpg, pv, bpv):
                ag, bag = agr.next()
                av, bav = avr.next()
                items = ((pg, bpg, ag, bag, j), (pv, bpv, av, bav, NJ + j))
                for (ps, bps, a_, ba_, ch) in items:
                    act(a_, ps, AF.Identity, (bps, b_cst), (ba_,),
                        bias=cst[:, CB + ch:CB + ch + 1], scale=cst[:, CW2 + ch:CW2 + ch + 1])
                for (ps, bps, a_, ba_, ch) in items:
                    stt(a_[:, 1:T], ps[:, 0:T - 1], cst[:, CW1 + ch:CW1 + ch + 1], a_[:, 1:T], ALU.mult, ALU.add,
                        (bps, ba_, b_cst), (ba_,))
                for (ps, bps, a_, ba_, ch) in items:
                    stt(a_[:, 2:T], ps[:, 0:T - 2], cst[:, CW0 + ch:CW0 + ch + 1], a_[:, 2:T], ALU.mult, ALU.add,
                        (bps, ba_, b_cst), (ba_,))
                if not first_in_seq:
                    for (ps, bps, a_, ba_, ch) in items:
                        tt(a_[:, 0:2], a_[:, 0:2], hc[:, ch, :], ALU.add, (ba_, b_hc), (ba_,), eng="pool")
                if not last_in_seq:
                    for (ps, bps, a_, ba_, ch) in items:
                        act(halo[l][:, ch, :], ps[:, T - 2:T], AF.Identity, (bps,), (b_halo[l][ch],))
                sg, bsg = sgr.next()
                act(sg, ag, AF.Silu, (bag,), (bsg,))
                tt(actT[:, j, :], sg, av, ALU.mult, (bsg, bav), (b_actT[j],), eng="pool")

            if not first_in_seq:
                bh = tuple(b_halo[l])
                tt(hc[:, :, 1], halo[l][:, :, 1], cst[:, CW0:CW0 + 44], ALU.mult, (*bh, b_cst), (b_hc,))
                tt(hc[:, :, 0], halo[l][:, :, 0], cst[:, CW0:CW0 + 44], ALU.mult, (*bh, b_cst), (b_hc,))
                tt(hctmp[:, :], halo[l][:, :, 1], cst[:, CW1:CW1 + 44], ALU.mult, (*bh, b_cst), (b_hctmp,))
                tt(hc[:, :, 0], hc[:, :, 0], hctmp[:, :], ALU.add, (b_hc, b_hctmp), (b_hc,))
            for i in range(11):
                wU, bU = w_acquire("UP%d" % i)
                vU = wU.rearrange("p (k n) -> p k n", k=8)
                if i == 0:
                    pss = [psr.next() for _ in range(4)]
                    offs = [0, 256, 128, 384]
                    kouter([p[0] for p in pss], lambda q, k: vU[:, k, offs[q]:offs[q] + 128], (bU,),
                           [p[1] for p in pss])
                    ffn_epilogue(0, pss[0][0], pss[0][1], pss[1][0], pss[1][1])
                    ffn_epilogue(1, pss[2][0], pss[2][1], pss[3][0], pss[3][1])
                else:
                    for jj in range(2):
                        j = 2 * i + jj
                        pg, bpg = psr.next()
                        mm_group(pg, [(vU[:, k, jj * 128:(jj + 1) * 128], hT[:, k, :]) for k in range(8)],
                                 (bU, *b_hT), (bpg,))
                        pv, bpv = psr.next()
                        mm_group(pv, [(vU[:, k, 256 + jj * 128:256 + (jj + 1) * 128], hT[:, k, :]) for k in range(8)],
                                 (bU, *b_hT), (bpv,))
                        ffn_epilogue(j, pg, bpg, pv, bpv)
                w_release()
            _ck(7)
            if nxt is not None:
                nxr, nbxr = xres[nxt[0] % 2], b_xres[nxt[0] % 2]
            sqbs = {}

            def down_tail(c, pd, bpd):
                if nxt is not None and c >= 1:
                    rms_sq_mm(sqbs[c - 1], c - 1)
                tt(xr[:, c, :], pd, xr[:, c, :], ALU.add, (bpd, bxr[c]), (bxr[c],))
                if nxt is not None:
                    sqbs[c] = rms_sq_act(nxr, nbxr, c)

            JS = 14
            first = []
            for c in range(4):
                wDn, bDn = w_acquire("DN%d" % c)
                vDn = wDn[:, 0:2816].rearrange("p (k n) -> p k n", k=NJ)
                pd, bpd = psr.next()
                mm_part(pd, [(vDn[:, j, :], actT[:, j, :]) for j in range(JS)], (bDn, *b_actT[0:JS]), (bpd,), True, False)
                first.append((vDn, bDn, pd, bpd))
            for c in range(4):
                vDn, bDn, pd, bpd = first[c]
                mm_part(pd, [(vDn[:, j, :], actT[:, j, :]) for j in range(JS, NJ)], (bDn, *b_actT[JS:NJ]), (bpd,), False, True)
                down_tail(c, pd, bpd)
                w_release()
            for c in range(4, 8):
                wDn, bDn = w_acquire("DN%d" % c)
                vDn = wDn[:, 0:2816].rearrange("p (k n) -> p k n", k=NJ)
                pd, bpd = psr.next()
                mm_group(pd, [(vDn[:, j, :], actT[:, j, :]) for j in range(NJ)], (bDn, *b_actT), (bpd,))
                down_tail(c, pd, bpd)
                w_release()
            if nxt is not None:
                rms_sq_mm(sqbs[7], 7)

            if last_layer:
                dst = yT_d[s_idx].rearrange("(c p) t -> p c t", p=128)[:, :, t0:t0 + T]
                S.dma("sp", "xs%d" % xi, lambda e: e.dma_start(out=dst, in_=xr[:, :, :]), tuple(bxr), ())

        plist = [(ti, li) for ti in range(n_tiles) for li in range(len(layers))]
        emit_xload(0)
        sq0 = [rms_sq_act(xres[0], b_xres[0], c) for c in range(4)]
        for c in range(8):
            rms_sq_mm(sq0[c] if c < 4 else rms_sq_act(xres[0], b_xres[0], c), c)
        try:
            for pi, (ti, li) in enumerate(plist):
                nxt = plist[pi + 1] if pi + 1 < len(plist) else None
                run_pass(ti, layers[li], li == 0, li == len(layers) - 1, nxt)
        except _Stop:
            dst = yT_d[0].rearrange("(c p) t -> p c t", p=128)[:, :, 0:T]
            S.dma("sp", "xs0", lambda e: e.dma_start(out=dst, in_=xres[0][:, :, :]), tuple(b_xres[0]), ())
        S.wait_all("sp", [b for i in range(2) for b in b_xres[i]])

        sems = {k: es.enter_context(nc.semaphore(k)) for k in S.semnames}
        block = es.enter_context(nc.Block())

        @block.tensor
        def _(e):
            S.replay("pe", e, sems)

        @block.scalar
        def _(e):
            S.replay("act", e, sems)

        @block.vector
        def _(e):
            S.replay("dve", e, sems)

        @block.gpsimd
        def _(e):
            S.replay("pool", e, sems)

        @block.sync
        def _(e):
            S.replay("sp", e, sems)
    return nc


def _pkn(w):
    kc = w.shape[0] // 128
    return np.ascontiguousarray(w.reshape(kc, 128, -1).transpose(1, 0, 2).reshape(128, -1))


def pack_weights(w_in, w_oa, w_ob, w_out, w_up, w_down):
    out = np.empty((NL, 128, WPL), np.float32)
    for l in range(NL):
        parts = {
            "A0": _pkn(w_in[l][:, 0:256]), "A1": _pkn(w_in[l][:, 256:512]), "B": _pkn(w_in[l][:, 512:768]),
            "C": _pkn(w_in[l][:, 768:1280]), "D": _pkn(w_in[l][:, 1280:1792]),
            "E": _pkn(w_in[l][:, 1792:2304]), "F": _pkn(w_in[l][:, 2304:2816]),
            "G": _pkn(w_in[l][:, 2816:3328]), "H": _pkn(w_in[l][:, 3328:3840]),
            "OAB0": np.concatenate([_pkn(w_oa[l][:, 0:512]), _pkn(w_ob[l][:, 0:512])], axis=1),
            "OAB1": np.concatenate([_pkn(w_oa[l][:, 512:1024]), _pkn(w_ob[l][:, 512:1024])], axis=1),
            "OUT0": _pkn(w_out[l][:, 0:512]), "OUT1": _pkn(w_out[l][:, 512:1024]),
        }
        for i in range(11):
            parts["UP%d" % i] = _pkn(np.concatenate(
                [w_up[l][:, 256 * i:256 * i + 256], w_up[l][:, DFF + 256 * i:DFF + 256 * i + 256]], axis=1))
        for c in range(8):
            parts["DN%d" % c] = _pkn(w_down[l][:, 128 * c:128 * c + 128])
        for nm, n in SLABS:
            off, _ = SLAB_OFF[nm]
            assert parts[nm].shape == (128, n), (nm, parts[nm].shape)
            out[l, :, off:off + n] = parts[nm]
    return out


def pack_consts(mix_norm, q_norm, k_norm, sinks, sgu_norm, w_s, b_s, ffn_norm, conv_w, conv_b):
    cst = np.zeros((128, NCST), np.float32)
    for l in range(NL):
        vb = C_VEC + 192 * l
        cst[:, vb:vb + 8] = mix_norm[l].reshape(8, 128).T
        cst[:, vb + 8:vb + 16] = ffn_norm[l].reshape(8, 128).T
        for tap in range(3):
            cst[:, vb + 16 + 44 * tap:vb + 16 + 44 * (tap + 1)] = conv_w[l, tap].reshape(44, 128).T
        cst[:, vb + 148:vb + 192] = conv_b[l].reshape(44, 128).T
        cst[0:64, C_QK + 2 * l] = q_norm[l]
        cst[0:64, C_QK + 2 * l + 1] = k_norm[l]
        cst[:, C_SGUG + 512 * l:C_SGUG + 512 * (l + 1)] = sgu_norm[l][None, :]
        for gp in range(4):
            cst[0:64, C_BSB + 512 * l + gp * 128:C_BSB + 512 * l + (gp + 1) * 128] = b_s[l, 2 * gp][None, :]
            cst[64:128, C_BSB + 512 * l + gp * 128:C_BSB + 512 * l + (gp + 1) * 128] = b_s[l, 2 * gp + 1][None, :]
    k = np.arange(128)[:, None]
    q = np.arange(128)[None, :]
    for g in range(2):
        for j in range(4):
            slope = 2.0 ** (-(4 * g + j + 1))
            dist_prev = q + 128 - k
            dist_cur = q - k
            bp = np.where(dist_prev < 128, -slope * dist_prev, -30000.0)
            bc = np.where(dist_cur >= 0, -slope * dist_cur, -30000.0)
            cst[:, C_ABIAS + (g * 2 + 0) * 512 + j * 128:C_ABIAS + (g * 2 + 0) * 512 + (j + 1) * 128] = bp
            cst[:, C_ABIAS + (g * 2 + 1) * 512 + j * 128:C_ABIAS + (g * 2 + 1) * 512 + (j + 1) * 128] = bc
    cst[:, C_TRIL:C_TRIL + 128] = (k <= q).astype(np.float32)
    srow = np.zeros((1, 2048), np.float32)
    for l in range(NL):
        for g in range(2):
            for sl in range(2):
                for pr in range(2):
                    base = (((l * 2 + g) * 2 + sl) * 2 + pr) * 128
                    srow[0, base:base + 128] = sinks[l, 4 * g + 2 * pr + sl]
    wst = np.ascontiguousarray(np.transpose(w_s, (3, 0, 1, 2)).reshape(128, NL * 8 * 128)).astype(np.float32)
    return cst, srow, wst


_NC_CACHE = {}
DBG_STOP = None


class _Stop(Exception):
    pass


_CUR = [0, 0]


def _ck(k):
    if DBG_STOP is not None and DBG_STOP == (_CUR[0], _CUR[1], k):
        raise _Stop()


def run(x, params, n_cores, layers=(0, 1)):
    B, S_, _ = x.shape
    n_seq = B // n_cores
    key = (n_seq, S_, tuple(layers))
    if key not in _NC_CACHE:
        _NC_CACHE[key] = build(n_seq, S_, layers)
    nc = _NC_CACHE[key]
    wts = pack_weights(params["w_in"], params["w_oa"], params["w_ob"], params["w_out"], params["w_up"],
                       params["w_down"])
    cst, srow, wst = pack_consts(params["mix_norm"], params["q_norm"], params["k_norm"], params["sinks"],
                                 params["sgu_norm"], params["w_s"], params["b_s"], params["ffn_norm"],
                                 params["conv_w"], params["conv_b"])
    in_maps = []
    for c in range(n_cores):
        xc = np.ascontiguousarray(np.transpose(x[c * n_seq:(c + 1) * n_seq], (0, 2, 1)))
        in_maps.append({"xT": xc, "wts": wts, "cst": cst, "srow": srow, "wst": wst})
    res = run_bass_kernel_spmd(nc, in_maps, core_ids=list(range(n_cores)))
    outs = [np.transpose(r["yT"], (0, 2, 1)) for r in res.results]
    return np.ascontiguousarray(np.concatenate(outs, axis=0)).astype(np.float32)


def kernel(**inputs):
    inputs = {k: np.asarray(v) for k, v in inputs.items()}
    x = inputs.pop("x").astype(np.float32)
    params = {k: v.astype(np.float32) for k, v in inputs.items()}
    return run(x, params, 8)
```

```python
from contextlib import ExitStack

import numpy as np
import concourse.bass as bass
import concourse.mybir as mybir
from concourse.bass_utils import run_bass_kernel_spmd

F32 = mybir.dt.float32
BF16 = mybir.dt.bfloat16
AF = mybir.ActivationFunctionType
ALU = mybir.AluOpType

D = 1024
NL = 2
NH = 8
NKV = 2
HD = 64
DFF = 2816
NJ = DFF // 128
T = 512
NB = T // 128
EPS = 1e-6
NBUF = 7
SLAB = 4096

SLABS = ([("A0", 2048), ("B", 2048), ("A1", 2048), ("C", 4096), ("D", 4096), ("E", 4096), ("G", 4096),
          ("OAB0", 4096), ("F", 4096), ("H", 4096), ("OAB1", 4096), ("OUT0", 4096), ("OUT1", 4096)]
         + [("UP%d" % i, 4096) for i in range(11)] + [("DN%d" % c, 2816) for c in range(8)])
SLAB_OFF = {}
_o = 0
for _n, _s in SLABS:
    SLAB_OFF[_n] = (_o, _s)
    _o += _s
WPL = _o

C_VEC = 0
C_QK = 384
C_SGUG = 392
C_BSB = C_SGUG + 1024
C_ABIAS = C_BSB + 1024
C_TRIL = C_ABIAS + 2048
NCST = C_TRIL + 128


class Buf:
    __slots__ = ("name", "lw", "rd", "excl")

    def __init__(self, name, excl=False):
        self.name = name
        self.lw = None
        self.rd = []
        self.excl = excl


class Sched:
    ENG = ("pe", "act", "dve", "pool", "sp")

    def __init__(self):
        self.streams = {e: [] for e in self.ENG}
        self.cnt = {}
        self.waited = {e: {} for e in self.ENG}
        self.semnames = []

    def newsem(self, key):
        self.cnt[key] = 0
        self.semnames.append(key)

    def _waits(self, eng, reads, writes):
        deps = {}
        def add(ev, raw):
            if ev is None:
                return
            k, v = ev
            if k == eng and (eng == "pe" or not raw):
                return
            if deps.get(k, 0) < v:
                deps[k] = v
        for b in reads:
            add(b.lw, True)
            if b.excl:
                for ev in b.rd:
                    add(ev, False)
        for b in writes:
            add(b.lw, False)
            for ev in b.rd:
                add(ev, False)
        for k, v in deps.items():
            if self.waited[eng].get(k, 0) >= v:
                continue
            self.waited[eng][k] = v
            self.streams[eng].append(("wait", k, v))

    def _commit(self, ev, reads, writes):
        for b in reads:
            b.rd.append(ev)
        for b in writes:
            b.lw = ev
            b.rd = []

    def op(self, eng, fn, reads=(), writes=()):
        self._waits(eng, reads, writes)
        self.cnt[eng] += 1
        self.streams[eng].append(("op", fn, eng, 1))
        self._commit((eng, self.cnt[eng]), reads, writes)

    def dma(self, eng, sem, fn, reads=(), writes=()):
        self._waits(eng, reads, writes)
        self.cnt[sem] += 16
        self.streams[eng].append(("op", fn, sem, 16))
        self._commit((sem, self.cnt[sem]), reads, writes)

    def wait_all(self, eng, bufs):
        self._waits(eng, bufs, bufs)

    def replay(self, eng, handle, sems):
        for it in self.streams[eng]:
            if it[0] == "wait":
                handle.wait_ge(sems[it[1]], it[2])
            else:
                ins = it[1](handle)
                ins.then_inc(sems[it[2]], it[3])


class Rot:
    def __init__(self, tiles, name):
        self.tiles = tiles
        self.bufs = [Buf("%s%d" % (name, i)) for i in range(len(tiles))]
        self.i = 0

    def next(self):
        i = self.i
        self.i = (i + 1) % len(self.tiles)
        return self.tiles[i], self.bufs[i]


def build(n_seq, seq_len, layers=(0, 1)):
    nc = bass.Bass("TRN2", target_bir_lowering=False)
    S = Sched()
    tiles_per_seq = seq_len // T
    n_tiles = n_seq * tiles_per_seq
    xT_d = nc.dram_tensor("xT", [n_seq, D, seq_len], F32, kind="ExternalInput").ap()
    wts_d = nc.dram_tensor("wts", [NL, 128, WPL], F32, kind="ExternalInput").ap()
    cst_d = nc.dram_tensor("cst", [128, NCST], F32, kind="ExternalInput").ap()
    srow_d = nc.dram_tensor("srow", [1, 2048], F32, kind="ExternalInput").ap()
    wst_d = nc.dram_tensor("wst", [128, 2048], F32, kind="ExternalInput").ap()
    yT_d = nc.dram_tensor("yT", [n_seq, D, seq_len], F32, kind="ExternalOutput").ap()
    wbf_d = nc.dram_tensor("wbf", [NL, 128, WPL], BF16).ap()

    es = ExitStack()
    with es:
        def sb(name, shape, dt):
            return es.enter_context(nc.sbuf_tensor("s_" + name, shape, dt))

        cst = sb("cst", [128, NCST], F32)
        srow = sb("srow", [128, 2048], BF16)
        wsT = sb("wsT", [128, 16, 128], BF16)
        ones = sb("ones", [128, 128], BF16)
        xres = [sb("xres%d" % i, [128, 8, T], F32) for i in range(2)]
        hT = sb("hT", [128, 8, T], BF16)
        sqt = sb("sqt", [128, 4, T], BF16)
        NFR = 10
        frt = sb("frt", [128, NFR, 512], F32)
        qT = sb("qT", [64, 8, T], BF16)
        kT = [sb("kT%d" % l, [64, 2, T + 128], BF16) for l in range(NL)]
        vtok = [sb("vtok%d" % l, [128, NB + 1, 128], BF16) for l in range(NL)]
        qsq = sb("qsq", [64, 5, T], BF16)
        junk = sb("junk", [128, 512], BF16)
        ssv = sb("ssv", [128, 8], F32)
        lnv = sb("lnv", [128, 8], F32)
        rv = sb("rv", [128, 8], F32)
        vn = sb("vn", [128, NB, 512], BF16)
        pT = sb("pT", [128, 6, 512], BF16)
        rden = sb("rden", [128, 4, 256], F32)
        actT = sb("actT", [128, NJ, T], BF16)
        mergedT = actT[:, 0:8, :]
        yattT = actT[:, 8:12, :]
        ysguT = actT[:, 12:16, :]
        uT = actT[:, 16:20, :]
        halo = [sb("halo%d" % l, [128, 2 * NJ, 2], F32) for l in range(NL)]
        hc = sb("hc", [128, 2 * NJ, 2], F32)
        hctmp = sb("hctmp", [128, 2 * NJ], F32)
        b_hc = Buf("hc")
        b_hctmp = Buf("hctmp")
        wslab = sb("wslab", [128, NBUF, SLAB], BF16)
        psb = [es.enter_context(nc.psum_tensor("ps%d" % i, [128, 512], F32)) for i in range(8)]

        b_cst = Buf("cst")
        b_srow = Buf("srow")
        b_wsT = Buf("wsT")
        b_ones = Buf("ones")
        b_xres = [[Buf("xres%d_%d" % (i, c)) for c in range(8)] for i in range(2)]
        b_hT = [Buf("hT%d" % c) for c in range(8)]
        b_qT = [Buf("qT%d" % h) for h in range(8)]
        b_kprev = [Buf("kprev%d" % l) for l in range(NL)]
        b_kcur = [[Buf("kcur%d_%d" % (l, g)) for g in range(2)] for l in range(NL)]
        b_vprev = [Buf("vprev%d" % l) for l in range(NL)]
        b_vcur = [Buf("vcur%d" % l) for l in range(NL)]
        b_junk = Buf("junk")
        b_ssv = [Buf("ssv%d" % i) for i in range(8)]
        b_lnv = [Buf("lnv%d" % i) for i in range(8)]
        b_rv = [Buf("rv%d" % i) for i in range(8)]
        b_vn = [Buf("vn%d" % b) for b in range(NB)]
        b_actT = [Buf("actT%d" % j) for j in range(NJ)]
        b_merged = b_actT[0:8]
        b_yatt = [[b_actT[8 + c]] * NB for c in range(4)]
        b_uT = b_actT[16:20]
        b_halo = [[Buf("halo%d_%d" % (l, ch)) for ch in range(2 * NJ)] for l in range(NL)]
        b_wslab = [Buf("wslab%d" % i) for i in range(NBUF)]
        b_ps = [Buf("ps%d" % i, excl=True) for i in range(8)]

        sqr = Rot([sqt[:, i, :] for i in range(4)], "sq")
        qsqr = Rot([qsq[:, i, :] for i in range(5)], "qsq")
        fr = Rot([frt[:, i, :] for i in range(NFR)], "fr")
        rqr = rq2r = gvr = efr = tmr = sar = sbr = t1r = t2r = agr = avr = sgr = fr
        srowf = frt[0:1, 0:4, :]
        wstage = frt[:, 4:8, :]
        pTr = Rot([pT[:, i, :] for i in range(6)], "pT")
        rdr = Rot([rden[:, i, :] for i in range(4)], "rden")
        psr = Rot([p[:, :] for p in psb[0:7]], "psr")
        psr.bufs = b_ps[0:7]
        ss_ps, b_ss = psb[7][:, :], b_ps[7]
        small_i = [0]

        for e in ("pe", "act", "dve", "pool"):
            S.newsem(e)
        for i in range(NBUF):
            S.newsem("w%d" % i)
            S.newsem("ws%d" % i)
        b_wd = {(l, nm): Buf("wd%d_%s" % (l, nm)) for l in range(NL) for (nm, _) in SLABS}
        for k in ("cst0", "cst1", "cst2", "xl0", "xl1", "xs0", "xs1"):
            S.newsem(k)

        def act(out, in_, func, reads, writes, bias=None, scale=None, accum_out=None):
            kw = {}
            if bias is not None:
                kw["bias"] = bias
            if scale is not None:
                kw["scale"] = scale
            if accum_out is not None:
                kw["accum_out"] = accum_out
            S.op("act", lambda e: e.activation(out=out, in_=in_, func=func, **kw), reads, writes)

        def tt(out, in0, in1, op, reads, writes, eng="dve"):
            S.op(eng, lambda e: e.tensor_tensor(out=out, in0=in0, in1=in1, op=op), reads, writes)

        def stt(out, in0, scalar, in1, op0, op1, reads, writes, eng="dve"):
            S.op(eng, lambda e: e.scalar_tensor_tensor(out=out, in0=in0, scalar=scalar, in1=in1,
                                                        op0=op0, op1=op1), reads, writes)

        def cp(out, in_, reads, writes, eng="dve"):
            S.op(eng, lambda e: e.tensor_copy(out=out, in_=in_), reads, writes)

        def mm_group(out, pairs, reads, writes):
            def fn(e):
                ins = None
                n = len(pairs)
                for i, (l, r) in enumerate(pairs):
                    ins = e.matmul(out, l, r, start=(i == 0), stop=(i == n - 1))
                return ins
            S.op("pe", fn, reads, writes)

        def mm_part(out, pairs, reads, writes, first, last):
            def fn(e):
                ins = None
                n = len(pairs)
                for i, (l, r) in enumerate(pairs):
                    ins = e.matmul(out, l, r, start=(first and i == 0), stop=(last and i == n - 1))
                return ins
            S.op("pe", fn, reads, writes)

        def mm_multi(groups, reads, writes):
            def fn(e):
                ins = None
                for out, pairs in groups:
                    n = len(pairs)
                    for i, (l, r) in enumerate(pairs):
                        ins = e.matmul(out, l, r, start=(i == 0), stop=(i == n - 1))
                return ins
            S.op("pe", fn, reads, writes)

        passes = [(ti, l) for ti in range(n_tiles) for l in layers]
        wseq = [(l, nm) for (_, l) in passes for (nm, _) in SLABS]
        wstate = {"issue": 0, "acq": 0}

        def w_issue():
            i = wstate["issue"]
            if i >= len(wseq):
                return
            wstate["issue"] = i + 1
            l, nm = wseq[i]
            off, n = SLAB_OFF[nm]
            slot = i % NBUF
            o = wslab[:, slot, 0:n]
            if passes[i // len(SLABS)][0] == 0:
                src = wts_d[l, :, off:off + n]
                S.dma("pool", "w%d" % slot, lambda e: e.dma_start(out=o, in_=src), (), (b_wslab[slot],))
            else:
                src = wbf_d[l, :, off:off + n]
                S.dma("sp", "w%d" % slot, lambda e: e.dma_start(out=o, in_=src), (b_wd[(l, nm)],),
                      (b_wslab[slot],))

        def w_acquire(expect):
            i = wstate["acq"]
            wstate["acq"] = i + 1
            assert wseq[i][1] == expect, (wseq[i], expect)
            slot = i % NBUF
            if passes[i // len(SLABS)][0] == 0 and n_tiles > 1:
                l, nm = wseq[i]
                off, n = SLAB_OFF[nm]
                dst = wbf_d[l, :, off:off + n]
                srcs = wslab[:, slot, 0:n]
                S.dma("sp", "ws%d" % slot, lambda e: e.dma_start(out=dst, in_=srcs), (b_wslab[slot],),
                      (b_wd[(l, nm)],))
            return wslab[:, slot, :], b_wslab[slot]

        def w_release(n=1):
            for _ in range(n):
                w_issue()

        S.dma("sp", "cst0", lambda e: e.dma_start(out=cst[:, :], in_=cst_d[:, :]), (), (b_cst,))
        S.dma("sp", "cst1", lambda e: e.dma_start(out=srowf, in_=srow_d.rearrange("o (a n) -> o a n", a=4)),
              (), tuple(fr.bufs[0:4]))
        S.dma("sp", "cst2", lambda e: e.dma_start(out=wstage, in_=wst_d.rearrange("p (a n) -> p a n", a=4)),
              (), tuple(fr.bufs[4:8]))
        for _ in range(NBUF):
            w_issue()
        S.op("dve", lambda e: e.memset(ones[:, :], 1.0), (), (b_ones,))
        S.op("dve", lambda e: e.memset(srow[:, :], 0.0), (), (b_srow,))
        act(srow[0:1, :].rearrange("o (a n) -> o a n", a=4), srowf, AF.Exp, tuple(fr.bufs[0:4]), (b_srow,))
        act(cst[:, C_ABIAS:C_ABIAS + 2048], cst[:, C_ABIAS:C_ABIAS + 2048], AF.Exp, (b_cst,), (b_cst,))
        for i in range(16):
            tt(wsT[:, i, :], wstage[:, i // 4, (i % 4) * 128:(i % 4 + 1) * 128], cst[:, C_TRIL:C_TRIL + 128], ALU.mult,
               (fr.bufs[4 + i // 4], b_cst), (b_wsT,))

        def rms_sq_act(xr, bxr, c):
            sq, bsq = sqr.next()
            act(sq, xr[:, c, :], AF.Square, (bxr[c],), (bsq,))
            return sq, bsq

        def rms_sq_mm(sqb, c):
            sq, bsq = sqb
            S.op("pe", (lambda e: e.matmul(ss_ps, ones[:, :], sq, start=(c == 0), stop=(c == 7))),
                 (bsq, b_ones), (b_ss,))

        def rms_finish(xr, bxr, gcol):
            ps, bps = ss_ps, b_ss
            rtmp, b_rtmp = fr.next()
            act(rtmp, ps, AF.Ln, (bps,), (b_rtmp,), bias=EPS, scale=1.0 / D)
            rstd, b_rstd = fr.next()
            act(rstd, rtmp, AF.Exp, (b_rtmp,), (b_rstd,), scale=-0.5)
            for c in range(8):
                stt(hT[:, c, :], xr[:, c, :], cst[:, gcol + c:gcol + c + 1], rstd, ALU.mult, ALU.mult,
                    (bxr[c], b_rstd, b_cst), (b_hT[c],))

        def headnorm_a(ps, bps, gcolumn, out, bout):
            sq, bsq = qsqr.next()
            act(sq[0:64, :], ps[0:64, :], AF.Square, (bps,), (bsq,))
            return lambda: headnorm_b(ps, bps, gcolumn, out, bout, sq, bsq)

        def headnorm_b(ps, bps, gcolumn, out, bout, sq, bsq):
            ps2, bps2 = psr.next()
            S.op("pe", lambda e: e.matmul(ps2[0:64, :], ones[0:64, 0:64], sq[0:64, :], start=True, stop=True),
                 (bsq, b_ones), (bps2,))
            r1, br1 = rqr.next()
            act(r1[0:64, :], ps2[0:64, :], AF.Ln, (bps2,), (br1,), bias=EPS, scale=1.0 / HD)
            r2, br2 = rq2r.next()
            act(r2[0:64, :], r1[0:64, :], AF.Exp, (br1,), (br2,), scale=-0.5)
            stt(out, ps[0:64, :], cst[0:64, gcolumn:gcolumn + 1], r2[0:64, :], ALU.mult, ALU.mult,
                (bps, br2, b_cst), (bout,))

        def emit_xload(ti):
            s_idx = ti // tiles_per_seq
            t0 = (ti % tiles_per_seq) * T
            xi = ti % 2
            src = xT_d[s_idx].rearrange("(c p) t -> p c t", p=128)[:, :, t0:t0 + T]
            dstt = xres[xi][:, :, :]
            S.dma("sp", "xl%d" % xi, lambda e: e.dma_start(out=dstt, in_=src), (), tuple(b_xres[xi]))

        def kouter(outs, lhs_fn, reads_w, wr_bufs):
            for k in range(8):
                groups = [(o, lhs_fn(i, k), hT[:, k, :]) for i, o in enumerate(outs)]
                def fn(e, groups=groups, k=k):
                    ins = None
                    for (o, l_, r_) in groups:
                        ins = e.matmul(o, l_, r_, start=(k == 0), stop=(k == 7))
                    return ins
                S.op("pe", fn, (*reads_w, b_hT[k]), tuple(wr_bufs))

        def run_pass(ti, l, first_layer, last_layer, nxt):
            _CUR[0], _CUR[1] = ti, l
            s_idx = ti // tiles_per_seq
            tt_i = ti % tiles_per_seq
            t0 = tt_i * T
            first_in_seq = tt_i == 0
            last_in_seq = tt_i == tiles_per_seq - 1
            xi = ti % 2
            xr = xres[xi]
            bxr = b_xres[xi]
            vbase = C_VEC + 192 * l
            G1, G2 = vbase, vbase + 8
            CW0, CW1, CW2, CB = vbase + 16, vbase + 60, vbase + 104, vbase + 148
            QG, KG = C_QK + 2 * l, C_QK + 2 * l + 1

            _ck(0)
            rms_finish(xr, bxr, G1)
            _ck(1)

            pend_hn = []

            def flush_hn():
                while pend_hn:
                    pend_hn.pop(0)()

            def proj_head(vW, bW, cols, gcolumn, out, bout):
                ps, bps = psr.next()
                mm_group(ps[0:64, :], [(vW[:, k, cols], hT[:, k, :]) for k in range(8)], (bW, *b_hT), (bps,))
                part_b = headnorm_a(ps, bps, gcolumn, out, bout)
                flush_hn()
                pend_hn.append(part_b)

            wA0, bA0 = w_acquire("A0")
            vA0 = wA0[:, 0:2048].rearrange("p (k n) -> p k n", k=8)
            qps = [psr.next() for _ in range(4)]
            kouter([p[0][0:64, :] for p in qps], lambda i, k: vA0[:, k, i * 64:(i + 1) * 64], (bA0,),
                   [p[1] for p in qps])
            for h in range(4):
                pend_hn.append(headnorm_a(qps[h][0], qps[h][1], QG, qT[:, h, :], b_qT[h]))
            w_release()
            _ck(2)

            wB, bB = w_acquire("B")
            vB = wB[:, 0:2048].rearrange("p (k n) -> p k n", k=8)
            for g in range(2):
                proj_head(vB, bB, slice(g * 64, (g + 1) * 64), KG, kT[l][:, g, 128:128 + T], b_kcur[l][g])
            ps, bps = psr.next()
            mm_multi([(ps[:, b * 128:(b + 1) * 128],
                       [(hT[:, k, b * 128:(b + 1) * 128], vB[:, k, 128:256]) for k in range(8)])
                      for b in range(NB)], (bB, *b_hT), (bps,))
            flush_hn()
            cp(vtok[l][:, 1:NB + 1, :], ps.rearrange("p (b n) -> p b n", b=NB), (bps,), (b_vcur[l],))
            w_release()

            def sgu_block(b):
                ps, bps = psr.next()
                groups = []
                for gp in range(4):
                    for sl in range(2):
                        g = 2 * gp + sl
                        groups.append((ps[64 * sl:64 * sl + 64, gp * 128:(gp + 1) * 128],
                                       [(vn[:, b, g * 64:(g + 1) * 64], wsT[:, l * 8 + g, :])]))
                mm_multi(groups, (b_vn[b], b_wsT), (bps,))
                tm, btm = tmr.next()
                tt(tm, ps, cst[:, C_BSB + 512 * l:C_BSB + 512 * l + 512], ALU.add, (bps, b_cst), (btm,))
                tt(ysguT[:, :, b * 128:(b + 1) * 128], tm.rearrange("p (a n) -> p a n", a=4),
                   uT[:, :, b * 128:(b + 1) * 128], ALU.mult, (btm, *b_uT), tuple(b_actT[12:16]))

            def att_stage1(b, g):
                halves = []
                if not (first_in_seq and b == 0):
                    halves.append(0)
                halves.append(1)
                pts = {}
                for hf in halves:
                    ps, bps = psr.next()
                    kcols = slice(128 * (b + hf), 128 * (b + hf) + 128)
                    kb = [b_kcur[l][g]] + ([b_kprev[l]] if (b == 0 and hf == 0) else [])
                    lhs_ = kT[l][:, g, kcols]
                    rhs_ = qT[:, 4 * g:4 * g + 4, b * 128:(b + 1) * 128]
                    out_ = ps.rearrange("p (a n) -> p a n", a=4)
                    S.op("pe", (lambda e, out_=out_, lhs_=lhs_, rhs_=rhs_: e.matmul(
                        out_, lhs_, rhs_, start=True, stop=True)),
                        (*kb, *b_qT[4 * g:4 * g + 4]), (bps,))
                    e_, be_ = efr.next()
                    act(e_, ps, AF.Exp, (bps,), (be_,), scale=0.125)
                    p_, bp_ = pTr.next()
                    col = C_ABIAS + (g * 2 + hf) * 512
                    tt(p_, e_, cst[:, col:col + 512], ALU.mult, (be_, b_cst), (bp_,), eng="pool")
                    pts[hf] = (p_, bp_)
                return halves, pts

            def att_stage2(b, g, halves, pts):
                yd, byd = psr.next()
                groups = []
                sbase = ((l * 2 + g) * 2) * 256
                for sl in range(2):
                    ypairs, dpairs = [], []
                    for hf in halves:
                        p_ = pts[hf][0]
                        rhs = p_.rearrange("p (pr s n) -> p pr s n", pr=2, s=2)[:, :, sl, :]
                        ypairs.append((vtok[l][:, b + hf, g * 64:(g + 1) * 64], rhs))
                        dpairs.append((ones[:, 0:64], rhs))
                    dpairs.append((ones[:, 0:64],
                                   srow[:, sbase + sl * 256:sbase + sl * 256 + 256].rearrange("p (a n) -> p a n", a=2)))
                    groups.append((yd[64 * sl:64 * sl + 64, 0:256].rearrange("p (a n) -> p a n", a=2), ypairs))
                    groups.append((yd[64 * sl:64 * sl + 64, 256:512].rearrange("p (a n) -> p a n", a=2), dpairs))
                vb = [b_vcur[l]] + ([b_vprev[l]] if b == 0 and 0 in halves else [])
                mm_multi(groups, (*[pts[hf][1] for hf in halves], *vb, b_ones, b_srow), (byd,))
                rl, brl = rdr.next()
                act(rl, yd[:, 256:512], AF.Ln, (byd,), (brl,))
                rd, brd = rdr.next()
                act(rd, rl, AF.Exp, (brl,), (brd,), scale=-1.0)
                tt(yattT[:, 2 * g:2 * g + 2, b * 128:(b + 1) * 128],
                   yd[:, 0:256].rearrange("p (a n) -> p a n", a=2),
                   rd.rearrange("p (a n) -> p a n", a=2), ALU.mult, (byd, brd),
                   (b_yatt[2 * g][b], b_yatt[2 * g + 1][b]))

            _ck(4)
            slabs = {}

            def get_slab(nm):
                if nm not in slabs:
                    w_, b_ = w_acquire(nm)
                    n_ = 2048 if nm == "A1" else 4096
                    slabs[nm] = (w_[:, 0:n_].rearrange("p (k n) -> p k n", k=8), b_)
                return slabs[nm]

            def u_qhead(h):
                vA1, bA1 = get_slab("A1")
                proj_head(vA1, bA1, slice((h - 4) * 64, (h - 3) * 64), QG, qT[:, h, :], b_qT[h])
                if h == 7:
                    w_release()

            def u_su(c):
                vC, bC = get_slab("C")
                flush_hn()
                ps, bps = psr.next()
                mm_group(ps, [(vC[:, k, c * 128:(c + 1) * 128], hT[:, k, :]) for k in range(8)],
                         (bC, *b_hT), (bps,))
                act(uT[:, c, :], ps, AF.Gelu_apprx_tanh, (bps,), (b_uT[c],))
                if c == 3:
                    w_release()

            def u_sv(b):
                vD, bD = get_slab("D")
                ps, bps = psr.next()
                mm_group(ps, [(hT[:, k, b * 128:(b + 1) * 128], vD[:, k, :]) for k in range(8)],
                         (bD, *b_hT), (bps,))
                g_, bg_ = gvr.next()
                act(g_, ps, AF.Gelu_apprx_tanh, (bps,), (bg_,))
                si = small_i[0]
                small_i[0] = (si + 1) % 8
                act(junk[:, :], g_, AF.Square, (bg_,), (b_junk, b_ssv[si]), accum_out=ssv[:, si:si + 1])
                act(lnv[:, si:si + 1], ssv[:, si:si + 1], AF.Ln, (b_ssv[si],), (b_lnv[si],), bias=EPS, scale=1.0 / 512)
                act(rv[:, si:si + 1], lnv[:, si:si + 1], AF.Exp, (b_lnv[si],), (b_rv[si],), scale=-0.5)
                stt(vn[:, b, :], g_, rv[:, si:si + 1], cst[:, C_SGUG + 512 * l:C_SGUG + 512 * l + 512],
                    ALU.mult, ALU.mult, (bg_, b_rv[si], b_cst), (b_vn[b],))
                if b == NB - 1:
                    w_release()

            units = ([lambda h=h: u_qhead(h) for h in range(4, 8)] + [lambda c=c: u_su(c) for c in range(4)]
                     + [lambda b=b: u_sv(b) for b in range(NB)])
            its = [(b, 0) for b in range(NB)] + [(b, 1) for b in range(NB)]
            pend = [att_stage1(*its[0]), att_stage1(*its[1])]
            for i, (b, g) in enumerate(its):
                if g == 0:
                    for _ in range(3):
                        units.pop(0)()
                else:
                    sgu_block(b)
                if i + 2 < len(its):
                    pend.append(att_stage1(*its[i + 2]))
                att_stage2(b, g, *pend.pop(0))
            assert not units
            _ck(3)
            if not last_in_seq:
                cp(kT[l][:, :, 0:128], kT[l][:, :, T:T + 128], (*b_kcur[l],), (b_kprev[l],))
                cp(vtok[l][:, 0, :], vtok[l][:, NB, :], (b_vcur[l],), (b_vprev[l],))

            _ck(5)
            for half in range(2):
                wE, bE = w_acquire("EF"[half])
                wG, bG = w_acquire("GH"[half])
                wO, bO = w_acquire("OAB%d" % half)
                vE = wE.rearrange("p (k n) -> p k n", k=8)
                vG = wG.rearrange("p (k n) -> p k n", k=8)
                vO = wO.rearrange("p (m k n) -> p m k n", m=2, k=4)
                for c4 in range(4):
                    c = 4 * half + c4
                    cs = slice(c4 * 128, (c4 + 1) * 128)
                    pga, bpga = psr.next()
                    mm_group(pga, [(vE[:, k, cs], hT[:, k, :]) for k in range(8)], (bE, *b_hT), (bpga,))
                    pgb, bpgb = psr.next()
                    mm_group(pgb, [(vG[:, k, cs], hT[:, k, :]) for k in range(8)], (bG, *b_hT), (bpgb,))
                    pa, bpa = psr.next()
                    mm_group(pa, [(vO[:, 0, kc, cs], yattT[:, kc, :]) for kc in range(4)],
                             (bO, *b_actT[8:12]), (bpa,))
                    pb, bpb = psr.next()
                    mm_group(pb, [(vO[:, 1, kc, cs], ysguT[:, kc, :]) for kc in range(4)], (bO, *b_actT[12:16]), (bpb,))
                    sa, bsa = sar.next()
                    act(sa, pga, AF.Sigmoid, (bpga,), (bsa,))
                    sb_, bsb_ = sbr.next()
                    act(sb_, pgb, AF.Sigmoid, (bpgb,), (bsb_,))
                    t1, bt1 = t1r.next()
                    tt(t1, pa, sa, ALU.mult, (bpa, bsa), (bt1,))
                    t2, bt2 = t2r.next()
                    tt(t2, pb, sb_, ALU.mult, (bpb, bsb_), (bt2,))
                    tt(mergedT[:, c, :], t1, t2, ALU.add, (bt1, bt2), (b_merged[c],), eng="pool")
                w_release(3)
            sqbs = {}
            for half in range(2):
                wO, bO = w_acquire("OUT%d" % half)
                vO = wO.rearrange("p (k n) -> p k n", k=8)
                for c4 in range(4):
                    c = 4 * half + c4
                    po, bpo = psr.next()
                    mm_group(po, [(vO[:, k, c4 * 128:(c4 + 1) * 128], mergedT[:, k, :]) for k in range(8)],
                             (bO, *b_merged), (bpo,))
                    if c >= 1:
                        rms_sq_mm(sqbs[c - 1], c - 1)
                    tt(xr[:, c, :], po, xr[:, c, :], ALU.add, (bpo, bxr[c]), (bxr[c],))
                    sqbs[c] = rms_sq_act(xr, bxr, c)
                w_release()
            rms_sq_mm(sqbs[7], 7)

            _ck(6)
            rms_finish(xr, bxr, G2)
            if last_layer and nxt is not None:
                emit_xload(nxt[0])

            def ffn_epilogue(j, pg, bpg, pv, bpv):
                ag, bag = agr.next()
                av, bav = avr.next()
                items = ((pg, bpg, ag, bag, j), (pv, bpv, av, bav, NJ + j))
                for (ps, bps, a_, ba_, ch) in items:
                    act(a_, ps, AF.Identity, (bps, b_cst), (ba_,),
                        bias=cst[:, CB + ch:CB + ch + 1], scale=cst[:, CW2 + ch:CW2 + ch + 1])
                for (ps, bps, a_, ba_, ch) in items:
                    stt(a_[:, 1:T], ps[:, 0:T - 1], cst[:, CW1 + ch:CW1 + ch + 1], a_[:, 1:T], ALU.mult, ALU.add,
                        (bps, ba_, b_cst), (ba_,))
                for (ps, bps, a_, ba_, ch) in items:
                    stt(a_[:, 2:T], ps[:, 0:T - 2], cst[:, CW0 + ch:CW0 + ch + 1], a_[:, 2:T], ALU.mult, ALU.add,
                        (bps, ba_, b_cst), (ba_,))
                if not first_in_seq:
                    for (ps, bps, a_, ba_, ch) in items:
                        tt(a_[:, 0:2], a_[:, 0:2], hc[:, ch, :], ALU.add, (ba_, b_hc), (ba_,), eng="pool")
                if not last_in_seq:
                    for (ps, bps, a_, ba_, ch) in items:
                        act(halo[l][:, ch, :], ps[:, T - 2:T], AF.Identity, (bps,), (b_halo[l][ch],))
                sg, bsg = sgr.next()
                act(sg, ag, AF.Silu, (bag,), (bsg,))
                tt(actT[:, j, :], sg, av, ALU.mult, (bsg, bav), (b_actT[j],), eng="pool")

            if not first_in_seq:
                bh = tuple(b_halo[l])
                tt(hc[:, :, 1], halo[l][:, :, 1], cst[:, CW0:CW0 + 44], ALU.mult, (*bh, b_cst), (b_hc,))
                tt(hc[:, :, 0], halo[l][:, :, 0], cst[:, CW0:CW0 + 44], ALU.mult, (*bh, b_cst), (b_hc,))
                tt(hctmp[:, :], halo[l][:, :, 1], cst[:, CW1:CW1 + 44], ALU.mult, (*bh, b_cst), (b_hctmp,))
                tt(hc[:, :, 0], hc[:, :, 0], hctmp[:, :], ALU.add, (b_hc, b_hctmp), (b_hc,))
            for i in range(11):
                wU, bU = w_acquire("UP%d" % i)
                vU = wU.rearrange("p (k n) -> p k n", k=8)
                if i == 0:
                    pss = [psr.next() for _ in range(4)]
                    offs = [0, 256, 128, 384]
                    kouter([p[0] for p in pss], lambda q, k: vU[:, k, offs[q]:offs[q] + 128], (bU,),
                           [p[1] for p in pss])
                    ffn_epilogue(0, pss[0][0], pss[0][1], pss[1][0], pss[1][1])
                    ffn_epilogue(1, pss[2][0], pss[2][1], pss[3][0], pss[3][1])
                else:
                    for jj in range(2):
                        j = 2 * i + jj
                        pg, bpg = psr.next()
                        mm_group(pg, [(vU[:, k, jj * 128:(jj + 1) * 128], hT[:, k, :]) for k in range(8)],
                                 (bU, *b_hT), (bpg,))
                        pv, bpv = psr.next()
                        mm_group(pv, [(vU[:, k, 256 + jj * 128:256 + (jj + 1) * 128], hT[:, k, :]) for k in range(8)],
                                 (bU, *b_hT), (bpv,))
                        ffn_epilogue(j, pg, bpg, pv, bpv)
                w_release()
            _ck(7)
            if nxt is not None:
                nxr, nbxr = xres[nxt[0] % 2], b_xres[nxt[0] % 2]
            sqbs = {}

            def down_tail(c, pd, bpd):
                if nxt is not None and c >= 1:
                    rms_sq_mm(sqbs[c - 1], c - 1)
                tt(xr[:, c, :], pd, xr[:, c, :], ALU.add, (bpd, bxr[c]), (bxr[c],))
                if nxt is not None:
                    sqbs[c] = rms_sq_act(nxr, nbxr, c)

            JS = 14
            first = []
            for c in range(4):
                wDn, bDn = w_acquire("DN%d" % c)
                vDn = wDn[:, 0:2816].rearrange("p (k n) -> p k n", k=NJ)
                pd, bpd = psr.next()
                mm_part(pd, [(vDn[:, j, :], actT[:, j, :]) for j in range(JS)], (bDn, *b_actT[0:JS]), (bpd,), True, False)
                first.append((vDn, bDn, pd, bpd))
            for c in range(4):
                vDn, bDn, pd, bpd = first[c]
                mm_part(pd, [(vDn[:, j, :], actT[:, j, :]) for j in range(JS, NJ)], (bDn, *b_actT[JS:NJ]), (bpd,), False, True)
                down_tail(c, pd, bpd)
                w_release()
            for c in range(4, 8):
                wDn, bDn = w_acquire("DN%d" % c)
                vDn = wDn[:, 0:2816].rearrange("p (k n) -> p k n", k=NJ)
                pd, bpd = psr.next()
                mm_group(pd, [(vDn[:, j, :], actT[:, j, :]) for j in range(NJ)], (bDn, *b_actT), (bpd,))
                down_tail(c, pd, bpd)
                w_release()
            if nxt is not None:
                rms_sq_mm(sqbs[7], 7)

            if last_layer:
                dst = yT_d[s_idx].rearrange("(c p) t -> p c t", p=128)[:, :, t0:t0 + T]
                S.dma("sp", "xs%d" % xi, lambda e: e.dma_start(out=dst, in_=xr[:, :, :]), tuple(bxr), ())

        plist = [(ti, li) for ti in range(n_tiles) for li in range(len(layers))]
        emit_xload(0)
        sq0 = [rms_sq_act(xres[0], b_xres[0], c) for c in range(4)]
        for c in range(8):
            rms_sq_mm(sq0[c] if c < 4 else rms_sq_act(xres[0], b_xres[0], c), c)
        try:
            for pi, (ti, li) in enumerate(plist):
                nxt = plist[pi + 1] if pi + 1 < len(plist) else None
                run_pass(ti, layers[li], li == 0, li == len(layers) - 1, nxt)
        except _Stop:
            dst = yT_d[0].rearrange("(c p) t -> p c t", p=128)[:, :, 0:T]
            S.dma("sp", "xs0", lambda e: e.dma_start(out=dst, in_=xres[0][:, :, :]), tuple(b_xres[0]), ())
        S.wait_all("sp", [b for i in range(2) for b in b_xres[i]])

        sems = {k: es.enter_context(nc.semaphore(k)) for k in S.semnames}
        block = es.enter_context(nc.Block())

        @block.tensor
        def _(e):
            S.replay("pe", e, sems)

        @block.scalar
        def _(e):
            S.replay("act", e, sems)

        @block.vector
        def _(e):
            S.replay("dve", e, sems)

        @block.gpsimd
        def _(e):
            S.replay("pool", e, sems)

        @block.sync
        def _(e):
            S.replay("sp", e, sems)
    return nc


def _pkn(w):
    kc = w.shape[0] // 128
    return np.ascontiguousarray(w.reshape(kc, 128, -1).transpose(1, 0, 2).reshape(128, -1))


def pack_weights(w_in, w_oa, w_ob, w_out, w_up, w_down):
    out = np.empty((NL, 128, WPL), np.float32)
    for l in range(NL):
        parts = {
            "A0": _pkn(w_in[l][:, 0:256]), "A1": _pkn(w_in[l][:, 256:512]), "B": _pkn(w_in[l][:, 512:768]),
            "C": _pkn(w_in[l][:, 768:1280]), "D": _pkn(w_in[l][:, 1280:1792]),
            "E": _pkn(w_in[l][:, 1792:2304]), "F": _pkn(w_in[l][:, 2304:2816]),
            "G": _pkn(w_in[l][:, 2816:3328]), "H": _pkn(w_in[l][:, 3328:3840]),
            "OAB0": np.concatenate([_pkn(w_oa[l][:, 0:512]), _pkn(w_ob[l][:, 0:512])], axis=1),
            "OAB1": np.concatenate([_pkn(w_oa[l][:, 512:1024]), _pkn(w_ob[l][:, 512:1024])], axis=1),
            "OUT0": _pkn(w_out[l][:, 0:512]), "OUT1": _pkn(w_out[l][:, 512:1024]),
        }
        for i in range(11):
            parts["UP%d" % i] = _pkn(np.concatenate(
                [w_up[l][:, 256 * i:256 * i + 256], w_up[l][:, DFF + 256 * i:DFF + 256 * i + 256]], axis=1))
        for c in range(8):
            parts["DN%d" % c] = _pkn(w_down[l][:, 128 * c:128 * c + 128])
        for nm, n in SLABS:
            off, _ = SLAB_OFF[nm]
            assert parts[nm].shape == (128, n), (nm, parts[nm].shape)
            out[l, :, off:off + n] = parts[nm]
    return out


def pack_consts(mix_norm, q_norm, k_norm, sinks, sgu_norm, w_s, b_s, ffn_norm, conv_w, conv_b):
    cst = np.zeros((128, NCST), np.float32)
    for l in range(NL):
        vb = C_VEC + 192 * l
        cst[:, vb:vb + 8] = mix_norm[l].reshape(8, 128).T
        cst[:, vb + 8:vb + 16] = ffn_norm[l].reshape(8, 128).T
        for tap in range(3):
            cst[:, vb + 16 + 44 * tap:vb + 16 + 44 * (tap + 1)] = conv_w[l, tap].reshape(44, 128).T
        cst[:, vb + 148:vb + 192] = conv_b[l].reshape(44, 128).T
        cst[0:64, C_QK + 2 * l] = q_norm[l]
        cst[0:64, C_QK + 2 * l + 1] = k_norm[l]
        cst[:, C_SGUG + 512 * l:C_SGUG + 512 * (l + 1)] = sgu_norm[l][None, :]
        for gp in range(4):
            cst[0:64, C_BSB + 512 * l + gp * 128:C_BSB + 512 * l + (gp + 1) * 128] = b_s[l, 2 * gp][None, :]
            cst[64:128, C_BSB + 512 * l + gp * 128:C_BSB + 512 * l + (gp + 1) * 128] = b_s[l, 2 * gp + 1][None, :]
    k = np.arange(128)[:, None]
    q = np.arange(128)[None, :]
    for g in range(2):
        for j in range(4):
            slope = 2.0 ** (-(4 * g + j + 1))
            dist_prev = q + 128 - k
            dist_cur = q - k
            bp = np.where(dist_prev < 128, -slope * dist_prev, -30000.0)
            bc = np.where(dist_cur >= 0, -slope * dist_cur, -30000.0)
            cst[:, C_ABIAS + (g * 2 + 0) * 512 + j * 128:C_ABIAS + (g * 2 + 0) * 512 + (j + 1) * 128] = bp
            cst[:, C_ABIAS + (g * 2 + 1) * 512 + j * 128:C_ABIAS + (g * 2 + 1) * 512 + (j + 1) * 128] = bc
    cst[:, C_TRIL:C_TRIL + 128] = (k <= q).astype(np.float32)
    srow = np.zeros((1, 2048), np.float32)
    for l in range(NL):
        for g in range(2):
            for sl in range(2):
                for pr in range(2):
                    base = (((l * 2 + g) * 2 + sl) * 2 + pr) * 128
                    srow[0, base:base + 128] = sinks[l, 4 * g + 2 * pr + sl]
    wst = np.ascontiguousarray(np.transpose(w_s, (3, 0, 1, 2)).reshape(128, NL * 8 * 128)).astype(np.float32)
    return cst, srow, wst


_NC_CACHE = {}
DBG_STOP = None


class _Stop(Exception):
    pass


_CUR = [0, 0]


def _ck(k):
    if DBG_STOP is not None and DBG_STOP == (_CUR[0], _CUR[1], k):
        raise _Stop()


def run(x, params, n_cores, layers=(0, 1)):
    B, S_, _ = x.shape
    n_seq = B // n_cores
    key = (n_seq, S_, tuple(layers))
    if key not in _NC_CACHE:
        _NC_CACHE[key] = build(n_seq, S_, layers)
    nc = _NC_CACHE[key]
    wts = pack_weights(params["w_in"], params["w_oa"], params["w_ob"], params["w_out"], params["w_up"],
                       params["w_down"])
    cst, srow, wst = pack_consts(params["mix_norm"], params["q_norm"], params["k_norm"], params["sinks"],
                                 params["sgu_norm"], params["w_s"], params["b_s"], params["ffn_norm"],
                                 params["conv_w"], params["conv_b"])
    in_maps = []
    for c in range(n_cores):
        xc = np.ascontiguousarray(np.transpose(x[c * n_seq:(c + 1) * n_seq], (0, 2, 1)))
        in_maps.append({"xT": xc, "wts": wts, "cst": cst, "srow": srow, "wst": wst})
    res = run_bass_kernel_spmd(nc, in_maps, core_ids=list(range(n_cores)))
    outs = [np.transpose(r["yT"], (0, 2, 1)) for r in res.results]
    return np.ascontiguousarray(np.concatenate(outs, axis=0)).astype(np.float32)


def kernel(**inputs):
    inputs = {k: np.asarray(v) for k, v in inputs.items()}
    x = inputs.pop("x").astype(np.float32)
    params = {k: v.astype(np.float32) for k, v in inputs.items()}
    return run(x, params, 8)
```

```python
from contextlib import ExitStack

import numpy as np
import concourse.bass as bass
import concourse.mybir as mybir
from concourse.bass_utils import run_bass_kernel_spmd

F32 = mybir.dt.float32
BF16 = mybir.dt.bfloat16
AF = mybir.ActivationFunctionType
ALU = mybir.AluOpType

D = 1024
NL = 2
NH = 8
NKV = 2
HD = 64
DFF = 2816
NJ = DFF // 128
T = 512
NB = T // 128
EPS = 1e-6
NBUF = 7
SLAB = 4096

SLABS = ([("A0", 2048), ("B", 2048), ("A1", 2048), ("C", 4096), ("D", 4096), ("E", 4096), ("G", 4096),
          ("OAB0", 4096), ("F", 4096), ("H", 4096), ("OAB1", 4096), ("OUT0", 4096), ("OUT1", 4096)]
         + [("UP%d" % i, 4096) for i in range(11)] + [("DN%d" % c, 2816) for c in range(8)])
SLAB_OFF = {}
_o = 0
for _n, _s in SLABS:
    SLAB_OFF[_n] = (_o, _s)
    _o += _s
WPL = _o

C_VEC = 0
C_QK = 384
C_SGUG = 392
C_BSB = C_SGUG + 1024
C_ABIAS = C_BSB + 1024
C_TRIL = C_ABIAS + 2048
NCST = C_TRIL + 128


class Buf:
    __slots__ = ("name", "lw", "rd", "excl")

    def __init__(self, name, excl=False):
        self.name = name
        self.lw = None
        self.rd = []
        self.excl = excl


class Sched:
    ENG = ("pe", "act", "dve", "pool", "sp")

    def __init__(self):
        self.streams = {e: [] for e in self.ENG}
        self.cnt = {}
        self.waited = {e: {} for e in self.ENG}
        self.semnames = []

    def newsem(self, key):
        self.cnt[key] = 0
        self.semnames.append(key)

    def _waits(self, eng, reads, writes):
        deps = {}
        def add(ev, raw):
            if ev is None:
                return
            k, v = ev
            if k == eng and (eng == "pe" or not raw):
                return
            if deps.get(k, 0) < v:
                deps[k] = v
        for b in reads:
            add(b.lw, True)
            if b.excl:
                for ev in b.rd:
                    add(ev, False)
        for b in writes:
            add(b.lw, False)
            for ev in b.rd:
                add(ev, False)
        for k, v in deps.items():
            if self.waited[eng].get(k, 0) >= v:
                continue
            self.waited[eng][k] = v
            self.streams[eng].append(("wait", k, v))

    def _commit(self, ev, reads, writes):
        for b in reads:
            b.rd.append(ev)
        for b in writes:
            b.lw = ev
            b.rd = []

    def op(self, eng, fn, reads=(), writes=()):
        self._waits(eng, reads, writes)
        self.cnt[eng] += 1
        self.streams[eng].append(("op", fn, eng, 1))
        self._commit((eng, self.cnt[eng]), reads, writes)

    def dma(self, eng, sem, fn, reads=(), writes=()):
        self._waits(eng, reads, writes)
        self.cnt[sem] += 16
        self.streams[eng].append(("op", fn, sem, 16))
        self._commit((sem, self.cnt[sem]), reads, writes)

    def wait_all(self, eng, bufs):
        self._waits(eng, bufs, bufs)

    def replay(self, eng, handle, sems):
        pend = None
        for it in self.streams[eng]:
            if it[0] == "wait":
                if pend is not None:
                    handle.wait_ge(sems[pend[1]], pend[2])
                pend = it
            else:
                res = it[1](handle)
                first, last = res if isinstance(res, tuple) else (res, res)
                if pend is not None:
                    first._wait_ge(sems[pend[1]], pend[2])
                    pend = None
                last.then_inc(sems[it[2]], it[3])
        if pend is not None:
            handle.wait_ge(sems[pend[1]], pend[2])


class Rot:
    def __init__(self, tiles, name):
        self.tiles = tiles
        self.bufs = [Buf("%s%d" % (name, i)) for i in range(len(tiles))]
        self.i = 0

    def next(self):
        i = self.i
        self.i = (i + 1) % len(self.tiles)
        return self.tiles[i], self.bufs[i]


def build(n_seq, seq_len, layers=(0, 1)):
    nc = bass.Bass("TRN2", target_bir_lowering=False)
    S = Sched()
    tiles_per_seq = seq_len // T
    n_tiles = n_seq * tiles_per_seq
    xT_d = nc.dram_tensor("xT", [n_seq, D, seq_len], F32, kind="ExternalInput").ap()
    wts_d = nc.dram_tensor("wts", [NL, 128, WPL], F32, kind="ExternalInput").ap()
    cst_d = nc.dram_tensor("cst", [128, NCST], F32, kind="ExternalInput").ap()
    srow_d = nc.dram_tensor("srow", [1, 2048], F32, kind="ExternalInput").ap()
    wst_d = nc.dram_tensor("wst", [128, 2048], F32, kind="ExternalInput").ap()
    yT_d = nc.dram_tensor("yT", [n_seq, D, seq_len], F32, kind="ExternalOutput").ap()
    wbf_d = nc.dram_tensor("wbf", [NL, 128, WPL], BF16).ap()

    es = ExitStack()
    with es:
        def sb(name, shape, dt):
            return es.enter_context(nc.sbuf_tensor("s_" + name, shape, dt))

        cst = sb("cst", [128, NCST], F32)
        srow = sb("srow", [128, 2048], BF16)
        wsT = sb("wsT", [128, 16, 128], BF16)
        ones = sb("ones", [128, 128], BF16)
        xres = [sb("xres%d" % i, [128, 8, T], F32) for i in range(2)]
        hT = sb("hT", [128, 8, T], BF16)
        sqt = sb("sqt", [128, 4, T], BF16)
        NFR = 10
        frt = sb("frt", [128, NFR, 512], F32)
        qT = sb("qT", [64, 8, T], BF16)
        kT = [sb("kT%d" % l, [64, 2, T + 128], BF16) for l in range(NL)]
        vtok = [sb("vtok%d" % l, [128, NB + 1, 128], BF16) for l in range(NL)]
        qsq = sb("qsq", [64, 5, T], BF16)
        junk = sb("junk", [128, 512], BF16)
        ssv = sb("ssv", [128, 8], F32)
        lnv = sb("lnv", [128, 8], F32)
        rv = sb("rv", [128, 8], F32)
        vn = sb("vn", [128, NB, 512], BF16)
        pT = sb("pT", [128, 6, 512], BF16)
        rden = sb("rden", [128, 4, 256], F32)
        actT = sb("actT", [128, NJ, T], BF16)
        mergedT = actT[:, 0:8, :]
        yattT = actT[:, 8:12, :]
        ysguT = actT[:, 12:16, :]
        uT = actT[:, 16:20, :]
        halo = [sb("halo%d" % l, [128, 2 * NJ, 2], F32) for l in range(NL)]
        hc = sb("hc", [128, 2 * NJ, 2], F32)
        hctmp = sb("hctmp", [128, 2 * NJ], F32)
        b_hc = Buf("hc")
        b_hctmp = Buf("hctmp")
        wslab = sb("wslab", [128, NBUF, SLAB], BF16)
        psb = [es.enter_context(nc.psum_tensor("ps%d" % i, [128, 512], F32)) for i in range(8)]

        b_cst = Buf("cst")
        b_srow = Buf("srow")
        b_wsT = Buf("wsT")
        b_ones = Buf("ones")
        b_xres = [[Buf("xres%d_%d" % (i, c)) for c in range(8)] for i in range(2)]
        b_hT = [Buf("hT%d" % c) for c in range(8)]
        b_qT = [Buf("qT%d" % h) for h in range(8)]
        b_kprev = [Buf("kprev%d" % l) for l in range(NL)]
        b_kcur = [[Buf("kcur%d_%d" % (l, g)) for g in range(2)] for l in range(NL)]
        b_vprev = [Buf("vprev%d" % l) for l in range(NL)]
        b_vcur = [Buf("vcur%d" % l) for l in range(NL)]
        b_junk = Buf("junk")
        b_ssv = [Buf("ssv%d" % i) for i in range(8)]
        b_lnv = [Buf("lnv%d" % i) for i in range(8)]
        b_rv = [Buf("rv%d" % i) for i in range(8)]
        b_vn = [Buf("vn%d" % b) for b in range(NB)]
        b_actT = [Buf("actT%d" % j) for j in range(NJ)]
        b_merged = b_actT[0:8]
        b_yatt = [[b_actT[8 + c]] * NB for c in range(4)]
        b_uT = b_actT[16:20]
        b_halo = [[Buf("halo%d_%d" % (l, ch)) for ch in range(2 * NJ)] for l in range(NL)]
        b_wslab = [Buf("wslab%d" % i) for i in range(NBUF)]
        b_ps = [Buf("ps%d" % i, excl=True) for i in range(8)]

        sqr = Rot([sqt[:, i, :] for i in range(4)], "sq")
        qsqr = Rot([qsq[:, i, :] for i in range(5)], "qsq")
        fr = Rot([frt[:, i, :] for i in range(NFR)], "fr")
        rqr = rq2r = gvr = efr = tmr = sar = sbr = t1r = t2r = agr = avr = sgr = fr
        srowf = frt[0:1, 0:4, :]
        wstage = frt[:, 4:8, :]
        pTr = Rot([pT[:, i, :] for i in range(6)], "pT")
        rdr = Rot([rden[:, i, :] for i in range(4)], "rden")
        psr = Rot([p[:, :] for p in psb[0:7]], "psr")
        psr.bufs = b_ps[0:7]
        ss_ps, b_ss = psb[7][:, :], b_ps[7]
        small_i = [0]

        for e in ("pe", "act", "dve", "pool"):
            S.newsem(e)
        for i in range(NBUF):
            S.newsem("w%d" % i)
            S.newsem("ws%d" % i)
        b_wd = {(l, nm): Buf("wd%d_%s" % (l, nm)) for l in range(NL) for (nm, _) in SLABS}
        for k in ("cst0", "cst1", "cst2", "xl0", "xl1", "xs0", "xs1"):
            S.newsem(k)

        def act(out, in_, func, reads, writes, bias=None, scale=None, accum_out=None):
            kw = {}
            if bias is not None:
                kw["bias"] = bias
            if scale is not None:
                kw["scale"] = scale
            if accum_out is not None:
                kw["accum_out"] = accum_out
            S.op("act", lambda e: e.activation(out=out, in_=in_, func=func, **kw), reads, writes)

        def tt(out, in0, in1, op, reads, writes, eng="dve"):
            S.op(eng, lambda e: e.tensor_tensor(out=out, in0=in0, in1=in1, op=op), reads, writes)

        def stt(out, in0, scalar, in1, op0, op1, reads, writes, eng="dve"):
            S.op(eng, lambda e: e.scalar_tensor_tensor(out=out, in0=in0, scalar=scalar, in1=in1,
                                                        op0=op0, op1=op1), reads, writes)

        def cp(out, in_, reads, writes, eng="dve"):
            S.op(eng, lambda e: e.tensor_copy(out=out, in_=in_), reads, writes)

        def mm_group(out, pairs, reads, writes):
            def fn(e):
                ins = first = None
                n = len(pairs)
                for i, (l, r) in enumerate(pairs):
                    ins = e.matmul(out, l, r, start=(i == 0), stop=(i == n - 1))
                    first = first or ins
                return first, ins
            S.op("pe", fn, reads, writes)

        def mm_part(out, pairs, reads, writes, first, last):
            def fn(e):
                ins = fi = None
                n = len(pairs)
                for i, (l, r) in enumerate(pairs):
                    ins = e.matmul(out, l, r, start=(first and i == 0), stop=(last and i == n - 1))
                    fi = fi or ins
                return fi, ins
            S.op("pe", fn, reads, writes)

        def mm_multi(groups, reads, writes):
            def fn(e):
                ins = first = None
                for out, pairs in groups:
                    n = len(pairs)
                    for i, (l, r) in enumerate(pairs):
                        ins = e.matmul(out, l, r, start=(i == 0), stop=(i == n - 1))
                        first = first or ins
                return first, ins
            S.op("pe", fn, reads, writes)

        passes = [(ti, l) for ti in range(n_tiles) for l in layers]
        wseq = [(l, nm) for (_, l) in passes for (nm, _) in SLABS]
        wstate = {"issue": 0, "acq": 0}

        def w_issue():
            i = wstate["issue"]
            if i >= len(wseq):
                return
            wstate["issue"] = i + 1
            l, nm = wseq[i]
            off, n = SLAB_OFF[nm]
            slot = i % NBUF
            o = wslab[:, slot, 0:n]
            if passes[i // len(SLABS)][0] == 0:
                src = wts_d[l, :, off:off + n]
                S.dma("pool", "w%d" % slot, lambda e: e.dma_start(out=o, in_=src), (), (b_wslab[slot],))
            else:
                src = wbf_d[l, :, off:off + n]
                S.dma("sp", "w%d" % slot, lambda e: e.dma_start(out=o, in_=src), (b_wd[(l, nm)],),
                      (b_wslab[slot],))

        def w_acquire(expect):
            i = wstate["acq"]
            wstate["acq"] = i + 1
            assert wseq[i][1] == expect, (wseq[i], expect)
            slot = i % NBUF
            if passes[i // len(SLABS)][0] == 0 and n_tiles > 1:
                l, nm = wseq[i]
                off, n = SLAB_OFF[nm]
                dst = wbf_d[l, :, off:off + n]
                srcs = wslab[:, slot, 0:n]
                S.dma("sp", "ws%d" % slot, lambda e: e.dma_start(out=dst, in_=srcs), (b_wslab[slot],),
                      (b_wd[(l, nm)],))
            return wslab[:, slot, :], b_wslab[slot]

        def w_release(n=1):
            for _ in range(n):
                w_issue()

        S.dma("sp", "cst0", lambda e: e.dma_start(out=cst[:, :], in_=cst_d[:, :]), (), (b_cst,))
        S.dma("sp", "cst1", lambda e: e.dma_start(out=srowf, in_=srow_d.rearrange("o (a n) -> o a n", a=4)),
              (), tuple(fr.bufs[0:4]))
        S.dma("sp", "cst2", lambda e: e.dma_start(out=wstage, in_=wst_d.rearrange("p (a n) -> p a n", a=4)),
              (), tuple(fr.bufs[4:8]))
        for _ in range(NBUF):
            w_issue()
        S.op("dve", lambda e: e.memset(ones[:, :], 1.0), (), (b_ones,))
        S.op("dve", lambda e: e.memset(srow[:, :], 0.0), (), (b_srow,))
        act(srow[0:1, :].rearrange("o (a n) -> o a n", a=4), srowf, AF.Exp, tuple(fr.bufs[0:4]), (b_srow,))
        act(cst[:, C_ABIAS:C_ABIAS + 2048], cst[:, C_ABIAS:C_ABIAS + 2048], AF.Exp, (b_cst,), (b_cst,))
        for i in range(16):
            tt(wsT[:, i, :], wstage[:, i // 4, (i % 4) * 128:(i % 4 + 1) * 128], cst[:, C_TRIL:C_TRIL + 128], ALU.mult,
               (fr.bufs[4 + i // 4], b_cst), (b_wsT,))

        def rms_sq_act(xr, bxr, c):
            sq, bsq = sqr.next()
            act(sq, xr[:, c, :], AF.Square, (bxr[c],), (bsq,))
            return sq, bsq

        def rms_sq_mm(sqb, c):
            sq, bsq = sqb
            S.op("pe", (lambda e: e.matmul(ss_ps, ones[:, :], sq, start=(c == 0), stop=(c == 7))),
                 (bsq, b_ones), (b_ss,))

        def rms_finish(xr, bxr, gcol):
            ps, bps = ss_ps, b_ss
            rtmp, b_rtmp = fr.next()
            act(rtmp, ps, AF.Ln, (bps,), (b_rtmp,), bias=EPS, scale=1.0 / D)
            rstd, b_rstd = fr.next()
            act(rstd, rtmp, AF.Exp, (b_rtmp,), (b_rstd,), scale=-0.5)
            for c in range(8):
                stt(hT[:, c, :], xr[:, c, :], cst[:, gcol + c:gcol + c + 1], rstd, ALU.mult, ALU.mult,
                    (bxr[c], b_rstd, b_cst), (b_hT[c],))

        def headnorm_a(ps, bps, gcolumn, out, bout):
            sq, bsq = qsqr.next()
            act(sq[0:64, :], ps[0:64, :], AF.Square, (bps,), (bsq,))
            return lambda: headnorm_b(ps, bps, gcolumn, out, bout, sq, bsq)

        def headnorm_b(ps, bps, gcolumn, out, bout, sq, bsq):
            ps2, bps2 = psr.next()
            S.op("pe", lambda e: e.matmul(ps2[0:64, :], ones[0:64, 0:64], sq[0:64, :], start=True, stop=True),
                 (bsq, b_ones), (bps2,))
            r1, br1 = rqr.next()
            act(r1[0:64, :], ps2[0:64, :], AF.Ln, (bps2,), (br1,), bias=EPS, scale=1.0 / HD)
            r2, br2 = rq2r.next()
            act(r2[0:64, :], r1[0:64, :], AF.Exp, (br1,), (br2,), scale=-0.5)
            stt(out, ps[0:64, :], cst[0:64, gcolumn:gcolumn + 1], r2[0:64, :], ALU.mult, ALU.mult,
                (bps, br2, b_cst), (bout,))

        def emit_xload(ti):
            s_idx = ti // tiles_per_seq
            t0 = (ti % tiles_per_seq) * T
            xi = ti % 2
            src = xT_d[s_idx].rearrange("(c p) t -> p c t", p=128)[:, :, t0:t0 + T]
            dstt = xres[xi][:, :, :]
            S.dma("sp", "xl%d" % xi, lambda e: e.dma_start(out=dstt, in_=src), (), tuple(b_xres[xi]))

        def kouter(outs, lhs_fn, reads_w, wr_bufs):
            for k in range(8):
                groups = [(o, lhs_fn(i, k), hT[:, k, :]) for i, o in enumerate(outs)]
                def fn(e, groups=groups, k=k):
                    ins = first = None
                    for (o, l_, r_) in groups:
                        ins = e.matmul(o, l_, r_, start=(k == 0), stop=(k == 7))
                        first = first or ins
                    return first, ins
                S.op("pe", fn, (*reads_w, b_hT[k]), tuple(wr_bufs))

        def run_pass(ti, l, first_layer, last_layer, nxt):
            _CUR[0], _CUR[1] = ti, l
            s_idx = ti // tiles_per_seq
            tt_i = ti % tiles_per_seq
            t0 = tt_i * T
            first_in_seq = tt_i == 0
            last_in_seq = tt_i == tiles_per_seq - 1
            xi = ti % 2
            xr = xres[xi]
            bxr = b_xres[xi]
            vbase = C_VEC + 192 * l
            G1, G2 = vbase, vbase + 8
            CW0, CW1, CW2, CB = vbase + 16, vbase + 60, vbase + 104, vbase + 148
            QG, KG = C_QK + 2 * l, C_QK + 2 * l + 1

            _ck(0)
            rms_finish(xr, bxr, G1)
            _ck(1)

            pend_hn = []

            def flush_hn():
                while pend_hn:
                    pend_hn.pop(0)()

            def proj_head(vW, bW, cols, gcolumn, out, bout):
                ps, bps = psr.next()
                mm_group(ps[0:64, :], [(vW[:, k, cols], hT[:, k, :]) for k in range(8)], (bW, *b_hT), (bps,))
                part_b = headnorm_a(ps, bps, gcolumn, out, bout)
                flush_hn()
                pend_hn.append(part_b)

            wA0, bA0 = w_acquire("A0")
            vA0 = wA0[:, 0:2048].rearrange("p (k n) -> p k n", k=8)
            qps = [psr.next() for _ in range(4)]
            kouter([p[0][0:64, :] for p in qps], lambda i, k: vA0[:, k, i * 64:(i + 1) * 64], (bA0,),
                   [p[1] for p in qps])
            for h in range(4):
                pend_hn.append(headnorm_a(qps[h][0], qps[h][1], QG, qT[:, h, :], b_qT[h]))
            w_release()
            _ck(2)

            wB, bB = w_acquire("B")
            vB = wB[:, 0:2048].rearrange("p (k n) -> p k n", k=8)
            for g in range(2):
                proj_head(vB, bB, slice(g * 64, (g + 1) * 64), KG, kT[l][:, g, 128:128 + T], b_kcur[l][g])
            ps, bps = psr.next()
            mm_multi([(ps[:, b * 128:(b + 1) * 128],
                       [(hT[:, k, b * 128:(b + 1) * 128], vB[:, k, 128:256]) for k in range(8)])
                      for b in range(NB)], (bB, *b_hT), (bps,))
            flush_hn()
            cp(vtok[l][:, 1:NB + 1, :], ps.rearrange("p (b n) -> p b n", b=NB), (bps,), (b_vcur[l],))
            w_release()

            def sgu_block(b):
                ps, bps = psr.next()
                groups = []
                for gp in range(4):
                    for sl in range(2):
                        g = 2 * gp + sl
                        groups.append((ps[64 * sl:64 * sl + 64, gp * 128:(gp + 1) * 128],
                                       [(vn[:, b, g * 64:(g + 1) * 64], wsT[:, l * 8 + g, :])]))
                mm_multi(groups, (b_vn[b], b_wsT), (bps,))
                tm, btm = tmr.next()
                tt(tm, ps, cst[:, C_BSB + 512 * l:C_BSB + 512 * l + 512], ALU.add, (bps, b_cst), (btm,))
                tt(ysguT[:, :, b * 128:(b + 1) * 128], tm.rearrange("p (a n) -> p a n", a=4),
                   uT[:, :, b * 128:(b + 1) * 128], ALU.mult, (btm, *b_uT), tuple(b_actT[12:16]))

            def att_stage1(b, g):
                halves = []
                if not (first_in_seq and b == 0):
                    halves.append(0)
                halves.append(1)
                pts = {}
                for hf in halves:
                    ps, bps = psr.next()
                    kcols = slice(128 * (b + hf), 128 * (b + hf) + 128)
                    kb = [b_kcur[l][g]] + ([b_kprev[l]] if (b == 0 and hf == 0) else [])
                    lhs_ = kT[l][:, g, kcols]
                    rhs_ = qT[:, 4 * g:4 * g + 4, b * 128:(b + 1) * 128]
                    out_ = ps.rearrange("p (a n) -> p a n", a=4)
                    S.op("pe", (lambda e, out_=out_, lhs_=lhs_, rhs_=rhs_: e.matmul(
                        out_, lhs_, rhs_, start=True, stop=True)),
                        (*kb, *b_qT[4 * g:4 * g + 4]), (bps,))
                    e_, be_ = efr.next()
                    act(e_, ps, AF.Exp, (bps,), (be_,), scale=0.125)
                    p_, bp_ = pTr.next()
                    col = C_ABIAS + (g * 2 + hf) * 512
                    tt(p_, e_, cst[:, col:col + 512], ALU.mult, (be_, b_cst), (bp_,), eng="pool")
                    pts[hf] = (p_, bp_)
                return halves, pts

            def att_stage2(b, g, halves, pts):
                yd, byd = psr.next()
                groups = []
                sbase = ((l * 2 + g) * 2) * 256
                for sl in range(2):
                    ypairs, dpairs = [], []
                    for hf in halves:
                        p_ = pts[hf][0]
                        rhs = p_.rearrange("p (pr s n) -> p pr s n", pr=2, s=2)[:, :, sl, :]
                        ypairs.append((vtok[l][:, b + hf, g * 64:(g + 1) * 64], rhs))
                        dpairs.append((ones[:, 0:64], rhs))
                    dpairs.append((ones[:, 0:64],
                                   srow[:, sbase + sl * 256:sbase + sl * 256 + 256].rearrange("p (a n) -> p a n", a=2)))
                    groups.append((yd[64 * sl:64 * sl + 64, 0:256].rearrange("p (a n) -> p a n", a=2), ypairs))
                    groups.append((yd[64 * sl:64 * sl + 64, 256:512].rearrange("p (a n) -> p a n", a=2), dpairs))
                vb = [b_vcur[l]] + ([b_vprev[l]] if b == 0 and 0 in halves else [])
                mm_multi(groups, (*[pts[hf][1] for hf in halves], *vb, b_ones, b_srow), (byd,))
                rl, brl = rdr.next()
                act(rl, yd[:, 256:512], AF.Ln, (byd,), (brl,))
                rd, brd = rdr.next()
                act(rd, rl, AF.Exp, (brl,), (brd,), scale=-1.0)
                tt(yattT[:, 2 * g:2 * g + 2, b * 128:(b + 1) * 128],
                   yd[:, 0:256].rearrange("p (a n) -> p a n", a=2),
                   rd.rearrange("p (a n) -> p a n", a=2), ALU.mult, (byd, brd),
                   (b_yatt[2 * g][b], b_yatt[2 * g + 1][b]))

            _ck(4)
            slabs = {}

            def get_slab(nm):
                if nm not in slabs:
                    w_, b_ = w_acquire(nm)
                    n_ = 2048 if nm == "A1" else 4096
                    slabs[nm] = (w_[:, 0:n_].rearrange("p (k n) -> p k n", k=8), b_)
                return slabs[nm]

            def u_qhead(h):
                vA1, bA1 = get_slab("A1")
                proj_head(vA1, bA1, slice((h - 4) * 64, (h - 3) * 64), QG, qT[:, h, :], b_qT[h])
                if h == 7:
                    w_release()

            def u_su(c):
                vC, bC = get_slab("C")
                flush_hn()
                ps, bps = psr.next()
                mm_group(ps, [(vC[:, k, c * 128:(c + 1) * 128], hT[:, k, :]) for k in range(8)],
                         (bC, *b_hT), (bps,))
                act(uT[:, c, :], ps, AF.Gelu_apprx_tanh, (bps,), (b_uT[c],))
                if c == 3:
                    w_release()

            def u_sv(b):
                vD, bD = get_slab("D")
                ps, bps = psr.next()
                mm_group(ps, [(hT[:, k, b * 128:(b + 1) * 128], vD[:, k, :]) for k in range(8)],
                         (bD, *b_hT), (bps,))
                g_, bg_ = gvr.next()
                act(g_, ps, AF.Gelu_apprx_tanh, (bps,), (bg_,))
                si = small_i[0]
                small_i[0] = (si + 1) % 8
                act(junk[:, :], g_, AF.Square, (bg_,), (b_junk, b_ssv[si]), accum_out=ssv[:, si:si + 1])
                act(lnv[:, si:si + 1], ssv[:, si:si + 1], AF.Ln, (b_ssv[si],), (b_lnv[si],), bias=EPS, scale=1.0 / 512)
                act(rv[:, si:si + 1], lnv[:, si:si + 1], AF.Exp, (b_lnv[si],), (b_rv[si],), scale=-0.5)
                stt(vn[:, b, :], g_, rv[:, si:si + 1], cst[:, C_SGUG + 512 * l:C_SGUG + 512 * l + 512],
                    ALU.mult, ALU.mult, (bg_, b_rv[si], b_cst), (b_vn[b],))
                if b == NB - 1:
                    w_release()

            units = ([lambda h=h: u_qhead(h) for h in range(4, 8)] + [lambda c=c: u_su(c) for c in range(4)]
                     + [lambda b=b: u_sv(b) for b in range(NB)])
            its = [(b, 0) for b in range(NB)] + [(b, 1) for b in range(NB)]
            pend = [att_stage1(*its[0]), att_stage1(*its[1])]
            for i, (b, g) in enumerate(its):
                if g == 0:
                    for _ in range(3):
                        units.pop(0)()
                else:
                    sgu_block(b)
                if i + 2 < len(its):
                    pend.append(att_stage1(*its[i + 2]))
                att_stage2(b, g, *pend.pop(0))
            assert not units
            _ck(3)
            if not last_in_seq:
                cp(kT[l][:, :, 0:128], kT[l][:, :, T:T + 128], (*b_kcur[l],), (b_kprev[l],))
                cp(vtok[l][:, 0, :], vtok[l][:, NB, :], (b_vcur[l],), (b_vprev[l],))

            _ck(5)
            for half in range(2):
                wE, bE = w_acquire("EF"[half])
                wG, bG = w_acquire("GH"[half])
                wO, bO = w_acquire("OAB%d" % half)
                vE = wE.rearrange("p (k n) -> p k n", k=8)
                vG = wG.rearrange("p (k n) -> p k n", k=8)
                vO = wO.rearrange("p (m k n) -> p m k n", m=2, k=4)
                for c4 in range(4):
                    c = 4 * half + c4
                    cs = slice(c4 * 128, (c4 + 1) * 128)
                    pga, bpga = psr.next()
                    mm_group(pga, [(vE[:, k, cs], hT[:, k, :]) for k in range(8)], (bE, *b_hT), (bpga,))
                    pgb, bpgb = psr.next()
                    mm_group(pgb, [(vG[:, k, cs], hT[:, k, :]) for k in range(8)], (bG, *b_hT), (bpgb,))
                    pa, bpa = psr.next()
                    mm_group(pa, [(vO[:, 0, kc, cs], yattT[:, kc, :]) for kc in range(4)],
                             (bO, *b_actT[8:12]), (bpa,))
                    pb, bpb = psr.next()
                    mm_group(pb, [(vO[:, 1, kc, cs], ysguT[:, kc, :]) for kc in range(4)], (bO, *b_actT[12:16]), (bpb,))
                    sa, bsa = sar.next()
                    act(sa, pga, AF.Sigmoid, (bpga,), (bsa,))
                    sb_, bsb_ = sbr.next()
                    act(sb_, pgb, AF.Sigmoid, (bpgb,), (bsb_,))
                    t1, bt1 = t1r.next()
                    tt(t1, pa, sa, ALU.mult, (bpa, bsa), (bt1,))
                    t2, bt2 = t2r.next()
                    tt(t2, pb, sb_, ALU.mult, (bpb, bsb_), (bt2,))
                    tt(mergedT[:, c, :], t1, t2, ALU.add, (bt1, bt2), (b_merged[c],), eng="pool")
                w_release(3)
            sqbs = {}
            for half in range(2):
                wO, bO = w_acquire("OUT%d" % half)
                vO = wO.rearrange("p (k n) -> p k n", k=8)
                for c4 in range(4):
                    c = 4 * half + c4
                    po, bpo = psr.next()
                    mm_group(po, [(vO[:, k, c4 * 128:(c4 + 1) * 128], mergedT[:, k, :]) for k in range(8)],
                             (bO, *b_merged), (bpo,))
                    if c >= 1:
                        rms_sq_mm(sqbs[c - 1], c - 1)
                    tt(xr[:, c, :], po, xr[:, c, :], ALU.add, (bpo, bxr[c]), (bxr[c],))
                    sqbs[c] = rms_sq_act(xr, bxr, c)
                w_release()
            rms_sq_mm(sqbs[7], 7)

            _ck(6)
            rms_finish(xr, bxr, G2)
            if last_layer and nxt is not None:
                emit_xload(nxt[0])

            def ffn_epilogue(j, pg, bpg, pv, bpv):
                ag, bag = agr.next()
                av, bav = avr.next()
                items = ((pg, bpg, ag, bag, j), (pv, bpv, av, bav, NJ + j))
                for (ps, bps, a_, ba_, ch) in items:
                    act(a_, ps, AF.Identity, (bps, b_cst), (ba_,),
                        bias=cst[:, CB + ch:CB + ch + 1], scale=cst[:, CW2 + ch:CW2 + ch + 1])
                for (ps, bps, a_, ba_, ch) in items:
                    stt(a_[:, 1:T], ps[:, 0:T - 1], cst[:, CW1 + ch:CW1 + ch + 1], a_[:, 1:T], ALU.mult, ALU.add,
                        (bps, ba_, b_cst), (ba_,))
                for (ps, bps, a_, ba_, ch) in items:
                    stt(a_[:, 2:T], ps[:, 0:T - 2], cst[:, CW0 + ch:CW0 + ch + 1], a_[:, 2:T], ALU.mult, ALU.add,
                        (bps, ba_, b_cst), (ba_,))
                if not first_in_seq:
                    for (ps, bps, a_, ba_, ch) in items:
                        tt(a_[:, 0:2], a_[:, 0:2], hc[:, ch, :], ALU.add, (ba_, b_hc), (ba_,), eng="pool")
                if not last_in_seq:
                    for (ps, bps, a_, ba_, ch) in items:
                        act(halo[l][:, ch, :], ps[:, T - 2:T], AF.Identity, (bps,), (b_halo[l][ch],))
                sg, bsg = sgr.next()
                act(sg, ag, AF.Silu, (bag,), (bsg,))
                tt(actT[:, j, :], sg, av, ALU.mult, (bsg, bav), (b_actT[j],), eng="pool")

            if not first_in_seq:
                bh = tuple(b_halo[l])
                tt(hc[:, :, 1], halo[l][:, :, 1], cst[:, CW0:CW0 + 44], ALU.mult, (*bh, b_cst), (b_hc,))
                tt(hc[:, :, 0], halo[l][:, :, 0], cst[:, CW0:CW0 + 44], ALU.mult, (*bh, b_cst), (b_hc,))
                tt(hctmp[:, :], halo[l][:, :, 1], cst[:, CW1:CW1 + 44], ALU.mult, (*bh, b_cst), (b_hctmp,))
                tt(hc[:, :, 0], hc[:, :, 0], hctmp[:, :], ALU.add, (b_hc, b_hctmp), (b_hc,))
            for i in range(11):
                wU, bU = w_acquire("UP%d" % i)
                vU = wU.rearrange("p (k n) -> p k n", k=8)
                if i == 0:
                    pss = [psr.next() for _ in range(4)]
                    offs = [0, 256, 128, 384]
                    kouter([p[0] for p in pss], lambda q, k: vU[:, k, offs[q]:offs[q] + 128], (bU,),
                           [p[1] for p in pss])
                    ffn_epilogue(0, pss[0][0], pss[0][1], pss[1][0], pss[1][1])
                    ffn_epilogue(1, pss[2][0], pss[2][1], pss[3][0], pss[3][1])
                else:
                    for jj in range(2):
                        j = 2 * i + jj
                        pg, bpg = psr.next()
                        mm_group(pg, [(vU[:, k, jj * 128:(jj + 1) * 128], hT[:, k, :]) for k in range(8)],
                                 (bU, *b_hT), (bpg,))
                        pv, bpv = psr.next()
                        mm_group(pv, [(vU[:, k, 256 + jj * 128:256 + (jj + 1) * 128], hT[:, k, :]) for k in range(8)],
                                 (bU, *b_hT), (bpv,))
                        ffn_epilogue(j, pg, bpg, pv, bpv)
                w_release()
            _ck(7)
            if nxt is not None:
                nxr, nbxr = xres[nxt[0] % 2], b_xres[nxt[0] % 2]
            sqbs = {}

            def down_tail(c, pd, bpd):
                if nxt is not None and c >= 1:
                    rms_sq_mm(sqbs[c - 1], c - 1)
                tt(xr[:, c, :], pd, xr[:, c, :], ALU.add, (bpd, bxr[c]), (bxr[c],))
                if nxt is not None:
                    sqbs[c] = rms_sq_act(nxr, nbxr, c)

            JS = 14
            first = []
            for c in range(4):
                wDn, bDn = w_acquire("DN%d" % c)
                vDn = wDn[:, 0:2816].rearrange("p (k n) -> p k n", k=NJ)
                pd, bpd = psr.next()
                mm_part(pd, [(vDn[:, j, :], actT[:, j, :]) for j in range(JS)], (bDn, *b_actT[0:JS]), (bpd,), True, False)
                first.append((vDn, bDn, pd, bpd))
            for c in range(4):
                vDn, bDn, pd, bpd = first[c]
                mm_part(pd, [(vDn[:, j, :], actT[:, j, :]) for j in range(JS, NJ)], (bDn, *b_actT[JS:NJ]), (bpd,), False, True)
                down_tail(c, pd, bpd)
                w_release()
            for c in range(4, 8):
                wDn, bDn = w_acquire("DN%d" % c)
                vDn = wDn[:, 0:2816].rearrange("p (k n) -> p k n", k=NJ)
                pd, bpd = psr.next()
                mm_group(pd, [(vDn[:, j, :], actT[:, j, :]) for j in range(NJ)], (bDn, *b_actT), (bpd,))
                down_tail(c, pd, bpd)
                w_release()
            if nxt is not None:
                rms_sq_mm(sqbs[7], 7)

            if last_layer:
                dst = yT_d[s_idx].rearrange("(c p) t -> p c t", p=128)[:, :, t0:t0 + T]
                S.dma("sp", "xs%d" % xi, lambda e: e.dma_start(out=dst, in_=xr[:, :, :]), tuple(bxr), ())

        plist = [(ti, li) for ti in range(n_tiles) for li in range(len(layers))]
        emit_xload(0)
        sq0 = [rms_sq_act(xres[0], b_xres[0], c) for c in range(4)]
        for c in range(8):
            rms_sq_mm(sq0[c] if c < 4 else rms_sq_act(xres[0], b_xres[0], c), c)
        try:
            for pi, (ti, li) in enumerate(plist):
                nxt = plist[pi + 1] if pi + 1 < len(plist) else None
                run_pass(ti, layers[li], li == 0, li == len(layers) - 1, nxt)
        except _Stop:
            dst = yT_d[0].rearrange("(c p) t -> p c t", p=128)[:, :, 0:T]
            S.dma("sp", "xs0", lambda e: e.dma_start(out=dst, in_=xres[0][:, :, :]), tuple(b_xres[0]), ())
        S.wait_all("sp", [b for i in range(2) for b in b_xres[i]])

        sems = {k: es.enter_context(nc.semaphore(k)) for k in S.semnames}
        block = es.enter_context(nc.Block())

        @block.tensor
        def _(e):
            S.replay("pe", e, sems)

        @block.scalar
        def _(e):
            S.replay("act", e, sems)

        @block.vector
        def _(e):
            S.replay("dve", e, sems)

        @block.gpsimd
        def _(e):
            S.replay("pool", e, sems)

        @block.sync
        def _(e):
            S.replay("sp", e, sems)
    return nc


def _pkn(w):
    kc = w.shape[0] // 128
    return np.ascontiguousarray(w.reshape(kc, 128, -1).transpose(1, 0, 2).reshape(128, -1))


def pack_weights(w_in, w_oa, w_ob, w_out, w_up, w_down):
    out = np.empty((NL, 128, WPL), np.float32)
    for l in range(NL):
        parts = {
            "A0": _pkn(w_in[l][:, 0:256]), "A1": _pkn(w_in[l][:, 256:512]), "B": _pkn(w_in[l][:, 512:768]),
            "C": _pkn(w_in[l][:, 768:1280]), "D": _pkn(w_in[l][:, 1280:1792]),
            "E": _pkn(w_in[l][:, 1792:2304]), "F": _pkn(w_in[l][:, 2304:2816]),
            "G": _pkn(w_in[l][:, 2816:3328]), "H": _pkn(w_in[l][:, 3328:3840]),
            "OAB0": np.concatenate([_pkn(w_oa[l][:, 0:512]), _pkn(w_ob[l][:, 0:512])], axis=1),
            "OAB1": np.concatenate([_pkn(w_oa[l][:, 512:1024]), _pkn(w_ob[l][:, 512:1024])], axis=1),
            "OUT0": _pkn(w_out[l][:, 0:512]), "OUT1": _pkn(w_out[l][:, 512:1024]),
        }
        for i in range(11):
            parts["UP%d" % i] = _pkn(np.concatenate(
                [w_up[l][:, 256 * i:256 * i + 256], w_up[l][:, DFF + 256 * i:DFF + 256 * i + 256]], axis=1))
        for c in range(8):
            parts["DN%d" % c] = _pkn(w_down[l][:, 128 * c:128 * c + 128])
        for nm, n in SLABS:
            off, _ = SLAB_OFF[nm]
            assert parts[nm].shape == (128, n), (nm, parts[nm].shape)
            out[l, :, off:off + n] = parts[nm]
    return out


def pack_consts(mix_norm, q_norm, k_norm, sinks, sgu_norm, w_s, b_s, ffn_norm, conv_w, conv_b):
    cst = np.zeros((128, NCST), np.float32)
    for l in range(NL):
        vb = C_VEC + 192 * l
        cst[:, vb:vb + 8] = mix_norm[l].reshape(8, 128).T
        cst[:, vb + 8:vb + 16] = ffn_norm[l].reshape(8, 128).T
        for tap in range(3):
            cst[:, vb + 16 + 44 * tap:vb + 16 + 44 * (tap + 1)] = conv_w[l, tap].reshape(44, 128).T
        cst[:, vb + 148:vb + 192] = conv_b[l].reshape(44, 128).T
        cst[0:64, C_QK + 2 * l] = q_norm[l]
        cst[0:64, C_QK + 2 * l + 1] = k_norm[l]
        cst[:, C_SGUG + 512 * l:C_SGUG + 512 * (l + 1)] = sgu_norm[l][None, :]
        for gp in range(4):
            cst[0:64, C_BSB + 512 * l + gp * 128:C_BSB + 512 * l + (gp + 1) * 128] = b_s[l, 2 * gp][None, :]
            cst[64:128, C_BSB + 512 * l + gp * 128:C_BSB + 512 * l + (gp + 1) * 128] = b_s[l, 2 * gp + 1][None, :]
    k = np.arange(128)[:, None]
    q = np.arange(128)[None, :]
    for g in range(2):
        for j in range(4):
            slope = 2.0 ** (-(4 * g + j + 1))
            dist_prev = q + 128 - k
            dist_cur = q - k
            bp = np.where(dist_prev < 128, -slope * dist_prev, -30000.0)
            bc = np.where(dist_cur >= 0, -slope * dist_cur, -30000.0)
            cst[:, C_ABIAS + (g * 2 + 0) * 512 + j * 128:C_ABIAS + (g * 2 + 0) * 512 + (j + 1) * 128] = bp
            cst[:, C_ABIAS + (g * 2 + 1) * 512 + j * 128:C_ABIAS + (g * 2 + 1) * 512 + (j + 1) * 128] = bc
    cst[:, C_TRIL:C_TRIL + 128] = (k <= q).astype(np.float32)
    srow = np.zeros((1, 2048), np.float32)
    for l in range(NL):
        for g in range(2):
            for sl in range(2):
                for pr in range(2):
                    base = (((l * 2 + g) * 2 + sl) * 2 + pr) * 128
                    srow[0, base:base + 128] = sinks[l, 4 * g + 2 * pr + sl]
    wst = np.ascontiguousarray(np.transpose(w_s, (3, 0, 1, 2)).reshape(128, NL * 8 * 128)).astype(np.float32)
    return cst, srow, wst


_NC_CACHE = {}
DBG_STOP = None


class _Stop(Exception):
    pass


_CUR = [0, 0]


def _ck(k):
    if DBG_STOP is not None and DBG_STOP == (_CUR[0], _CUR[1], k):
        raise _Stop()


def run(x, params, n_cores, layers=(0, 1)):
    B, S_, _ = x.shape
    n_seq = B // n_cores
    key = (n_seq, S_, tuple(layers))
    if key not in _NC_CACHE:
        _NC_CACHE[key] = build(n_seq, S_, layers)
    nc = _NC_CACHE[key]
    wts = pack_weights(params["w_in"], params["w_oa"], params["w_ob"], params["w_out"], params["w_up"],
                       params["w_down"])
    cst, srow, wst = pack_consts(params["mix_norm"], params["q_norm"], params["k_norm"], params["sinks"],
                                 params["sgu_norm"], params["w_s"], params["b_s"], params["ffn_norm"],
                                 params["conv_w"], params["conv_b"])
    in_maps = []
    for c in range(n_cores):
        xc = np.ascontiguousarray(np.transpose(x[c * n_seq:(c + 1) * n_seq], (0, 2, 1)))
        in_maps.append({"xT": xc, "wts": wts, "cst": cst, "srow": srow, "wst": wst})
    res = run_bass_kernel_spmd(nc, in_maps, core_ids=list(range(n_cores)))
    outs = [np.transpose(r["yT"], (0, 2, 1)) for r in res.results]
    return np.ascontiguousarray(np.concatenate(outs, axis=0)).astype(np.float32)


def kernel(**inputs):
    inputs = {k: np.asarray(v) for k, v in inputs.items()}
    x = inputs.pop("x").astype(np.float32)
    params = {k: v.astype(np.float32) for k, v in inputs.items()}
    return run(x, params, 8)
```

```python
from contextlib import ExitStack

import numpy as np
import concourse.bass as bass
import concourse.mybir as mybir
from concourse.bass_utils import run_bass_kernel_spmd

F32 = mybir.dt.float32
BF16 = mybir.dt.bfloat16
AF = mybir.ActivationFunctionType
ALU = mybir.AluOpType

D = 1024
NL = 2
NH = 8
NKV = 2
HD = 64
DFF = 2816
NJ = DFF // 128
T = 512
NB = T // 128
EPS = 1e-6
NBUF = 7
SLAB = 4096

SLABS = ([("A0", 2048), ("B", 2048), ("A1", 2048), ("C", 4096), ("D", 4096), ("E", 4096), ("G", 4096),
          ("OAB0", 4096), ("F", 4096), ("H", 4096), ("OAB1", 4096), ("OUT0", 4096), ("OUT1", 4096)]
         + [("UP%d" % i, 4096) for i in range(11)] + [("DN%d" % c, 2816) for c in range(8)])
SLAB_OFF = {}
_o = 0
for _n, _s in SLABS:
    SLAB_OFF[_n] = (_o, _s)
    _o += _s
WPL = _o

C_VEC = 0
C_QK = 384
C_SGUG = 392
C_BSB = C_SGUG + 1024
C_ABIAS = C_BSB + 1024
C_TRIL = C_ABIAS + 2048
NCST = C_TRIL + 128


class Buf:
    __slots__ = ("name", "lw", "rd", "excl")

    def __init__(self, name, excl=False):
        self.name = name
        self.lw = None
        self.rd = []
        self.excl = excl


class Sched:
    ENG = ("pe", "act", "dve", "pool", "sp")

    def __init__(self):
        self.streams = {e: [] for e in self.ENG}
        self.cnt = {}
        self.waited = {e: {} for e in self.ENG}
        self.semnames = []
        self.snap = {}

    def newsem(self, key):
        self.cnt[key] = 0
        self.semnames.append(key)

    def _waits(self, eng, reads, writes):
        deps = {}
        def add(ev, raw):
            if ev is None:
                return
            k, v = ev
            if k == eng and (eng == "pe" or not raw):
                return
            if deps.get(k, 0) < v:
                deps[k] = v
        for b in reads:
            add(b.lw, True)
            if b.excl:
                for ev in b.rd:
                    add(ev, False)
        for b in writes:
            add(b.lw, False)
            for ev in b.rd:
                add(ev, False)
        wd = self.waited[eng]
        for k, v in sorted(deps.items(), key=lambda kv: -kv[1]):
            if wd.get(k, 0) >= v:
                continue
            wd[k] = v
            self.streams[eng].append(("wait", k, v))
            for k2, v2 in self.snap.get((k, v), {}).items():
                if wd.get(k2, 0) < v2:
                    wd[k2] = v2

    def _commit(self, ev, reads, writes, eng=None):
        if eng is not None:
            self.snap[ev] = dict(self.waited[eng])
        for b in reads:
            b.rd.append(ev)
        for b in writes:
            b.lw = ev
            b.rd = []

    def op(self, eng, fn, reads=(), writes=()):
        self._waits(eng, reads, writes)
        self.cnt[eng] += 1
        self.streams[eng].append(("op", fn, eng, 1))
        self._commit((eng, self.cnt[eng]), reads, writes, eng)

    def dma(self, eng, sem, fn, reads=(), writes=()):
        self._waits(eng, reads, writes)
        self.cnt[sem] += 16
        self.streams[eng].append(("op", fn, sem, 16))
        self._commit((sem, self.cnt[sem]), reads, writes, eng)

    def wait_all(self, eng, bufs):
        self._waits(eng, bufs, bufs)

    def replay(self, eng, handle, sems):
        pend = None
        for it in self.streams[eng]:
            if it[0] == "wait":
                if pend is not None:
                    handle.wait_ge(sems[pend[1]], pend[2])
                pend = it
            else:
                res = it[1](handle)
                first, last = res if isinstance(res, tuple) else (res, res)
                if pend is not None:
                    first._wait_ge(sems[pend[1]], pend[2])
                    pend = None
                last.then_inc(sems[it[2]], it[3])
        if pend is not None:
            handle.wait_ge(sems[pend[1]], pend[2])


class Rot:
    def __init__(self, tiles, name):
        self.tiles = tiles
        self.bufs = [Buf("%s%d" % (name, i)) for i in range(len(tiles))]
        self.i = 0

    def next(self):
        i = self.i
        self.i = (i + 1) % len(self.tiles)
        return self.tiles[i], self.bufs[i]


def build(n_seq, seq_len, layers=(0, 1)):
    nc = bass.Bass("TRN2", target_bir_lowering=False)
    S = Sched()
    tiles_per_seq = seq_len // T
    n_tiles = n_seq * tiles_per_seq
    xT_d = nc.dram_tensor("xT", [n_seq, D, seq_len], F32, kind="ExternalInput").ap()
    wts_d = nc.dram_tensor("wts", [NL, 128, WPL], F32, kind="ExternalInput").ap()
    cst_d = nc.dram_tensor("cst", [128, NCST], F32, kind="ExternalInput").ap()
    srow_d = nc.dram_tensor("srow", [1, 2048], F32, kind="ExternalInput").ap()
    wst_d = nc.dram_tensor("wst", [128, 2048], F32, kind="ExternalInput").ap()
    yT_d = nc.dram_tensor("yT", [n_seq, D, seq_len], F32, kind="ExternalOutput").ap()
    wbf_d = nc.dram_tensor("wbf", [NL, 128, WPL], BF16).ap()

    es = ExitStack()
    with es:
        def sb(name, shape, dt):
            return es.enter_context(nc.sbuf_tensor("s_" + name, shape, dt))

        cst = sb("cst", [128, NCST], F32)
        srow = sb("srow", [128, 2048], BF16)
        wsT = sb("wsT", [128, 16, 128], BF16)
        ones = sb("ones", [128, 128], BF16)
        xres = [sb("xres%d" % i, [128, 8, T], F32) for i in range(2)]
        hT = sb("hT", [128, 8, T], BF16)
        sqt = sb("sqt", [128, 4, T], BF16)
        NFR = 10
        frt = sb("frt", [128, NFR, 512], F32)
        qT = sb("qT", [64, 8, T], BF16)
        kT = [sb("kT%d" % l, [64, 2, T + 128], BF16) for l in range(NL)]
        vtok = [sb("vtok%d" % l, [128, NB + 1, 128], BF16) for l in range(NL)]
        qsq = sb("qsq", [64, 5, T], BF16)
        junk = sb("junk", [128, 512], BF16)
        ssv = sb("ssv", [128, 8], F32)
        lnv = sb("lnv", [128, 8], F32)
        rv = sb("rv", [128, 8], F32)
        vn = sb("vn", [128, NB, 512], BF16)
        pT = sb("pT", [128, 6, 512], BF16)
        rden = sb("rden", [128, 4, 256], F32)
        actT = sb("actT", [128, NJ, T], BF16)
        mergedT = actT[:, 0:8, :]
        yattT = actT[:, 8:12, :]
        ysguT = actT[:, 12:16, :]
        uT = actT[:, 16:20, :]
        halo = [sb("halo%d" % l, [128, 2 * NJ, 2], F32) for l in range(NL)]
        hc = sb("hc", [128, 2 * NJ, 2], F32)
        hctmp = sb("hctmp", [128, 2 * NJ], F32)
        b_hc = Buf("hc")
        b_hctmp = Buf("hctmp")
        wslab = sb("wslab", [128, NBUF, SLAB], BF16)
        psb = [es.enter_context(nc.psum_tensor("ps%d" % i, [128, 512], F32)) for i in range(8)]

        b_cst = Buf("cst")
        b_srow = Buf("srow")
        b_wsT = Buf("wsT")
        b_ones = Buf("ones")
        b_xres = [[Buf("xres%d_%d" % (i, c)) for c in range(8)] for i in range(2)]
        b_hT = [Buf("hT%d" % c) for c in range(8)]
        b_qT = [Buf("qT%d" % h) for h in range(8)]
        b_kprev = [Buf("kprev%d" % l) for l in range(NL)]
        b_kcur = [[Buf("kcur%d_%d" % (l, g)) for g in range(2)] for l in range(NL)]
        b_vprev = [Buf("vprev%d" % l) for l in range(NL)]
        b_vcur = [Buf("vcur%d" % l) for l in range(NL)]
        b_junk = Buf("junk")
        b_ssv = [Buf("ssv%d" % i) for i in range(8)]
        b_lnv = [Buf("lnv%d" % i) for i in range(8)]
        b_rv = [Buf("rv%d" % i) for i in range(8)]
        b_vn = [Buf("vn%d" % b) for b in range(NB)]
        b_actT = [Buf("actT%d" % j) for j in range(NJ)]
        b_merged = b_actT[0:8]
        b_yatt = [[b_actT[8 + c]] * NB for c in range(4)]
        b_uT = b_actT[16:20]
        b_halo = [[Buf("halo%d_%d" % (l, ch)) for ch in range(2 * NJ)] for l in range(NL)]
        b_wslab = [Buf("wslab%d" % i) for i in range(NBUF)]
        b_ps = [Buf("ps%d" % i, excl=True) for i in range(8)]

        sqr = Rot([sqt[:, i, :] for i in range(4)], "sq")
        qsqr = Rot([qsq[:, i, :] for i in range(5)], "qsq")
        fr = Rot([frt[:, i, :] for i in range(NFR)], "fr")
        rqr = rq2r = gvr = efr = tmr = sar = sbr = t1r = t2r = agr = avr = sgr = fr
        srowf = frt[0:1, 0:4, :]
        wstage = frt[:, 4:8, :]
        pTr = Rot([pT[:, i, :] for i in range(6)], "pT")
        rdr = Rot([rden[:, i, :] for i in range(4)], "rden")
        psr = Rot([p[:, :] for p in psb[0:7]], "psr")
        psr.bufs = b_ps[0:7]
        ss_ps, b_ss = psb[7][:, :], b_ps[7]
        small_i = [0]

        for e in ("pe", "act", "dve", "pool"):
            S.newsem(e)
        for i in range(NBUF):
            S.newsem("w%d" % i)
            S.newsem("ws%d" % i)
        b_wd = {(l, nm): Buf("wd%d_%s" % (l, nm)) for l in range(NL) for (nm, _) in SLABS}
        for k in ("cst0", "cst1", "cst2", "xl0", "xl1", "xs0", "xs1"):
            S.newsem(k)

        def act(out, in_, func, reads, writes, bias=None, scale=None, accum_out=None):
            kw = {}
            if bias is not None:
                kw["bias"] = bias
            if scale is not None:
                kw["scale"] = scale
            if accum_out is not None:
                kw["accum_out"] = accum_out
            S.op("act", lambda e: e.activation(out=out, in_=in_, func=func, **kw), reads, writes)

        def tt(out, in0, in1, op, reads, writes, eng="dve"):
            S.op(eng, lambda e: e.tensor_tensor(out=out, in0=in0, in1=in1, op=op), reads, writes)

        def stt(out, in0, scalar, in1, op0, op1, reads, writes, eng="dve"):
            S.op(eng, lambda e: e.scalar_tensor_tensor(out=out, in0=in0, scalar=scalar, in1=in1,
                                                        op0=op0, op1=op1), reads, writes)

        def cp(out, in_, reads, writes, eng="dve"):
            S.op(eng, lambda e: e.tensor_copy(out=out, in_=in_), reads, writes)

        def mm_group(out, pairs, reads, writes):
            def fn(e):
                ins = first = None
                n = len(pairs)
                for i, (l, r) in enumerate(pairs):
                    ins = e.matmul(out, l, r, start=(i == 0), stop=(i == n - 1))
                    first = first or ins
                return first, ins
            S.op("pe", fn, reads, writes)

        def mm_part(out, pairs, reads, writes, first, last):
            def fn(e):
                ins = fi = None
                n = len(pairs)
                for i, (l, r) in enumerate(pairs):
                    ins = e.matmul(out, l, r, start=(first and i == 0), stop=(last and i == n - 1))
                    fi = fi or ins
                return fi, ins
            S.op("pe", fn, reads, writes)

        def mm_multi(groups, reads, writes):
            def fn(e):
                ins = first = None
                for out, pairs in groups:
                    n = len(pairs)
                    for i, (l, r) in enumerate(pairs):
                        ins = e.matmul(out, l, r, start=(i == 0), stop=(i == n - 1))
                        first = first or ins
                return first, ins
            S.op("pe", fn, reads, writes)

        passes = [(ti, l) for ti in range(n_tiles) for l in layers]
        wseq = [(l, nm) for (_, l) in passes for (nm, _) in SLABS]
        wstate = {"issue": 0, "acq": 0}

        def w_issue():
            i = wstate["issue"]
            if i >= len(wseq):
                return
            wstate["issue"] = i + 1
            l, nm = wseq[i]
            off, n = SLAB_OFF[nm]
            slot = i % NBUF
            o = wslab[:, slot, 0:n]
            if passes[i // len(SLABS)][0] == 0:
                src = wts_d[l, :, off:off + n]
                S.dma("pool", "w%d" % slot, lambda e: e.dma_start(out=o, in_=src), (), (b_wslab[slot],))
            else:
                src = wbf_d[l, :, off:off + n]
                S.dma("sp", "w%d" % slot, lambda e: e.dma_start(out=o, in_=src), (b_wd[(l, nm)],),
                      (b_wslab[slot],))

        def w_acquire(expect):
            i = wstate["acq"]
            wstate["acq"] = i + 1
            assert wseq[i][1] == expect, (wseq[i], expect)
            slot = i % NBUF
            if passes[i // len(SLABS)][0] == 0 and n_tiles > 1:
                l, nm = wseq[i]
                off, n = SLAB_OFF[nm]
                dst = wbf_d[l, :, off:off + n]
                srcs = wslab[:, slot, 0:n]
                S.dma("sp", "ws%d" % slot, lambda e: e.dma_start(out=dst, in_=srcs), (b_wslab[slot],),
                      (b_wd[(l, nm)],))
            return wslab[:, slot, :], b_wslab[slot]

        def w_release(n=1):
            for _ in range(n):
                w_issue()

        S.dma("sp", "cst0", lambda e: e.dma_start(out=cst[:, :], in_=cst_d[:, :]), (), (b_cst,))
        S.dma("sp", "cst1", lambda e: e.dma_start(out=srowf, in_=srow_d.rearrange("o (a n) -> o a n", a=4)),
              (), tuple(fr.bufs[0:4]))
        S.dma("sp", "cst2", lambda e: e.dma_start(out=wstage, in_=wst_d.rearrange("p (a n) -> p a n", a=4)),
              (), tuple(fr.bufs[4:8]))
        for _ in range(NBUF):
            w_issue()
        S.op("dve", lambda e: e.memset(ones[:, :], 1.0), (), (b_ones,))
        S.op("dve", lambda e: e.memset(srow[:, :], 0.0), (), (b_srow,))
        act(srow[0:1, :].rearrange("o (a n) -> o a n", a=4), srowf, AF.Exp, tuple(fr.bufs[0:4]), (b_srow,))
        act(cst[:, C_ABIAS:C_ABIAS + 2048], cst[:, C_ABIAS:C_ABIAS + 2048], AF.Exp, (b_cst,), (b_cst,))
        for i in range(16):
            tt(wsT[:, i, :], wstage[:, i // 4, (i % 4) * 128:(i % 4 + 1) * 128], cst[:, C_TRIL:C_TRIL + 128], ALU.mult,
               (fr.bufs[4 + i // 4], b_cst), (b_wsT,))

        def rms_sq_act(xr, bxr, c):
            sq, bsq = sqr.next()
            act(sq, xr[:, c, :], AF.Square, (bxr[c],), (bsq,))
            return sq, bsq

        def rms_sq_mm(sqb, c):
            sq, bsq = sqb
            S.op("pe", (lambda e: e.matmul(ss_ps, ones[:, :], sq, start=(c == 0), stop=(c == 7))),
                 (bsq, b_ones), (b_ss,))

        def rms_finish(xr, bxr, gcol):
            ps, bps = ss_ps, b_ss
            rtmp, b_rtmp = fr.next()
            act(rtmp, ps, AF.Ln, (bps,), (b_rtmp,), bias=EPS, scale=1.0 / D)
            rstd, b_rstd = fr.next()
            act(rstd, rtmp, AF.Exp, (b_rtmp,), (b_rstd,), scale=-0.5)
            for c in range(8):
                stt(hT[:, c, :], xr[:, c, :], cst[:, gcol + c:gcol + c + 1], rstd, ALU.mult, ALU.mult,
                    (bxr[c], b_rstd, b_cst), (b_hT[c],))

        def headnorm_a(ps, bps, gcolumn, out, bout):
            sq, bsq = qsqr.next()
            act(sq[0:64, :], ps[0:64, :], AF.Square, (bps,), (bsq,))
            return lambda: headnorm_b(ps, bps, gcolumn, out, bout, sq, bsq)

        def headnorm_b(ps, bps, gcolumn, out, bout, sq, bsq):
            ps2, bps2 = psr.next()
            S.op("pe", lambda e: e.matmul(ps2[0:64, :], ones[0:64, 0:64], sq[0:64, :], start=True, stop=True),
                 (bsq, b_ones), (bps2,))
            r1, br1 = rqr.next()
            act(r1[0:64, :], ps2[0:64, :], AF.Ln, (bps2,), (br1,), bias=EPS, scale=1.0 / HD)
            r2, br2 = rq2r.next()
            act(r2[0:64, :], r1[0:64, :], AF.Exp, (br1,), (br2,), scale=-0.5)
            stt(out, ps[0:64, :], cst[0:64, gcolumn:gcolumn + 1], r2[0:64, :], ALU.mult, ALU.mult,
                (bps, br2, b_cst), (bout,))

        def emit_xload(ti):
            s_idx = ti // tiles_per_seq
            t0 = (ti % tiles_per_seq) * T
            xi = ti % 2
            src = xT_d[s_idx].rearrange("(c p) t -> p c t", p=128)[:, :, t0:t0 + T]
            dstt = xres[xi][:, :, :]
            S.dma("sp", "xl%d" % xi, lambda e: e.dma_start(out=dstt, in_=src), (), tuple(b_xres[xi]))

        def kouter(outs, lhs_fn, reads_w, wr_bufs):
            for k in range(8):
                groups = [(o, lhs_fn(i, k), hT[:, k, :]) for i, o in enumerate(outs)]
                def fn(e, groups=groups, k=k):
                    ins = first = None
                    for (o, l_, r_) in groups:
                        ins = e.matmul(o, l_, r_, start=(k == 0), stop=(k == 7))
                        first = first or ins
                    return first, ins
                S.op("pe", fn, (*reads_w, b_hT[k]), tuple(wr_bufs))

        def run_pass(ti, l, first_layer, last_layer, nxt):
            _CUR[0], _CUR[1] = ti, l
            s_idx = ti // tiles_per_seq
            tt_i = ti % tiles_per_seq
            t0 = tt_i * T
            first_in_seq = tt_i == 0
            last_in_seq = tt_i == tiles_per_seq - 1
            xi = ti % 2
            xr = xres[xi]
            bxr = b_xres[xi]
            vbase = C_VEC + 192 * l
            G1, G2 = vbase, vbase + 8
            CW0, CW1, CW2, CB = vbase + 16, vbase + 60, vbase + 104, vbase + 148
            QG, KG = C_QK + 2 * l, C_QK + 2 * l + 1

            _ck(0)
            rms_finish(xr, bxr, G1)
            _ck(1)

            pend_hn = []

            def flush_hn():
                while pend_hn:
                    pend_hn.pop(0)()

            def proj_head(vW, bW, cols, gcolumn, out, bout):
                ps, bps = psr.next()
                mm_group(ps[0:64, :], [(vW[:, k, cols], hT[:, k, :]) for k in range(8)], (bW, *b_hT), (bps,))
                part_b = headnorm_a(ps, bps, gcolumn, out, bout)
                flush_hn()
                pend_hn.append(part_b)

            wA0, bA0 = w_acquire("A0")
            vA0 = wA0[:, 0:2048].rearrange("p (k n) -> p k n", k=8)
            qps = [psr.next() for _ in range(4)]
            kouter([p[0][0:64, :] for p in qps], lambda i, k: vA0[:, k, i * 64:(i + 1) * 64], (bA0,),
                   [p[1] for p in qps])
            for h in range(4):
                pend_hn.append(headnorm_a(qps[h][0], qps[h][1], QG, qT[:, h, :], b_qT[h]))
            w_release()
            _ck(2)

            wB, bB = w_acquire("B")
            vB = wB[:, 0:2048].rearrange("p (k n) -> p k n", k=8)
            for g in range(2):
                proj_head(vB, bB, slice(g * 64, (g + 1) * 64), KG, kT[l][:, g, 128:128 + T], b_kcur[l][g])
            ps, bps = psr.next()
            mm_multi([(ps[:, b * 128:(b + 1) * 128],
                       [(hT[:, k, b * 128:(b + 1) * 128], vB[:, k, 128:256]) for k in range(8)])
                      for b in range(NB)], (bB, *b_hT), (bps,))
            flush_hn()
            cp(vtok[l][:, 1:NB + 1, :], ps.rearrange("p (b n) -> p b n", b=NB), (bps,), (b_vcur[l],))
            w_release()

            def sgu_block(b):
                ps, bps = psr.next()
                groups = []
                for gp in range(4):
                    for sl in range(2):
                        g = 2 * gp + sl
                        groups.append((ps[64 * sl:64 * sl + 64, gp * 128:(gp + 1) * 128],
                                       [(vn[:, b, g * 64:(g + 1) * 64], wsT[:, l * 8 + g, :])]))
                mm_multi(groups, (b_vn[b], b_wsT), (bps,))
                tm, btm = tmr.next()
                tt(tm, ps, cst[:, C_BSB + 512 * l:C_BSB + 512 * l + 512], ALU.add, (bps, b_cst), (btm,))
                tt(ysguT[:, :, b * 128:(b + 1) * 128], tm.rearrange("p (a n) -> p a n", a=4),
                   uT[:, :, b * 128:(b + 1) * 128], ALU.mult, (btm, *b_uT), tuple(b_actT[12:16]))

            def att_stage1(b, g):
                halves = []
                if not (first_in_seq and b == 0):
                    halves.append(0)
                halves.append(1)
                pts = {}
                for hf in halves:
                    ps, bps = psr.next()
                    kcols = slice(128 * (b + hf), 128 * (b + hf) + 128)
                    kb = [b_kcur[l][g]] + ([b_kprev[l]] if (b == 0 and hf == 0) else [])
                    lhs_ = kT[l][:, g, kcols]
                    rhs_ = qT[:, 4 * g:4 * g + 4, b * 128:(b + 1) * 128]
                    out_ = ps.rearrange("p (a n) -> p a n", a=4)
                    S.op("pe", (lambda e, out_=out_, lhs_=lhs_, rhs_=rhs_: e.matmul(
                        out_, lhs_, rhs_, start=True, stop=True)),
                        (*kb, *b_qT[4 * g:4 * g + 4]), (bps,))
                    e_, be_ = efr.next()
                    act(e_, ps, AF.Exp, (bps,), (be_,), scale=0.125)
                    p_, bp_ = pTr.next()
                    col = C_ABIAS + (g * 2 + hf) * 512
                    tt(p_, e_, cst[:, col:col + 512], ALU.mult, (be_, b_cst), (bp_,), eng="pool")
                    pts[hf] = (p_, bp_)
                return halves, pts

            def att_stage2(b, g, halves, pts):
                yd, byd = psr.next()
                groups = []
                sbase = ((l * 2 + g) * 2) * 256
                for sl in range(2):
                    ypairs, dpairs = [], []
                    for hf in halves:
                        p_ = pts[hf][0]
                        rhs = p_.rearrange("p (pr s n) -> p pr s n", pr=2, s=2)[:, :, sl, :]
                        ypairs.append((vtok[l][:, b + hf, g * 64:(g + 1) * 64], rhs))
                        dpairs.append((ones[:, 0:64], rhs))
                    dpairs.append((ones[:, 0:64],
                                   srow[:, sbase + sl * 256:sbase + sl * 256 + 256].rearrange("p (a n) -> p a n", a=2)))
                    groups.append((yd[64 * sl:64 * sl + 64, 0:256].rearrange("p (a n) -> p a n", a=2), ypairs))
                    groups.append((yd[64 * sl:64 * sl + 64, 256:512].rearrange("p (a n) -> p a n", a=2), dpairs))
                vb = [b_vcur[l]] + ([b_vprev[l]] if b == 0 and 0 in halves else [])
                mm_multi(groups, (*[pts[hf][1] for hf in halves], *vb, b_ones, b_srow), (byd,))
                rl, brl = rdr.next()
                act(rl, yd[:, 256:512], AF.Ln, (byd,), (brl,))
                rd, brd = rdr.next()
                act(rd, rl, AF.Exp, (brl,), (brd,), scale=-1.0)
                tt(yattT[:, 2 * g:2 * g + 2, b * 128:(b + 1) * 128],
                   yd[:, 0:256].rearrange("p (a n) -> p a n", a=2),
                   rd.rearrange("p (a n) -> p a n", a=2), ALU.mult, (byd, brd),
                   (b_yatt[2 * g][b], b_yatt[2 * g + 1][b]))

            _ck(4)
            slabs = {}

            def get_slab(nm):
                if nm not in slabs:
                    w_, b_ = w_acquire(nm)
                    n_ = 2048 if nm == "A1" else 4096
                    slabs[nm] = (w_[:, 0:n_].rearrange("p (k n) -> p k n", k=8), b_)
                return slabs[nm]

            def u_qhead(h):
                vA1, bA1 = get_slab("A1")
                proj_head(vA1, bA1, slice((h - 4) * 64, (h - 3) * 64), QG, qT[:, h, :], b_qT[h])
                if h == 7:
                    w_release()

            def u_su(c):
                vC, bC = get_slab("C")
                flush_hn()
                ps, bps = psr.next()
                mm_group(ps, [(vC[:, k, c * 128:(c + 1) * 128], hT[:, k, :]) for k in range(8)],
                         (bC, *b_hT), (bps,))
                act(uT[:, c, :], ps, AF.Gelu_apprx_tanh, (bps,), (b_uT[c],))
                if c == 3:
                    w_release()

            def u_sv(b):
                vD, bD = get_slab("D")
                ps, bps = psr.next()
                mm_group(ps, [(hT[:, k, b * 128:(b + 1) * 128], vD[:, k, :]) for k in range(8)],
                         (bD, *b_hT), (bps,))
                g_, bg_ = gvr.next()
                act(g_, ps, AF.Gelu_apprx_tanh, (bps,), (bg_,))
                si = small_i[0]
                small_i[0] = (si + 1) % 8
                act(junk[:, :], g_, AF.Square, (bg_,), (b_junk, b_ssv[si]), accum_out=ssv[:, si:si + 1])
                act(lnv[:, si:si + 1], ssv[:, si:si + 1], AF.Ln, (b_ssv[si],), (b_lnv[si],), bias=EPS, scale=1.0 / 512)
                act(rv[:, si:si + 1], lnv[:, si:si + 1], AF.Exp, (b_lnv[si],), (b_rv[si],), scale=-0.5)
                stt(vn[:, b, :], g_, rv[:, si:si + 1], cst[:, C_SGUG + 512 * l:C_SGUG + 512 * l + 512],
                    ALU.mult, ALU.mult, (bg_, b_rv[si], b_cst), (b_vn[b],))
                if b == NB - 1:
                    w_release()

            units = ([lambda h=h: u_qhead(h) for h in range(4, 8)] + [lambda c=c: u_su(c) for c in range(4)]
                     + [lambda b=b: u_sv(b) for b in range(NB)])
            its = [(b, 0) for b in range(NB)] + [(b, 1) for b in range(NB)]
            pend = [att_stage1(*its[0]), att_stage1(*its[1])]
            for i, (b, g) in enumerate(its):
                if g == 0:
                    for _ in range(3):
                        units.pop(0)()
                else:
                    sgu_block(b)
                if i + 2 < len(its):
                    pend.append(att_stage1(*its[i + 2]))
                att_stage2(b, g, *pend.pop(0))
            assert not units
            _ck(3)
            if not last_in_seq:
                cp(kT[l][:, :, 0:128], kT[l][:, :, T:T + 128], (*b_kcur[l],), (b_kprev[l],))
                cp(vtok[l][:, 0, :], vtok[l][:, NB, :], (b_vcur[l],), (b_vprev[l],))

            _ck(5)
            for half in range(2):
                wE, bE = w_acquire("EF"[half])
                wG, bG = w_acquire("GH"[half])
                wO, bO = w_acquire("OAB%d" % half)
                vE = wE.rearrange("p (k n) -> p k n", k=8)
                vG = wG.rearrange("p (k n) -> p k n", k=8)
                vO = wO.rearrange("p (m k n) -> p m k n", m=2, k=4)
                for c4 in range(4):
                    c = 4 * half + c4
                    cs = slice(c4 * 128, (c4 + 1) * 128)
                    pga, bpga = psr.next()
                    mm_group(pga, [(vE[:, k, cs], hT[:, k, :]) for k in range(8)], (bE, *b_hT), (bpga,))
                    pgb, bpgb = psr.next()
                    mm_group(pgb, [(vG[:, k, cs], hT[:, k, :]) for k in range(8)], (bG, *b_hT), (bpgb,))
                    pa, bpa = psr.next()
                    mm_group(pa, [(vO[:, 0, kc, cs], yattT[:, kc, :]) for kc in range(4)],
                             (bO, *b_actT[8:12]), (bpa,))
                    pb, bpb = psr.next()
                    mm_group(pb, [(vO[:, 1, kc, cs], ysguT[:, kc, :]) for kc in range(4)], (bO, *b_actT[12:16]), (bpb,))
                    sa, bsa = sar.next()
                    act(sa, pga, AF.Sigmoid, (bpga,), (bsa,))
                    sb_, bsb_ = sbr.next()
                    act(sb_, pgb, AF.Sigmoid, (bpgb,), (bsb_,))
                    t1, bt1 = t1r.next()
                    tt(t1, pa, sa, ALU.mult, (bpa, bsa), (bt1,))
                    t2, bt2 = t2r.next()
                    tt(t2, pb, sb_, ALU.mult, (bpb, bsb_), (bt2,))
                    tt(mergedT[:, c, :], t1, t2, ALU.add, (bt1, bt2), (b_merged[c],), eng="pool")
                w_release(3)
            sqbs = {}
            for half in range(2):
                wO, bO = w_acquire("OUT%d" % half)
                vO = wO.rearrange("p (k n) -> p k n", k=8)
                for c4 in range(4):
                    c = 4 * half + c4
                    po, bpo = psr.next()
                    mm_group(po, [(vO[:, k, c4 * 128:(c4 + 1) * 128], mergedT[:, k, :]) for k in range(8)],
                             (bO, *b_merged), (bpo,))
                    if c >= 1:
                        rms_sq_mm(sqbs[c - 1], c - 1)
                    tt(xr[:, c, :], po, xr[:, c, :], ALU.add, (bpo, bxr[c]), (bxr[c],))
                    sqbs[c] = rms_sq_act(xr, bxr, c)
                w_release()
            rms_sq_mm(sqbs[7], 7)

            _ck(6)
            rms_finish(xr, bxr, G2)
            if last_layer and nxt is not None:
                emit_xload(nxt[0])

            def ffn_epilogue(j, pg, bpg, pv, bpv):
                ag, bag = agr.next()
                av, bav = avr.next()
                items = ((pg, bpg, ag, bag, j), (pv, bpv, av, bav, NJ + j))
                for (ps, bps, a_, ba_, ch) in items:
                    act(a_, ps, AF.Identity, (bps, b_cst), (ba_,),
                        bias=cst[:, CB + ch:CB + ch + 1], scale=cst[:, CW2 + ch:CW2 + ch + 1])
                for (ps, bps, a_, ba_, ch) in items:
                    stt(a_[:, 1:T], ps[:, 0:T - 1], cst[:, CW1 + ch:CW1 + ch + 1], a_[:, 1:T], ALU.mult, ALU.add,
                        (bps, ba_, b_cst), (ba_,))
                for (ps, bps, a_, ba_, ch) in items:
                    stt(a_[:, 2:T], ps[:, 0:T - 2], cst[:, CW0 + ch:CW0 + ch + 1], a_[:, 2:T], ALU.mult, ALU.add,
                        (bps, ba_, b_cst), (ba_,))
                if not first_in_seq:
                    for (ps, bps, a_, ba_, ch) in items:
                        tt(a_[:, 0:2], a_[:, 0:2], hc[:, ch, :], ALU.add, (ba_, b_hc), (ba_,), eng="pool")
                if not last_in_seq:
                    for (ps, bps, a_, ba_, ch) in items:
                        act(halo[l][:, ch, :], ps[:, T - 2:T], AF.Identity, (bps,), (b_halo[l][ch],))
                sg, bsg = sgr.next()
                act(sg, ag, AF.Silu, (bag,), (bsg,))
                tt(actT[:, j, :], sg, av, ALU.mult, (bsg, bav), (b_actT[j],), eng="pool")

            if not first_in_seq:
                bh = tuple(b_halo[l])
                tt(hc[:, :, 1], halo[l][:, :, 1], cst[:, CW0:CW0 + 44], ALU.mult, (*bh, b_cst), (b_hc,))
                tt(hc[:, :, 0], halo[l][:, :, 0], cst[:, CW0:CW0 + 44], ALU.mult, (*bh, b_cst), (b_hc,))
                tt(hctmp[:, :], halo[l][:, :, 1], cst[:, CW1:CW1 + 44], ALU.mult, (*bh, b_cst), (b_hctmp,))
                tt(hc[:, :, 0], hc[:, :, 0], hctmp[:, :], ALU.add, (b_hc, b_hctmp), (b_hc,))
            for i in range(11):
                wU, bU = w_acquire("UP%d" % i)
                vU = wU.rearrange("p (k n) -> p k n", k=8)
                if i == 0:
                    pss = [psr.next() for _ in range(4)]
                    offs = [0, 256, 128, 384]
                    kouter([p[0] for p in pss], lambda q, k: vU[:, k, offs[q]:offs[q] + 128], (bU,),
                           [p[1] for p in pss])
                    ffn_epilogue(0, pss[0][0], pss[0][1], pss[1][0], pss[1][1])
                    ffn_epilogue(1, pss[2][0], pss[2][1], pss[3][0], pss[3][1])
                else:
                    for jj in range(2):
                        j = 2 * i + jj
                        pg, bpg = psr.next()
                        mm_group(pg, [(vU[:, k, jj * 128:(jj + 1) * 128], hT[:, k, :]) for k in range(8)],
                                 (bU, *b_hT), (bpg,))
                        pv, bpv = psr.next()
                        mm_group(pv, [(vU[:, k, 256 + jj * 128:256 + (jj + 1) * 128], hT[:, k, :]) for k in range(8)],
                                 (bU, *b_hT), (bpv,))
                        ffn_epilogue(j, pg, bpg, pv, bpv)
                w_release()
            _ck(7)
            if nxt is not None:
                nxr, nbxr = xres[nxt[0] % 2], b_xres[nxt[0] % 2]
            sqbs = {}

            def down_tail(c, pd, bpd):
                if nxt is not None and c >= 1:
                    rms_sq_mm(sqbs[c - 1], c - 1)
                tt(xr[:, c, :], pd, xr[:, c, :], ALU.add, (bpd, bxr[c]), (bxr[c],))
                if nxt is not None:
                    sqbs[c] = rms_sq_act(nxr, nbxr, c)

            JS = 14
            first = []
            for c in range(4):
                wDn, bDn = w_acquire("DN%d" % c)
                vDn = wDn[:, 0:2816].rearrange("p (k n) -> p k n", k=NJ)
                pd, bpd = psr.next()
                mm_part(pd, [(vDn[:, j, :], actT[:, j, :]) for j in range(JS)], (bDn, *b_actT[0:JS]), (bpd,), True, False)
                first.append((vDn, bDn, pd, bpd))
            for c in range(4):
                vDn, bDn, pd, bpd = first[c]
                mm_part(pd, [(vDn[:, j, :], actT[:, j, :]) for j in range(JS, NJ)], (bDn, *b_actT[JS:NJ]), (bpd,), False, True)
                down_tail(c, pd, bpd)
                w_release()
            for c in range(4, 8):
                wDn, bDn = w_acquire("DN%d" % c)
                vDn = wDn[:, 0:2816].rearrange("p (k n) -> p k n", k=NJ)
                pd, bpd = psr.next()
                mm_group(pd, [(vDn[:, j, :], actT[:, j, :]) for j in range(NJ)], (bDn, *b_actT), (bpd,))
                down_tail(c, pd, bpd)
                w_release()
            if nxt is not None:
                rms_sq_mm(sqbs[7], 7)

            if last_layer:
                dst = yT_d[s_idx].rearrange("(c p) t -> p c t", p=128)[:, :, t0:t0 + T]
                S.dma("sp", "xs%d" % xi, lambda e: e.dma_start(out=dst, in_=xr[:, :, :]), tuple(bxr), ())

        plist = [(ti, li) for ti in range(n_tiles) for li in range(len(layers))]
        emit_xload(0)
        sq0 = [rms_sq_act(xres[0], b_xres[0], c) for c in range(4)]
        for c in range(8):
            rms_sq_mm(sq0[c] if c < 4 else rms_sq_act(xres[0], b_xres[0], c), c)
        try:
            for pi, (ti, li) in enumerate(plist):
                nxt = plist[pi + 1] if pi + 1 < len(plist) else None
                run_pass(ti, layers[li], li == 0, li == len(layers) - 1, nxt)
        except _Stop:
            dst = yT_d[0].rearrange("(c p) t -> p c t", p=128)[:, :, 0:T]
            S.dma("sp", "xs0", lambda e: e.dma_start(out=dst, in_=xres[0][:, :, :]), tuple(b_xres[0]), ())
        S.wait_all("sp", [b for i in range(2) for b in b_xres[i]])

        sems = {k: es.enter_context(nc.semaphore(k)) for k in S.semnames}
        block = es.enter_context(nc.Block())

        @block.tensor
        def _(e):
            S.replay("pe", e, sems)

        @block.scalar
        def _(e):
            S.replay("act", e, sems)

        @block.vector
        def _(e):
            S.replay("dve", e, sems)

        @block.gpsimd
        def _(e):
            S.replay("pool", e, sems)

        @block.sync
        def _(e):
            S.replay("sp", e, sems)
    return nc


def _pkn(w):
    kc = w.shape[0] // 128
    return np.ascontiguousarray(w.reshape(kc, 128, -1).transpose(1, 0, 2).reshape(128, -1))


def pack_weights(w_in, w_oa, w_ob, w_out, w_up, w_down):
    out = np.empty((NL, 128, WPL), np.float32)
    for l in range(NL):
        parts = {
            "A0": _pkn(w_in[l][:, 0:256]), "A1": _pkn(w_in[l][:, 256:512]), "B": _pkn(w_in[l][:, 512:768]),
            "C": _pkn(w_in[l][:, 768:1280]), "D": _pkn(w_in[l][:, 1280:1792]),
            "E": _pkn(w_in[l][:, 1792:2304]), "F": _pkn(w_in[l][:, 2304:2816]),
            "G": _pkn(w_in[l][:, 2816:3328]), "H": _pkn(w_in[l][:, 3328:3840]),
            "OAB0": np.concatenate([_pkn(w_oa[l][:, 0:512]), _pkn(w_ob[l][:, 0:512])], axis=1),
            "OAB1": np.concatenate([_pkn(w_oa[l][:, 512:1024]), _pkn(w_ob[l][:, 512:1024])], axis=1),
            "OUT0": _pkn(w_out[l][:, 0:512]), "OUT1": _pkn(w_out[l][:, 512:1024]),
        }
        for i in range(11):
            parts["UP%d" % i] = _pkn(np.concatenate(
                [w_up[l][:, 256 * i:256 * i + 256], w_up[l][:, DFF + 256 * i:DFF + 256 * i + 256]], axis=1))
        for c in range(8):
            parts["DN%d" % c] = _pkn(w_down[l][:, 128 * c:128 * c + 128])
        for nm, n in SLABS:
            off, _ = SLAB_OFF[nm]
            assert parts[nm].shape == (128, n), (nm, parts[nm].shape)
            out[l, :, off:off + n] = parts[nm]
    return out


def pack_consts(mix_norm, q_norm, k_norm, sinks, sgu_norm, w_s, b_s, ffn_norm, conv_w, conv_b):
    cst = np.zeros((128, NCST), np.float32)
    for l in range(NL):
        vb = C_VEC + 192 * l
        cst[:, vb:vb + 8] = mix_norm[l].reshape(8, 128).T
        cst[:, vb + 8:vb + 16] = ffn_norm[l].reshape(8, 128).T
        for tap in range(3):
            cst[:, vb + 16 + 44 * tap:vb + 16 + 44 * (tap + 1)] = conv_w[l, tap].reshape(44, 128).T
        cst[:, vb + 148:vb + 192] = conv_b[l].reshape(44, 128).T
        cst[0:64, C_QK + 2 * l] = q_norm[l]
        cst[0:64, C_QK + 2 * l + 1] = k_norm[l]
        cst[:, C_SGUG + 512 * l:C_SGUG + 512 * (l + 1)] = sgu_norm[l][None, :]
        for gp in range(4):
            cst[0:64, C_BSB + 512 * l + gp * 128:C_BSB + 512 * l + (gp + 1) * 128] = b_s[l, 2 * gp][None, :]
            cst[64:128, C_BSB + 512 * l + gp * 128:C_BSB + 512 * l + (gp + 1) * 128] = b_s[l, 2 * gp + 1][None, :]
    k = np.arange(128)[:, None]
    q = np.arange(128)[None, :]
    for g in range(2):
        for j in range(4):
            slope = 2.0 ** (-(4 * g + j + 1))
            dist_prev = q + 128 - k
            dist_cur = q - k
            bp = np.where(dist_prev < 128, -slope * dist_prev, -30000.0)
            bc = np.where(dist_cur >= 0, -slope * dist_cur, -30000.0)
            cst[:, C_ABIAS + (g * 2 + 0) * 512 + j * 128:C_ABIAS + (g * 2 + 0) * 512 + (j + 1) * 128] = bp
            cst[:, C_ABIAS + (g * 2 + 1) * 512 + j * 128:C_ABIAS + (g * 2 + 1) * 512 + (j + 1) * 128] = bc
    cst[:, C_TRIL:C_TRIL + 128] = (k <= q).astype(np.float32)
    srow = np.zeros((1, 2048), np.float32)
    for l in range(NL):
        for g in range(2):
            for sl in range(2):
                for pr in range(2):
                    base = (((l * 2 + g) * 2 + sl) * 2 + pr) * 128
                    srow[0, base:base + 128] = sinks[l, 4 * g + 2 * pr + sl]
    wst = np.ascontiguousarray(np.transpose(w_s, (3, 0, 1, 2)).reshape(128, NL * 8 * 128)).astype(np.float32)
    return cst, srow, wst


_NC_CACHE = {}
DBG_STOP = None


class _Stop(Exception):
    pass


_CUR = [0, 0]


def _ck(k):
    if DBG_STOP is not None and DBG_STOP == (_CUR[0], _CUR[1], k):
        raise _Stop()


def run(x, params, n_cores, layers=(0, 1)):
    B, S_, _ = x.shape
    n_seq = B // n_cores
    key = (n_seq, S_, tuple(layers))
    if key not in _NC_CACHE:
        _NC_CACHE[key] = build(n_seq, S_, layers)
    nc = _NC_CACHE[key]
    wts = pack_weights(params["w_in"], params["w_oa"], params["w_ob"], params["w_out"], params["w_up"],
                       params["w_down"])
    cst, srow, wst = pack_consts(params["mix_norm"], params["q_norm"], params["k_norm"], params["sinks"],
                                 params["sgu_norm"], params["w_s"], params["b_s"], params["ffn_norm"],
                                 params["conv_w"], params["conv_b"])
    in_maps = []
    for c in range(n_cores):
        xc = np.ascontiguousarray(np.transpose(x[c * n_seq:(c + 1) * n_seq], (0, 2, 1)))
        in_maps.append({"xT": xc, "wts": wts, "cst": cst, "srow": srow, "wst": wst})
    res = run_bass_kernel_spmd(nc, in_maps, core_ids=list(range(n_cores)))
    outs = [np.transpose(r["yT"], (0, 2, 1)) for r in res.results]
    return np.ascontiguousarray(np.concatenate(outs, axis=0)).astype(np.float32)


def kernel(**inputs):
    inputs = {k: np.asarray(v) for k, v in inputs.items()}
    x = inputs.pop("x").astype(np.float32)
    params = {k: v.astype(np.float32) for k, v in inputs.items()}
    return run(x, params, 8)
```

```python
from contextlib import ExitStack

import numpy as np
import concourse.bass as bass
import concourse.mybir as mybir
from concourse.bass_utils import run_bass_kernel_spmd

F32 = mybir.dt.float32
BF16 = mybir.dt.bfloat16
AF = mybir.ActivationFunctionType
ALU = mybir.AluOpType

D = 1024
NL = 2
NH = 8
NKV = 2
HD = 64
DFF = 2816
NJ = DFF // 128
T = 512
NB = T // 128
EPS = 1e-6
NBUF = 7
SLAB = 4096

SLABS = ([("A0", 2048), ("B", 3072), ("A1", 2048), ("C", 4096), ("D", 4096), ("E", 4096), ("G", 4096),
          ("OAB0", 4096), ("F", 4096), ("H", 4096), ("OAB1", 4096), ("OUT0", 4096), ("OUT1", 4096)]
         + [("UP%d" % i, 4096) for i in range(11)] + [("DN%d" % c, 2816) for c in range(8)])
SLAB_OFF = {}
_o = 0
for _n, _s in SLABS:
    SLAB_OFF[_n] = (_o, _s)
    _o += _s
WPL = _o

C_VEC = 0
C_QK = 384
C_SGUG = 392
C_BSB = C_SGUG + 1024
C_ABIAS = C_BSB + 1024
C_TRIL = C_ABIAS + 2048
NCST = C_TRIL + 128


class Buf:
    __slots__ = ("name", "lw", "rd", "excl")

    def __init__(self, name, excl=False):
        self.name = name
        self.lw = None
        self.rd = []
        self.excl = excl


class Sched:
    ENG = ("pe", "act", "dve", "pool", "sp")

    def __init__(self):
        self.streams = {e: [] for e in self.ENG}
        self.cnt = {}
        self.waited = {e: {} for e in self.ENG}
        self.semnames = []
        self.snap = {}

    def newsem(self, key):
        self.cnt[key] = 0
        self.semnames.append(key)

    def _waits(self, eng, reads, writes):
        deps = {}
        def add(ev, raw):
            if ev is None:
                return
            k, v = ev
            if k == eng and (eng == "pe" or not raw):
                return
            if deps.get(k, 0) < v:
                deps[k] = v
        for b in reads:
            add(b.lw, True)
            if b.excl:
                for ev in b.rd:
                    add(ev, False)
        for b in writes:
            add(b.lw, False)
            for ev in b.rd:
                add(ev, False)
        wd = self.waited[eng]
        for k, v in sorted(deps.items(), key=lambda kv: -kv[1]):
            if wd.get(k, 0) >= v:
                continue
            wd[k] = v
            self.streams[eng].append(("wait", k, v))
            for k2, v2 in self.snap.get((k, v), {}).items():
                if wd.get(k2, 0) < v2:
                    wd[k2] = v2

    def _commit(self, ev, reads, writes, eng=None):
        if eng is not None:
            self.snap[ev] = dict(self.waited[eng])
        for b in reads:
            b.rd.append(ev)
        for b in writes:
            b.lw = ev
            b.rd = []

    def op(self, eng, fn, reads=(), writes=()):
        self._waits(eng, reads, writes)
        self.cnt[eng] += 1
        self.streams[eng].append(("op", fn, eng, 1))
        self._commit((eng, self.cnt[eng]), reads, writes, eng)

    def dma(self, eng, sem, fn, reads=(), writes=()):
        self._waits(eng, reads, writes)
        self.cnt[sem] += 16
        self.streams[eng].append(("op", fn, sem, 16))
        self._commit((sem, self.cnt[sem]), reads, writes, eng)

    def wait_all(self, eng, bufs):
        self._waits(eng, bufs, bufs)

    def replay(self, eng, handle, sems):
        pend = None
        for it in self.streams[eng]:
            if it[0] == "wait":
                if pend is not None:
                    handle.wait_ge(sems[pend[1]], pend[2])
                pend = it
            else:
                res = it[1](handle)
                first, last = res if isinstance(res, tuple) else (res, res)
                if pend is not None:
                    first._wait_ge(sems[pend[1]], pend[2])
                    pend = None
                last.then_inc(sems[it[2]], it[3])
        if pend is not None:
            handle.wait_ge(sems[pend[1]], pend[2])


class Rot:
    def __init__(self, tiles, name):
        self.tiles = tiles
        self.bufs = [Buf("%s%d" % (name, i)) for i in range(len(tiles))]
        self.i = 0

    def next(self):
        i = self.i
        self.i = (i + 1) % len(self.tiles)
        return self.tiles[i], self.bufs[i]


def build(n_seq, seq_len, layers=(0, 1)):
    nc = bass.Bass("TRN2", target_bir_lowering=False)
    S = Sched()
    tiles_per_seq = seq_len // T
    n_tiles = n_seq * tiles_per_seq
    xT_d = nc.dram_tensor("xT", [n_seq, D, seq_len], F32, kind="ExternalInput").ap()
    wts_d = nc.dram_tensor("wts", [NL, 128, WPL], F32, kind="ExternalInput").ap()
    cst_d = nc.dram_tensor("cst", [128, NCST], F32, kind="ExternalInput").ap()
    srow_d = nc.dram_tensor("srow", [1, 2048], F32, kind="ExternalInput").ap()
    wst_d = nc.dram_tensor("wst", [128, 2048], F32, kind="ExternalInput").ap()
    yT_d = nc.dram_tensor("yT", [n_seq, D, seq_len], F32, kind="ExternalOutput").ap()
    wbf_d = nc.dram_tensor("wbf", [NL, 128, WPL], BF16).ap()

    es = ExitStack()
    with es:
        def sb(name, shape, dt):
            return es.enter_context(nc.sbuf_tensor("s_" + name, shape, dt))

        cst = sb("cst", [128, NCST], F32)
        srow = sb("srow", [128, 2048], BF16)
        wsT = sb("wsT", [128, 16, 128], BF16)
        ones = sb("ones", [128, 128], BF16)
        xres = [sb("xres%d" % i, [128, 8, T], F32) for i in range(2)]
        hT = sb("hT", [128, 8, T], BF16)
        sqt = sb("sqt", [128, 4, T], BF16)
        NFR = 10
        frt = sb("frt", [128, NFR, 512], F32)
        qT = sb("qT", [128, 4, T], BF16)
        kT = [sb("kT%d" % l, [128, 2, T + 128], BF16) for l in range(NL)]
        onesbd = sb("onesbd", [128, 128], BF16)
        vtok = [sb("vtok%d" % l, [128, NB + 1, 128], BF16) for l in range(NL)]
        qsq = sb("qsq", [128, 3, T], BF16)
        junk = sb("junk", [128, 512], BF16)
        ssv = sb("ssv", [128, 8], F32)
        lnv = sb("lnv", [128, 8], F32)
        rv = sb("rv", [128, 8], F32)
        vn = sb("vn", [128, NB, 512], BF16)
        pT = sb("pT", [128, 6, 512], BF16)
        rden = sb("rden", [128, 4, 256], F32)
        actT = sb("actT", [128, NJ, T], BF16)
        mergedT = actT[:, 0:8, :]
        yattT = actT[:, 8:12, :]
        ysguT = actT[:, 12:16, :]
        uT = actT[:, 16:20, :]
        halo = [sb("halo%d" % l, [128, 2 * NJ, 2], F32) for l in range(NL)]
        hc = sb("hc", [128, 2 * NJ, 2], F32)
        hctmp = sb("hctmp", [128, 2 * NJ], F32)
        b_hc = Buf("hc")
        b_hctmp = Buf("hctmp")
        wslab = sb("wslab", [128, NBUF, SLAB], BF16)
        psb = [es.enter_context(nc.psum_tensor("ps%d" % i, [128, 512], F32)) for i in range(8)]

        b_cst = Buf("cst")
        b_srow = Buf("srow")
        b_wsT = Buf("wsT")
        b_ones = Buf("ones")
        b_xres = [[Buf("xres%d_%d" % (i, c)) for c in range(8)] for i in range(2)]
        b_hT = [Buf("hT%d" % c) for c in range(8)]
        b_qT = [Buf("qT%d" % h) for h in range(4)]
        b_onesbd = Buf("onesbd")
        b_kprev = [Buf("kprev%d" % l) for l in range(NL)]
        b_kcur = [[Buf("kcur%d_%d" % (l, g)) for g in range(2)] for l in range(NL)]
        b_vprev = [Buf("vprev%d" % l) for l in range(NL)]
        b_vcur = [Buf("vcur%d" % l) for l in range(NL)]
        b_junk = Buf("junk")
        b_ssv = [Buf("ssv%d" % i) for i in range(8)]
        b_lnv = [Buf("lnv%d" % i) for i in range(8)]
        b_rv = [Buf("rv%d" % i) for i in range(8)]
        b_vn = [Buf("vn%d" % b) for b in range(NB)]
        b_actT = [Buf("actT%d" % j) for j in range(NJ)]
        b_merged = b_actT[0:8]
        b_yatt = [[b_actT[8 + c]] * NB for c in range(4)]
        b_uT = b_actT[16:20]
        b_halo = [[Buf("halo%d_%d" % (l, ch)) for ch in range(2 * NJ)] for l in range(NL)]
        b_wslab = [Buf("wslab%d" % i) for i in range(NBUF)]
        b_ps = [Buf("ps%d" % i, excl=True) for i in range(8)]

        sqr = Rot([sqt[:, i, :] for i in range(4)], "sq")
        qsqr = Rot([qsq[:, i, :] for i in range(3)], "qsq")
        fr = Rot([frt[:, i, :] for i in range(NFR)], "fr")
        rqr = rq2r = gvr = efr = tmr = sar = sbr = t1r = t2r = agr = avr = sgr = fr
        srowf = frt[0:1, 0:4, :]
        wstage = frt[:, 4:8, :]
        pTr = Rot([pT[:, i, :] for i in range(6)], "pT")
        rdr = Rot([rden[:, i, :] for i in range(4)], "rden")
        psr = Rot([p[:, :] for p in psb[0:7]], "psr")
        psr.bufs = b_ps[0:7]
        ss_ps, b_ss = psb[7][:, :], b_ps[7]
        small_i = [0]

        for e in ("pe", "act", "dve", "pool"):
            S.newsem(e)
        for i in range(NBUF):
            S.newsem("w%d" % i)
            S.newsem("ws%d" % i)
        b_wd = {(l, nm): Buf("wd%d_%s" % (l, nm)) for l in range(NL) for (nm, _) in SLABS}
        for k in ("cst0", "cst1", "cst2", "xl0", "xl1", "xs0", "xs1"):
            S.newsem(k)

        def act(out, in_, func, reads, writes, bias=None, scale=None, accum_out=None):
            kw = {}
            if bias is not None:
                kw["bias"] = bias
            if scale is not None:
                kw["scale"] = scale
            if accum_out is not None:
                kw["accum_out"] = accum_out
            S.op("act", lambda e: e.activation(out=out, in_=in_, func=func, **kw), reads, writes)

        def tt(out, in0, in1, op, reads, writes, eng="dve"):
            S.op(eng, lambda e: e.tensor_tensor(out=out, in0=in0, in1=in1, op=op), reads, writes)

        def stt(out, in0, scalar, in1, op0, op1, reads, writes, eng="dve"):
            S.op(eng, lambda e: e.scalar_tensor_tensor(out=out, in0=in0, scalar=scalar, in1=in1,
                                                        op0=op0, op1=op1), reads, writes)

        def cp(out, in_, reads, writes, eng="dve"):
            S.op(eng, lambda e: e.tensor_copy(out=out, in_=in_), reads, writes)

        def mm_group(out, pairs, reads, writes):
            def fn(e):
                ins = first = None
                n = len(pairs)
                for i, (l, r) in enumerate(pairs):
                    ins = e.matmul(out, l, r, start=(i == 0), stop=(i == n - 1))
                    first = first or ins
                return first, ins
            S.op("pe", fn, reads, writes)

        def mm_part(out, pairs, reads, writes, first, last):
            def fn(e):
                ins = fi = None
                n = len(pairs)
                for i, (l, r) in enumerate(pairs):
                    ins = e.matmul(out, l, r, start=(first and i == 0), stop=(last and i == n - 1))
                    fi = fi or ins
                return fi, ins
            S.op("pe", fn, reads, writes)

        def mm_multi(groups, reads, writes):
            def fn(e):
                ins = first = None
                for out, pairs in groups:
                    n = len(pairs)
                    for i, (l, r) in enumerate(pairs):
                        ins = e.matmul(out, l, r, start=(i == 0), stop=(i == n - 1))
                        first = first or ins
                return first, ins
            S.op("pe", fn, reads, writes)

        passes = [(ti, l) for ti in range(n_tiles) for l in layers]
        wseq = [(l, nm) for (_, l) in passes for (nm, _) in SLABS]
        wstate = {"issue": 0, "acq": 0}

        def w_issue():
            i = wstate["issue"]
            if i >= len(wseq):
                return
            wstate["issue"] = i + 1
            l, nm = wseq[i]
            off, n = SLAB_OFF[nm]
            slot = i % NBUF
            o = wslab[:, slot, 0:n]
            if passes[i // len(SLABS)][0] == 0:
                src = wts_d[l, :, off:off + n]
                S.dma("pool", "w%d" % slot, lambda e: e.dma_start(out=o, in_=src), (), (b_wslab[slot],))
            else:
                src = wbf_d[l, :, off:off + n]
                S.dma("sp", "w%d" % slot, lambda e: e.dma_start(out=o, in_=src), (b_wd[(l, nm)],),
                      (b_wslab[slot],))

        def w_acquire(expect):
            i = wstate["acq"]
            wstate["acq"] = i + 1
            assert wseq[i][1] == expect, (wseq[i], expect)
            slot = i % NBUF
            if passes[i // len(SLABS)][0] == 0 and n_tiles > 1:
                l, nm = wseq[i]
                off, n = SLAB_OFF[nm]
                dst = wbf_d[l, :, off:off + n]
                srcs = wslab[:, slot, 0:n]
                S.dma("sp", "ws%d" % slot, lambda e: e.dma_start(out=dst, in_=srcs), (b_wslab[slot],),
                      (b_wd[(l, nm)],))
            return wslab[:, slot, :], b_wslab[slot]

        def w_release(n=1):
            for _ in range(n):
                w_issue()

        S.dma("sp", "cst0", lambda e: e.dma_start(out=cst[:, :], in_=cst_d[:, :]), (), (b_cst,))
        S.dma("sp", "cst1", lambda e: e.dma_start(out=srowf, in_=srow_d.rearrange("o (a n) -> o a n", a=4)),
              (), tuple(fr.bufs[0:4]))
        S.dma("sp", "cst2", lambda e: e.dma_start(out=wstage, in_=wst_d.rearrange("p (a n) -> p a n", a=4)),
              (), tuple(fr.bufs[4:8]))
        for _ in range(NBUF):
            w_issue()
        S.op("dve", lambda e: e.memset(ones[:, :], 1.0), (), (b_ones,))
        S.op("dve", lambda e: e.memset(onesbd[:, :], 0.0), (), (b_onesbd,))
        S.op("dve", lambda e: e.memset(onesbd[0:64, 0:64], 1.0), (), (b_onesbd,))
        S.op("dve", lambda e: e.memset(onesbd[64:128, 64:128], 1.0), (), (b_onesbd,))
        S.op("dve", lambda e: e.memset(srow[:, :], 0.0), (), (b_srow,))
        act(srow[0:1, :].rearrange("o (a n) -> o a n", a=4), srowf, AF.Exp, tuple(fr.bufs[0:4]), (b_srow,))
        act(cst[:, C_ABIAS:C_ABIAS + 2048], cst[:, C_ABIAS:C_ABIAS + 2048], AF.Exp, (b_cst,), (b_cst,))
        for i in range(16):
            tt(wsT[:, i, :], wstage[:, i // 4, (i % 4) * 128:(i % 4 + 1) * 128], cst[:, C_TRIL:C_TRIL + 128], ALU.mult,
               (fr.bufs[4 + i // 4], b_cst), (b_wsT,))

        def rms_sq_act(xr, bxr, c):
            sq, bsq = sqr.next()
            act(sq, xr[:, c, :], AF.Square, (bxr[c],), (bsq,))
            return sq, bsq

        def rms_sq_mm(sqb, c):
            sq, bsq = sqb
            S.op("pe", (lambda e: e.matmul(ss_ps, ones[:, :], sq, start=(c == 0), stop=(c == 7))),
                 (bsq, b_ones), (b_ss,))

        def rms_finish(xr, bxr, gcol):
            ps, bps = ss_ps, b_ss
            rtmp, b_rtmp = fr.next()
            act(rtmp, ps, AF.Ln, (bps,), (b_rtmp,), bias=EPS, scale=1.0 / D)
            rstd, b_rstd = fr.next()
            act(rstd, rtmp, AF.Exp, (b_rtmp,), (b_rstd,), scale=-0.5)
            for c in range(8):
                stt(hT[:, c, :], xr[:, c, :], cst[:, gcol + c:gcol + c + 1], rstd, ALU.mult, ALU.mult,
                    (bxr[c], b_rstd, b_cst), (b_hT[c],))

        def headnorm_a(ps, bps, gcolumn, out, bout):
            sq, bsq = qsqr.next()
            act(sq, ps, AF.Square, (bps,), (bsq,))
            return lambda: headnorm_b(ps, bps, gcolumn, out, bout, sq, bsq)

        def headnorm_b(ps, bps, gcolumn, out, bout, sq, bsq):
            ps2, bps2 = psr.next()
            S.op("pe", lambda e: e.matmul(ps2, onesbd[:, :], sq, start=True, stop=True),
                 (bsq, b_onesbd), (bps2,))
            r1, br1 = rqr.next()
            act(r1, ps2, AF.Ln, (bps2,), (br1,), bias=EPS, scale=1.0 / HD)
            r2, br2 = rq2r.next()
            act(r2, r1, AF.Exp, (br1,), (br2,), scale=-0.5)
            stt(out, ps, cst[:, gcolumn:gcolumn + 1], r2, ALU.mult, ALU.mult,
                (bps, br2, b_cst), tuple(bout))

        def emit_xload(ti):
            s_idx = ti // tiles_per_seq
            t0 = (ti % tiles_per_seq) * T
            xi = ti % 2
            src = xT_d[s_idx].rearrange("(c p) t -> p c t", p=128)[:, :, t0:t0 + T]
            dstt = xres[xi][:, :, :]
            S.dma("sp", "xl%d" % xi, lambda e: e.dma_start(out=dstt, in_=src), (), tuple(b_xres[xi]))

        def kouter(outs, lhs_fn, reads_w, wr_bufs):
            for k in range(8):
                groups = [(o, lhs_fn(i, k), hT[:, k, :]) for i, o in enumerate(outs)]
                def fn(e, groups=groups, k=k):
                    ins = first = None
                    for (o, l_, r_) in groups:
                        ins = e.matmul(o, l_, r_, start=(k == 0), stop=(k == 7))
                        first = first or ins
                    return first, ins
                S.op("pe", fn, (*reads_w, b_hT[k]), tuple(wr_bufs))

        def run_pass(ti, l, first_layer, last_layer, nxt):
            _CUR[0], _CUR[1] = ti, l
            s_idx = ti // tiles_per_seq
            tt_i = ti % tiles_per_seq
            t0 = tt_i * T
            first_in_seq = tt_i == 0
            last_in_seq = tt_i == tiles_per_seq - 1
            xi = ti % 2
            xr = xres[xi]
            bxr = b_xres[xi]
            vbase = C_VEC + 192 * l
            G1, G2 = vbase, vbase + 8
            CW0, CW1, CW2, CB = vbase + 16, vbase + 60, vbase + 104, vbase + 148
            QG, KG = C_QK + 2 * l, C_QK + 2 * l + 1

            _ck(0)
            rms_finish(xr, bxr, G1)
            _ck(1)

            pend_hn = []

            def flush_hn():
                while pend_hn:
                    pend_hn.pop(0)()

            def proj_head(vW, bW, cols, gcolumn, out, bout):
                ps, bps = psr.next()
                mm_group(ps, [(vW[:, k, cols], hT[:, k, :]) for k in range(8)], (bW, *b_hT), (bps,))
                part_b = headnorm_a(ps, bps, gcolumn, out, bout)
                flush_hn()
                pend_hn.append(part_b)

            wA0, bA0 = w_acquire("A0")
            vA0 = wA0[:, 0:2048].rearrange("p (k n) -> p k n", k=8)
            qps = [psr.next() for _ in range(2)]
            kouter([p[0] for p in qps], lambda i, k: vA0[:, k, i * 128:(i + 1) * 128], (bA0,),
                   [p[1] for p in qps])
            for h in range(2):
                pend_hn.append(headnorm_a(qps[h][0], qps[h][1], QG, qT[:, h, :], (b_qT[h],)))
            w_release()
            _ck(2)

            wB, bB = w_acquire("B")
            vB = wB[:, 0:3072].rearrange("p (k n) -> p k n", k=8)
            for g in range(2):
                proj_head(vB, bB, slice(g * 128, (g + 1) * 128), KG, kT[l][:, g, 128:128 + T], (b_kcur[l][g],))
            ps, bps = psr.next()
            mm_multi([(ps[:, b * 128:(b + 1) * 128],
                       [(hT[:, k, b * 128:(b + 1) * 128], vB[:, k, 256:384]) for k in range(8)])
                      for b in range(NB)], (bB, *b_hT), (bps,))
            flush_hn()
            cp(vtok[l][:, 1:NB + 1, :], ps.rearrange("p (b n) -> p b n", b=NB), (bps,), (b_vcur[l],))
            w_release()

            def sgu_block(b):
                ps, bps = psr.next()
                groups = []
                for gp in range(4):
                    for sl in range(2):
                        g = 2 * gp + sl
                        groups.append((ps[64 * sl:64 * sl + 64, gp * 128:(gp + 1) * 128],
                                       [(vn[:, b, g * 64:(g + 1) * 64], wsT[:, l * 8 + g, :])]))
                mm_multi(groups, (b_vn[b], b_wsT), (bps,))
                tm, btm = tmr.next()
                tt(tm, ps, cst[:, C_BSB + 512 * l:C_BSB + 512 * l + 512], ALU.add, (bps, b_cst), (btm,))
                tt(ysguT[:, :, b * 128:(b + 1) * 128], tm.rearrange("p (a n) -> p a n", a=4),
                   uT[:, :, b * 128:(b + 1) * 128], ALU.mult, (btm, *b_uT), tuple(b_actT[12:16]))

            def att_stage1(b, g):
                halves = []
                if not (first_in_seq and b == 0):
                    halves.append(0)
                halves.append(1)
                c0 = 256 * halves[0]
                banks = [psr.next(), psr.next()]
                kb = [b_kcur[l][g]] + ([b_kprev[l]] if (b == 0 and 0 in halves) else [])
                for hf in halves:
                    kcols = slice(128 * (b + hf), 128 * (b + hf) + 128)
                    for sl in range(2):
                        ps, bps = banks[sl]
                        lhs_ = kT[l][64 * sl:64 * sl + 64, g, kcols]
                        rhs_ = qT[64 * sl:64 * sl + 64, 2 * g:2 * g + 2, b * 128:(b + 1) * 128]
                        out_ = ps[:, hf * 256:hf * 256 + 256].rearrange("p (a n) -> p a n", a=2)
                        S.op("pe", (lambda e, out_=out_, lhs_=lhs_, rhs_=rhs_: e.matmul(
                            out_, lhs_, rhs_, start=True, stop=True)),
                            (*kb, b_qT[2 * g], b_qT[2 * g + 1]), (bps,))
                pts = {}
                for sl in range(2):
                    ps, bps = banks[sl]
                    e_, be_ = efr.next()
                    act(e_[:, c0:512], ps[:, c0:512], AF.Exp, (bps,), (be_,), scale=0.125)
                    p_, bp_ = pTr.next()
                    col = C_ABIAS + (g * 2 + sl) * 512
                    tt(p_[:, c0:512], e_[:, c0:512], cst[:, col + c0:col + 512], ALU.mult, (be_, b_cst), (bp_,),
                       eng="pool")
                    pts[sl] = (p_, bp_)
                return halves, pts

            def att_stage2(b, g, halves, pts):
                yd, byd = psr.next()
                groups = []
                sbase = ((l * 2 + g) * 2) * 256
                for sl in range(2):
                    ypairs, dpairs = [], []
                    for hf in halves:
                        rhs = pts[sl][0][:, hf * 256:hf * 256 + 256].rearrange("p (a n) -> p a n", a=2)
                        ypairs.append((vtok[l][:, b + hf, g * 64:(g + 1) * 64], rhs))
                        dpairs.append((ones[:, 0:64], rhs))
                    dpairs.append((ones[:, 0:64],
                                   srow[:, sbase + sl * 256:sbase + sl * 256 + 256].rearrange("p (a n) -> p a n", a=2)))
                    groups.append((yd[64 * sl:64 * sl + 64, 0:256].rearrange("p (a n) -> p a n", a=2), ypairs))
                    groups.append((yd[64 * sl:64 * sl + 64, 256:512].rearrange("p (a n) -> p a n", a=2), dpairs))
                vb = [b_vcur[l]] + ([b_vprev[l]] if b == 0 and 0 in halves else [])
                mm_multi(groups, (pts[0][1], pts[1][1], *vb, b_ones, b_srow), (byd,))
                rl, brl = rdr.next()
                act(rl, yd[:, 256:512], AF.Ln, (byd,), (brl,))
                rd, brd = rdr.next()
                act(rd, rl, AF.Exp, (brl,), (brd,), scale=-1.0)
                tt(yattT[:, 2 * g:2 * g + 2, b * 128:(b + 1) * 128],
                   yd[:, 0:256].rearrange("p (a n) -> p a n", a=2),
                   rd.rearrange("p (a n) -> p a n", a=2), ALU.mult, (byd, brd),
                   (b_yatt[2 * g][b], b_yatt[2 * g + 1][b]))

            _ck(4)
            slabs = {}

            def get_slab(nm):
                if nm not in slabs:
                    w_, b_ = w_acquire(nm)
                    n_ = 2048 if nm == "A1" else 4096
                    slabs[nm] = (w_[:, 0:n_].rearrange("p (k n) -> p k n", k=8), b_)
                return slabs[nm]

            def u_qhead(h):
                vA1, bA1 = get_slab("A1")
                proj_head(vA1, bA1, slice((h - 2) * 128, (h - 1) * 128), QG, qT[:, h, :], (b_qT[h],))
                if h == 3:
                    w_release()

            def u_su(c):
                vC, bC = get_slab("C")
                flush_hn()
                ps, bps = psr.next()
                mm_group(ps, [(vC[:, k, c * 128:(c + 1) * 128], hT[:, k, :]) for k in range(8)],
                         (bC, *b_hT), (bps,))
                act(uT[:, c, :], ps, AF.Gelu_apprx_tanh, (bps,), (b_uT[c],))
                if c == 3:
                    w_release()

            def u_sv(b):
                vD, bD = get_slab("D")
                ps, bps = psr.next()
                mm_group(ps, [(hT[:, k, b * 128:(b + 1) * 128], vD[:, k, :]) for k in range(8)],
                         (bD, *b_hT), (bps,))
                g_, bg_ = gvr.next()
                act(g_, ps, AF.Gelu_apprx_tanh, (bps,), (bg_,))
                si = small_i[0]
                small_i[0] = (si + 1) % 8
                act(junk[:, :], g_, AF.Square, (bg_,), (b_junk, b_ssv[si]), accum_out=ssv[:, si:si + 1])
                act(lnv[:, si:si + 1], ssv[:, si:si + 1], AF.Ln, (b_ssv[si],), (b_lnv[si],), bias=EPS, scale=1.0 / 512)
                act(rv[:, si:si + 1], lnv[:, si:si + 1], AF.Exp, (b_lnv[si],), (b_rv[si],), scale=-0.5)
                stt(vn[:, b, :], g_, rv[:, si:si + 1], cst[:, C_SGUG + 512 * l:C_SGUG + 512 * l + 512],
                    ALU.mult, ALU.mult, (bg_, b_rv[si], b_cst), (b_vn[b],))
                if b == NB - 1:
                    w_release()

            units = ([lambda h=h: u_qhead(h) for h in range(2, 4)] + [lambda c=c: u_su(c) for c in range(4)]
                     + [lambda b=b: u_sv(b) for b in range(NB)])
            its = [(b, 0) for b in range(NB)] + [(b, 1) for b in range(NB)]
            pend = [att_stage1(*its[0]), att_stage1(*its[1])]
            for i, (b, g) in enumerate(its):
                if g == 0:
                    for _ in range(3 if b < 2 else 2):
                        units.pop(0)()
                else:
                    sgu_block(b)
                if i + 2 < len(its):
                    pend.append(att_stage1(*its[i + 2]))
                att_stage2(b, g, *pend.pop(0))
            assert not units
            _ck(3)
            if not last_in_seq:
                cp(kT[l][:, :, 0:128], kT[l][:, :, T:T + 128], (*b_kcur[l],), (b_kprev[l],))
                cp(vtok[l][:, 0, :], vtok[l][:, NB, :], (b_vcur[l],), (b_vprev[l],))

            _ck(5)
            for half in range(2):
                wE, bE = w_acquire("EF"[half])
                wG, bG = w_acquire("GH"[half])
                wO, bO = w_acquire("OAB%d" % half)
                vE = wE.rearrange("p (k n) -> p k n", k=8)
                vG = wG.rearrange("p (k n) -> p k n", k=8)
                vO = wO.rearrange("p (m k n) -> p m k n", m=2, k=4)
                for c4 in range(4):
                    c = 4 * half + c4
                    cs = slice(c4 * 128, (c4 + 1) * 128)
                    pga, bpga = psr.next()
                    mm_group(pga, [(vE[:, k, cs], hT[:, k, :]) for k in range(8)], (bE, *b_hT), (bpga,))
                    pgb, bpgb = psr.next()
                    mm_group(pgb, [(vG[:, k, cs], hT[:, k, :]) for k in range(8)], (bG, *b_hT), (bpgb,))
                    pa, bpa = psr.next()
                    mm_group(pa, [(vO[:, 0, kc, cs], yattT[:, kc, :]) for kc in range(4)],
                             (bO, *b_actT[8:12]), (bpa,))
                    pb, bpb = psr.next()
                    mm_group(pb, [(vO[:, 1, kc, cs], ysguT[:, kc, :]) for kc in range(4)], (bO, *b_actT[12:16]), (bpb,))
                    sa, bsa = sar.next()
                    act(sa, pga, AF.Sigmoid, (bpga,), (bsa,))
                    sb_, bsb_ = sbr.next()
                    act(sb_, pgb, AF.Sigmoid, (bpgb,), (bsb_,))
                    t1, bt1 = t1r.next()
                    tt(t1, pa, sa, ALU.mult, (bpa, bsa), (bt1,))
                    t2, bt2 = t2r.next()
                    tt(t2, pb, sb_, ALU.mult, (bpb, bsb_), (bt2,))
                    tt(mergedT[:, c, :], t1, t2, ALU.add, (bt1, bt2), (b_merged[c],), eng="pool")
                w_release(3)
            sqbs = {}
            for half in range(2):
                wO, bO = w_acquire("OUT%d" % half)
                vO = wO.rearrange("p (k n) -> p k n", k=8)
                for c4 in range(4):
                    c = 4 * half + c4
                    po, bpo = psr.next()
                    mm_group(po, [(vO[:, k, c4 * 128:(c4 + 1) * 128], mergedT[:, k, :]) for k in range(8)],
                             (bO, *b_merged), (bpo,))
                    if c >= 1:
                        rms_sq_mm(sqbs[c - 1], c - 1)
                    tt(xr[:, c, :], po, xr[:, c, :], ALU.add, (bpo, bxr[c]), (bxr[c],))
                    sqbs[c] = rms_sq_act(xr, bxr, c)
                w_release()
            rms_sq_mm(sqbs[7], 7)

            _ck(6)
            rms_finish(xr, bxr, G2)
            if last_layer and nxt is not None:
                emit_xload(nxt[0])

            def ffn_epilogue(j, pg, bpg, pv, bpv):
                ag, bag = agr.next()
                av, bav = avr.next()
                items = ((pg, bpg, ag, bag, j), (pv, bpv, av, bav, NJ + j))
                for (ps, bps, a_, ba_, ch) in items:
                    act(a_, ps, AF.Identity, (bps, b_cst), (ba_,),
                        bias=cst[:, CB + ch:CB + ch + 1], scale=cst[:, CW2 + ch:CW2 + ch + 1])
                for (ps, bps, a_, ba_, ch) in items:
                    stt(a_[:, 1:T], ps[:, 0:T - 1], cst[:, CW1 + ch:CW1 + ch + 1], a_[:, 1:T], ALU.mult, ALU.add,
                        (bps, ba_, b_cst), (ba_,))
                for (ps, bps, a_, ba_, ch) in items:
                    stt(a_[:, 2:T], ps[:, 0:T - 2], cst[:, CW0 + ch:CW0 + ch + 1], a_[:, 2:T], ALU.mult, ALU.add,
                        (bps, ba_, b_cst), (ba_,))
                if not first_in_seq:
                    for (ps, bps, a_, ba_, ch) in items:
                        tt(a_[:, 0:2], a_[:, 0:2], hc[:, ch, :], ALU.add, (ba_, b_hc), (ba_,), eng="pool")
                if not last_in_seq:
                    for (ps, bps, a_, ba_, ch) in items:
                        act(halo[l][:, ch, :], ps[:, T - 2:T], AF.Identity, (bps,), (b_halo[l][ch],))
                sg, bsg = sgr.next()
                act(sg, ag, AF.Silu, (bag,), (bsg,))
                tt(actT[:, j, :], sg, av, ALU.mult, (bsg, bav), (b_actT[j],), eng="pool")

            if not first_in_seq:
                bh = tuple(b_halo[l])
                tt(hc[:, :, 1], halo[l][:, :, 1], cst[:, CW0:CW0 + 44], ALU.mult, (*bh, b_cst), (b_hc,))
                tt(hc[:, :, 0], halo[l][:, :, 0], cst[:, CW0:CW0 + 44], ALU.mult, (*bh, b_cst), (b_hc,))
                tt(hctmp[:, :], halo[l][:, :, 1], cst[:, CW1:CW1 + 44], ALU.mult, (*bh, b_cst), (b_hctmp,))
                tt(hc[:, :, 0], hc[:, :, 0], hctmp[:, :], ALU.add, (b_hc, b_hctmp), (b_hc,))
            for i in range(11):
                wU, bU = w_acquire("UP%d" % i)
                vU = wU.rearrange("p (k n) -> p k n", k=8)
                if i == 0:
                    pss = [psr.next() for _ in range(4)]
                    offs = [0, 256, 128, 384]
                    kouter([p[0] for p in pss], lambda q, k: vU[:, k, offs[q]:offs[q] + 128], (bU,),
                           [p[1] for p in pss])
                    ffn_epilogue(0, pss[0][0], pss[0][1], pss[1][0], pss[1][1])
                    ffn_epilogue(1, pss[2][0], pss[2][1], pss[3][0], pss[3][1])
                else:
                    for jj in range(2):
                        j = 2 * i + jj
                        pg, bpg = psr.next()
                        mm_group(pg, [(vU[:, k, jj * 128:(jj + 1) * 128], hT[:, k, :]) for k in range(8)],
                                 (bU, *b_hT), (bpg,))
                        pv, bpv = psr.next()
                        mm_group(pv, [(vU[:, k, 256 + jj * 128:256 + (jj + 1) * 128], hT[:, k, :]) for k in range(8)],
                                 (bU, *b_hT), (bpv,))
                        ffn_epilogue(j, pg, bpg, pv, bpv)
                w_release()
            _ck(7)
            if nxt is not None:
                nxr, nbxr = xres[nxt[0] % 2], b_xres[nxt[0] % 2]
            sqbs = {}

            def down_tail(c, pd, bpd):
                if nxt is not None and c >= 1:
                    rms_sq_mm(sqbs[c - 1], c - 1)
                tt(xr[:, c, :], pd, xr[:, c, :], ALU.add, (bpd, bxr[c]), (bxr[c],))
                if nxt is not None:
                    sqbs[c] = rms_sq_act(nxr, nbxr, c)

            JS = 14
            first = []
            for c in range(4):
                wDn, bDn = w_acquire("DN%d" % c)
                vDn = wDn[:, 0:2816].rearrange("p (k n) -> p k n", k=NJ)
                pd, bpd = psr.next()
                mm_part(pd, [(vDn[:, j, :], actT[:, j, :]) for j in range(JS)], (bDn, *b_actT[0:JS]), (bpd,), True, False)
                first.append((vDn, bDn, pd, bpd))
            for c in range(4):
                vDn, bDn, pd, bpd = first[c]
                mm_part(pd, [(vDn[:, j, :], actT[:, j, :]) for j in range(JS, NJ)], (bDn, *b_actT[JS:NJ]), (bpd,), False, True)
                down_tail(c, pd, bpd)
                w_release()
            for c in range(4, 8):
                wDn, bDn = w_acquire("DN%d" % c)
                vDn = wDn[:, 0:2816].rearrange("p (k n) -> p k n", k=NJ)
                pd, bpd = psr.next()
                mm_group(pd, [(vDn[:, j, :], actT[:, j, :]) for j in range(NJ)], (bDn, *b_actT), (bpd,))
                down_tail(c, pd, bpd)
                w_release()
            if nxt is not None:
                rms_sq_mm(sqbs[7], 7)

            if last_layer:
                dst = yT_d[s_idx].rearrange("(c p) t -> p c t", p=128)[:, :, t0:t0 + T]
                S.dma("sp", "xs%d" % xi, lambda e: e.dma_start(out=dst, in_=xr[:, :, :]), tuple(bxr), ())

        plist = [(ti, li) for ti in range(n_tiles) for li in range(len(layers))]
        emit_xload(0)
        sq0 = [rms_sq_act(xres[0], b_xres[0], c) for c in range(4)]
        for c in range(8):
            rms_sq_mm(sq0[c] if c < 4 else rms_sq_act(xres[0], b_xres[0], c), c)
        try:
            for pi, (ti, li) in enumerate(plist):
                nxt = plist[pi + 1] if pi + 1 < len(plist) else None
                run_pass(ti, layers[li], li == 0, li == len(layers) - 1, nxt)
        except _Stop:
            dst = yT_d[0].rearrange("(c p) t -> p c t", p=128)[:, :, 0:T]
            S.dma("sp", "xs0", lambda e: e.dma_start(out=dst, in_=xres[0][:, :, :]), tuple(b_xres[0]), ())
        S.wait_all("sp", [b for i in range(2) for b in b_xres[i]])

        sems = {k: es.enter_context(nc.semaphore(k)) for k in S.semnames}
        block = es.enter_context(nc.Block())

        @block.tensor
        def _(e):
            S.replay("pe", e, sems)

        @block.scalar
        def _(e):
            S.replay("act", e, sems)

        @block.vector
        def _(e):
            S.replay("dve", e, sems)

        @block.gpsimd
        def _(e):
            S.replay("pool", e, sems)

        @block.sync
        def _(e):
            S.replay("sp", e, sems)
    return nc


def _pkn(w):
    kc = w.shape[0] // 128
    return np.ascontiguousarray(w.reshape(kc, 128, -1).transpose(1, 0, 2).reshape(128, -1))


def pack_weights(w_in, w_oa, w_ob, w_out, w_up, w_down):
    out = np.empty((NL, 128, WPL), np.float32)
    for l in range(NL):
        parts = {
            "A0": _pkn(w_in[l][:, 0:256]), "A1": _pkn(w_in[l][:, 256:512]), "B": _pkn(np.concatenate([w_in[l][:, 512:576], w_in[l][:, 512:576], w_in[l][:, 576:640],
                                     w_in[l][:, 576:640], w_in[l][:, 640:768]], axis=1)),
            "C": _pkn(w_in[l][:, 768:1280]), "D": _pkn(w_in[l][:, 1280:1792]),
            "E": _pkn(w_in[l][:, 1792:2304]), "F": _pkn(w_in[l][:, 2304:2816]),
            "G": _pkn(w_in[l][:, 2816:3328]), "H": _pkn(w_in[l][:, 3328:3840]),
            "OAB0": np.concatenate([_pkn(w_oa[l][:, 0:512]), _pkn(w_ob[l][:, 0:512])], axis=1),
            "OAB1": np.concatenate([_pkn(w_oa[l][:, 512:1024]), _pkn(w_ob[l][:, 512:1024])], axis=1),
            "OUT0": _pkn(w_out[l][:, 0:512]), "OUT1": _pkn(w_out[l][:, 512:1024]),
        }
        for i in range(11):
            parts["UP%d" % i] = _pkn(np.concatenate(
                [w_up[l][:, 256 * i:256 * i + 256], w_up[l][:, DFF + 256 * i:DFF + 256 * i + 256]], axis=1))
        for c in range(8):
            parts["DN%d" % c] = _pkn(w_down[l][:, 128 * c:128 * c + 128])
        for nm, n in SLABS:
            off, _ = SLAB_OFF[nm]
            assert parts[nm].shape == (128, n), (nm, parts[nm].shape)
            out[l, :, off:off + n] = parts[nm]
    return out


def pack_consts(mix_norm, q_norm, k_norm, sinks, sgu_norm, w_s, b_s, ffn_norm, conv_w, conv_b):
    cst = np.zeros((128, NCST), np.float32)
    for l in range(NL):
        vb = C_VEC + 192 * l
        cst[:, vb:vb + 8] = mix_norm[l].reshape(8, 128).T
        cst[:, vb + 8:vb + 16] = ffn_norm[l].reshape(8, 128).T
        for tap in range(3):
            cst[:, vb + 16 + 44 * tap:vb + 16 + 44 * (tap + 1)] = conv_w[l, tap].reshape(44, 128).T
        cst[:, vb + 148:vb + 192] = conv_b[l].reshape(44, 128).T
        cst[0:64, C_QK + 2 * l] = q_norm[l]
        cst[64:128, C_QK + 2 * l] = q_norm[l]
        cst[0:64, C_QK + 2 * l + 1] = k_norm[l]
        cst[64:128, C_QK + 2 * l + 1] = k_norm[l]
        cst[:, C_SGUG + 512 * l:C_SGUG + 512 * (l + 1)] = sgu_norm[l][None, :]
        for gp in range(4):
            cst[0:64, C_BSB + 512 * l + gp * 128:C_BSB + 512 * l + (gp + 1) * 128] = b_s[l, 2 * gp][None, :]
            cst[64:128, C_BSB + 512 * l + gp * 128:C_BSB + 512 * l + (gp + 1) * 128] = b_s[l, 2 * gp + 1][None, :]
    k = np.arange(128)[:, None]
    q = np.arange(128)[None, :]
    for g in range(2):
        for j in range(4):
            slope = 2.0 ** (-(4 * g + j + 1))
            dist_prev = q + 128 - k
            dist_cur = q - k
            bp = np.where(dist_prev < 128, -slope * dist_prev, -30000.0)
            bc = np.where(dist_cur >= 0, -slope * dist_cur, -30000.0)
            pr_, sl_ = j // 2, j % 2
            o_ = C_ABIAS + (g * 2 + sl_) * 512 + pr_ * 128
            cst[:, o_:o_ + 128] = bp
            cst[:, o_ + 256:o_ + 384] = bc
    cst[:, C_TRIL:C_TRIL + 128] = (k <= q).astype(np.float32)
    srow = np.zeros((1, 2048), np.float32)
    for l in range(NL):
        for g in range(2):
            for sl in range(2):
                for pr in range(2):
                    base = (((l * 2 + g) * 2 + sl) * 2 + pr) * 128
                    srow[0, base:base + 128] = sinks[l, 4 * g + 2 * pr + sl]
    wst = np.ascontiguousarray(np.transpose(w_s, (3, 0, 1, 2)).reshape(128, NL * 8 * 128)).astype(np.float32)
    return cst, srow, wst


_NC_CACHE = {}
DBG_STOP = None


class _Stop(Exception):
    pass


_CUR = [0, 0]


def _ck(k):
    if DBG_STOP is not None and DBG_STOP == (_CUR[0], _CUR[1], k):
        raise _Stop()


def run(x, params, n_cores, layers=(0, 1)):
    B, S_, _ = x.shape
    n_seq = B // n_cores
    key = (n_seq, S_, tuple(layers))
    if key not in _NC_CACHE:
        _NC_CACHE[key] = build(n_seq, S_, layers)
    nc = _NC_CACHE[key]
    wts = pack_weights(params["w_in"], params["w_oa"], params["w_ob"], params["w_out"], params["w_up"],
                       params["w_down"])
    cst, srow, wst = pack_consts(params["mix_norm"], params["q_norm"], params["k_norm"], params["sinks"],
                                 params["sgu_norm"], params["w_s"], params["b_s"], params["ffn_norm"],
                                 params["conv_w"], params["conv_b"])
    in_maps = []
    for c in range(n_cores):
        xc = np.ascontiguousarray(np.transpose(x[c * n_seq:(c + 1) * n_seq], (0, 2, 1)))
        in_maps.append({"xT": xc, "wts": wts, "cst": cst, "srow": srow, "wst": wst})
    res = run_bass_kernel_spmd(nc, in_maps, core_ids=list(range(n_cores)))
    outs = [np.transpose(r["yT"], (0, 2, 1)) for r in res.results]
    return np.ascontiguousarray(np.concatenate(outs, axis=0)).astype(np.float32)


def kernel(**inputs):
    inputs = {k: np.asarray(v) for k, v in inputs.items()}
    x = inputs.pop("x").astype(np.float32)
    params = {k: v.astype(np.float32) for k, v in inputs.items()}
    return run(x, params, 8)
```

```python
from contextlib import ExitStack

import numpy as np
import concourse.bass as bass
import concourse.mybir as mybir
from concourse.bass_utils import run_bass_kernel_spmd

F32 = mybir.dt.float32
BF16 = mybir.dt.bfloat16
AF = mybir.ActivationFunctionType
ALU = mybir.AluOpType

D = 1024
NL = 2
NH = 8
NKV = 2
HD = 64
DFF = 2816
NJ = DFF // 128
T = 512
NB = T // 128
EPS = 1e-6
NBUF = 7
SLAB = 4096

SLABS = ([("A0", 2048), ("B", 3072), ("A1", 2048), ("C", 4096), ("D", 4096), ("E", 4096), ("G", 4096),
          ("OAB0", 4096), ("F", 4096), ("H", 4096), ("OAB1", 4096), ("OUT0", 4096), ("OUT1", 4096)]
         + [("UP%d" % i, 4096) for i in range(11)] + [("DN%d" % c, 2816) for c in range(8)])
SLAB_OFF = {}
_o = 0
for _n, _s in SLABS:
    SLAB_OFF[_n] = (_o, _s)
    _o += _s
WPL = _o

C_VEC = 0
C_QK = 384
C_SGUG = 392
C_BSB = C_SGUG + 1024
C_ABIAS = C_BSB + 1024
C_TRIL = C_ABIAS + 2048
NCST = C_TRIL + 128


class Buf:
    __slots__ = ("name", "lw", "rd", "excl")

    def __init__(self, name, excl=False):
        self.name = name
        self.lw = None
        self.rd = []
        self.excl = excl


class Sched:
    ENG = ("pe", "act", "dve", "pool", "sp")

    def __init__(self):
        self.streams = {e: [] for e in self.ENG}
        self.cnt = {}
        self.waited = {e: {} for e in self.ENG}
        self.semnames = []
        self.snap = {}

    def newsem(self, key):
        self.cnt[key] = 0
        self.semnames.append(key)

    def _waits(self, eng, reads, writes):
        deps = {}
        def add(ev, raw):
            if ev is None:
                return
            k, v = ev
            if k == eng and (eng == "pe" or not raw):
                return
            if deps.get(k, 0) < v:
                deps[k] = v
        for b in reads:
            add(b.lw, True)
            if b.excl:
                for ev in b.rd:
                    add(ev, False)
        for b in writes:
            add(b.lw, False)
            for ev in b.rd:
                add(ev, False)
        wd = self.waited[eng]
        for k, v in sorted(deps.items(), key=lambda kv: -kv[1]):
            if wd.get(k, 0) >= v:
                continue
            wd[k] = v
            self.streams[eng].append(("wait", k, v))
            for k2, v2 in self.snap.get((k, v), {}).items():
                if wd.get(k2, 0) < v2:
                    wd[k2] = v2

    def _commit(self, ev, reads, writes, eng=None):
        if eng is not None:
            self.snap[ev] = dict(self.waited[eng])
        for b in reads:
            b.rd.append(ev)
        for b in writes:
            b.lw = ev
            b.rd = []

    def op(self, eng, fn, reads=(), writes=()):
        self._waits(eng, reads, writes)
        self.cnt[eng] += 1
        self.streams[eng].append(("op", fn, eng, 1))
        self._commit((eng, self.cnt[eng]), reads, writes, eng)

    def dma(self, eng, sem, fn, reads=(), writes=()):
        self._waits(eng, reads, writes)
        self.cnt[sem] += 16
        self.streams[eng].append(("op", fn, sem, 16))
        self._commit((sem, self.cnt[sem]), reads, writes, eng)

    def wait_all(self, eng, bufs):
        self._waits(eng, bufs, bufs)

    def replay(self, eng, handle, sems):
        pend = None
        for it in self.streams[eng]:
            if it[0] == "wait":
                if pend is not None:
                    handle.wait_ge(sems[pend[1]], pend[2])
                pend = it
            else:
                res = it[1](handle)
                first, last = res if isinstance(res, tuple) else (res, res)
                if pend is not None:
                    first._wait_ge(sems[pend[1]], pend[2])
                    pend = None
                last.then_inc(sems[it[2]], it[3])
        if pend is not None:
            handle.wait_ge(sems[pend[1]], pend[2])


class Rot:
    def __init__(self, tiles, name):
        self.tiles = tiles
        self.bufs = [Buf("%s%d" % (name, i)) for i in range(len(tiles))]
        self.i = 0

    def next(self):
        i = self.i
        self.i = (i + 1) % len(self.tiles)
        return self.tiles[i], self.bufs[i]


def build(n_seq, seq_len, layers=(0, 1)):
    nc = bass.Bass("TRN2", target_bir_lowering=False)
    S = Sched()
    tiles_per_seq = seq_len // T
    n_tiles = n_seq * tiles_per_seq
    xT_d = nc.dram_tensor("xT", [n_seq, D, seq_len], F32, kind="ExternalInput").ap()
    wts_d = nc.dram_tensor("wts", [NL, 128, WPL], F32, kind="ExternalInput").ap()
    cst_d = nc.dram_tensor("cst", [128, NCST], F32, kind="ExternalInput").ap()
    srow_d = nc.dram_tensor("srow", [1, 2048], F32, kind="ExternalInput").ap()
    wst_d = nc.dram_tensor("wst", [128, 2048], F32, kind="ExternalInput").ap()
    yT_d = nc.dram_tensor("yT", [n_seq, D, seq_len], F32, kind="ExternalOutput").ap()
    wbf_d = nc.dram_tensor("wbf", [NL, 128, WPL], BF16).ap()

    es = ExitStack()
    with es:
        def sb(name, shape, dt):
            return es.enter_context(nc.sbuf_tensor("s_" + name, shape, dt))

        cst = sb("cst", [128, NCST], F32)
        srow = sb("srow", [128, 2048], BF16)
        wsT = sb("wsT", [128, 16, 128], BF16)
        ones = sb("ones", [128, 128], BF16)
        xres = [sb("xres%d" % i, [128, 8, T], F32) for i in range(2)]
        hT = sb("hT", [128, 8, T], BF16)
        sqt = sb("sqt", [128, 4, T], BF16)
        NFR = 10
        frt = sb("frt", [128, NFR, 512], F32)
        qT = sb("qT", [128, 4, T], BF16)
        kT = [sb("kT%d" % l, [128, 2, T + 128], BF16) for l in range(NL)]
        onesbd = sb("onesbd", [128, 128], BF16)
        vtok = [sb("vtok%d" % l, [128, NB + 1, 128], BF16) for l in range(NL)]
        qsq = sb("qsq", [128, 3, T], BF16)
        junk = sb("junk", [128, 512], BF16)
        ssv = sb("ssv", [128, 8], F32)
        lnv = sb("lnv", [128, 8], F32)
        rv = sb("rv", [128, 8], F32)
        vn = sb("vn", [128, NB, 512], BF16)
        pT = sb("pT", [128, 6, 512], BF16)
        rden = sb("rden", [128, 4, 256], F32)
        actT = sb("actT", [128, NJ, T], BF16)
        mergedT = actT[:, 0:8, :]
        yattT = actT[:, 8:12, :]
        ysguT = actT[:, 12:16, :]
        uT = actT[:, 16:20, :]
        halo = [sb("halo%d" % l, [128, 2 * NJ, 2], F32) for l in range(NL)]
        hc = sb("hc", [128, 2 * NJ, 2], F32)
        hctmp = sb("hctmp", [128, 2 * NJ], F32)
        b_hc = Buf("hc")
        b_hctmp = Buf("hctmp")
        wslab = sb("wslab", [128, NBUF, SLAB], BF16)
        psb = [es.enter_context(nc.psum_tensor("ps%d" % i, [128, 512], F32)) for i in range(8)]

        b_cst = Buf("cst")
        b_srow = Buf("srow")
        b_wsT = Buf("wsT")
        b_ones = Buf("ones")
        b_xres = [[Buf("xres%d_%d" % (i, c)) for c in range(8)] for i in range(2)]
        b_hT = [Buf("hT%d" % c) for c in range(8)]
        b_qT = [Buf("qT%d" % h) for h in range(4)]
        b_onesbd = Buf("onesbd")
        b_kprev = [Buf("kprev%d" % l) for l in range(NL)]
        b_kcur = [[Buf("kcur%d_%d" % (l, g)) for g in range(2)] for l in range(NL)]
        b_vprev = [Buf("vprev%d" % l) for l in range(NL)]
        b_vcur = [Buf("vcur%d" % l) for l in range(NL)]
        b_junk = Buf("junk")
        b_ssv = [Buf("ssv%d" % i) for i in range(8)]
        b_lnv = [Buf("lnv%d" % i) for i in range(8)]
        b_rv = [Buf("rv%d" % i) for i in range(8)]
        b_vn = [Buf("vn%d" % b) for b in range(NB)]
        b_actT = [Buf("actT%d" % j) for j in range(NJ)]
        b_merged = b_actT[0:8]
        b_yatt = [[b_actT[8 + c]] * NB for c in range(4)]
        b_uT = b_actT[16:20]
        b_halo = [[Buf("halo%d_%d" % (l, ch)) for ch in range(2 * NJ)] for l in range(NL)]
        b_wslab = [Buf("wslab%d" % i) for i in range(NBUF)]
        b_ps = [Buf("ps%d" % i, excl=True) for i in range(8)]

        sqr = Rot([sqt[:, i, :] for i in range(4)], "sq")
        qsqr = Rot([qsq[:, i, :] for i in range(3)], "qsq")
        fr = Rot([frt[:, i, :] for i in range(NFR)], "fr")
        rqr = rq2r = gvr = efr = tmr = sar = sbr = t1r = t2r = agr = avr = sgr = fr
        srowf = frt[0:1, 0:4, :]
        wstage = frt[:, 4:8, :]
        pTr = Rot([pT[:, i, :] for i in range(6)], "pT")
        rdr = Rot([rden[:, i, :] for i in range(4)], "rden")
        psr = Rot([p[:, :] for p in psb[0:7]], "psr")
        psr.bufs = b_ps[0:7]
        ss_ps, b_ss = psb[7][:, :], b_ps[7]
        small_i = [0]

        for e in ("pe", "act", "dve", "pool"):
            S.newsem(e)
        for i in range(NBUF):
            S.newsem("w%d" % i)
            S.newsem("ws%d" % i)
        b_wd = {(l, nm): Buf("wd%d_%s" % (l, nm)) for l in range(NL) for (nm, _) in SLABS}
        for k in ("cst0", "cst1", "cst2", "xl0", "xl1", "xs0", "xs1"):
            S.newsem(k)

        def act(out, in_, func, reads, writes, bias=None, scale=None, accum_out=None):
            kw = {}
            if bias is not None:
                kw["bias"] = bias
            if scale is not None:
                kw["scale"] = scale
            if accum_out is not None:
                kw["accum_out"] = accum_out
            S.op("act", lambda e: e.activation(out=out, in_=in_, func=func, **kw), reads, writes)

        def tt(out, in0, in1, op, reads, writes, eng="dve"):
            S.op(eng, lambda e: e.tensor_tensor(out=out, in0=in0, in1=in1, op=op), reads, writes)

        def stt(out, in0, scalar, in1, op0, op1, reads, writes, eng="dve"):
            S.op(eng, lambda e: e.scalar_tensor_tensor(out=out, in0=in0, scalar=scalar, in1=in1,
                                                        op0=op0, op1=op1), reads, writes)

        def cp(out, in_, reads, writes, eng="dve"):
            S.op(eng, lambda e: e.tensor_copy(out=out, in_=in_), reads, writes)

        def mm_group(out, pairs, reads, writes):
            def fn(e):
                ins = first = None
                n = len(pairs)
                for i, (l, r) in enumerate(pairs):
                    ins = e.matmul(out, l, r, start=(i == 0), stop=(i == n - 1))
                    first = first or ins
                return first, ins
            S.op("pe", fn, reads, writes)

        def mm_part(out, pairs, reads, writes, first, last):
            def fn(e):
                ins = fi = None
                n = len(pairs)
                for i, (l, r) in enumerate(pairs):
                    ins = e.matmul(out, l, r, start=(first and i == 0), stop=(last and i == n - 1))
                    fi = fi or ins
                return fi, ins
            S.op("pe", fn, reads, writes)

        def mm_multi(groups, reads, writes):
            def fn(e):
                ins = first = None
                for out, pairs in groups:
                    n = len(pairs)
                    for i, (l, r) in enumerate(pairs):
                        ins = e.matmul(out, l, r, start=(i == 0), stop=(i == n - 1))
                        first = first or ins
                return first, ins
            S.op("pe", fn, reads, writes)

        passes = [(ti, l) for ti in range(n_tiles) for l in layers]
        wseq = [(l, nm) for (_, l) in passes for (nm, _) in SLABS]
        wstate = {"issue": 0, "acq": 0}

        def w_issue():
            i = wstate["issue"]
            if i >= len(wseq):
                return
            wstate["issue"] = i + 1
            l, nm = wseq[i]
            off, n = SLAB_OFF[nm]
            slot = i % NBUF
            o = wslab[:, slot, 0:n]
            if passes[i // len(SLABS)][0] == 0:
                src = wts_d[l, :, off:off + n]
                S.dma("pool", "w%d" % slot, lambda e: e.dma_start(out=o, in_=src), (), (b_wslab[slot],))
            else:
                src = wbf_d[l, :, off:off + n]
                S.dma("sp", "w%d" % slot, lambda e: e.dma_start(out=o, in_=src), (b_wd[(l, nm)],),
                      (b_wslab[slot],))

        def w_acquire(expect):
            i = wstate["acq"]
            wstate["acq"] = i + 1
            assert wseq[i][1] == expect, (wseq[i], expect)
            slot = i % NBUF
            if passes[i // len(SLABS)][0] == 0 and n_tiles > 1:
                l, nm = wseq[i]
                off, n = SLAB_OFF[nm]
                dst = wbf_d[l, :, off:off + n]
                srcs = wslab[:, slot, 0:n]
                S.dma("sp", "ws%d" % slot, lambda e: e.dma_start(out=dst, in_=srcs), (b_wslab[slot],),
                      (b_wd[(l, nm)],))
            return wslab[:, slot, :], b_wslab[slot]

        def w_release(n=1):
            for _ in range(n):
                w_issue()

        S.dma("sp", "cst0", lambda e: e.dma_start(out=cst[:, :], in_=cst_d[:, :]), (), (b_cst,))
        S.dma("sp", "cst1", lambda e: e.dma_start(out=srowf, in_=srow_d.rearrange("o (a n) -> o a n", a=4)),
              (), tuple(fr.bufs[0:4]))
        S.dma("sp", "cst2", lambda e: e.dma_start(out=wstage, in_=wst_d.rearrange("p (a n) -> p a n", a=4)),
              (), tuple(fr.bufs[4:8]))
        for _ in range(NBUF):
            w_issue()
        S.op("dve", lambda e: e.memset(ones[:, :], 1.0), (), (b_ones,))
        S.op("dve", lambda e: e.memset(onesbd[:, :], 0.0), (), (b_onesbd,))
        S.op("dve", lambda e: e.memset(onesbd[0:64, 0:64], 1.0), (), (b_onesbd,))
        S.op("dve", lambda e: e.memset(onesbd[64:128, 64:128], 1.0), (), (b_onesbd,))
        S.op("dve", lambda e: e.memset(srow[:, :], 0.0), (), (b_srow,))
        act(srow[0:1, :].rearrange("o (a n) -> o a n", a=4), srowf, AF.Exp, tuple(fr.bufs[0:4]), (b_srow,))
        act(cst[:, C_ABIAS:C_ABIAS + 2048], cst[:, C_ABIAS:C_ABIAS + 2048], AF.Exp, (b_cst,), (b_cst,))
        for i in range(16):
            tt(wsT[:, i, :], wstage[:, i // 4, (i % 4) * 128:(i % 4 + 1) * 128], cst[:, C_TRIL:C_TRIL + 128], ALU.mult,
               (fr.bufs[4 + i // 4], b_cst), (b_wsT,))

        def rms_sq_act(xr, bxr, c):
            sq, bsq = sqr.next()
            act(sq, xr[:, c, :], AF.Square, (bxr[c],), (bsq,))
            return sq, bsq

        def rms_sq_mm(sqb, c):
            sq, bsq = sqb
            S.op("pe", (lambda e: e.matmul(ss_ps, ones[:, :], sq, start=(c == 0), stop=(c == 7))),
                 (bsq, b_ones), (b_ss,))

        def rms_finish(xr, bxr, gcol):
            ps, bps = ss_ps, b_ss
            rtmp, b_rtmp = fr.next()
            act(rtmp, ps, AF.Ln, (bps,), (b_rtmp,), bias=EPS, scale=1.0 / D)
            rstd, b_rstd = fr.next()
            act(rstd, rtmp, AF.Exp, (b_rtmp,), (b_rstd,), scale=-0.5)
            for c in range(8):
                stt(hT[:, c, :], xr[:, c, :], cst[:, gcol + c:gcol + c + 1], rstd, ALU.mult, ALU.mult,
                    (bxr[c], b_rstd, b_cst), (b_hT[c],))

        def headnorm_a(ps, bps, gcolumn, out, bout):
            sq, bsq = qsqr.next()
            act(sq, ps, AF.Square, (bps,), (bsq,))
            return lambda: headnorm_b(ps, bps, gcolumn, out, bout, sq, bsq)

        def headnorm_b(ps, bps, gcolumn, out, bout, sq, bsq):
            ps2, bps2 = psr.next()
            S.op("pe", lambda e: e.matmul(ps2, onesbd[:, :], sq, start=True, stop=True),
                 (bsq, b_onesbd), (bps2,))
            r1, br1 = rqr.next()
            act(r1, ps2, AF.Ln, (bps2,), (br1,), bias=EPS, scale=1.0 / HD)
            r2, br2 = rq2r.next()
            act(r2, r1, AF.Exp, (br1,), (br2,), scale=-0.5)
            stt(out, ps, cst[:, gcolumn:gcolumn + 1], r2, ALU.mult, ALU.mult,
                (bps, br2, b_cst), tuple(bout))

        def emit_xload(ti):
            s_idx = ti // tiles_per_seq
            t0 = (ti % tiles_per_seq) * T
            xi = ti % 2
            src = xT_d[s_idx].rearrange("(c p) t -> p c t", p=128)[:, :, t0:t0 + T]
            dstt = xres[xi][:, :, :]
            S.dma("sp", "xl%d" % xi, lambda e: e.dma_start(out=dstt, in_=src), (), tuple(b_xres[xi]))

        def kouter(outs, lhs_fn, reads_w, wr_bufs):
            for k in range(8):
                groups = [(o, lhs_fn(i, k), hT[:, k, :]) for i, o in enumerate(outs)]
                def fn(e, groups=groups, k=k):
                    ins = first = None
                    for (o, l_, r_) in groups:
                        ins = e.matmul(o, l_, r_, start=(k == 0), stop=(k == 7))
                        first = first or ins
                    return first, ins
                S.op("pe", fn, (*reads_w, b_hT[k]), tuple(wr_bufs))

        def run_pass(ti, l, first_layer, last_layer, nxt):
            _CUR[0], _CUR[1] = ti, l
            s_idx = ti // tiles_per_seq
            tt_i = ti % tiles_per_seq
            t0 = tt_i * T
            first_in_seq = tt_i == 0
            last_in_seq = tt_i == tiles_per_seq - 1
            xi = ti % 2
            xr = xres[xi]
            bxr = b_xres[xi]
            vbase = C_VEC + 192 * l
            G1, G2 = vbase, vbase + 8
            CW0, CW1, CW2, CB = vbase + 16, vbase + 60, vbase + 104, vbase + 148
            QG, KG = C_QK + 2 * l, C_QK + 2 * l + 1

            _ck(0)
            rms_finish(xr, bxr, G1)
            _ck(1)

            pend_hn = []

            def flush_hn():
                while pend_hn:
                    pend_hn.pop(0)()

            def proj_head(vW, bW, cols, gcolumn, out, bout):
                ps, bps = psr.next()
                mm_group(ps, [(vW[:, k, cols], hT[:, k, :]) for k in range(8)], (bW, *b_hT), (bps,))
                part_b = headnorm_a(ps, bps, gcolumn, out, bout)
                flush_hn()
                pend_hn.append(part_b)

            wA0, bA0 = w_acquire("A0")
            vA0 = wA0[:, 0:2048].rearrange("p (k n) -> p k n", k=8)
            qps = [psr.next() for _ in range(2)]
            kouter([p[0] for p in qps], lambda i, k: vA0[:, k, i * 128:(i + 1) * 128], (bA0,),
                   [p[1] for p in qps])
            for h in range(2):
                pend_hn.append(headnorm_a(qps[h][0], qps[h][1], QG, qT[:, h, :], (b_qT[h],)))
            w_release()
            _ck(2)

            wB, bB = w_acquire("B")
            vB = wB[:, 0:3072].rearrange("p (k n) -> p k n", k=8)
            for g in range(2):
                proj_head(vB, bB, slice(g * 128, (g + 1) * 128), KG, kT[l][:, g, 128:128 + T], (b_kcur[l][g],))
            ps, bps = psr.next()
            mm_multi([(ps[:, b * 128:(b + 1) * 128],
                       [(hT[:, k, b * 128:(b + 1) * 128], vB[:, k, 256:384]) for k in range(8)])
                      for b in range(NB)], (bB, *b_hT), (bps,))
            flush_hn()
            cp(vtok[l][:, 1:NB + 1, :], ps.rearrange("p (b n) -> p b n", b=NB), (bps,), (b_vcur[l],))
            w_release()

            def sgu_block(b):
                ps, bps = psr.next()
                groups = []
                for gp in range(4):
                    for sl in range(2):
                        g = 2 * gp + sl
                        groups.append((ps[64 * sl:64 * sl + 64, gp * 128:(gp + 1) * 128],
                                       [(vn[:, b, g * 64:(g + 1) * 64], wsT[:, l * 8 + g, :])]))
                mm_multi(groups, (b_vn[b], b_wsT), (bps,))
                tm, btm = tmr.next()
                tt(tm, ps, cst[:, C_BSB + 512 * l:C_BSB + 512 * l + 512], ALU.add, (bps, b_cst), (btm,))
                tt(ysguT[:, :, b * 128:(b + 1) * 128], tm.rearrange("p (a n) -> p a n", a=4),
                   uT[:, :, b * 128:(b + 1) * 128], ALU.mult, (btm, *b_uT), tuple(b_actT[12:16]))

            def att_stage1(b, g):
                halves = []
                if not (first_in_seq and b == 0):
                    halves.append(0)
                halves.append(1)
                c0 = 256 * halves[0]
                banks = [psr.next(), psr.next()]
                kb = [b_kcur[l][g]] + ([b_kprev[l]] if (b == 0 and 0 in halves) else [])
                for hf in halves:
                    kcols = slice(128 * (b + hf), 128 * (b + hf) + 128)
                    for sl in range(2):
                        ps, bps = banks[sl]
                        lhs_ = kT[l][64 * sl:64 * sl + 64, g, kcols]
                        rhs_ = qT[64 * sl:64 * sl + 64, 2 * g:2 * g + 2, b * 128:(b + 1) * 128]
                        out_ = ps[:, hf * 256:hf * 256 + 256].rearrange("p (a n) -> p a n", a=2)
                        S.op("pe", (lambda e, out_=out_, lhs_=lhs_, rhs_=rhs_: e.matmul(
                            out_, lhs_, rhs_, start=True, stop=True)),
                            (*kb, b_qT[2 * g], b_qT[2 * g + 1]), (bps,))
                pts = {}
                for sl in range(2):
                    ps, bps = banks[sl]
                    e_, be_ = efr.next()
                    act(e_[:, c0:512], ps[:, c0:512], AF.Exp, (bps,), (be_,), scale=0.125)
                    p_, bp_ = pTr.next()
                    col = C_ABIAS + (g * 2 + sl) * 512
                    tt(p_[:, c0:512], e_[:, c0:512], cst[:, col + c0:col + 512], ALU.mult, (be_, b_cst), (bp_,),
                       eng="pool")
                    pts[sl] = (p_, bp_)
                return halves, pts

            def att_stage2(b, g, halves, pts):
                yd, byd = psr.next()
                groups = []
                sbase = ((l * 2 + g) * 2) * 256
                for sl in range(2):
                    ypairs, dpairs = [], []
                    for hf in halves:
                        rhs = pts[sl][0][:, hf * 256:hf * 256 + 256].rearrange("p (a n) -> p a n", a=2)
                        ypairs.append((vtok[l][:, b + hf, g * 64:(g + 1) * 64], rhs))
                        dpairs.append((ones[:, 0:64], rhs))
                    dpairs.append((ones[:, 0:64],
                                   srow[:, sbase + sl * 256:sbase + sl * 256 + 256].rearrange("p (a n) -> p a n", a=2)))
                    groups.append((yd[64 * sl:64 * sl + 64, 0:256].rearrange("p (a n) -> p a n", a=2), ypairs))
                    groups.append((yd[64 * sl:64 * sl + 64, 256:512].rearrange("p (a n) -> p a n", a=2), dpairs))
                vb = [b_vcur[l]] + ([b_vprev[l]] if b == 0 and 0 in halves else [])
                mm_multi(groups, (pts[0][1], pts[1][1], *vb, b_ones, b_srow), (byd,))
                rd, brd = rdr.next()
                if g == 0:
                    S.op("dve", lambda e, rd=rd, yd=yd: e.reciprocal(out=rd, in_=yd[:, 256:512]), (byd,), (brd,))
                else:
                    rl, brl = rdr.next()
                    act(rl, yd[:, 256:512], AF.Ln, (byd,), (brl,))
                    act(rd, rl, AF.Exp, (brl,), (brd,), scale=-1.0)
                tt(yattT[:, 2 * g:2 * g + 2, b * 128:(b + 1) * 128],
                   yd[:, 0:256].rearrange("p (a n) -> p a n", a=2),
                   rd.rearrange("p (a n) -> p a n", a=2), ALU.mult, (byd, brd),
                   (b_yatt[2 * g][b], b_yatt[2 * g + 1][b]))

            _ck(4)
            slabs = {}

            def get_slab(nm):
                if nm not in slabs:
                    w_, b_ = w_acquire(nm)
                    n_ = 2048 if nm == "A1" else 4096
                    slabs[nm] = (w_[:, 0:n_].rearrange("p (k n) -> p k n", k=8), b_)
                return slabs[nm]

            def u_qhead(h):
                vA1, bA1 = get_slab("A1")
                proj_head(vA1, bA1, slice((h - 2) * 128, (h - 1) * 128), QG, qT[:, h, :], (b_qT[h],))
                if h == 3:
                    w_release()

            def u_su(c):
                vC, bC = get_slab("C")
                flush_hn()
                ps, bps = psr.next()
                mm_group(ps, [(vC[:, k, c * 128:(c + 1) * 128], hT[:, k, :]) for k in range(8)],
                         (bC, *b_hT), (bps,))
                act(uT[:, c, :], ps, AF.Gelu_apprx_tanh, (bps,), (b_uT[c],))
                if c == 3:
                    w_release()

            def u_sv(b):
                vD, bD = get_slab("D")
                ps, bps = psr.next()
                mm_group(ps, [(hT[:, k, b * 128:(b + 1) * 128], vD[:, k, :]) for k in range(8)],
                         (bD, *b_hT), (bps,))
                g_, bg_ = gvr.next()
                act(g_, ps, AF.Gelu_apprx_tanh, (bps,), (bg_,))
                si = small_i[0]
                small_i[0] = (si + 1) % 8
                S.op("dve", lambda e, g_=g_, si=si: e.scalar_tensor_tensor(
                    out=junk[:, :], in0=g_, scalar=1.0, in1=g_, op0=ALU.mult, op1=ALU.mult,
                    accum_out=ssv[:, si:si + 1]), (bg_,), (b_junk, b_ssv[si]))
                act(lnv[:, si:si + 1], ssv[:, si:si + 1], AF.Ln, (b_ssv[si],), (b_lnv[si],), bias=EPS, scale=1.0 / 512)
                act(rv[:, si:si + 1], lnv[:, si:si + 1], AF.Exp, (b_lnv[si],), (b_rv[si],), scale=-0.5)
                stt(vn[:, b, :], g_, rv[:, si:si + 1], cst[:, C_SGUG + 512 * l:C_SGUG + 512 * l + 512],
                    ALU.mult, ALU.mult, (bg_, b_rv[si], b_cst), (b_vn[b],))
                if b == NB - 1:
                    w_release()

            units = ([lambda h=h: u_qhead(h) for h in range(2, 4)] + [lambda c=c: u_su(c) for c in range(4)]
                     + [lambda b=b: u_sv(b) for b in range(NB)])
            its = [(b, 0) for b in range(NB)] + [(b, 1) for b in range(NB)]
            pend = [att_stage1(*its[0]), att_stage1(*its[1])]
            for i, (b, g) in enumerate(its):
                if g == 0:
                    for _ in range(3 if b < 2 else 2):
                        units.pop(0)()
                else:
                    sgu_block(b)
                if i + 2 < len(its):
                    pend.append(att_stage1(*its[i + 2]))
                att_stage2(b, g, *pend.pop(0))
            assert not units
            _ck(3)
            if not last_in_seq:
                cp(kT[l][:, :, 0:128], kT[l][:, :, T:T + 128], (*b_kcur[l],), (b_kprev[l],))
                cp(vtok[l][:, 0, :], vtok[l][:, NB, :], (b_vcur[l],), (b_vprev[l],))

            _ck(5)
            for half in range(2):
                wE, bE = w_acquire("EF"[half])
                wG, bG = w_acquire("GH"[half])
                wO, bO = w_acquire("OAB%d" % half)
                vE = wE.rearrange("p (k n) -> p k n", k=8)
                vG = wG.rearrange("p (k n) -> p k n", k=8)
                vO = wO.rearrange("p (m k n) -> p m k n", m=2, k=4)
                for c4 in range(4):
                    c = 4 * half + c4
                    cs = slice(c4 * 128, (c4 + 1) * 128)
                    pga, bpga = psr.next()
                    mm_group(pga, [(vE[:, k, cs], hT[:, k, :]) for k in range(8)], (bE, *b_hT), (bpga,))
                    pgb, bpgb = psr.next()
                    mm_group(pgb, [(vG[:, k, cs], hT[:, k, :]) for k in range(8)], (bG, *b_hT), (bpgb,))
                    pa, bpa = psr.next()
                    mm_group(pa, [(vO[:, 0, kc, cs], yattT[:, kc, :]) for kc in range(4)],
                             (bO, *b_actT[8:12]), (bpa,))
                    pb, bpb = psr.next()
                    mm_group(pb, [(vO[:, 1, kc, cs], ysguT[:, kc, :]) for kc in range(4)], (bO, *b_actT[12:16]), (bpb,))
                    sa, bsa = sar.next()
                    act(sa, pga, AF.Sigmoid, (bpga,), (bsa,))
                    sb_, bsb_ = sbr.next()
                    act(sb_, pgb, AF.Sigmoid, (bpgb,), (bsb_,))
                    t1, bt1 = t1r.next()
                    tt(t1, pa, sa, ALU.mult, (bpa, bsa), (bt1,))
                    t2, bt2 = t2r.next()
                    tt(t2, pb, sb_, ALU.mult, (bpb, bsb_), (bt2,))
                    tt(mergedT[:, c, :], t1, t2, ALU.add, (bt1, bt2), (b_merged[c],), eng="pool")
                w_release(3)
            sqbs = {}
            for half in range(2):
                wO, bO = w_acquire("OUT%d" % half)
                vO = wO.rearrange("p (k n) -> p k n", k=8)
                for c4 in range(4):
                    c = 4 * half + c4
                    po, bpo = psr.next()
                    mm_group(po, [(vO[:, k, c4 * 128:(c4 + 1) * 128], mergedT[:, k, :]) for k in range(8)],
                             (bO, *b_merged), (bpo,))
                    if c >= 1:
                        rms_sq_mm(sqbs[c - 1], c - 1)
                    tt(xr[:, c, :], po, xr[:, c, :], ALU.add, (bpo, bxr[c]), (bxr[c],))
                    sqbs[c] = rms_sq_act(xr, bxr, c)
                w_release()
            rms_sq_mm(sqbs[7], 7)

            _ck(6)
            rms_finish(xr, bxr, G2)
            if last_layer and nxt is not None:
                emit_xload(nxt[0])

            def ffn_epilogue(j, pg, bpg, pv, bpv):
                ag, bag = agr.next()
                av, bav = avr.next()
                items = ((pg, bpg, ag, bag, j), (pv, bpv, av, bav, NJ + j))
                for (ps, bps, a_, ba_, ch) in items:
                    act(a_, ps, AF.Identity, (bps, b_cst), (ba_,),
                        bias=cst[:, CB + ch:CB + ch + 1], scale=cst[:, CW2 + ch:CW2 + ch + 1])
                for (ps, bps, a_, ba_, ch) in items:
                    stt(a_[:, 1:T], ps[:, 0:T - 1], cst[:, CW1 + ch:CW1 + ch + 1], a_[:, 1:T], ALU.mult, ALU.add,
                        (bps, ba_, b_cst), (ba_,))
                for (ps, bps, a_, ba_, ch) in items:
                    stt(a_[:, 2:T], ps[:, 0:T - 2], cst[:, CW0 + ch:CW0 + ch + 1], a_[:, 2:T], ALU.mult, ALU.add,
                        (bps, ba_, b_cst), (ba_,))
                if not first_in_seq:
                    for (ps, bps, a_, ba_, ch) in items:
                        tt(a_[:, 0:2], a_[:, 0:2], hc[:, ch, :], ALU.add, (ba_, b_hc), (ba_,), eng="pool")
                if not last_in_seq:
                    for (ps, bps, a_, ba_, ch) in items:
                        act(halo[l][:, ch, :], ps[:, T - 2:T], AF.Identity, (bps,), (b_halo[l][ch],))
                sg, bsg = sgr.next()
                act(sg, ag, AF.Silu, (bag,), (bsg,))
                tt(actT[:, j, :], sg, av, ALU.mult, (bsg, bav), (b_actT[j],), eng="pool")

            if not first_in_seq:
                bh = tuple(b_halo[l])
                tt(hc[:, :, 1], halo[l][:, :, 1], cst[:, CW0:CW0 + 44], ALU.mult, (*bh, b_cst), (b_hc,))
                tt(hc[:, :, 0], halo[l][:, :, 0], cst[:, CW0:CW0 + 44], ALU.mult, (*bh, b_cst), (b_hc,))
                tt(hctmp[:, :], halo[l][:, :, 1], cst[:, CW1:CW1 + 44], ALU.mult, (*bh, b_cst), (b_hctmp,))
                tt(hc[:, :, 0], hc[:, :, 0], hctmp[:, :], ALU.add, (b_hc, b_hctmp), (b_hc,))
            for i in range(11):
                wU, bU = w_acquire("UP%d" % i)
                vU = wU.rearrange("p (k n) -> p k n", k=8)
                if i == 0:
                    pss = [psr.next() for _ in range(4)]
                    offs = [0, 256, 128, 384]
                    kouter([p[0] for p in pss], lambda q, k: vU[:, k, offs[q]:offs[q] + 128], (bU,),
                           [p[1] for p in pss])
                    ffn_epilogue(0, pss[0][0], pss[0][1], pss[1][0], pss[1][1])
                    ffn_epilogue(1, pss[2][0], pss[2][1], pss[3][0], pss[3][1])
                else:
                    for jj in range(2):
                        j = 2 * i + jj
                        pg, bpg = psr.next()
                        mm_group(pg, [(vU[:, k, jj * 128:(jj + 1) * 128], hT[:, k, :]) for k in range(8)],
                                 (bU, *b_hT), (bpg,))
                        pv, bpv = psr.next()
                        mm_group(pv, [(vU[:, k, 256 + jj * 128:256 + (jj + 1) * 128], hT[:, k, :]) for k in range(8)],
                                 (bU, *b_hT), (bpv,))
                        ffn_epilogue(j, pg, bpg, pv, bpv)
                w_release()
            _ck(7)
            if nxt is not None:
                nxr, nbxr = xres[nxt[0] % 2], b_xres[nxt[0] % 2]
            sqbs = {}

            def down_tail(c, pd, bpd):
                if nxt is not None and c >= 1:
                    rms_sq_mm(sqbs[c - 1], c - 1)
                tt(xr[:, c, :], pd, xr[:, c, :], ALU.add, (bpd, bxr[c]), (bxr[c],))
                if nxt is not None:
                    sqbs[c] = rms_sq_act(nxr, nbxr, c)

            JS = 14
            first = []
            for c in range(4):
                wDn, bDn = w_acquire("DN%d" % c)
                vDn = wDn[:, 0:2816].rearrange("p (k n) -> p k n", k=NJ)
                pd, bpd = psr.next()
                mm_part(pd, [(vDn[:, j, :], actT[:, j, :]) for j in range(JS)], (bDn, *b_actT[0:JS]), (bpd,), True, False)
                first.append((vDn, bDn, pd, bpd))
            for c in range(4):
                vDn, bDn, pd, bpd = first[c]
                mm_part(pd, [(vDn[:, j, :], actT[:, j, :]) for j in range(JS, NJ)], (bDn, *b_actT[JS:NJ]), (bpd,), False, True)
                down_tail(c, pd, bpd)
                w_release()
            for c in range(4, 8):
                wDn, bDn = w_acquire("DN%d" % c)
                vDn = wDn[:, 0:2816].rearrange("p (k n) -> p k n", k=NJ)
                pd, bpd = psr.next()
                mm_group(pd, [(vDn[:, j, :], actT[:, j, :]) for j in range(NJ)], (bDn, *b_actT), (bpd,))
                down_tail(c, pd, bpd)
                w_release()
            if nxt is not None:
                rms_sq_mm(sqbs[7], 7)

            if last_layer:
                dst = yT_d[s_idx].rearrange("(c p) t -> p c t", p=128)[:, :, t0:t0 + T]
                S.dma("sp", "xs%d" % xi, lambda e: e.dma_start(out=dst, in_=xr[:, :, :]), tuple(bxr), ())

        plist = [(ti, li) for ti in range(n_tiles) for li in range(len(layers))]
        emit_xload(0)
        sq0 = [rms_sq_act(xres[0], b_xres[0], c) for c in range(4)]
        for c in range(8):
            rms_sq_mm(sq0[c] if c < 4 else rms_sq_act(xres[0], b_xres[0], c), c)
        try:
            for pi, (ti, li) in enumerate(plist):
                nxt = plist[pi + 1] if pi + 1 < len(plist) else None
                run_pass(ti, layers[li], li == 0, li == len(layers) - 1, nxt)
        except _Stop:
            dst = yT_d[0].rearrange("(c p) t -> p c t", p=128)[:, :, 0:T]
            S.dma("sp", "xs0", lambda e: e.dma_start(out=dst, in_=xres[0][:, :, :]), tuple(b_xres[0]), ())
        S.wait_all("sp", [b for i in range(2) for b in b_xres[i]])

        sems = {k: es.enter_context(nc.semaphore(k)) for k in S.semnames}
        block = es.enter_context(nc.Block())

        @block.tensor
        def _(e):
            S.replay("pe", e, sems)

        @block.scalar
        def _(e):
            S.replay("act", e, sems)

        @block.vector
        def _(e):
            S.replay("dve", e, sems)

        @block.gpsimd
        def _(e):
            S.replay("pool", e, sems)

        @block.sync
        def _(e):
            S.replay("sp", e, sems)
    return nc


def _pkn(w):
    kc = w.shape[0] // 128
    return np.ascontiguousarray(w.reshape(kc, 128, -1).transpose(1, 0, 2).reshape(128, -1))


def pack_weights(w_in, w_oa, w_ob, w_out, w_up, w_down):
    out = np.empty((NL, 128, WPL), np.float32)
    for l in range(NL):
        parts = {
            "A0": _pkn(w_in[l][:, 0:256]), "A1": _pkn(w_in[l][:, 256:512]), "B": _pkn(np.concatenate([w_in[l][:, 512:576], w_in[l][:, 512:576], w_in[l][:, 576:640],
                                     w_in[l][:, 576:640], w_in[l][:, 640:768]], axis=1)),
            "C": _pkn(w_in[l][:, 768:1280]), "D": _pkn(w_in[l][:, 1280:1792]),
            "E": _pkn(w_in[l][:, 1792:2304]), "F": _pkn(w_in[l][:, 2304:2816]),
            "G": _pkn(w_in[l][:, 2816:3328]), "H": _pkn(w_in[l][:, 3328:3840]),
            "OAB0": np.concatenate([_pkn(w_oa[l][:, 0:512]), _pkn(w_ob[l][:, 0:512])], axis=1),
            "OAB1": np.concatenate([_pkn(w_oa[l][:, 512:1024]), _pkn(w_ob[l][:, 512:1024])], axis=1),
            "OUT0": _pkn(w_out[l][:, 0:512]), "OUT1": _pkn(w_out[l][:, 512:1024]),
        }
        for i in range(11):
            parts["UP%d" % i] = _pkn(np.concatenate(
                [w_up[l][:, 256 * i:256 * i + 256], w_up[l][:, DFF + 256 * i:DFF + 256 * i + 256]], axis=1))
        for c in range(8):
            parts["DN%d" % c] = _pkn(w_down[l][:, 128 * c:128 * c + 128])
        for nm, n in SLABS:
            off, _ = SLAB_OFF[nm]
            assert parts[nm].shape == (128, n), (nm, parts[nm].shape)
            out[l, :, off:off + n] = parts[nm]
    return out


def pack_consts(mix_norm, q_norm, k_norm, sinks, sgu_norm, w_s, b_s, ffn_norm, conv_w, conv_b):
    cst = np.zeros((128, NCST), np.float32)
    for l in range(NL):
        vb = C_VEC + 192 * l
        cst[:, vb:vb + 8] = mix_norm[l].reshape(8, 128).T
        cst[:, vb + 8:vb + 16] = ffn_norm[l].reshape(8, 128).T
        for tap in range(3):
            cst[:, vb + 16 + 44 * tap:vb + 16 + 44 * (tap + 1)] = conv_w[l, tap].reshape(44, 128).T
        cst[:, vb + 148:vb + 192] = conv_b[l].reshape(44, 128).T
        cst[0:64, C_QK + 2 * l] = q_norm[l]
        cst[64:128, C_QK + 2 * l] = q_norm[l]
        cst[0:64, C_QK + 2 * l + 1] = k_norm[l]
        cst[64:128, C_QK + 2 * l + 1] = k_norm[l]
        cst[:, C_SGUG + 512 * l:C_SGUG + 512 * (l + 1)] = sgu_norm[l][None, :]
        for gp in range(4):
            cst[0:64, C_BSB + 512 * l + gp * 128:C_BSB + 512 * l + (gp + 1) * 128] = b_s[l, 2 * gp][None, :]
            cst[64:128, C_BSB + 512 * l + gp * 128:C_BSB + 512 * l + (gp + 1) * 128] = b_s[l, 2 * gp + 1][None, :]
    k = np.arange(128)[:, None]
    q = np.arange(128)[None, :]
    for g in range(2):
        for j in range(4):
            slope = 2.0 ** (-(4 * g + j + 1))
            dist_prev = q + 128 - k
            dist_cur = q - k
            bp = np.where(dist_prev < 128, -slope * dist_prev, -30000.0)
            bc = np.where(dist_cur >= 0, -slope * dist_cur, -30000.0)
            pr_, sl_ = j // 2, j % 2
            o_ = C_ABIAS + (g * 2 + sl_) * 512 + pr_ * 128
            cst[:, o_:o_ + 128] = bp
            cst[:, o_ + 256:o_ + 384] = bc
    cst[:, C_TRIL:C_TRIL + 128] = (k <= q).astype(np.float32)
    srow = np.zeros((1, 2048), np.float32)
    for l in range(NL):
        for g in range(2):
            for sl in range(2):
                for pr in range(2):
                    base = (((l * 2 + g) * 2 + sl) * 2 + pr) * 128
                    srow[0, base:base + 128] = sinks[l, 4 * g + 2 * pr + sl]
    wst = np.ascontiguousarray(np.transpose(w_s, (3, 0, 1, 2)).reshape(128, NL * 8 * 128)).astype(np.float32)
    return cst, srow, wst


_NC_CACHE = {}
DBG_STOP = None


class _Stop(Exception):
    pass


_CUR = [0, 0]


def _ck(k):
    if DBG_STOP is not None and DBG_STOP == (_CUR[0], _CUR[1], k):
        raise _Stop()


def run(x, params, n_cores, layers=(0, 1)):
    B, S_, _ = x.shape
    n_seq = B // n_cores
    key = (n_seq, S_, tuple(layers))
    if key not in _NC_CACHE:
        _NC_CACHE[key] = build(n_seq, S_, layers)
    nc = _NC_CACHE[key]
    wts = pack_weights(params["w_in"], params["w_oa"], params["w_ob"], params["w_out"], params["w_up"],
                       params["w_down"])
    cst, srow, wst = pack_consts(params["mix_norm"], params["q_norm"], params["k_norm"], params["sinks"],
                                 params["sgu_norm"], params["w_s"], params["b_s"], params["ffn_norm"],
                                 params["conv_w"], params["conv_b"])
    in_maps = []
    for c in range(n_cores):
        xc = np.ascontiguousarray(np.transpose(x[c * n_seq:(c + 1) * n_seq], (0, 2, 1)))
        in_maps.append({"xT": xc, "wts": wts, "cst": cst, "srow": srow, "wst": wst})
    res = run_bass_kernel_spmd(nc, in_maps, core_ids=list(range(n_cores)))
    outs = [np.transpose(r["yT"], (0, 2, 1)) for r in res.results]
    return np.ascontiguousarray(np.concatenate(outs, axis=0)).astype(np.float32)


def kernel(**inputs):
    inputs = {k: np.asarray(v) for k, v in inputs.items()}
    x = inputs.pop("x").astype(np.float32)
    params = {k: v.astype(np.float32) for k, v in inputs.items()}
    return run(x, params, 8)
```

```python
from contextlib import ExitStack

import numpy as np
import concourse.bass as bass
import concourse.mybir as mybir
from concourse.bass_utils import run_bass_kernel_spmd

F32 = mybir.dt.float32
BF16 = mybir.dt.bfloat16
AF = mybir.ActivationFunctionType
ALU = mybir.AluOpType

D = 1024
NL = 2
NH = 8
NKV = 2
HD = 64
DFF = 2816
NJ = DFF // 128
T = 512
NB = T // 128
EPS = 1e-6
NBUF = 7
SLAB = 4096

SLABS = ([("A0", 2048), ("B", 3072), ("A1", 2048), ("C", 4096), ("D", 4096), ("E", 4096), ("G", 4096),
          ("OAB0", 4096), ("F", 4096), ("H", 4096), ("OAB1", 4096), ("OUT0", 4096), ("OUT1", 4096)]
         + [("UP%d" % i, 4096) for i in range(11)] + [("DN%d" % c, 2816) for c in range(8)])
SLAB_OFF = {}
_o = 0
for _n, _s in SLABS:
    SLAB_OFF[_n] = (_o, _s)
    _o += _s
WPL = _o

C_VEC = 0
C_QK = 384
C_SGUG = 392
C_BSB = C_SGUG + 1024
C_ABIAS = C_BSB + 1024
C_TRIL = C_ABIAS + 2048
NCST = C_TRIL + 128


class Buf:
    __slots__ = ("name", "lw", "rd", "excl")

    def __init__(self, name, excl=False):
        self.name = name
        self.lw = None
        self.rd = []
        self.excl = excl


class Sched:
    ENG = ("pe", "act", "dve", "pool", "sp")

    def __init__(self):
        self.streams = {e: [] for e in self.ENG}
        self.cnt = {}
        self.waited = {e: {} for e in self.ENG}
        self.semnames = []
        self.snap = {}

    def newsem(self, key):
        self.cnt[key] = 0
        self.semnames.append(key)

    def _waits(self, eng, reads, writes):
        deps = {}
        def add(ev, raw):
            if ev is None:
                return
            k, v = ev
            if k == eng and (eng == "pe" or not raw):
                return
            if deps.get(k, 0) < v:
                deps[k] = v
        for b in reads:
            add(b.lw, True)
            if b.excl:
                for ev in b.rd:
                    add(ev, False)
        for b in writes:
            add(b.lw, True)
            for ev in b.rd:
                add(ev, False)
        wd = self.waited[eng]
        for k, v in sorted(deps.items(), key=lambda kv: -kv[1]):
            if wd.get(k, 0) >= v:
                continue
            wd[k] = v
            self.streams[eng].append(("wait", k, v))
            for k2, v2 in self.snap.get((k, v), {}).items():
                if wd.get(k2, 0) < v2:
                    wd[k2] = v2

    def _commit(self, ev, reads, writes, eng=None):
        if eng is not None:
            self.snap[ev] = dict(self.waited[eng])
        for b in reads:
            b.rd.append(ev)
        for b in writes:
            b.lw = ev
            b.rd = []

    def op(self, eng, fn, reads=(), writes=()):
        self._waits(eng, reads, writes)
        self.cnt[eng] += 1
        self.streams[eng].append(("op", fn, eng, 1))
        self._commit((eng, self.cnt[eng]), reads, writes, eng)

    def dma(self, eng, sem, fn, reads=(), writes=()):
        self._waits(eng, reads, writes)
        self.cnt[sem] += 16
        self.streams[eng].append(("op", fn, sem, 16))
        self._commit((sem, self.cnt[sem]), reads, writes, eng)

    def wait_all(self, eng, bufs):
        self._waits(eng, bufs, bufs)

    def replay(self, eng, handle, sems):
        pend = None
        for it in self.streams[eng]:
            if it[0] == "wait":
                if pend is not None:
                    handle.wait_ge(sems[pend[1]], pend[2])
                pend = it
            else:
                res = it[1](handle)
                first, last = res if isinstance(res, tuple) else (res, res)
                if pend is not None:
                    first._wait_ge(sems[pend[1]], pend[2])
                    pend = None
                last.then_inc(sems[it[2]], it[3])
        if pend is not None:
            handle.wait_ge(sems[pend[1]], pend[2])


class Rot:
    def __init__(self, tiles, name):
        self.tiles = tiles
        self.bufs = [Buf("%s%d" % (name, i)) for i in range(len(tiles))]
        self.i = 0

    def next(self):
        i = self.i
        self.i = (i + 1) % len(self.tiles)
        return self.tiles[i], self.bufs[i]


def build(n_seq, seq_len, layers=(0, 1)):
    nc = bass.Bass("TRN2", target_bir_lowering=False)
    S = Sched()
    tiles_per_seq = seq_len // T
    n_tiles = n_seq * tiles_per_seq
    xT_d = nc.dram_tensor("xT", [n_seq, D, seq_len], F32, kind="ExternalInput").ap()
    wts_d = nc.dram_tensor("wts", [NL, 128, WPL], F32, kind="ExternalInput").ap()
    cst_d = nc.dram_tensor("cst", [128, NCST], F32, kind="ExternalInput").ap()
    srow_d = nc.dram_tensor("srow", [1, 2048], F32, kind="ExternalInput").ap()
    wst_d = nc.dram_tensor("wst", [128, 2048], F32, kind="ExternalInput").ap()
    yT_d = nc.dram_tensor("yT", [n_seq, D, seq_len], F32, kind="ExternalOutput").ap()
    wbf_d = nc.dram_tensor("wbf", [NL, 128, WPL], BF16).ap()

    es = ExitStack()
    with es:
        def sb(name, shape, dt):
            return es.enter_context(nc.sbuf_tensor("s_" + name, shape, dt))

        cst = sb("cst", [128, NCST], F32)
        srow = sb("srow", [128, 2048], BF16)
        wsT = sb("wsT", [128, 16, 128], BF16)
        ones = sb("ones", [128, 128], BF16)
        xres = [sb("xres%d" % i, [128, 8, T], F32) for i in range(2)]
        hT = sb("hT", [128, 8, T], BF16)
        sqt = sb("sqt", [128, 4, T], BF16)
        NFR = 10
        frt = sb("frt", [128, NFR, 512], F32)
        qT = sb("qT", [128, 4, T], BF16)
        kT = [sb("kT%d" % l, [128, 2, T + 128], BF16) for l in range(NL)]
        onesbd = sb("onesbd", [128, 128], BF16)
        vtok = [sb("vtok%d" % l, [128, NB + 1, 128], BF16) for l in range(NL)]
        qsq = sb("qsq", [128, 3, T], BF16)
        junk = sb("junk", [128, 512], BF16)
        ssv = sb("ssv", [128, 8], F32)
        lnv = sb("lnv", [128, 8], F32)
        rv = sb("rv", [128, 8], F32)
        vn = sb("vn", [128, NB, 512], BF16)
        pT = sb("pT", [128, 6, 512], BF16)
        rden = sb("rden", [128, 4, 256], F32)
        actT = sb("actT", [128, NJ, T], BF16)
        mergedT = actT[:, 0:8, :]
        yattT = actT[:, 8:12, :]
        ysguT = actT[:, 12:16, :]
        uT = actT[:, 16:20, :]
        halo = [sb("halo%d" % l, [128, 2 * NJ, 2], F32) for l in range(NL)]
        hc = sb("hc", [128, 2 * NJ, 2], F32)
        hctmp = sb("hctmp", [128, 2 * NJ], F32)
        b_hc = Buf("hc")
        b_hctmp = Buf("hctmp")
        wslab = sb("wslab", [128, NBUF, SLAB], BF16)
        psb = [es.enter_context(nc.psum_tensor("ps%d" % i, [128, 512], F32)) for i in range(8)]

        b_cst = Buf("cst")
        b_srow = Buf("srow")
        b_wsT = Buf("wsT")
        b_ones = Buf("ones")
        b_xres = [[Buf("xres%d_%d" % (i, c)) for c in range(8)] for i in range(2)]
        b_hT = [Buf("hT%d" % c) for c in range(8)]
        b_qT = [Buf("qT%d" % h) for h in range(4)]
        b_onesbd = Buf("onesbd")
        b_kprev = [Buf("kprev%d" % l) for l in range(NL)]
        b_kcur = [[Buf("kcur%d_%d" % (l, g)) for g in range(2)] for l in range(NL)]
        b_vprev = [Buf("vprev%d" % l) for l in range(NL)]
        b_vcur = [Buf("vcur%d" % l) for l in range(NL)]
        b_junk = Buf("junk")
        b_ssv = [Buf("ssv%d" % i) for i in range(8)]
        b_lnv = [Buf("lnv%d" % i) for i in range(8)]
        b_rv = [Buf("rv%d" % i) for i in range(8)]
        b_vn = [Buf("vn%d" % b) for b in range(NB)]
        b_actT = [Buf("actT%d" % j) for j in range(NJ)]
        b_merged = b_actT[0:8]
        b_yatt = [[b_actT[8 + c]] * NB for c in range(4)]
        b_uT = b_actT[16:20]
        b_halo = [[Buf("halo%d_%d" % (l, ch)) for ch in range(2 * NJ)] for l in range(NL)]
        b_wslab = [Buf("wslab%d" % i) for i in range(NBUF)]
        b_ps = [Buf("ps%d" % i, excl=True) for i in range(8)]

        sqr = Rot([sqt[:, i, :] for i in range(4)], "sq")
        qsqr = Rot([qsq[:, i, :] for i in range(3)], "qsq")
        fr = Rot([frt[:, i, :] for i in range(NFR)], "fr")
        rqr = rq2r = gvr = efr = tmr = sar = sbr = t1r = t2r = agr = avr = sgr = fr
        srowf = frt[0:1, 0:4, :]
        wstage = frt[:, 4:8, :]
        pTr = Rot([pT[:, i, :] for i in range(6)], "pT")
        rdr = Rot([rden[:, i, :] for i in range(4)], "rden")
        psr = Rot([p[:, :] for p in psb[0:7]], "psr")
        psr.bufs = b_ps[0:7]
        ss_ps, b_ss = psb[7][:, :], b_ps[7]
        small_i = [0]

        for e in ("pe", "act", "dve", "pool"):
            S.newsem(e)
        for i in range(NBUF):
            S.newsem("w%d" % i)
            S.newsem("ws%d" % i)
        b_wd = {(l, nm): Buf("wd%d_%s" % (l, nm)) for l in range(NL) for (nm, _) in SLABS}
        for k in ("cst0", "cst1", "cst2", "xl0", "xl1", "xs0", "xs1"):
            S.newsem(k)

        def act(out, in_, func, reads, writes, bias=None, scale=None, accum_out=None):
            kw = {}
            if bias is not None:
                kw["bias"] = bias
            if scale is not None:
                kw["scale"] = scale
            if accum_out is not None:
                kw["accum_out"] = accum_out
            S.op("act", lambda e: e.activation(out=out, in_=in_, func=func, **kw), reads, writes)

        def tt(out, in0, in1, op, reads, writes, eng="dve"):
            S.op(eng, lambda e: e.tensor_tensor(out=out, in0=in0, in1=in1, op=op), reads, writes)

        def stt(out, in0, scalar, in1, op0, op1, reads, writes, eng="dve"):
            S.op(eng, lambda e: e.scalar_tensor_tensor(out=out, in0=in0, scalar=scalar, in1=in1,
                                                        op0=op0, op1=op1), reads, writes)

        def cp(out, in_, reads, writes, eng="dve"):
            S.op(eng, lambda e: e.tensor_copy(out=out, in_=in_), reads, writes)

        def mm_group(out, pairs, reads, writes):
            def fn(e):
                ins = first = None
                n = len(pairs)
                for i, (l, r) in enumerate(pairs):
                    ins = e.matmul(out, l, r, start=(i == 0), stop=(i == n - 1))
                    first = first or ins
                return first, ins
            S.op("pe", fn, reads, writes)

        def mm_part(out, pairs, reads, writes, first, last):
            def fn(e):
                ins = fi = None
                n = len(pairs)
                for i, (l, r) in enumerate(pairs):
                    ins = e.matmul(out, l, r, start=(first and i == 0), stop=(last and i == n - 1))
                    fi = fi or ins
                return fi, ins
            S.op("pe", fn, reads, writes)

        def mm_multi(groups, reads, writes):
            def fn(e):
                ins = first = None
                for out, pairs in groups:
                    n = len(pairs)
                    for i, (l, r) in enumerate(pairs):
                        ins = e.matmul(out, l, r, start=(i == 0), stop=(i == n - 1))
                        first = first or ins
                return first, ins
            S.op("pe", fn, reads, writes)

        passes = [(ti, l) for ti in range(n_tiles) for l in layers]
        wseq = [(l, nm) for (_, l) in passes for (nm, _) in SLABS]
        wstate = {"issue": 0, "acq": 0}

        def w_issue():
            i = wstate["issue"]
            if i >= len(wseq):
                return
            wstate["issue"] = i + 1
            l, nm = wseq[i]
            off, n = SLAB_OFF[nm]
            slot = i % NBUF
            o = wslab[:, slot, 0:n]
            if passes[i // len(SLABS)][0] == 0:
                src = wts_d[l, :, off:off + n]
                S.dma("pool", "w%d" % slot, lambda e: e.dma_start(out=o, in_=src), (), (b_wslab[slot],))
            else:
                src = wbf_d[l, :, off:off + n]
                S.dma("sp", "w%d" % slot, lambda e: e.dma_start(out=o, in_=src), (b_wd[(l, nm)],),
                      (b_wslab[slot],))

        def w_acquire(expect):
            i = wstate["acq"]
            wstate["acq"] = i + 1
            assert wseq[i][1] == expect, (wseq[i], expect)
            slot = i % NBUF
            if passes[i // len(SLABS)][0] == 0 and n_tiles > 1:
                l, nm = wseq[i]
                off, n = SLAB_OFF[nm]
                dst = wbf_d[l, :, off:off + n]
                srcs = wslab[:, slot, 0:n]
                S.dma("sp", "ws%d" % slot, lambda e: e.dma_start(out=dst, in_=srcs), (b_wslab[slot],),
                      (b_wd[(l, nm)],))
            return wslab[:, slot, :], b_wslab[slot]

        def w_release(n=1):
            for _ in range(n):
                w_issue()

        S.dma("sp", "cst0", lambda e: e.dma_start(out=cst[:, :], in_=cst_d[:, :]), (), (b_cst,))
        S.dma("sp", "cst1", lambda e: e.dma_start(out=srowf, in_=srow_d.rearrange("o (a n) -> o a n", a=4)),
              (), tuple(fr.bufs[0:4]))
        S.dma("sp", "cst2", lambda e: e.dma_start(out=wstage, in_=wst_d.rearrange("p (a n) -> p a n", a=4)),
              (), tuple(fr.bufs[4:8]))
        for _ in range(NBUF):
            w_issue()
        S.op("dve", lambda e: e.memset(ones[:, :], 1.0), (), (b_ones,))
        S.op("dve", lambda e: e.memset(onesbd[:, :], 0.0), (), (b_onesbd,))
        S.op("dve", lambda e: e.memset(onesbd[0:64, 0:64], 1.0), (), (b_onesbd,))
        S.op("dve", lambda e: e.memset(onesbd[64:128, 64:128], 1.0), (), (b_onesbd,))
        S.op("dve", lambda e: e.memset(srow[:, :], 0.0), (), (b_srow,))
        act(srow[0:1, :].rearrange("o (a n) -> o a n", a=4), srowf, AF.Exp, tuple(fr.bufs[0:4]), (b_srow,))
        act(cst[:, C_ABIAS:C_ABIAS + 2048], cst[:, C_ABIAS:C_ABIAS + 2048], AF.Exp, (b_cst,), (b_cst,))
        for i in range(16):
            tt(wsT[:, i, :], wstage[:, i // 4, (i % 4) * 128:(i % 4 + 1) * 128], cst[:, C_TRIL:C_TRIL + 128], ALU.mult,
               (fr.bufs[4 + i // 4], b_cst), (b_wsT,))

        def rms_sq_act(xr, bxr, c):
            sq, bsq = sqr.next()
            act(sq, xr[:, c, :], AF.Square, (bxr[c],), (bsq,))
            return sq, bsq

        def rms_sq_mm(sqb, c):
            sq, bsq = sqb
            S.op("pe", (lambda e: e.matmul(ss_ps, ones[:, :], sq, start=(c == 0), stop=(c == 7))),
                 (bsq, b_ones), (b_ss,))

        def rms_finish(xr, bxr, gcol):
            ps, bps = ss_ps, b_ss
            rtmp, b_rtmp = fr.next()
            act(rtmp, ps, AF.Ln, (bps,), (b_rtmp,), bias=EPS, scale=1.0 / D)
            rstd, b_rstd = fr.next()
            act(rstd, rtmp, AF.Exp, (b_rtmp,), (b_rstd,), scale=-0.5)
            for c in range(8):
                stt(hT[:, c, :], xr[:, c, :], cst[:, gcol + c:gcol + c + 1], rstd, ALU.mult, ALU.mult,
                    (bxr[c], b_rstd, b_cst), (b_hT[c],))

        def headnorm_a(ps, bps, gcolumn, out, bout):
            sq, bsq = qsqr.next()
            act(sq, ps, AF.Square, (bps,), (bsq,))
            return lambda: headnorm_b(ps, bps, gcolumn, out, bout, sq, bsq)

        def headnorm_b(ps, bps, gcolumn, out, bout, sq, bsq):
            ps2, bps2 = psr.next()
            S.op("pe", lambda e: e.matmul(ps2, onesbd[:, :], sq, start=True, stop=True),
                 (bsq, b_onesbd), (bps2,))
            r1, br1 = rqr.next()
            act(r1, ps2, AF.Ln, (bps2,), (br1,), bias=EPS, scale=1.0 / HD)
            r2, br2 = rq2r.next()
            act(r2, r1, AF.Exp, (br1,), (br2,), scale=-0.5)
            stt(out, ps, cst[:, gcolumn:gcolumn + 1], r2, ALU.mult, ALU.mult,
                (bps, br2, b_cst), tuple(bout))

        def emit_xload(ti):
            s_idx = ti // tiles_per_seq
            t0 = (ti % tiles_per_seq) * T
            xi = ti % 2
            src = xT_d[s_idx].rearrange("(c p) t -> p c t", p=128)[:, :, t0:t0 + T]
            dstt = xres[xi][:, :, :]
            S.dma("sp", "xl%d" % xi, lambda e: e.dma_start(out=dstt, in_=src), (), tuple(b_xres[xi]))

        def kouter(outs, lhs_fn, reads_w, wr_bufs):
            for k in range(8):
                groups = [(o, lhs_fn(i, k), hT[:, k, :]) for i, o in enumerate(outs)]
                def fn(e, groups=groups, k=k):
                    ins = first = None
                    for (o, l_, r_) in groups:
                        ins = e.matmul(o, l_, r_, start=(k == 0), stop=(k == 7))
                        first = first or ins
                    return first, ins
                S.op("pe", fn, (*reads_w, b_hT[k]), tuple(wr_bufs))

        def run_pass(ti, l, first_layer, last_layer, nxt):
            _CUR[0], _CUR[1] = ti, l
            s_idx = ti // tiles_per_seq
            tt_i = ti % tiles_per_seq
            t0 = tt_i * T
            first_in_seq = tt_i == 0
            last_in_seq = tt_i == tiles_per_seq - 1
            xi = ti % 2
            xr = xres[xi]
            bxr = b_xres[xi]
            vbase = C_VEC + 192 * l
            G1, G2 = vbase, vbase + 8
            CW0, CW1, CW2, CB = vbase + 16, vbase + 60, vbase + 104, vbase + 148
            QG, KG = C_QK + 2 * l, C_QK + 2 * l + 1

            _ck(0)
            rms_finish(xr, bxr, G1)
            _ck(1)

            pend_hn = []

            def flush_hn():
                while pend_hn:
                    pend_hn.pop(0)()

            def proj_head(vW, bW, cols, gcolumn, out, bout):
                ps, bps = psr.next()
                mm_group(ps, [(vW[:, k, cols], hT[:, k, :]) for k in range(8)], (bW, *b_hT), (bps,))
                part_b = headnorm_a(ps, bps, gcolumn, out, bout)
                flush_hn()
                pend_hn.append(part_b)

            wA0, bA0 = w_acquire("A0")
            vA0 = wA0[:, 0:2048].rearrange("p (k n) -> p k n", k=8)
            qps = [psr.next() for _ in range(2)]
            kouter([p[0] for p in qps], lambda i, k: vA0[:, k, i * 128:(i + 1) * 128], (bA0,),
                   [p[1] for p in qps])
            for h in range(2):
                pend_hn.append(headnorm_a(qps[h][0], qps[h][1], QG, qT[:, h, :], (b_qT[h],)))
            w_release()
            _ck(2)

            wB, bB = w_acquire("B")
            vB = wB[:, 0:3072].rearrange("p (k n) -> p k n", k=8)
            for g in range(2):
                proj_head(vB, bB, slice(g * 128, (g + 1) * 128), KG, kT[l][:, g, 128:128 + T], (b_kcur[l][g],))
            ps, bps = psr.next()
            mm_multi([(ps[:, b * 128:(b + 1) * 128],
                       [(hT[:, k, b * 128:(b + 1) * 128], vB[:, k, 256:384]) for k in range(8)])
                      for b in range(NB)], (bB, *b_hT), (bps,))
            flush_hn()
            cp(vtok[l][:, 1:NB + 1, :], ps.rearrange("p (b n) -> p b n", b=NB), (bps,), (b_vcur[l],))
            w_release()

            def sgu_block(b):
                ps, bps = psr.next()
                groups = []
                for gp in range(4):
                    for sl in range(2):
                        g = 2 * gp + sl
                        groups.append((ps[64 * sl:64 * sl + 64, gp * 128:(gp + 1) * 128],
                                       [(vn[:, b, g * 64:(g + 1) * 64], wsT[:, l * 8 + g, :])]))
                mm_multi(groups, (b_vn[b], b_wsT), (bps,))
                tm, btm = tmr.next()
                tt(tm, ps, cst[:, C_BSB + 512 * l:C_BSB + 512 * l + 512], ALU.add, (bps, b_cst), (btm,))
                tt(ysguT[:, :, b * 128:(b + 1) * 128], tm.rearrange("p (a n) -> p a n", a=4),
                   uT[:, :, b * 128:(b + 1) * 128], ALU.mult, (btm, *b_uT), tuple(b_actT[12:16]))

            def att_stage1(b, g):
                halves = []
                if not (first_in_seq and b == 0):
                    halves.append(0)
                halves.append(1)
                c0 = 256 * halves[0]
                banks = [psr.next(), psr.next()]
                kb = [b_kcur[l][g]] + ([b_kprev[l]] if (b == 0 and 0 in halves) else [])
                for hf in halves:
                    kcols = slice(128 * (b + hf), 128 * (b + hf) + 128)
                    for sl in range(2):
                        ps, bps = banks[sl]
                        lhs_ = kT[l][64 * sl:64 * sl + 64, g, kcols]
                        rhs_ = qT[64 * sl:64 * sl + 64, 2 * g:2 * g + 2, b * 128:(b + 1) * 128]
                        out_ = ps[:, hf * 256:hf * 256 + 256].rearrange("p (a n) -> p a n", a=2)
                        S.op("pe", (lambda e, out_=out_, lhs_=lhs_, rhs_=rhs_: e.matmul(
                            out_, lhs_, rhs_, start=True, stop=True)),
                            (*kb, b_qT[2 * g], b_qT[2 * g + 1]), (bps,))
                pts = {}
                for sl in range(2):
                    ps, bps = banks[sl]
                    e_, be_ = efr.next()
                    act(e_[:, c0:512], ps[:, c0:512], AF.Exp, (bps,), (be_,), scale=0.125)
                    p_, bp_ = pTr.next()
                    col = C_ABIAS + (g * 2 + sl) * 512
                    tt(p_[:, c0:512], e_[:, c0:512], cst[:, col + c0:col + 512], ALU.mult, (be_, b_cst), (bp_,),
                       eng="pool")
                    pts[sl] = (p_, bp_)
                return halves, pts

            def att_stage2(b, g, halves, pts):
                yd, byd = psr.next()
                groups = []
                sbase = ((l * 2 + g) * 2) * 256
                for sl in range(2):
                    ypairs, dpairs = [], []
                    for hf in halves:
                        rhs = pts[sl][0][:, hf * 256:hf * 256 + 256].rearrange("p (a n) -> p a n", a=2)
                        ypairs.append((vtok[l][:, b + hf, g * 64:(g + 1) * 64], rhs))
                        dpairs.append((ones[:, 0:64], rhs))
                    dpairs.append((ones[:, 0:64],
                                   srow[:, sbase + sl * 256:sbase + sl * 256 + 256].rearrange("p (a n) -> p a n", a=2)))
                    groups.append((yd[64 * sl:64 * sl + 64, 0:256].rearrange("p (a n) -> p a n", a=2), ypairs))
                    groups.append((yd[64 * sl:64 * sl + 64, 256:512].rearrange("p (a n) -> p a n", a=2), dpairs))
                vb = [b_vcur[l]] + ([b_vprev[l]] if b == 0 and 0 in halves else [])
                mm_multi(groups, (pts[0][1], pts[1][1], *vb, b_ones, b_srow), (byd,))
                rd, brd = rdr.next()
                if g == 0:
                    S.op("dve", lambda e, rd=rd, yd=yd: e.reciprocal(out=rd, in_=yd[:, 256:512]), (byd,), (brd,))
                else:
                    rl, brl = rdr.next()
                    act(rl, yd[:, 256:512], AF.Ln, (byd,), (brl,))
                    act(rd, rl, AF.Exp, (brl,), (brd,), scale=-1.0)
                tt(yattT[:, 2 * g:2 * g + 2, b * 128:(b + 1) * 128],
                   yd[:, 0:256].rearrange("p (a n) -> p a n", a=2),
                   rd.rearrange("p (a n) -> p a n", a=2), ALU.mult, (byd, brd),
                   (b_yatt[2 * g][b], b_yatt[2 * g + 1][b]))

            _ck(4)
            slabs = {}

            def get_slab(nm):
                if nm not in slabs:
                    w_, b_ = w_acquire(nm)
                    n_ = 2048 if nm == "A1" else 4096
                    slabs[nm] = (w_[:, 0:n_].rearrange("p (k n) -> p k n", k=8), b_)
                return slabs[nm]

            def u_qhead(h):
                vA1, bA1 = get_slab("A1")
                proj_head(vA1, bA1, slice((h - 2) * 128, (h - 1) * 128), QG, qT[:, h, :], (b_qT[h],))
                if h == 3:
                    w_release()

            def u_su(c):
                vC, bC = get_slab("C")
                flush_hn()
                ps, bps = psr.next()
                mm_group(ps, [(vC[:, k, c * 128:(c + 1) * 128], hT[:, k, :]) for k in range(8)],
                         (bC, *b_hT), (bps,))
                act(uT[:, c, :], ps, AF.Gelu_apprx_tanh, (bps,), (b_uT[c],))
                if c == 3:
                    w_release()

            def u_sv(b):
                vD, bD = get_slab("D")
                ps, bps = psr.next()
                mm_group(ps, [(hT[:, k, b * 128:(b + 1) * 128], vD[:, k, :]) for k in range(8)],
                         (bD, *b_hT), (bps,))
                g_, bg_ = gvr.next()
                act(g_, ps, AF.Gelu_apprx_tanh, (bps,), (bg_,))
                si = small_i[0]
                small_i[0] = (si + 1) % 8
                S.op("dve", lambda e, g_=g_, si=si: e.scalar_tensor_tensor(
                    out=junk[:, :], in0=g_, scalar=1.0, in1=g_, op0=ALU.mult, op1=ALU.mult,
                    accum_out=ssv[:, si:si + 1]), (bg_,), (b_junk, b_ssv[si]))
                act(lnv[:, si:si + 1], ssv[:, si:si + 1], AF.Ln, (b_ssv[si],), (b_lnv[si],), bias=EPS, scale=1.0 / 512)
                act(rv[:, si:si + 1], lnv[:, si:si + 1], AF.Exp, (b_lnv[si],), (b_rv[si],), scale=-0.5)
                stt(vn[:, b, :], g_, rv[:, si:si + 1], cst[:, C_SGUG + 512 * l:C_SGUG + 512 * l + 512],
                    ALU.mult, ALU.mult, (bg_, b_rv[si], b_cst), (b_vn[b],))
                if b == NB - 1:
                    w_release()

            units = ([lambda h=h: u_qhead(h) for h in range(2, 4)] + [lambda c=c: u_su(c) for c in range(4)]
                     + [lambda b=b: u_sv(b) for b in range(NB)])
            its = [(b, 0) for b in range(NB)] + [(b, 1) for b in range(NB)]
            pend = [att_stage1(*its[0]), att_stage1(*its[1])]
            for i, (b, g) in enumerate(its):
                if g == 0:
                    for _ in range(3 if b < 2 else 2):
                        units.pop(0)()
                else:
                    sgu_block(b)
                if i + 2 < len(its):
                    pend.append(att_stage1(*its[i + 2]))
                att_stage2(b, g, *pend.pop(0))
            assert not units
            _ck(3)
            if not last_in_seq:
                cp(kT[l][:, :, 0:128], kT[l][:, :, T:T + 128], (*b_kcur[l],), (b_kprev[l],))
                cp(vtok[l][:, 0, :], vtok[l][:, NB, :], (b_vcur[l],), (b_vprev[l],))

            _ck(5)
            for half in range(2):
                wE, bE = w_acquire("EF"[half])
                wG, bG = w_acquire("GH"[half])
                wO, bO = w_acquire("OAB%d" % half)
                vE = wE.rearrange("p (k n) -> p k n", k=8)
                vG = wG.rearrange("p (k n) -> p k n", k=8)
                vO = wO.rearrange("p (m k n) -> p m k n", m=2, k=4)
                for c4 in range(4):
                    c = 4 * half + c4
                    cs = slice(c4 * 128, (c4 + 1) * 128)
                    pga, bpga = psr.next()
                    mm_group(pga, [(vE[:, k, cs], hT[:, k, :]) for k in range(8)], (bE, *b_hT), (bpga,))
                    pgb, bpgb = psr.next()
                    mm_group(pgb, [(vG[:, k, cs], hT[:, k, :]) for k in range(8)], (bG, *b_hT), (bpgb,))
                    pa, bpa = psr.next()
                    mm_group(pa, [(vO[:, 0, kc, cs], yattT[:, kc, :]) for kc in range(4)],
                             (bO, *b_actT[8:12]), (bpa,))
                    pb, bpb = psr.next()
                    mm_group(pb, [(vO[:, 1, kc, cs], ysguT[:, kc, :]) for kc in range(4)], (bO, *b_actT[12:16]), (bpb,))
                    sa, bsa = sar.next()
                    act(sa, pga, AF.Sigmoid, (bpga,), (bsa,))
                    sb_, bsb_ = sbr.next()
                    act(sb_, pgb, AF.Sigmoid, (bpgb,), (bsb_,))
                    t1, bt1 = t1r.next()
                    tt(t1, pa, sa, ALU.mult, (bpa, bsa), (bt1,))
                    t2, bt2 = t2r.next()
                    tt(t2, pb, sb_, ALU.mult, (bpb, bsb_), (bt2,))
                    tt(mergedT[:, c, :], t1, t2, ALU.add, (bt1, bt2), (b_merged[c],), eng="pool")
                w_release(3)
            sqbs = {}
            for half in range(2):
                wO, bO = w_acquire("OUT%d" % half)
                vO = wO.rearrange("p (k n) -> p k n", k=8)
                for c4 in range(4):
                    c = 4 * half + c4
                    po, bpo = psr.next()
                    mm_group(po, [(vO[:, k, c4 * 128:(c4 + 1) * 128], mergedT[:, k, :]) for k in range(8)],
                             (bO, *b_merged), (bpo,))
                    if c >= 1:
                        rms_sq_mm(sqbs[c - 1], c - 1)
                    tt(xr[:, c, :], po, xr[:, c, :], ALU.add, (bpo, bxr[c]), (bxr[c],))
                    sqbs[c] = rms_sq_act(xr, bxr, c)
                w_release()
            rms_sq_mm(sqbs[7], 7)

            _ck(6)
            rms_finish(xr, bxr, G2)
            if last_layer and nxt is not None:
                emit_xload(nxt[0])

            def ffn_epilogue(j, pg, bpg, pv, bpv):
                ag, bag = agr.next()
                av, bav = avr.next()
                items = ((pg, bpg, ag, bag, j), (pv, bpv, av, bav, NJ + j))
                for (ps, bps, a_, ba_, ch) in items:
                    act(a_, ps, AF.Identity, (bps, b_cst), (ba_,),
                        bias=cst[:, CB + ch:CB + ch + 1], scale=cst[:, CW2 + ch:CW2 + ch + 1])
                for (ps, bps, a_, ba_, ch) in items:
                    stt(a_[:, 1:T], ps[:, 0:T - 1], cst[:, CW1 + ch:CW1 + ch + 1], a_[:, 1:T], ALU.mult, ALU.add,
                        (bps, ba_, b_cst), (ba_,))
                for (ps, bps, a_, ba_, ch) in items:
                    stt(a_[:, 2:T], ps[:, 0:T - 2], cst[:, CW0 + ch:CW0 + ch + 1], a_[:, 2:T], ALU.mult, ALU.add,
                        (bps, ba_, b_cst), (ba_,))
                if not first_in_seq:
                    for (ps, bps, a_, ba_, ch) in items:
                        tt(a_[:, 0:2], a_[:, 0:2], hc[:, ch, :], ALU.add, (ba_, b_hc), (ba_,), eng="pool")
                if not last_in_seq:
                    for (ps, bps, a_, ba_, ch) in items:
                        act(halo[l][:, ch, :], ps[:, T - 2:T], AF.Identity, (bps,), (b_halo[l][ch],))
                sg, bsg = sgr.next()
                act(sg, ag, AF.Silu, (bag,), (bsg,))
                tt(actT[:, j, :], sg, av, ALU.mult, (bsg, bav), (b_actT[j],), eng="pool")

            if not first_in_seq:
                bh = tuple(b_halo[l])
                tt(hc[:, :, 1], halo[l][:, :, 1], cst[:, CW0:CW0 + 44], ALU.mult, (*bh, b_cst), (b_hc,))
                tt(hc[:, :, 0], halo[l][:, :, 0], cst[:, CW0:CW0 + 44], ALU.mult, (*bh, b_cst), (b_hc,))
                tt(hctmp[:, :], halo[l][:, :, 1], cst[:, CW1:CW1 + 44], ALU.mult, (*bh, b_cst), (b_hctmp,))
                tt(hc[:, :, 0], hc[:, :, 0], hctmp[:, :], ALU.add, (b_hc, b_hctmp), (b_hc,))
            for i in range(11):
                wU, bU = w_acquire("UP%d" % i)
                vU = wU.rearrange("p (k n) -> p k n", k=8)
                if i == 0:
                    pss = [psr.next() for _ in range(4)]
                    offs = [0, 256, 128, 384]
                    kouter([p[0] for p in pss], lambda q, k: vU[:, k, offs[q]:offs[q] + 128], (bU,),
                           [p[1] for p in pss])
                    ffn_epilogue(0, pss[0][0], pss[0][1], pss[1][0], pss[1][1])
                    ffn_epilogue(1, pss[2][0], pss[2][1], pss[3][0], pss[3][1])
                else:
                    for jj in range(2):
                        j = 2 * i + jj
                        pg, bpg = psr.next()
                        mm_group(pg, [(vU[:, k, jj * 128:(jj + 1) * 128], hT[:, k, :]) for k in range(8)],
                                 (bU, *b_hT), (bpg,))
                        pv, bpv = psr.next()
                        mm_group(pv, [(vU[:, k, 256 + jj * 128:256 + (jj + 1) * 128], hT[:, k, :]) for k in range(8)],
                                 (bU, *b_hT), (bpv,))
                        ffn_epilogue(j, pg, bpg, pv, bpv)
                w_release()
            _ck(7)
            if nxt is not None:
                nxr, nbxr = xres[nxt[0] % 2], b_xres[nxt[0] % 2]
            sqbs = {}

            def down_tail(c, pd, bpd):
                if nxt is not None and c >= 1:
                    rms_sq_mm(sqbs[c - 1], c - 1)
                tt(xr[:, c, :], pd, xr[:, c, :], ALU.add, (bpd, bxr[c]), (bxr[c],))
                if nxt is not None:
                    sqbs[c] = rms_sq_act(nxr, nbxr, c)

            JS = 14
            first = []
            for c in range(4):
                wDn, bDn = w_acquire("DN%d" % c)
                vDn = wDn[:, 0:2816].rearrange("p (k n) -> p k n", k=NJ)
                pd, bpd = psr.next()
                mm_part(pd, [(vDn[:, j, :], actT[:, j, :]) for j in range(JS)], (bDn, *b_actT[0:JS]), (bpd,), True, False)
                first.append((vDn, bDn, pd, bpd))
            for c in range(4):
                vDn, bDn, pd, bpd = first[c]
                mm_part(pd, [(vDn[:, j, :], actT[:, j, :]) for j in range(JS, NJ)], (bDn, *b_actT[JS:NJ]), (bpd,), False, True)
                down_tail(c, pd, bpd)
                w_release()
            for c in range(4, 8):
                wDn, bDn = w_acquire("DN%d" % c)
                vDn = wDn[:, 0:2816].rearrange("p (k n) -> p k n", k=NJ)
                pd, bpd = psr.next()
                mm_group(pd, [(vDn[:, j, :], actT[:, j, :]) for j in range(NJ)], (bDn, *b_actT), (bpd,))
                down_tail(c, pd, bpd)
                w_release()
            if nxt is not None:
                rms_sq_mm(sqbs[7], 7)

            if last_layer:
                dst = yT_d[s_idx].rearrange("(c p) t -> p c t", p=128)[:, :, t0:t0 + T]
                S.dma("sp", "xs%d" % xi, lambda e: e.dma_start(out=dst, in_=xr[:, :, :]), tuple(bxr), ())

        plist = [(ti, li) for ti in range(n_tiles) for li in range(len(layers))]
        emit_xload(0)
        sq0 = [rms_sq_act(xres[0], b_xres[0], c) for c in range(4)]
        for c in range(8):
            rms_sq_mm(sq0[c] if c < 4 else rms_sq_act(xres[0], b_xres[0], c), c)
        try:
            for pi, (ti, li) in enumerate(plist):
                nxt = plist[pi + 1] if pi + 1 < len(plist) else None
                run_pass(ti, layers[li], li == 0, li == len(layers) - 1, nxt)
        except _Stop:
            dst = yT_d[0].rearrange("(c p) t -> p c t", p=128)[:, :, 0:T]
            S.dma("sp", "xs0", lambda e: e.dma_start(out=dst, in_=xres[0][:, :, :]), tuple(b_xres[0]), ())
        S.wait_all("sp", [b for i in range(2) for b in b_xres[i]])

        sems = {k: es.enter_context(nc.semaphore(k)) for k in S.semnames}
        block = es.enter_context(nc.Block())

        @block.tensor
        def _(e):
            S.replay("pe", e, sems)

        @block.scalar
        def _(e):
            S.replay("act", e, sems)

        @block.vector
        def _(e):
            S.replay("dve", e, sems)

        @block.gpsimd
        def _(e):
            S.replay("pool", e, sems)

        @block.sync
        def _(e):
            S.replay("sp", e, sems)
    return nc


def _pkn(w):
    kc = w.shape[0] // 128
    return np.ascontiguousarray(w.reshape(kc, 128, -1).transpose(1, 0, 2).reshape(128, -1))


def pack_weights(w_in, w_oa, w_ob, w_out, w_up, w_down):
    out = np.empty((NL, 128, WPL), np.float32)
    for l in range(NL):
        parts = {
            "A0": _pkn(w_in[l][:, 0:256]), "A1": _pkn(w_in[l][:, 256:512]), "B": _pkn(np.concatenate([w_in[l][:, 512:576], w_in[l][:, 512:576], w_in[l][:, 576:640],
                                     w_in[l][:, 576:640], w_in[l][:, 640:768]], axis=1)),
            "C": _pkn(w_in[l][:, 768:1280]), "D": _pkn(w_in[l][:, 1280:1792]),
            "E": _pkn(w_in[l][:, 1792:2304]), "F": _pkn(w_in[l][:, 2304:2816]),
            "G": _pkn(w_in[l][:, 2816:3328]), "H": _pkn(w_in[l][:, 3328:3840]),
            "OAB0": np.concatenate([_pkn(w_oa[l][:, 0:512]), _pkn(w_ob[l][:, 0:512])], axis=1),
            "OAB1": np.concatenate([_pkn(w_oa[l][:, 512:1024]), _pkn(w_ob[l][:, 512:1024])], axis=1),
            "OUT0": _pkn(w_out[l][:, 0:512]), "OUT1": _pkn(w_out[l][:, 512:1024]),
        }
        for i in range(11):
            parts["UP%d" % i] = _pkn(np.concatenate(
                [w_up[l][:, 256 * i:256 * i + 256], w_up[l][:, DFF + 256 * i:DFF + 256 * i + 256]], axis=1))
        for c in range(8):
            parts["DN%d" % c] = _pkn(w_down[l][:, 128 * c:128 * c + 128])
        for nm, n in SLABS:
            off, _ = SLAB_OFF[nm]
            assert parts[nm].shape == (128, n), (nm, parts[nm].shape)
            out[l, :, off:off + n] = parts[nm]
    return out


def pack_consts(mix_norm, q_norm, k_norm, sinks, sgu_norm, w_s, b_s, ffn_norm, conv_w, conv_b):
    cst = np.zeros((128, NCST), np.float32)
    for l in range(NL):
        vb = C_VEC + 192 * l
        cst[:, vb:vb + 8] = mix_norm[l].reshape(8, 128).T
        cst[:, vb + 8:vb + 16] = ffn_norm[l].reshape(8, 128).T
        for tap in range(3):
            cst[:, vb + 16 + 44 * tap:vb + 16 + 44 * (tap + 1)] = conv_w[l, tap].reshape(44, 128).T
        cst[:, vb + 148:vb + 192] = conv_b[l].reshape(44, 128).T
        cst[0:64, C_QK + 2 * l] = q_norm[l]
        cst[64:128, C_QK + 2 * l] = q_norm[l]
        cst[0:64, C_QK + 2 * l + 1] = k_norm[l]
        cst[64:128, C_QK + 2 * l + 1] = k_norm[l]
        cst[:, C_SGUG + 512 * l:C_SGUG + 512 * (l + 1)] = sgu_norm[l][None, :]
        for gp in range(4):
            cst[0:64, C_BSB + 512 * l + gp * 128:C_BSB + 512 * l + (gp + 1) * 128] = b_s[l, 2 * gp][None, :]
            cst[64:128, C_BSB + 512 * l + gp * 128:C_BSB + 512 * l + (gp + 1) * 128] = b_s[l, 2 * gp + 1][None, :]
    k = np.arange(128)[:, None]
    q = np.arange(128)[None, :]
    for g in range(2):
        for j in range(4):
            slope = 2.0 ** (-(4 * g + j + 1))
            dist_prev = q + 128 - k
            dist_cur = q - k
            bp = np.where(dist_prev < 128, -slope * dist_prev, -30000.0)
            bc = np.where(dist_cur >= 0, -slope * dist_cur, -30000.0)
            pr_, sl_ = j // 2, j % 2
            o_ = C_ABIAS + (g * 2 + sl_) * 512 + pr_ * 128
            cst[:, o_:o_ + 128] = bp
            cst[:, o_ + 256:o_ + 384] = bc
    cst[:, C_TRIL:C_TRIL + 128] = (k <= q).astype(np.float32)
    srow = np.zeros((1, 2048), np.float32)
    for l in range(NL):
        for g in range(2):
            for sl in range(2):
                for pr in range(2):
                    base = (((l * 2 + g) * 2 + sl) * 2 + pr) * 128
                    srow[0, base:base + 128] = sinks[l, 4 * g + 2 * pr + sl]
    wst = np.ascontiguousarray(np.transpose(w_s, (3, 0, 1, 2)).reshape(128, NL * 8 * 128)).astype(np.float32)
    return cst, srow, wst


_NC_CACHE = {}
DBG_STOP = None


class _Stop(Exception):
    pass


_CUR = [0, 0]


def _ck(k):
    if DBG_STOP is not None and DBG_STOP == (_CUR[0], _CUR[1], k):
        raise _Stop()


def run(x, params, n_cores, layers=(0, 1)):
    B, S_, _ = x.shape
    n_seq = B // n_cores
    key = (n_seq, S_, tuple(layers))
    if key not in _NC_CACHE:
        _NC_CACHE[key] = build(n_seq, S_, layers)
    nc = _NC_CACHE[key]
    wts = pack_weights(params["w_in"], params["w_oa"], params["w_ob"], params["w_out"], params["w_up"],
                       params["w_down"])
    cst, srow, wst = pack_consts(params["mix_norm"], params["q_norm"], params["k_norm"], params["sinks"],
                                 params["sgu_norm"], params["w_s"], params["b_s"], params["ffn_norm"],
                                 params["conv_w"], params["conv_b"])
    in_maps = []
    for c in range(n_cores):
        xc = np.ascontiguousarray(np.transpose(x[c * n_seq:(c + 1) * n_seq], (0, 2, 1)))
        in_maps.append({"xT": xc, "wts": wts, "cst": cst, "srow": srow, "wst": wst})
    res = run_bass_kernel_spmd(nc, in_maps, core_ids=list(range(n_cores)))
    outs = [np.transpose(r["yT"], (0, 2, 1)) for r in res.results]
    return np.ascontiguousarray(np.concatenate(outs, axis=0)).astype(np.float32)


def kernel(**inputs):
    inputs = {k: np.asarray(v) for k, v in inputs.items()}
    x = inputs.pop("x").astype(np.float32)
    params = {k: v.astype(np.float32) for k, v in inputs.items()}
    return run(x, params, 8)
```
